# Optimizing a Trainium2 kernel written in Bass

```python
import math
import jax, jax.numpy as jnp
from jax import lax
import numpy as np

D_MODEL = 1024
BATCH = 16
SEQ = 2048
DEPTH = 2
DEC_BATCH = 32
DEC_SEQ = 1
PAST_LEN = 16384
PAGE_SIZE = 128

HEAD_DIM = 64
MIX_W = D_MODEL
GROUP_W = MIX_W // 4
N_NSA_HEADS = GROUP_W // HEAD_DIM
N_NSA_KV = 1
NSA_G = N_NSA_HEADS // N_NSA_KV
L_CMP = 32
L_SLC = 64
N_SEL = 16
WINDOW = 512
CMP_HID = 4 * HEAD_DIM
Q_BLOCK = 64
N_HG_HEADS = GROUP_W // HEAD_DIM
HG_DK = HEAD_DIM
HG_DV = HEAD_DIM
HG_CHUNK = 64
POOL_SIZES = (2, 4, 8, 16)
POOL_CH = GROUP_W // len(POOL_SIZES)
POOL_BUF = max(POOL_SIZES) - 1
N_MEM = 256
N_MEM_HEADS = 4
MEM_HD = GROUP_W // N_MEM_HEADS
ROPE_THETA = 10000.0
EPS = 1e-6
IN_WIDTHS = (N_NSA_HEADS * HEAD_DIM, 6 * N_NSA_KV * HEAD_DIM, 3 * N_NSA_HEADS,
             GROUP_W, GROUP_W, GROUP_W, GROUP_W, GROUP_W, MIX_W)
IN_W = sum(IN_WIDTHS)

kernel_name = 'hybrid_nsa_hgrn2_pool_memory_decode_step'


def rms_norm(x, g):
    xf = x.astype(jnp.float32)
    y = xf * lax.rsqrt(jnp.mean(xf * xf, axis=-1, keepdims=True) + EPS)
    return (y * g.astype(jnp.float32)).astype(x.dtype)


def rope(x, pos):
    half = x.shape[-1] // 2
    inv = ROPE_THETA ** (-jnp.arange(half, dtype=jnp.float32) / half)
    ang = pos.astype(jnp.float32)[:, None] * inv[None, :]
    cos = jnp.cos(ang)[None, :, None, :]
    sin = jnp.sin(ang)[None, :, None, :]
    xf = x.astype(jnp.float32)
    x1, x2 = xf[..., :half], xf[..., half:]
    return jnp.concatenate([x1 * cos - x2 * sin, x2 * cos + x1 * sin], axis=-1).astype(x.dtype)


def masked_softmax(s, mask, axis):
    s = jnp.where(mask, s.astype(jnp.float32), -jnp.inf)
    m = jnp.max(s, axis=axis, keepdims=True)
    m = jnp.where(jnp.isfinite(m), m, 0.0)
    p = jnp.exp(s - m)
    return p / jnp.maximum(jnp.sum(p, axis=axis, keepdims=True), 1e-30)


def compress_rows(rows, pe, w1, w2):
    B, L, Hk, d = rows.shape
    n = L // L_CMP
    blk = rows[:, :n * L_CMP].reshape(B, n, L_CMP, Hk, d) + pe[:, None, :].astype(rows.dtype)
    flat = blk.transpose(0, 1, 3, 2, 4).reshape(B, n, Hk, L_CMP * d)
    return jax.nn.silu(flat @ w1) @ w2


def nsa_mix(q, kc, vc, ks, vs, kw, vw, gates, q_pos0, kw_pos0, cmp_pe, cmp_w1, cmp_w2):
    B, T, Hk, G, d = q.shape
    L = kc.shape[1]
    scale = d ** -0.5
    tpos = q_pos0 + jnp.arange(T)
    kcc = compress_rows(kc, cmp_pe[0], cmp_w1[0], cmp_w2[0])
    vcc = compress_rows(vc, cmp_pe[1], cmp_w1[1], cmp_w2[1])
    n_cmp = kcc.shape[1]
    done = (jnp.arange(n_cmp) + 1) * L_CMP - 1 <= tpos[:, None]
    s_c = jnp.einsum('bthgd,bnhd->bthgn', q, kcc) * scale
    p_c = masked_softmax(s_c, done[None, :, None, None, :], -1)
    o_c = jnp.einsum('bthgn,bnhd->bthgd', p_c.astype(vcc.dtype), vcc)
    ratio = L_SLC // L_CMP
    n_slc = -(-L // L_SLC)
    imp = jnp.sum(p_c, axis=3)
    imp = jnp.pad(imp, ((0, 0), (0, 0), (0, 0), (0, ratio * n_slc - n_cmp)))
    imp = imp.reshape(B, T, Hk, n_slc, ratio).sum(-1)
    blk_t = (tpos // L_SLC)[:, None]
    j = jnp.arange(n_slc)[None, :]
    forced = (j == 0) | (j == blk_t) | (j == blk_t - 1)
    avail = j <= blk_t
    score = jnp.where(avail[None, :, None, :],
                      jnp.where(forced[None, :, None, :], jnp.inf, imp), -jnp.inf)
    k_sel = min(N_SEL, n_slc)
    _, idx = lax.top_k(score, k_sel)
    sel_ok = idx <= blk_t[None, :, :, None]
    pad = n_slc * L_SLC - L

    def to_sel_blocks(a):
        a = jnp.pad(a, ((0, 0), (0, pad), (0, 0), (0, 0)))
        return a.reshape(B, n_slc, L_SLC, Hk, d).transpose(0, 3, 1, 2, 4)

    ks_b, vs_b = to_sel_blocks(ks), to_sel_blocks(vs)
    kw_p = jnp.pad(kw, ((0, 0), (WINDOW, 0), (0, 0), (0, 0)))
    vw_p = jnp.pad(vw, ((0, 0), (WINDOW, 0), (0, 0), (0, 0)))
    qb = Q_BLOCK if T % Q_BLOCK == 0 else T
    nq = T // qb

    def split_q(a):
        return jnp.moveaxis(a.reshape((B, nq, qb) + a.shape[2:]), 1, 0)

    gather_blocks = jax.vmap(jax.vmap(lambda blocks, ix: blocks[ix]))

    def block_fn(args):
        qq, ii, ok, i0 = args
        tq = q_pos0 + i0 + jnp.arange(qb)
        ix = ii.transpose(0, 2, 1, 3)
        kg = gather_blocks(ks_b, ix)
        vg = gather_blocks(vs_b, ix)
        s_s = jnp.einsum('bthgd,bhtkpd->bthgkp', qq, kg) * scale
        kpos = ii[..., None] * L_SLC + jnp.arange(L_SLC)
        m_s = ok[..., None] & (kpos <= tq[None, :, None, None, None])
        p_s = masked_softmax(s_s, m_s[:, :, :, None], (-2, -1))
        o_s = jnp.einsum('bthgkp,bhtkpd->bthgd', p_s.astype(vg.dtype), vg)
        st = q_pos0 + i0 - kw_pos0
        kk = lax.dynamic_slice_in_dim(kw_p, st, WINDOW + qb, axis=1)
        vv = lax.dynamic_slice_in_dim(vw_p, st, WINDOW + qb, axis=1)
        wpos = q_pos0 + i0 - WINDOW + jnp.arange(WINDOW + qb)
        m_w = ((wpos[None, :] >= kw_pos0) & (wpos[None, :] <= tq[:, None])
               & (wpos[None, :] > tq[:, None] - WINDOW))
        s_w = jnp.einsum('bthgd,bshd->bthgs', qq, kk) * scale
        p_w = masked_softmax(s_w, m_w[None, :, None, None, :], -1)
        o_w = jnp.einsum('bthgs,bshd->bthgd', p_w.astype(vv.dtype), vv)
        return o_s, o_w

    starts = jnp.arange(nq, dtype=jnp.int32) * qb
    o_s, o_w = lax.map(block_fn, (split_q(q), split_q(idx), split_q(sel_ok), starts))
    o_s = jnp.moveaxis(o_s, 0, 1).reshape(B, T, Hk, G, d)
    o_w = jnp.moveaxis(o_w, 0, 1).reshape(B, T, Hk, G, d)
    o = gates[..., 0:1] * o_c + gates[..., 1:2] * o_s + gates[..., 2:3] * o_w
    return o.reshape(B, T, Hk * G * d)


def hgrn_scan(q, k, v, logf, S0):
    B, T, H, dk = q.shape
    C = min(HG_CHUNK, T)
    nC = -(-T // C)
    pad = nC * C - T

    def prep(a):
        a = jnp.pad(a.astype(jnp.float32), ((0, 0), (0, pad), (0, 0), (0, 0)))
        return a.reshape(B, nC, C, H, a.shape[-1]).transpose(1, 0, 3, 2, 4)

    causal = jnp.tril(jnp.ones((C, C), dtype=bool))

    def step(S, inp):
        qc, kc, vc, gc = inp
        Gc = jnp.cumsum(gc, axis=2)
        diff = Gc[:, :, :, None, :] - Gc[:, :, None, :, :]
        dec = jnp.exp(jnp.where(causal[None, None, :, :, None], diff, -jnp.inf))
        A = jnp.einsum('bhtd,bhtsd->bhts', qc, dec * kc[:, :, None, :, :])
        o = jnp.einsum('bhts,bhsv->bhtv', A, vc) + jnp.einsum('bhtd,bhdv->bhtv', qc * jnp.exp(Gc), S)
        Gl = Gc[:, :, -1:, :]
        S = jnp.exp(Gl[:, :, 0, :, None]) * S + jnp.einsum('bhsd,bhsv->bhdv', kc * jnp.exp(Gl - Gc), vc)
        return S, o

    S, o = lax.scan(step, S0.astype(jnp.float32), (prep(q), prep(k), prep(v), prep(logf)))
    o = o.transpose(1, 0, 3, 2, 4).reshape(B, nC * C, H, -1)[:, :T]
    return o, S


def pool_mix(u, buf, pos0, pool_w, pool_scale):
    B, T, C = u.shape
    seq = u if buf is None else jnp.concatenate([buf.astype(u.dtype), u], axis=1)
    off = seq.shape[1] - T
    wmax = max(POOL_SIZES)
    cs = jnp.pad(jnp.cumsum(seq.astype(jnp.float32), axis=1), ((0, 0), (wmax, 0), (0, 0)))
    idx = off + jnp.arange(T) + wmax
    pos = pos0 + jnp.arange(T)
    parts = []
    for g, wsz in enumerate(POOL_SIZES):
        c0 = g * POOL_CH
        wsum = cs[:, idx, c0:c0 + POOL_CH] - cs[:, idx - wsz, c0:c0 + POOL_CH]
        cnt = jnp.minimum(wsz, pos + 1).astype(jnp.float32)
        parts.append(wsum / cnt[None, :, None] - u[:, :, c0:c0 + POOL_CH].astype(jnp.float32))
    dlt = jnp.stack(parts, axis=2)
    out = jnp.einsum('btgc,gcd->btgd', dlt, pool_w.astype(jnp.float32)).reshape(B, T, C)
    out = out * pool_scale.astype(jnp.float32)
    return out.astype(u.dtype), seq[:, -POOL_BUF:]


def layer_forward(x, pos0, mem, past, lb, norm_g, w_in, w_out, nsa_qn, nsa_kn, cmp_pe, cmp_w1, cmp_w2,
                  hg_on, pool_w, pool_scale, mem_norm, w_mem_kv, mem_qn, mem_kn):
    B, T, _ = x.shape
    dt = x.dtype
    h = rms_norm(x, norm_g)
    proj = h @ w_in
    parts, o = [], 0
    for wd in IN_WIDTHS:
        parts.append(proj[..., o:o + wd])
        o += wd
    q_n, kv_n, g_n, hq, hf, hi, u_pool, q_m, z = parts
    pos = pos0 + jnp.arange(T)

    q = rope(rms_norm(q_n.reshape(B, T, N_NSA_HEADS, HEAD_DIM), nsa_qn), pos)
    q = q.reshape(B, T, N_NSA_KV, NSA_G, HEAD_DIM)
    kv = kv_n.reshape(B, T, 6, N_NSA_KV, HEAD_DIM)
    keys = rms_norm(kv[:, :, 0::2], nsa_kn[:, None, :])
    keys = rope(keys.reshape(B, T, 3 * N_NSA_KV, HEAD_DIM), pos).reshape(B, T, 3, N_NSA_KV, HEAD_DIM)
    vals = kv[:, :, 1::2]
    nsa_rows = jnp.stack([keys[:, :, 0], vals[:, :, 0], keys[:, :, 1], vals[:, :, 1]], axis=2)
    win_rows = jnp.stack([keys[:, :, 2], vals[:, :, 2]], axis=2)
    if past is None:
        rows_all, win_all, kw_pos0, wb = nsa_rows, win_rows, pos0, min(WINDOW, T)
    else:
        rows_all = jnp.concatenate([past['nsa'].astype(dt), nsa_rows], axis=1)
        win_all = jnp.concatenate([past['win'].astype(dt), win_rows], axis=1)
        wb = past['win'].shape[1]
        kw_pos0 = pos0 - wb
    gates = jax.nn.sigmoid(g_n.astype(jnp.float32)).astype(dt).reshape(B, T, N_NSA_KV, NSA_G, 3)
    o_nsa = nsa_mix(q, rows_all[:, :, 0], rows_all[:, :, 1], rows_all[:, :, 2], rows_all[:, :, 3],
                    win_all[:, :, 0], win_all[:, :, 1], gates, pos0, kw_pos0, cmp_pe, cmp_w1, cmp_w2)
    new_win = win_all[:, -wb:]

    fr = hf.astype(jnp.float32).reshape(B, T, N_HG_HEADS, HG_DK)
    lbh = lb.reshape(N_HG_HEADS, HG_DK)
    logf = jnp.logaddexp(jnp.log(lbh), jnp.log1p(-lbh) + jax.nn.log_sigmoid(fr))
    k_in = (1.0 - lbh) * jax.nn.sigmoid(-fr)
    S0 = jnp.zeros((B, N_HG_HEADS, HG_DK, HG_DV), jnp.float32) if past is None else past['hgrn']
    o_hg, S_new = hgrn_scan(hq.reshape(B, T, N_HG_HEADS, HG_DK), k_in,
                            hi.reshape(B, T, N_HG_HEADS, HG_DV), logf, S0)
    o_hg = rms_norm(o_hg, hg_on.reshape(N_HG_HEADS, HG_DV)).reshape(B, T, GROUP_W).astype(dt)

    o_pool, new_pool = pool_mix(u_pool, None if past is None else past['pool'], pos0, pool_w, pool_scale)

    qm = rms_norm(q_m.reshape(B, T, N_MEM_HEADS, MEM_HD), mem_qn)
    if past is None:
        mkv = (rms_norm(mem, mem_norm) @ w_mem_kv).reshape(B, N_MEM, 2, N_MEM_HEADS, MEM_HD)
        mem_kv = jnp.stack([rms_norm(mkv[:, :, 0], mem_kn), mkv[:, :, 1]], axis=2)
    else:
        mem_kv = past['mem'].astype(dt)
    s_m = jnp.einsum('bthd,bmhd->bthm', qm, mem_kv[:, :, 0]).astype(jnp.float32) * MEM_HD ** -0.5
    p_m = jax.nn.softmax(s_m, axis=-1).astype(dt)
    o_mem = jnp.einsum('bthm,bmhd->bthd', p_m, mem_kv[:, :, 1]).reshape(B, T, GROUP_W)

    y = jnp.concatenate([o_nsa, o_hg, o_pool, o_mem], axis=-1) * jax.nn.silu(z)
    x = x + y @ w_out
    return x, nsa_rows, new_win, S_new.astype(dt), new_pool, mem_kv


def setup_inputs(seed: int = 0) -> dict:
    key = jax.random.key(seed)
    ks = jax.random.split(key, 26)

    def nrm(k, shape, s=1.0):
        return jax.random.normal(k, shape, jnp.float32) * s

    n_pages = PAST_LEN // PAGE_SIZE
    n_used = DEC_BATCH * n_pages
    n_phys = n_used + n_used // 4
    wb = min(WINDOW, PAST_LEN)
    page_table = jax.random.permutation(ks[8], n_phys)[:n_used].reshape(DEC_BATCH, n_pages).astype(jnp.int32)
    return {
        'x_prompt': nrm(ks[0], (BATCH, SEQ, D_MODEL)),
        'x_sample': nrm(ks[1], (DEC_BATCH, DEC_SEQ, D_MODEL)),
        'mem_prompt': nrm(ks[2], (BATCH, N_MEM, D_MODEL)),
        'cache_nsa': nrm(ks[3], (DEPTH, n_phys, PAGE_SIZE, 4, N_NSA_KV, HEAD_DIM)),
        'cache_nsa_win': nrm(ks[4], (DEPTH, DEC_BATCH, wb, 2, N_NSA_KV, HEAD_DIM)),
        'state_hgrn': nrm(ks[5], (DEPTH, DEC_BATCH, N_HG_HEADS, HG_DK, HG_DV), 0.3),
        'state_pool': nrm(ks[6], (DEPTH, DEC_BATCH, POOL_BUF, GROUP_W)),
        'cache_mem': nrm(ks[7], (DEPTH, DEC_BATCH, N_MEM, 2, N_MEM_HEADS, MEM_HD)),
        'page_table': page_table,
        'norm_g': 1.0 + nrm(ks[9], (DEPTH, D_MODEL), 0.02),
        'w_in': nrm(ks[10], (DEPTH, D_MODEL, IN_W), D_MODEL ** -0.5),
        'w_out': nrm(ks[11], (DEPTH, MIX_W, D_MODEL), MIX_W ** -0.5),
        'nsa_qn': 1.0 + nrm(ks[12], (DEPTH, HEAD_DIM), 0.02),
        'nsa_kn': 1.0 + nrm(ks[13], (DEPTH, 3, HEAD_DIM), 0.02),
        'cmp_pe': nrm(ks[14], (DEPTH, 2, L_CMP, HEAD_DIM), 0.1),
        'cmp_w1': nrm(ks[15], (DEPTH, 2, L_CMP * HEAD_DIM, CMP_HID), (L_CMP * HEAD_DIM) ** -0.5),
        'cmp_w2': nrm(ks[16], (DEPTH, 2, CMP_HID, HEAD_DIM), CMP_HID ** -0.5),
        'hg_lb': nrm(ks[17], (DEPTH, GROUP_W), 1.0),
        'hg_on': 1.0 + nrm(ks[18], (DEPTH, GROUP_W), 0.02),
        'pool_w': nrm(ks[19], (DEPTH, len(POOL_SIZES), POOL_CH, POOL_CH), POOL_CH ** -0.5),
        'pool_scale': 1.0 + nrm(ks[20], (DEPTH, GROUP_W), 0.02),
        'mem_norm': 1.0 + nrm(ks[21], (DEPTH, D_MODEL), 0.02),
        'w_mem_kv': nrm(ks[22], (DEPTH, D_MODEL, 2 * GROUP_W), D_MODEL ** -0.5),
        'mem_qn': 1.0 + nrm(ks[23], (DEPTH, MEM_HD), 0.02),
        'mem_kn': 1.0 + nrm(ks[24], (DEPTH, MEM_HD), 0.02),
    }


def reference(x_prompt, x_sample, mem_prompt, cache_nsa, cache_nsa_win, state_hgrn, state_pool, cache_mem,
              page_table, norm_g, w_in, w_out, nsa_qn, nsa_kn, cmp_pe, cmp_w1, cmp_w2, hg_lb, hg_on,
              pool_w, pool_scale, mem_norm, w_mem_kv, mem_qn, mem_kn):
    lbs = jnp.cumsum(jax.nn.softmax(hg_lb.astype(jnp.float32), axis=0), axis=0)
    lbs = lbs - lbs[0:1]
    dec_b, n_pages = page_table.shape
    xp, xs = x_prompt, x_sample
    p_nsa, p_win, p_hg, p_pool, p_mem = [], [], [], [], []
    s_nsa, s_win, s_hg, s_pool = [], [], [], []
    for l in range(DEPTH):
        lw = (norm_g[l], w_in[l], w_out[l], nsa_qn[l], nsa_kn[l], cmp_pe[l], cmp_w1[l], cmp_w2[l],
              hg_on[l], pool_w[l], pool_scale[l], mem_norm[l], w_mem_kv[l], mem_qn[l], mem_kn[l])
        xp, a, b, c, d, e = layer_forward(xp, 0, mem_prompt, None, lbs[l], *lw)
        p_nsa.append(a); p_win.append(b); p_hg.append(c); p_pool.append(d); p_mem.append(e)
        past = {
            'nsa': cache_nsa[l][page_table].reshape(dec_b, n_pages * PAGE_SIZE, 4, N_NSA_KV, HEAD_DIM),
            'win': cache_nsa_win[l],
            'hgrn': state_hgrn[l],
            'pool': state_pool[l],
            'mem': cache_mem[l],
        }
        xs, a, b, c, d, _ = layer_forward(xs, PAST_LEN, None, past, lbs[l], *lw)
        s_nsa.append(a); s_win.append(b); s_hg.append(c); s_pool.append(d)
    return (xp, xs,
            jnp.stack(p_nsa), jnp.stack(p_win), jnp.stack(p_hg), jnp.stack(p_pool), jnp.stack(p_mem),
            jnp.stack(s_nsa), jnp.stack(s_win), jnp.stack(s_hg), jnp.stack(s_pool))
```

```python
import contextlib
import math
import numpy as np
import ml_dtypes
import concourse.bass as bass
import concourse.mybir as mybir
from concourse.bass_utils import run_bass_kernel_spmd

F32 = mybir.dt.float32
BF16 = mybir.dt.bfloat16
I32 = mybir.dt.int32
ALU = mybir.AluOpType
AF = mybir.ActivationFunctionType
AX = mybir.AxisListType

NCORES = 8
D = 1024
NW = 2956
SEQ = 2048
NT = SEQ // 128
DEPTH = 2
EPS = 1e-6
NEG = -30000.0
PB = 2
SB_ = 4
PAST = 16384
NPHYS = 5120

WPERM = [(0, 0, 320), (320, 384, 64), (384, 512, 64), (448, 1676, 256), (704, 320, 64), (768, 448, 64),
         (832, 576, 64), (896, 640, 12), (908, 652, 1024), (1932, 1932, 1024)]
C_Q, C_KC, C_KS, C_KW, C_QM, C_VC, C_VS, C_VW, C_G, C_HQ, C_HF, C_HI, C_U, C_Z = (
    0, 256, 320, 384, 448, 704, 768, 832, 896, 908, 1164, 1420, 1676, 1932)


class T:
    __slots__ = ("h", "name", "last_w", "readers", "dsem", "dcnt")

    def __init__(self, h, name):
        self.h = h
        self.name = name
        self.last_w = None
        self.readers = {}
        self.dsem = None
        self.dcnt = 0

    def __getitem__(self, k):
        return self.h[k]


class KB:
    def __init__(self, nc, stack):
        self.nc = nc
        self.stack = stack
        self.eng = {"pe": nc.tensor, "act": nc.scalar, "dve": nc.vector, "pool": nc.gpsimd, "sp": nc.sync}
        self.sem = {}
        self.cnt = {}
        self.waited = {}
        for en in self.eng:
            self.sem[en] = stack.enter_context(nc.semaphore("sem_" + en))
            self.cnt[en] = 0
            self.waited[en] = {}
        self.nd = 0
        self.out_marks = []
        self.n_inst = 0
        self.free_dsems = []

    def sb(self, name, shape, dt, stack=None):
        t = T((stack or self.stack).enter_context(self.nc.sbuf_tensor(name, list(shape), dt)), name)
        return t

    def ps(self, name, shape, dt=F32):
        return T(self.stack.enter_context(self.nc.psum_tensor(name, list(shape), dt)), name)

    def dram(self, name, shape, dt, kind=None):
        if kind is None:
            h = self.nc.dram_tensor(name, list(shape), dt)
        else:
            h = self.nc.dram_tensor(name, list(shape), dt, kind=kind)
        return T(h, name)

    def _wait(self, en, deps):
        w = self.waited[en]
        e = self.eng[en]
        best = {}
        for d in deps:
            if d is None:
                continue
            k, c = d
            if best.get(k, 0) < c:
                best[k] = c
        for k, c in best.items():
            if k == "pe" and en == "pe":
                continue
            if w.get(k, 0) < c:
                e.wait_ge(self.sem[k], c)
                w[k] = c

    def _deps(self, reads, writes):
        deps = []
        for t in reads:
            deps.append(t.last_w)
        for t in writes:
            deps.append(t.last_w)
            deps.extend(t.readers.items())
        return deps

    def _mark(self, mark, reads, writes):
        k, c = mark
        for t in writes:
            t.last_w = mark
            t.readers = {}
        for t in reads:
            if not any(t is w_ for w_ in writes):
                if t.readers.get(k, 0) < c:
                    t.readers[k] = c

    @staticmethod
    def _n(ts):
        return [getattr(t, "t", t) for t in ts]

    def op(self, en, fn, reads=(), writes=()):
        reads, writes = self._n(reads), self._n(writes)
        self._wait(en, self._deps(reads, writes))
        inst = fn(self.eng[en])
        self.cnt[en] += 1
        inst.then_inc(self.sem[en], 1)
        self._mark((en, self.cnt[en]), reads, writes)
        self.n_inst += 1
        return inst

    def mm(self, fn, reads=(), writes=(), last=True):
        reads, writes = self._n(reads), self._n(writes)
        self._wait("pe", self._deps(reads, writes))
        inst = fn(self.eng["pe"])
        self.n_inst += 1
        if last:
            self.cnt["pe"] += 1
            inst.then_inc(self.sem["pe"], 1)
            self._mark(("pe", self.cnt["pe"]), reads, writes)
        else:
            self._mark(("pe", self.cnt["pe"] + 1), reads, ())
        return inst

    def _dsem(self, t):
        if t.dsem is None:
            key = "d%d" % self.nd
            self.nd += 1
            self.sem[key] = self.stack.enter_context(self.nc.semaphore("sem_" + key))
            t.dsem = key
        return t.dsem

    def dma(self, q, out_t, out_ap, in_t, in_ap, home=None, is_output=False, **kw):
        out_t, in_t = getattr(out_t, "t", out_t), getattr(in_t, "t", in_t)
        if home is not None:
            home = getattr(home, "t", home)
        self._wait(q, self._deps([in_t], [out_t]))
        if home is None:
            home = out_t
        key = self._dsem(home)
        inst = self.eng[q].dma_start(out=out_ap, in_=in_ap, **kw)
        home.dcnt += 16
        inst.then_inc(self.sem[key], 16)
        mark = (key, home.dcnt)
        self._mark(mark, [in_t], [out_t])
        if is_output:
            self.out_marks.append(mark)
        self.n_inst += 1
        return inst

    def gather(self, out_t, out_ap, in_t, in_ap, idx_t, idx_ap):
        self._wait("pool", self._deps([in_t, idx_t], [out_t]))
        key = self._dsem(out_t)
        inst = self.nc.gpsimd.indirect_dma_start(out=out_ap, out_offset=None, in_=in_ap,
                                                 in_offset=bass.IndirectOffsetOnAxis(ap=idx_ap, axis=0))
        out_t.dcnt += 16
        inst.then_inc(self.sem[key], 16)
        self._mark((key, out_t.dcnt), [in_t, idx_t], [out_t])
        self.n_inst += 1

    def barrier(self):
        marks = [(en, c) for en, c in self.cnt.items() if c > 0]
        for en in self.eng:
            self._wait(en, marks)

    def finish(self):
        self._wait("sp", self.out_marks)
        self._wait("sp", [(en, c) for en, c in self.cnt.items() if c > 0 and en != "sp"])


def V(t, off, dims, p0=0, npart=None):
    full = t.h[:].ap
    pstep = full[0][0]
    if npart is None:
        npart = full[0][1] - p0
    return bass.AP(t.h, p0 * pstep + off, [[pstep, npart]] + [list(d) for d in dims])


def host_consts():
    c = {}
    half = 32
    inv = (10000.0 ** (-np.arange(half, dtype=np.float32) / half)).astype(np.float32)
    pos = np.arange(SEQ, dtype=np.float32)
    ang = (pos[:, None] * inv[None, :]).astype(np.float32)
    cs = np.cos(ang).astype(np.float32).reshape(NT, 128, half).transpose(1, 0, 2)
    sn = np.sin(ang).astype(np.float32).reshape(NT, 128, half).transpose(1, 0, 2)
    c["cos_p"] = np.ascontiguousarray(cs)
    c["sin_p"] = np.ascontiguousarray(sn)
    angs = (np.float32(PAST) * inv).astype(np.float32)
    c["cs_s"] = np.stack([np.cos(angs), np.sin(angs)]).astype(np.float32).reshape(1, 2, half)
    t = np.arange(SEQ)[:, None]
    n = np.arange(64)[None, :]
    done = ((n + 1) * 32 - 1) <= t
    c["cmpb"] = np.ascontiguousarray(np.where(done, 0.0, NEG).astype(np.float32).reshape(NT, 128, 64).transpose(1, 0, 2))
    j = np.arange(32)[None, :]
    blk = (t // 64)
    forced = (j == 0) | (j == blk) | (j == blk - 1)
    avail = j <= blk
    sb = np.where(avail, np.where(forced, 1e4, 0.0), -1e4).astype(np.float32)
    c["selb"] = np.ascontiguousarray(sb.reshape(NT, 128, 32).transpose(1, 0, 2))
    key = np.arange(SEQ)[None, :]
    c["e32"] = (np.arange(32)[:, None] == key // 64).astype(ml_dtypes.bfloat16)
    kk = np.arange(128)[:, None]
    tt = np.arange(128)[None, :]
    c["causb"] = np.where(kk > tt, NEG, 0.0).astype(ml_dtypes.bfloat16)
    c["edgeb"] = np.where(kk <= tt, NEG, 0.0).astype(ml_dtypes.bfloat16)
    c["identb"] = np.eye(128, dtype=np.float32).astype(ml_dtypes.bfloat16)
    c["identf"] = np.eye(128, dtype=np.float32)
    same = (kk // 64) == (tt // 64)
    c["bdmask"] = (same & (kk <= tt)).astype(np.float32)
    c["uincl"] = (same & (kk <= tt)).astype(np.float32)
    c["urest"] = (same & (kk > tt)).astype(np.float32)
    c["cmask"] = np.stack([(np.arange(128) < 64), (np.arange(128) >= 64)], 1).astype(np.float32)
    ws = (2, 4, 8, 16)
    bg = np.zeros((128, 4, 128), np.float32)
    b0 = np.zeros((128, 4, 128), np.float32)
    bp = np.zeros((128, 4, 128), np.float32)
    for g, w in enumerate(ws):
        for tq in range(128):
            for s in range(max(0, tq - w + 1), tq + 1):
                bg[s, g, tq] += 1.0 / w
                b0[s, g, tq] += 1.0 / min(w, tq + 1)
            bg[tq, g, tq] -= 1.0
            b0[tq, g, tq] -= 1.0
            for s in range(128):
                if s + 128 - 128 >= 0 and (s - 128) > tq - w:
                    bp[s, g, tq] = 1.0 / w
    c["pmod"] = np.stack([(np.arange(128) % 64) + l_ * NPHYS * 64 for l_ in range(DEPTH)], 1).astype(np.float32)
    c["nident"] = (1.0 - np.eye(128)).astype(np.float32)
    r4 = np.zeros((128, 128), np.float32)
    r4[0:4, :] = 1.0
    c["rows4"] = r4
    bs = np.zeros((60, 4, 128), np.float32)
    bself = np.zeros((128, 4, 128), np.float32)
    for g, w in enumerate(ws):
        for s_ in range(4):
            for r_ in range(15):
                if r_ >= 16 - w:
                    bs[s_ * 15 + r_, g, s_] = 1.0 / w
        bself[:, g, :] = np.eye(128) * (1.0 / w - 1.0)
    c["bs_s"] = bs.astype(ml_dtypes.bfloat16)
    c["bself"] = bself.astype(ml_dtypes.bfloat16)
    sbs = np.zeros((128, 256), np.float32)
    sbs[:, 0] = 1e4
    sbs[:, 255] = 1e4
    c["selb_s"] = sbs
    c["bandg"] = bg.astype(ml_dtypes.bfloat16)
    c["band0"] = b0.astype(ml_dtypes.bfloat16)
    c["bandp"] = bp.astype(ml_dtypes.bfloat16)
    return c


CONST_SPECS = [("cos_p", [128, NT, 32], F32), ("sin_p", [128, NT, 32], F32), ("cs_s", [1, 2, 32], F32),
               ("cmpb", [128, NT, 64], F32), ("selb", [128, NT, 32], F32), ("e32", [32, SEQ], BF16),
               ("causb", [128, 128], BF16), ("edgeb", [128, 128], BF16), ("identb", [128, 128], BF16),
               ("identf", [128, 128], F32), ("bdmask", [128, 128], F32), ("uincl", [128, 128], F32),
               ("urest", [128, 128], F32), ("cmask", [128, 2], F32), ("bandg", [128, 4, 128], BF16),
               ("band0", [128, 4, 128], BF16), ("bandp", [128, 4, 128], BF16), ("pmod", [128, 2], F32),
               ("nident", [128, 128], F32), ("rows4", [128, 128], F32), ("bs_s", [60, 4, 128], BF16),
               ("bself", [128, 4, 128], BF16), ("selb_s", [128, 256], F32)]


import os
STAGE = int(os.environ.get("KSTAGE", "99"))
NTILES_DBG = int(os.environ.get("KTILES", "16"))


SKIP_IN = set()
SUB = int(os.environ.get('KSUB', '99'))
PENG = os.environ.get('KPENG', 'dve')


def build_program(do_sample=bool(int(os.environ.get("KSAMPLE", "1")))):
    nc = bass.Bass("TRN2", target_bir_lowering=False)
    with contextlib.ExitStack() as st:
        kb = KB(nc, st)
        di = {}

        def din(name, shape, dt=F32):
            di[name] = kb.dram(name, shape, dt, kind="ExternalInput")
            return di[name]

        def dout(name, shape):
            di[name] = kb.dram(name, shape, F32, kind="ExternalOutput")
            return di[name]

        xp = din("xp", [PB, SEQ, D])
        memp = din("memp", [PB, 256, D])
        xs = din("xs", [SB_, D])
        cnsa = din("cnsa", [DEPTH * NPHYS * 64, 512]) if do_sample else None
        cwin = din("cwin", [DEPTH, SB_, 512, 128])
        shg = din("shg", [DEPTH, SB_, 4, 64, 64])
        spool = din("spool", [DEPTH, SB_, 15, 256])
        cmem = din("cmem", [DEPTH, SB_, 256, 512])
        ptb = din("ptb", [128, SB_, 64], I32)
        norm_g = din("norm_g", [DEPTH, D])
        w_in = din("w_in", [DEPTH, D, NW])
        w_out = din("w_out", [DEPTH, D, D])
        nsa_qn = din("nsa_qn", [DEPTH, 64])
        nsa_kn = din("nsa_kn", [DEPTH, 3, 64])
        cmp_pe = din("cmp_pe", [DEPTH, 2, 32, 64])
        cmp_w1 = din("cmp_w1", [DEPTH, 2, 2048, 256])
        cmp_w2 = din("cmp_w2", [DEPTH, 2, 256, 64])
        hg_lb = din("hg_lb", [DEPTH, 256])
        hg_on = din("hg_on", [DEPTH, 256])
        pool_w = din("pool_w", [DEPTH, 4, 64, 64])
        pool_scale = din("pool_scale", [DEPTH, 256])
        mem_norm = din("mem_norm", [DEPTH, D])
        w_mem_kv = din("w_mem_kv", [DEPTH, D, 512])
        mem_qn = din("mem_qn", [DEPTH, 64])
        mem_kn = din("mem_kn", [DEPTH, 64])
        cd = {}
        for name, shape, dt in CONST_SPECS:
            cd[name] = din("c_" + name, shape, dt)

        y_p = dout("y_p", [PB, SEQ, D])
        y_s = dout("y_s", [SB_, D])
        o_nsa_p = dout("o_nsa_p", [DEPTH, PB, SEQ, 256])
        o_win_p = dout("o_win_p", [DEPTH, PB, 512, 128])
        o_hg_p = dout("o_hg_p", [DEPTH, PB, 4, 64, 64])
        o_pool_p = dout("o_pool_p", [DEPTH, PB, 15, 256])
        o_mem_p = dout("o_mem_p", [DEPTH, PB, 256, 512])
        o_nsa_s = dout("o_nsa_s", [DEPTH, SB_, 256])
        o_win_s = dout("o_win_s", [DEPTH, SB_, 512, 128])
        o_hg_s = dout("o_hg_s", [DEPTH, SB_, 4, 64, 64])
        o_pool_s = dout("o_pool_s", [DEPTH, SB_, 15, 256])
        DBG = False
        xmid = kb.dram("xmid", [PB, SEQ, D], F32)
        global SKIP_IN
        SKIP_IN = set() if do_sample else {"cnsa"}

        cs = {}
        NONRES = ("cos_p", "sin_p", "cmpb", "selb", "uincl", "cs_s")
        for name, shape, dt in CONST_SPECS:
            if name in NONRES:
                continue
            cs[name] = kb.sb("k_" + name, shape, dt)
            kb.dma("sp", cs[name], cs[name][:], cd[name], cd[name][:])
        cs["uincl"] = cs["bdmask"]
        rope_t = kb.sb("rope_t", [128, 2, 32], F32)
        cmpb_t = kb.sb("cmpb_t", [128, 64], F32)
        selb_t = kb.sb("selb_t", [128, 32], F32)
        identb, identf = cs["identb"], cs["identf"]

        PF = [kb.ps("pf%d" % i, [128, 512], F32) for i in range(6)]
        PT = [kb.ps("pt%d" % i, [128, 1024], BF16) for i in range(2)]

        W_in = kb.sb("W_in", [128, 8, NW], BF16)
        W_out = kb.sb("W_out", [128, 8, D], BF16)
        W1 = kb.sb("W1", [128, 2, 16, 256], BF16)
        W2 = kb.sb("W2", [128, 2, 2, 64], BF16)
        PW = kb.sb("PW", [64, 4, 64], BF16)
        PE2 = kb.sb("PE2", [128, 2, 16], BF16)
        gcol = kb.sb("gcol", [128, 8], F32)
        mgcol = kb.sb("mgcol", [128, 8], F32)
        gain11 = kb.sb("gain11", [128, 11, 64], F32)
        mkn_b = kb.sb("mkn_b", [128, 64], F32)
        hgon_b = kb.sb("hgon_b", [128, 256], F32)
        psc_b = kb.sb("psc_b", [128, 256], F32)
        lbraw = kb.sb("lbraw", [128, 2, 256], F32)
        lb_b = kb.sb("lb_b", [128, 256], F32)
        oml_b = kb.sb("oml_b", [128, 256], F32)
        biasT = kb.sb("biasT", [128, 2, 2], F32)
        lbt = [kb.sb("lbt%d" % i, [128, 256], F32) for i in range(3)]

        def load_weights(l):
            for (d0, s0, n) in WPERM:
                kb.dma("pool", W_in, W_in[:, :, d0:d0 + n], w_in,
                       w_in[l, :, s0:s0 + n].rearrange("(kc p) n -> p kc n", p=128))
            kb.dma("pool", W_out, W_out[:], w_out, w_out[l].rearrange("(kc p) n -> p kc n", p=128))
            for a in range(2):
                kb.dma("pool", W1, W1[:, a], cmp_w1, cmp_w1[l, a].rearrange("(c p) h -> p c h", p=128))
                kb.dma("pool", W2, W2[:, a], cmp_w2, cmp_w2[l, a].rearrange("(hc p) d -> p hc d", p=128))
                for j in range(2):
                    kb.dma("pool", PE2, PE2[j * 64:(j + 1) * 64, a, :], cmp_pe,
                           cmp_pe[l, a, j::2, :].rearrange("c d -> d c"), allow_slow_non_contiguous=True)
            kb.dma("pool", PW, PW[:], pool_w, pool_w[l].rearrange("g c d -> c g d"))
            kb.dma("sp", gcol, gcol[:], norm_g, norm_g[l].rearrange("(kc p) -> p kc", p=128), allow_slow_non_contiguous=True)
            kb.dma("sp", mgcol, mgcol[:], mem_norm, mem_norm[l].rearrange("(kc p) -> p kc", p=128), allow_slow_non_contiguous=True)
            for h in range(4):
                kb.dma("sp", gain11, gain11[:, h, :], nsa_qn, bass.AP(nsa_qn.h, l * 64, [[0, 128], [1, 64]]))
                kb.dma("sp", gain11, gain11[:, 7 + h, :], mem_qn, bass.AP(mem_qn.h, l * 64, [[0, 128], [1, 64]]))
            kb.dma("sp", gain11, gain11[:, 4:7, :], nsa_kn, bass.AP(nsa_kn.h, l * 192, [[0, 128], [64, 3], [1, 64]]))
            kb.dma("sp", mkn_b, mkn_b[:], mem_kn, bass.AP(mem_kn.h, l * 64, [[0, 128], [1, 64]]))
            kb.dma("sp", hgon_b, hgon_b[:], hg_on, bass.AP(hg_on.h, l * 256, [[0, 128], [1, 256]]))
            kb.dma("sp", psc_b, psc_b[:], pool_scale, bass.AP(pool_scale.h, l * 256, [[0, 128], [1, 256]]))
            kb.dma("sp", lbraw, lbraw[:], hg_lb, bass.AP(hg_lb.h, 0, [[0, 128], [256, 2], [1, 256]]))
            if l == 0:
                kb.op("dve", lambda e: e.memset(lb_b[:], 0.0), writes=[lb_b])
                kb.op("dve", lambda e: e.memset(oml_b[:], 1.0), writes=[oml_b])
            else:
                kb.op("dve", lambda e: e.tensor_tensor(lbt[0][:], lbraw[:, 1, :], lbraw[:, 0, :], ALU.subtract),
                      reads=[lbraw], writes=[lbt[0]])
                kb.op("act", lambda e: e.activation(lb_b[:], lbt[0][:], AF.Sigmoid), reads=[lbt[0]], writes=[lb_b])
                kb.op("dve", lambda e: e.tensor_scalar(oml_b[:], lb_b[:], -1.0, 1.0, ALU.mult, ALU.add),
                      reads=[lb_b], writes=[oml_b])
            for a in range(2):
                for hc in range(2):
                    for c in range(16):
                        kb.mm(lambda e, a=a, hc=hc, c=c: e.matmul(PF[0][:, a * 2 + hc:a * 2 + hc + 1],
                                                                  W1[:, a, c, hc * 128:(hc + 1) * 128],
                                                                  PE2[:, a, c:c + 1], start=(c == 0), stop=(c == 15)),
                              reads=[W1, PE2], writes=[PF[0]], last=(c == 15))
            kb.op("act", lambda e: e.copy(biasT[:].rearrange("p a h -> p (a h)"), PF[0][:, 0:4]), reads=[PF[0]], writes=[biasT])

        x_t = kb.sb("x_t", [128, D], F32)
        xn = kb.sb("xn", [128, D], BF16)
        hT = kb.sb("hT", [128, 8, 128], BF16)
        proj = kb.sb("proj", [128, NW], F32)
        st1 = [kb.sb("st1_%d" % i, [128, 1], F32) for i in range(4)]
        s11 = [kb.sb("s11_%d" % i, [128, 11], F32) for i in range(4)]
        qkg = kb.sb("qkg", [128, 11, 64], F32)
        qkr = kb.sb("qkr", [128, 7, 64], F32)
        rows = kb.sb("rows", [128, 4, 64], F32)
        wrows = kb.sb("wrows", [128, 2, 64], F32)
        qk_bf = kb.sb("qk_bf", [128, 11, 64], BF16)
        qT_all = kb.sb("qT_all", [64, 4, 128], BF16)
        qmT = kb.sb("qmT", [64, 4, 128], BF16)
        KTW = kb.sb("KTW", [128, 2 * SEQ], BF16)
        V1 = kb.sb("V1", [128, 2, NT, 65], BF16)
        CC = kb.sb("CC", [64, 2, 64], BF16)
        stg = kb.sb("stg", [128, 2, 2, 64], BF16)
        kT2 = kb.sb("kT2", [128, 2, 64], BF16)
        hidT = kb.sb("hidT", [128, 2, 2, 4], BF16)
        pcb = kb.sb("pcb", [128, 4, 64], BF16)
        s4 = [kb.sb("s4_%d" % i, [128, 4], F32) for i in range(8)]
        pTc = kb.sb("pTc", [64, 4, 128], BF16)
        vcc = kb.sb("vcc", [64, 64], BF16)
        imp = kb.sb("imp", [128, 32], F32)
        score = kb.sb("score", [128, 32], F32)
        rank = kb.sb("rank", [128, 32], F32)
        negsel = kb.sb("negsel", [128, 64], BF16)
        kb.op("pool", lambda e: e.memset(negsel[:], 0.0), writes=[negsel])
        negselT = kb.sb("negselT", [32, 128], BF16)
        expT = [kb.sb("expT%d" % i, [128, 512], BF16) for i in range(2)]
        oT = [kb.sb("oT0", [65, 512], F32)] * 2
        gts = kb.sb("gts", [128, 12], F32)
        onsa = [kb.sb("onsa%d" % i, [128, 4, 64], F32) for i in range(4)]
        ycat = kb.sb("ycat", [128, D], F32)
        hg = [kb.sb("hg%d" % i, [128, 256], F32) for i in range(6)]
        hgb = [kb.sb("hgb%d" % i, [128, 256], BF16) for i in range(3)]
        khz = kb.sb("khz", [128, 2, 256], BF16)
        qtT = kb.sb("qtT", [64, 4, 128], BF16)
        qtTz = kb.sb("qtTz", [64, 4, 2, 128], BF16)
        ktT = kb.sb("ktT", [64, 4, 128], BF16)
        ATs = kb.sb("ATs", [128, 4, 128], BF16)
        Sst = [kb.sb("Sst%d" % i, [64, 4, 64], F32) for i in range(3)]
        Sbf = [kb.sb("Sbf%d" % i, [64, 4, 64], BF16) for i in range(2)]
        eGl = kb.sb("eGl", [64, 4, 2], F32)
        ubf = [kb.sb("ubf%d" % i, [128, 256], BF16) for i in range(2)]
        dltT = kb.sb("dltT", [64, 4, 128], BF16)
        memKT = kb.sb("memKT", [64, 4, 256], BF16)
        memV1 = kb.sb("memV1", [128, 2, 4, 65], BF16)
        sz = kb.sb("sz", [128, D], F32)
        xo = kb.sb("xo", [128, D], F32)

        class AV:
            def __init__(self, t, ap):
                self.t = t
                self.ap = ap

            def __getitem__(self, k):
                return self.ap[k]

        def alias(t, ap):
            v = AV(t, ap)
            return v
        junk, junk_t = alias(sz, sz[:]), sz
        sq11, sq11_t = alias(sz, sz[:, 0:704]), sz
        cmpt, cmpt_t = alias(sz, sz[:].rearrange("p (a b) -> p a b", a=32)), sz
        qkn, qkn_t = alias(ycat, ycat[:, 0:704].rearrange("p (h d) -> p h d", h=11)), ycat
        mrow, mrow_t = alias(xo, xo[:, 0:512].rearrange("p (a h d) -> p a h d", a=2, h=4)), xo
        mkv, mkv_t = alias(ycat, ycat[:, 0:512]), ycat
        rt = [alias(hg[i], hg[i][:, 0:224].rearrange("p (h d) -> p h d", h=7)) for i in range(4)]
        sm = [alias(hg[i], hg[i][:].rearrange("p (h d) -> p h d", h=4)) for i in range(3)]
        pcf = alias(hg[3], hg[3][:].rearrange("p (h d) -> p h d", h=4))
        eM = expT
        yg = xn
        yT = hT

        def rms_rstd(src_ssq, n, eps_t):
            a, b, c_ = eps_t
            kb.op("dve", lambda e: e.tensor_scalar(a[:], src_ssq[:], 1.0 / n, EPS, ALU.mult, ALU.add), reads=[src_ssq], writes=[a])
            kb.op("act", lambda e: e.activation(b[:], a[:], AF.Sqrt), reads=[a], writes=[b])
            kb.op("dve", lambda e: e.reciprocal(c_[:], b[:]), reads=[b], writes=[c_])
            return c_

        def norm_transpose(src_t, gc):
            kb.op("act", lambda e: e.activation(junk[:], src_t[:], AF.Square, accum_out=st1[0][:]), reads=[src_t], writes=[junk, st1[0]])
            r = rms_rstd(st1[0], D, st1[1:4])
            kb.op("dve", lambda e: e.tensor_scalar(xn[:], src_t[:], r[:], None, ALU.mult), reads=[src_t, r], writes=[xn])
            for kc in range(8):
                kb.mm(lambda e, kc=kc: e.transpose(PT[0][:, kc * 128:(kc + 1) * 128], xn[:, kc * 128:(kc + 1) * 128], identb[:]),
                      reads=[xn, identb], writes=[PT[0]], last=(kc == 7))
            kb.op("dve", lambda e: e.tensor_tensor(hT[:], PT[0][:].rearrange("p (k t) -> p k t", k=8),
                                                   V(gc, 0, [[1, 8], [0, 128]]), ALU.mult), reads=[PT[0], gc], writes=[hT])

        evac_flip = [0]

        def evac(dst_t, dst_ap, src_t, src_ap):
            evac_flip[0] ^= 1
            if evac_flip[0] or os.environ.get("KEVAC", "act") == "act":
                kb.op("act", lambda e: e.copy(dst_ap, src_ap), reads=[src_t], writes=[dst_t])
            else:
                kb.op("dve", lambda e: e.tensor_copy(dst_ap, src_ap), reads=[src_t], writes=[dst_t])

        kT3 = kb.sb("kT3", [64, 3, 128], BF16)
        qT_pad = kb.sb("qT_pad", [64, 128], BF16)
        Gs = [khz, ATs]
        Gv = [khz[:].rearrange("p a b -> p (a b)"), ATs[:].rearrange("p a b -> p (a b)")]
        stg1 = stg
        hidc = kb.sb("hidc", [128, 2, 2, 128], BF16)
        kccT_1 = alias(ktT, ktT[:].rearrange("p h t -> p (h t)"))
        vcc_1 = pcb
        idx_l = kb.sb("idx_l", [128, SB_, 64], I32)
        ptb_sb = kb.sb("ptb_sb", [128, SB_, 64], I32)
        Ef = kb.sb("Ef", [128, 128], F32)
        ETs = [kb.sb("ETs%d" % i_, [128, 128], BF16) for i_ in range(2)]
        Wt = alias(hgb[1], hgb[1][:].rearrange("p (k d) -> p k d", k=4))
        KwT = pTc
        V1w = kb.sb("V1w", [128, 4, 65], BF16)
        V1n = kb.sb("V1n", [128, 2, 65], BF16)
        Mt = kb.sb("Mt", [128, 2, 256], BF16)
        spb = kb.sb("spb", [60, 256], BF16)
        pT_s = alias(expT[0], expT[0][:].rearrange("p (c t) -> p c t", c=4))
        m8a = kb.sb("m8a", [128, 16], F32)
        m8b = kb.sb("m8b", [128, 16], F32)
        selk = kb.sb("selk", [128, 64], F32)
        sh_f = alias(onsa[2], onsa[2][:].rearrange("p h d -> p (h d)"))
        sh_k = alias(onsa[3], onsa[3][:].rearrange("p h d -> p (h d)"))
        fkT = kb.sb("fkT", [64, 2, 4, 4], F32)
        ETn = kb.sb("ETn", [128, 32], BF16)
        den4 = [kb.sb("den4_%d" % i_, [128, 4], F32) for i_ in range(3)]
        xsmid = kb.dram("xsmid", [SB_, D], F32)

        def acc(k, par):
            if k < 4:
                t = x_t if par == 0 else xo
                return t, t[:, k * 256:(k + 1) * 256]
            return sz, sz[:, par * 256:(par + 1) * 256]

        def sample_init():
            kb.op("pool", lambda e: e.memset(x_t[:], 0.0), writes=[x_t])
            kb.dma("sp", ptb_sb, ptb_sb[:], ptb, ptb[:])
            for l_ in range(DEPTH):
                pass

        def sample_idx(l):
            kb.op("dve", lambda e: e.tensor_scalar(idx_l[:], ptb_sb[:], 64.0, cs["pmod"][:, l:l + 1], ALU.mult, ALU.add),
                  reads=[ptb_sb, cs["pmod"]], writes=[idx_l])

        def sample_pass1(l, s):
            KC2 = KTW[:].rearrange("p (a m) -> p a m", a=2)
            if True:
                for ch in range(4):
                    for ppl in range(16):
                        pp = ch * 16 + ppl
                        g = Gs[pp % 2]
                        gq = Gv[pp % 2]
                        kb.gather(g, gq, cnsa, cnsa[:, :], idx_l, idx_l[:, s, pp:pp + 1])
                        gv = gq.rearrange("p (j r d) -> p r j d", j=2, r=4)
                        kb.op("dve", lambda e, gv=gv: e.tensor_copy(stg1[:], gv[:, 0:2, :, :]), reads=[g], writes=[stg1])
                        for a in range(2):
                            kb.mm(lambda e, a=a: e.transpose(PT[0][:, a * 128:(a + 1) * 128], stg1[:, a].rearrange("p j d -> p (j d)"), identb[:]),
                                  reads=[stg1, identb], writes=[PT[0]], last=(a == 1))
                        kb.op("dve", lambda e, ppl=ppl: e.tensor_copy(KC2[:, :, ppl * 128:(ppl + 1) * 128],
                                                                      PT[0][:, 0:256].rearrange("p (a m) -> p a m", a=2)),
                              reads=[PT[0]], writes=[KTW])
                    for a in range(2):
                        for hc in range(2):
                            pb_ = PF[(a * 2 + hc) % 2]
                            for c in range(16):
                                kb.mm(lambda e, a=a, hc=hc, c=c, pb_=pb_: e.matmul(pb_[:, 0:128], W1[:, a, c, hc * 128:(hc + 1) * 128],
                                                                                   V(KTW, a * 2048 + c, [[16, 128]]), start=(c == 0), stop=(c == 15)),
                                      reads=[W1, KTW], writes=[pb_], last=(c == 15))
                            kb.op("act", lambda e, a=a, hc=hc, pb_=pb_: e.activation(hidc[:, a, hc, :], pb_[:, 0:128], AF.Silu, bias=biasT[:, a, hc:hc + 1]),
                                  reads=[pb_, biasT], writes=[hidc])
                    for hc in range(2):
                        kb.mm(lambda e, hc=hc: e.matmul(PF[4][0:64, 0:128], W2[:, 0, hc, :], hidc[:, 0, hc, :], start=(hc == 0), stop=(hc == 1)),
                              reads=[W2, hidc], writes=[PF[4]], last=(hc == 1))
                    kb.op("dve", lambda e, s=s, ch=ch: e.tensor_copy(kccT_1[:, ch * 128:(ch + 1) * 128], PF[4][0:64, 0:128]), reads=[PF[4]], writes=[kccT_1])
                    for hc in range(2):
                        kb.mm(lambda e, hc=hc: e.matmul(PF[2][:, 0:64], hidc[:, 1, hc, :], W2[:, 1, hc, :], start=(hc == 0), stop=(hc == 1)),
                              reads=[W2, hidc], writes=[PF[2]], last=(hc == 1))
                    kb.op("dve", lambda e, s=s, ch=ch: e.tensor_copy(vcc_1[:, ch, :], PF[2][:, 0:64]), reads=[PF[2]], writes=[vcc_1])

        def nsa_branch_finish(s, k):
            kb.op("pool", lambda e: e.memset(oT[0][:], 0.0), writes=[oT[0]])
            kb.op("dve", lambda e: e.tensor_copy(V(oT[0], s, [[128, 4]], npart=65), PF[5][0:65, 0:4]), reads=[PF[5]], writes=[oT[0]])
            for h in range(4):
                kb.mm(lambda e, h=h: e.transpose(PF[2][:, h * 65:(h + 1) * 65], oT[0][:, h * 128:(h + 1) * 128], identf[0:65, 0:65]),
                      reads=[oT[0], identf], writes=[PF[2]], last=(h == 3))
            pv = PF[2][:, 0:260].rearrange("p (h d) -> p h d", h=4)
            kb.op("dve", lambda e: e.tensor_scalar(den4[0][:], pv[:, :, 64], cs["nident"][:, s:s + 1], None, ALU.add),
                  reads=[PF[2], cs["nident"]], writes=[den4[0]])
            kb.op("dve", lambda e: e.reciprocal(den4[1][:], den4[0][:]), reads=[den4[0]], writes=[den4[1]])
            kb.op("dve", lambda e: e.tensor_tensor(onsa[0][:], pv[:, :, 0:64], V(den4[1], 0, [[1, 4], [0, 64]]), ALU.mult),
                  reads=[PF[2], den4[1]], writes=[onsa[0]])
            to, ao = acc(k, s % 2)
            tn, an = acc(k, (s + 1) % 2)
            kb.op("dve", lambda e: e.tensor_tensor(an, ao, onsa[0][:].rearrange("p h d -> p (h d)"), ALU.add), reads=[to, onsa[0]], writes=[tn])

        def sample_mixers(l):
            for k in range(5):
                t0_, a0_ = acc(k, 0)
                kb.op("pool", lambda e, a0_=a0_: e.memset(a0_, 0.0), writes=[t0_])
            kb.op("pool", lambda e: e.memset(qT_pad[:], 0.0), writes=[qT_pad])
            kb.op("pool", lambda e: e.memset(V1n[:], 1.0), writes=[V1n])
            kb.op("dve", lambda e: e.tensor_copy(V1n[:, :, 0:64], proj[:, C_VS:C_VS + 128].rearrange("p (a d) -> p a d", a=2)), reads=[proj], writes=[V1n])
            kb.op("act", lambda e: e.activation(hg[0][:], proj[:, C_HF:C_HF + 256], AF.Sigmoid), reads=[proj], writes=[hg[0]])
            kb.op("dve", lambda e: e.tensor_tensor(hg[5][:], hg[0][:], oml_b[:], ALU.mult), reads=[hg[0], oml_b], writes=[hg[5]])
            kb.op("dve", lambda e: e.tensor_tensor(sh_f[:], hg[5][:], lb_b[:], ALU.add), reads=[hg[5], lb_b], writes=[sh_f])
            kb.op("dve", lambda e: e.tensor_scalar(sh_k[:], sh_f[:], -1.0, 1.0, ALU.mult, ALU.add), reads=[sh_f], writes=[sh_k])
            for ti, tsrc in enumerate([sh_f, sh_k]):
                for h in range(4):
                    kb.mm(lambda e, h=h, tsrc=tsrc: e.transpose(PF[4][0:64, h * 128:(h + 1) * 128], tsrc[:, h * 64:(h + 1) * 64], identf[:]),
                          reads=[tsrc, identf], writes=[PF[4]], last=(h == 3))
                kb.op("dve", lambda e, ti=ti: e.tensor_copy(fkT[:, ti], PF[4][0:64, :].rearrange("p (h t) -> p h t", h=4)[:, :, 0:4]), reads=[PF[4]], writes=[fkT])
            kb.op("dve", lambda e: e.tensor_copy(hgb[0][:], proj[:, C_HQ:C_HQ + 256]), reads=[proj], writes=[hgb[0]])
            for h in range(4):
                kb.mm(lambda e, h=h: e.transpose(PT[1][0:64, h * 128:(h + 1) * 128], hgb[0][:, h * 64:(h + 1) * 64], identb[:]),
                      reads=[hgb[0], identb], writes=[PT[1]], last=(h == 3))
            kb.op("dve", lambda e: e.tensor_copy(qtT[:], PT[1][0:64, 0:512].rearrange("p (h t) -> p h t", h=4)), reads=[PT[1]], writes=[qtT])

            sm_s = ycat[:, 0:512]
            p_s = ycat[:, 512:1024]
            pb_s = xn[:, 0:512]
            for s in range(SB_):
                oh = identf[:, s:s + 1]
                sample_pass1(l, s)
                kb.op("dve", lambda e, s=s: e.tensor_copy(qT_pad[:, 0:4], V(qT_all, s, [[128, 4]])), reads=[qT_all], writes=[qT_pad])
                kb.mm(lambda e, s=s: e.matmul(PF[3][:], qT_pad[:], kccT_1[:], start=True, stop=True), reads=[qT_pad, kccT_1], writes=[PF[3]])
                kb.op("dve", lambda e: e.tensor_reduce(st1[0][:], PF[3][:], AX.X, ALU.max), reads=[PF[3]], writes=[st1[0]])
                kb.op("dve", lambda e: e.tensor_scalar(sm_s, PF[3][:], st1[0][:], None, ALU.subtract), reads=[PF[3], st1[0]], writes=[ycat])
                kb.op("act", lambda e: e.activation(p_s, sm_s, AF.Exp, accum_out=st1[1][:]), reads=[ycat], writes=[ycat, st1[1]])
                kb.op("dve", lambda e: e.reciprocal(st1[2][:], st1[1][:]), reads=[st1[1]], writes=[st1[2]])
                kb.op("dve", lambda e: e.tensor_scalar(sm_s, p_s, st1[2][:], None, ALU.mult), reads=[ycat, st1[2]], writes=[ycat])
                kb.op("dve", lambda e: e.tensor_copy(pb_s, sm_s), reads=[ycat], writes=[xn])
                pv2 = sm_s.rearrange("p (j r) -> p j r", r=2)
                kb.op("dve", lambda e: e.tensor_tensor(hg[4][:], pv2[:, :, 0], pv2[:, :, 1], ALU.add), reads=[ycat], writes=[hg[4]])
                kb.mm(lambda e: e.matmul(PF[0][:, 0:256], cs["rows4"][:], hg[4][:], start=True, stop=True), reads=[cs["rows4"], hg[4]], writes=[PF[0]])
                kb.op("dve", lambda e: e.tensor_tensor(hg[1][:], PF[0][:, 0:256], cs["selb_s"][:], ALU.add), reads=[PF[0], cs["selb_s"]], writes=[hg[1]])
                kb.op("dve", lambda e: e.max(m8a[:, 0:8], hg[1][:]), reads=[hg[1]], writes=[m8a])
                kb.op("dve", lambda e: e.match_replace(hg[2][:], m8a[:, 0:8], hg[1][:], -1e9), reads=[hg[1], m8a], writes=[hg[2]])
                kb.op("dve", lambda e: e.max(m8b[:, 0:8], hg[2][:]), reads=[hg[2]], writes=[m8b])
                kb.op("dve", lambda e: e.tensor_scalar(hg[3][:], hg[1][:], m8b[:, 6:7], None, ALU.is_ge), reads=[hg[1], m8b], writes=[hg[3]])
                for g4 in range(4):
                    kb.op("dve", lambda e, g4=g4: e.tensor_copy(selk[32 * g4:32 * (g4 + 1), :], V(hg[3], g4, [[4, 64]], p0=32 * g4, npart=32)),
                          reads=[hg[3]], writes=[selk])
                for c in range(4):
                    kb.mm(lambda e, c=c: e.transpose(PT[0][:, c * 128:(c + 1) * 128], pb_s[:, c * 128:(c + 1) * 128], identb[:]),
                          reads=[xn, identb], writes=[PT[0]], last=(c == 3))
                kb.op("dve", lambda e: e.tensor_copy(pT_s[:], PT[0][:, 0:512].rearrange("p (c t) -> p c t", c=4)), reads=[PT[0]], writes=[pT_s])
                for c in range(4):
                    kb.mm(lambda e, c=c, s=s: e.matmul(PF[4][0:64, 0:4], vcc_1[:, c, :], pT_s[:, c, 0:4], start=(c == 0), stop=(c == 3)),
                          reads=[vcc_1, pT_s], writes=[PF[4]], last=(c == 3))
                kb.op("pool", lambda e: e.memset(oT[0][:], 0.0), writes=[oT[0]])
                kb.op("dve", lambda e, s=s: e.tensor_copy(V(oT[0], s, [[128, 4]], npart=64), PF[4][0:64, 0:4]), reads=[PF[4]], writes=[oT[0]])
                for h in range(4):
                    kb.mm(lambda e, h=h: e.transpose(PF[3][:, h * 64:(h + 1) * 64], oT[0][0:64, h * 128:(h + 1) * 128], identf[0:64, 0:64]),
                          reads=[oT[0], identf], writes=[PF[3]], last=(h == 3))
                to, ao = acc(0, s % 2)
                tn, an = acc(0, (s + 1) % 2)
                kb.op("dve", lambda e, ao=ao, an=an: e.tensor_tensor(an, ao, PF[3][:, 0:256], ALU.add), reads=[to, PF[3]], writes=[tn])
                kb.op("pool", lambda e: e.memset(V1[:], 1.0), writes=[V1])
                V1c = V1[:].rearrange("p a t d -> p (a t) d")
                KsT = KTW[0:64, :].rearrange("p (j m) -> p j m", j=2)
                first = True
                for ch in range(4):
                    for ppl in range(16):
                        pp = ch * 16 + ppl
                        g = Gs[pp % 2]
                        gq = Gv[pp % 2]
                        kb.gather(g, gq, cnsa, cnsa[:, :], idx_l, idx_l[:, s, pp:pp + 1])
                        for j in range(2):
                            kb.mm(lambda e, j=j, gq=gq: e.transpose(PT[1][0:64, j * 128:(j + 1) * 128], gq[:, j * 256 + 128:j * 256 + 192], identb[:]),
                                  reads=[g, identb], writes=[PT[1]], last=(j == 1))
                        kb.op("dve", lambda e, ppl=ppl: e.tensor_copy(KsT[:, :, ppl * 128:(ppl + 1) * 128],
                                                                      PT[1][0:64, 0:256].rearrange("p (j t) -> p j t", j=2)), reads=[PT[1]], writes=[KTW])
                        g4v = gq.rearrange("p (j r d) -> p j r d", j=2, r=4)
                        kb.op("dve", lambda e, ppl=ppl, g4v=g4v: e.tensor_copy(V1c[:, ppl * 2:ppl * 2 + 2, 0:64], g4v[:, :, 3, :]), reads=[g], writes=[V1])
                    psc = PF[ch % 2]
                    for t_ in range(32):
                        ppl, j = divmod(t_, 2)
                        kb.mm(lambda e, t_=t_, ppl=ppl, j=j, psc=psc: e.matmul(psc[:, t_ * 4:(t_ + 1) * 4], KsT[:, j, ppl * 128:(ppl + 1) * 128], qT_pad[:, 0:4],
                                                                            start=True, stop=True), reads=[KTW, qT_pad], writes=[psc], last=(t_ == 31))
                    kb.op("act", lambda e, psc=psc: e.activation(Ef[:], psc[:, 0:128], AF.Exp), reads=[psc], writes=[Ef])
                    et = ETs[ch % 2]
                    kb.op("dve", lambda e, et=et, ch=ch: e.tensor_tensor(et[:].rearrange("p (a b) -> p a b", b=8), Ef[:].rearrange("p (a b) -> p a b", b=8),
                                                                       V(selk, ch * 16, [[1, 16], [0, 8]]), ALU.mult), reads=[Ef, selk], writes=[et])
                    for t_ in range(32):
                        kb.mm(lambda e, t_=t_, et=et, first=first: e.matmul(PF[5][0:65, 0:4], V1c[:, t_, :], et[:, t_ * 4:(t_ + 1) * 4],
                                                                          start=(first and t_ == 0), stop=False), reads=[V1, et], writes=[PF[5]], last=False)
                    first = False
                kb.mm(lambda e: e.matmul(PF[3][:, 0:4], kT3[:, 1, :], qT_pad[:, 0:4], start=True, stop=True), reads=[kT3, qT_pad], writes=[PF[3]])
                kb.op("act", lambda e: e.activation(Ef[:, 0:4], PF[3][:, 0:4], AF.Exp), reads=[PF[3]], writes=[Ef])
                kb.op("dve", lambda e, oh=oh: e.tensor_scalar(ETn[:, 0:4], Ef[:, 0:4], oh, None, ALU.mult), reads=[Ef, identf], writes=[ETn])
                kb.mm(lambda e: e.matmul(PF[5][0:65, 0:4], V1n[:, 0, :], ETn[:, 0:4], start=False, stop=True), reads=[V1n, ETn], writes=[PF[5]])
                nsa_branch_finish(s, 1)
                kb.dma("pool", Wt, Wt[:], cwin, cwin[l, s].rearrange("(kt p) c -> p kt c", p=128)[:, :, 0:64])
                for kt in range(4):
                    kb.mm(lambda e, kt=kt: e.transpose(PT[1][0:64, kt * 128:(kt + 1) * 128], Wt[:, kt, :], identb[:]),
                          reads=[Wt, identb], writes=[PT[1]], last=(kt == 3))
                kb.op("dve", lambda e: e.tensor_copy(KwT[:], PT[1][0:64, 0:512].rearrange("p (k t) -> p k t", k=4)), reads=[PT[1]], writes=[KwT])
                kb.op("pool", lambda e: e.memset(V1w[:], 1.0), writes=[V1w])
                kb.dma("pool", V1w, V1w[:, :, 0:64], cwin, cwin[l, s].rearrange("(kt p) c -> p kt c", p=128)[:, :, 64:128])
                kb.op("pool", lambda e: e.memset(V1w[0:1, 0, :], 0.0), writes=[V1w])
                for kt in range(4):
                    kb.mm(lambda e, kt=kt: e.matmul(PF[3][:, kt * 4:(kt + 1) * 4], KwT[:, kt, :], qT_pad[:, 0:4], start=True, stop=True),
                          reads=[KwT, qT_pad], writes=[PF[3]], last=False)
                kb.mm(lambda e: e.matmul(PF[3][:, 16:20], kT3[:, 2, :], qT_pad[:, 0:4], start=True, stop=True), reads=[kT3, qT_pad], writes=[PF[3]])
                kb.op("act", lambda e: e.activation(Ef[:, 0:20], PF[3][:, 0:20], AF.Exp), reads=[PF[3]], writes=[Ef])
                kb.op("dve", lambda e: e.tensor_copy(ETn[:, 0:16], Ef[:, 0:16]), reads=[Ef], writes=[ETn])
                kb.op("dve", lambda e, oh=oh: e.tensor_scalar(ETn[:, 16:20], Ef[:, 16:20], oh, None, ALU.mult), reads=[Ef, identf], writes=[ETn])
                for kt in range(4):
                    kb.mm(lambda e, kt=kt: e.matmul(PF[5][0:65, 0:4], V1w[:, kt, :], ETn[:, kt * 4:(kt + 1) * 4], start=(kt == 0), stop=False),
                          reads=[V1w, ETn], writes=[PF[5]], last=False)
                kb.mm(lambda e: e.matmul(PF[5][0:65, 0:4], V1n[:, 1, :], ETn[:, 16:20], start=False, stop=True), reads=[V1n, ETn], writes=[PF[5]])
                nsa_branch_finish(s, 2)
                kb.dma("sp", Sst[0], Sst[0][:], shg, shg[l, s].rearrange("h k v -> k h v"))
                kb.mm(lambda e, s=s: e.matmul(PF[4][0:64, 0:256], V(identf, s, [[0, 64]]), proj[:, C_HI:C_HI + 256], start=True, stop=True),
                      reads=[identf, proj], writes=[PF[4]])
                kb.op("dve", lambda e, s=s: e.tensor_tensor(Sst[1][:], Sst[0][:], V(fkT, s, [[4, 4], [0, 64]]), ALU.mult), reads=[Sst[0], fkT], writes=[Sst[1]])
                kb.op("dve", lambda e, s=s: e.tensor_tensor(Sst[2][:], PF[4][0:64, 0:256].rearrange("p (h d) -> p h d", h=4),
                                                           V(fkT, 16 + s, [[4, 4], [0, 64]]), ALU.mult), reads=[PF[4], fkT], writes=[Sst[2]])
                kb.op("dve", lambda e: e.tensor_tensor(Sst[0][:], Sst[1][:], Sst[2][:], ALU.add), reads=[Sst[1], Sst[2]], writes=[Sst[0]])
                kb.dma("sp", o_hg_s, o_hg_s[l, s].rearrange("h k v -> k h v"), Sst[0], Sst[0][:], home=Sst[0], is_output=True)
                kb.op("act", lambda e: e.copy(Sbf[0][:], Sst[0][:]), reads=[Sst[0]], writes=[Sbf[0]])
                for h in range(4):
                    kb.mm(lambda e, h=h: e.matmul(PF[2][:, h * 64:(h + 1) * 64], qtT[:, h, :], Sbf[0][:, h, :], start=True, stop=True),
                          reads=[qtT, Sbf[0]], writes=[PF[2]], last=(h == 3))
                to, ao = acc(3, s % 2)
                tn, an = acc(3, (s + 1) % 2)
                kb.op("dve", lambda e, ao=ao, an=an, oh=oh: e.scalar_tensor_tensor(an, PF[2][:, 0:256], oh, ao, ALU.mult, ALU.add),
                      reads=[PF[2], identf, to], writes=[tn])
                kb.dma("pool", Mt, Mt[:], cmem, cmem[l, s].rearrange("(mc p) c -> p mc c", p=128)[:, :, 0:256])
                for mc in range(2):
                    for h in range(4):
                        kb.mm(lambda e, mc=mc, h=h: e.transpose(PT[1][0:64, (mc * 4 + h) * 128:(mc * 4 + h + 1) * 128], Mt[:, mc, h * 64:(h + 1) * 64], identb[:]),
                              reads=[Mt, identb], writes=[PT[1]], last=(mc == 1 and h == 3))
                kb.op("dve", lambda e: e.tensor_copy(memKT[:].rearrange("p h (mc t) -> p mc h t", mc=2),
                                                     PT[1][0:64, :].rearrange("p (mc h t) -> p mc h t", mc=2, h=4)), reads=[PT[1]], writes=[memKT])
                kb.op("pool", lambda e: e.memset(memV1[:], 1.0), writes=[memV1])
                for mc_ in range(2):
                    kb.dma("pool", memV1, memV1[:, mc_, :, 0:64], cmem,
                           cmem[l, s, mc_ * 128:(mc_ + 1) * 128, 256:512].rearrange("p (h d) -> p h d", h=4))
                for mc in range(2):
                    for h in range(4):
                        kb.mm(lambda e, mc=mc, h=h: e.matmul(PF[mc][:, h * 128:(h + 1) * 128], memKT[:, h, mc * 128:(mc + 1) * 128], qmT[:, h, :],
                                                             start=True, stop=True), reads=[memKT, qmT], writes=[PF[mc]], last=(h == 3))
                    kb.op("act", lambda e, mc=mc: e.activation(eM[mc][:], PF[mc][:], AF.Exp), reads=[PF[mc]], writes=[eM[mc]])
                for h in range(4):
                    for mc in range(2):
                        kb.mm(lambda e, mc=mc, h=h: e.matmul(PF[3][:, h * 65:(h + 1) * 65], eM[mc][:, h * 128:(h + 1) * 128], memV1[:, mc, h, :],
                                                             start=(mc == 0), stop=(mc == 1)), reads=[eM[mc], memV1], writes=[PF[3]], last=(mc == 1 and h == 3))
                pvm = PF[3][:, 0:260].rearrange("p (h d) -> p h d", h=4)
                kb.op("dve", lambda e: e.reciprocal(den4[2][:], pvm[:, :, 64]), reads=[PF[3]], writes=[den4[2]])
                kb.op("dve", lambda e: e.tensor_tensor(onsa[1][:], pvm[:, :, 0:64], V(den4[2], 0, [[1, 4], [0, 64]]), ALU.mult), reads=[PF[3], den4[2]], writes=[onsa[1]])
                to, ao = acc(4, s % 2)
                tn, an = acc(4, (s + 1) % 2)
                kb.op("dve", lambda e, ao=ao, an=an, oh=oh: e.scalar_tensor_tensor(an, onsa[1][:].rearrange("p h d -> p (h d)"), oh, ao, ALU.mult, ALU.add),
                      reads=[onsa[1], identf, to], writes=[tn])
            t0, aC = acc(0, 0)
            _, aS = acc(1, 0)
            _, aW = acc(2, 0)
            _, aH = acc(3, 0)
            tm, aM = acc(4, 0)
            kb.op("act", lambda e: e.activation(gts[:], proj[:, C_G:C_G + 12], AF.Sigmoid), reads=[proj], writes=[gts])
            for bi, ab in enumerate([aC, aS, aW]):
                kb.op("dve", lambda e, bi=bi, ab=ab: e.tensor_tensor(onsa[bi][:], ab.rearrange("p (h d) -> p h d", h=4), V(gts, bi, [[3, 4], [0, 64]]), ALU.mult),
                      reads=[t0, gts], writes=[onsa[bi]])
            kb.op("dve", lambda e: e.tensor_tensor(onsa[3][:], onsa[0][:], onsa[1][:], ALU.add), reads=[onsa[0], onsa[1]], writes=[onsa[3]])
            kb.op("dve", lambda e: e.tensor_tensor(ycat[:, 0:256].rearrange("p (h d) -> p h d", h=4), onsa[3][:], onsa[2][:], ALU.add),
                  reads=[onsa[3], onsa[2]], writes=[ycat])
            kb.op("dve", lambda e: e.tensor_tensor(hg[1][:], aH, aH, ALU.mult), reads=[t0], writes=[hg[1]])
            kb.op("dve", lambda e: e.tensor_reduce(s4[0][:], hg[1][:].rearrange("p (h d) -> p h d", h=4), AX.X, ALU.add), reads=[hg[1]], writes=[s4[0]])
            r_ = rms_rstd(s4[0], 64, s4[1:4])
            kb.op("dve", lambda e: e.tensor_tensor(hg[5][:].rearrange("p (h d) -> p h d", h=4), aH.rearrange("p (h d) -> p h d", h=4),
                                                   V(r_, 0, [[1, 4], [0, 64]]), ALU.mult), reads=[t0, r_], writes=[hg[5]])
            kb.op("dve", lambda e: e.tensor_tensor(ycat[:, 256:512], hg[5][:], hgon_b[:], ALU.mult), reads=[hg[5], hgon_b], writes=[ycat])
            kb.op("dve", lambda e: e.tensor_copy(ycat[:, 768:1024], aM), reads=[tm], writes=[ycat])
            kb.dma("pool", spb, spb[:], spool, spool[l].rearrange("s r c -> (s r) c"))
            kb.op("act", lambda e: e.copy(ubf[0][:], proj[:, C_U:C_U + 256]), reads=[proj], writes=[ubf[0]])
            for g in range(4):
                kb.mm(lambda e, g=g: e.matmul(PF[4][0:64, g * 128:(g + 1) * 128], spb[:, g * 64:(g + 1) * 64], cs["bs_s"][:, g, :], start=True, stop=False),
                      reads=[spb, cs["bs_s"]], writes=[PF[4]], last=False)
                kb.mm(lambda e, g=g: e.matmul(PF[4][0:64, g * 128:(g + 1) * 128], ubf[0][:, g * 64:(g + 1) * 64], cs["bself"][:, g, :], start=False, stop=True),
                      reads=[ubf[0], cs["bself"]], writes=[PF[4]], last=(g == 3))
            kb.op("act", lambda e: e.copy(dltT[:], PF[4][0:64, :].rearrange("p (g t) -> p g t", g=4)), reads=[PF[4]], writes=[dltT])
            for g in range(4):
                kb.mm(lambda e, g=g: e.matmul(PF[1][:, g * 64:(g + 1) * 64], dltT[:, g, :], PW[:, g, :], start=True, stop=True),
                      reads=[dltT, PW], writes=[PF[1]], last=(g == 3))
            kb.op("dve", lambda e: e.tensor_tensor(ycat[:, 512:768], PF[1][:, 0:256], psc_b[:], ALU.mult), reads=[PF[1], psc_b], writes=[ycat])

        def prompt_mem(l, b):
            for mc in range(2):
                kb.dma("sp", x_t, x_t[:], memp, memp[b, mc * 128:(mc + 1) * 128, :])
                norm_transpose(x_t, mgcol)
                for kc in range(8):
                    kb.mm(lambda e, kc=kc: e.matmul(PF[0][:], hT[:, kc, :], KTW[:, kc * 512:(kc + 1) * 512], start=(kc == 0), stop=(kc == 7)),
                          reads=[hT, KTW], writes=[PF[0]], last=(kc == 7))
                kb.op("act", lambda e: e.copy(mkv[:], PF[0][:]), reads=[PF[0]], writes=[mkv])
                kb.op("dve", lambda e: e.tensor_tensor(hg[0][:], mkv[:, 0:256], mkv[:, 0:256], ALU.mult), reads=[mkv], writes=[hg[0]])
                kb.op("dve", lambda e: e.tensor_reduce(s4[0][:], hg[0][:].rearrange("p (h d) -> p h d", h=4), AX.X, ALU.add),
                      reads=[hg[0]], writes=[s4[0]])
                r = rms_rstd(s4[0], 64, s4[1:4])
                kb.op("dve", lambda e: e.tensor_tensor(hg[1][:].rearrange("p (h d) -> p h d", h=4),
                                                       mkv[:, 0:256].rearrange("p (h d) -> p h d", h=4),
                                                       V(r, 0, [[1, 4], [0, 64]]), ALU.mult), reads=[mkv, r], writes=[hg[1]])
                kb.op("dve", lambda e: e.tensor_tensor(mrow[:, 0], hg[1][:].rearrange("p (h d) -> p h d", h=4),
                                                       V(mkn_b, 0, [[0, 4], [1, 64]]), ALU.mult), reads=[hg[1], mkn_b], writes=[mrow])
                kb.op("pool", lambda e: e.tensor_copy(mrow[:, 1], mkv[:, 256:512].rearrange("p (h d) -> p h d", h=4)),
                      reads=[mkv], writes=[mrow])
                kb.dma("sp", o_mem_p, o_mem_p[l, b, mc * 128:(mc + 1) * 128, :], mrow, mrow[:].rearrange("p a h d -> p (a h d)"),
                       home=mrow, is_output=True)
                kb.op("act", lambda e: e.copy(hgb[0][:].rearrange("p (h d) -> p h d", h=4), mrow[:, 0]), reads=[mrow], writes=[hgb[0]])
                kb.op("dve", lambda e, mc=mc: e.tensor_copy(memV1[:, mc, :, 0:64], mrow[:, 1]), reads=[mrow], writes=[memV1])
                for h in range(4):
                    kb.mm(lambda e, h=h: e.transpose(PT[1][0:64, h * 128:(h + 1) * 128], hgb[0][:, h * 64:(h + 1) * 64], identb[:]),
                          reads=[hgb[0], identb], writes=[PT[1]], last=(h == 3))
                kb.op("act", lambda e, mc=mc: e.copy(memKT[:, :, mc * 128:(mc + 1) * 128],
                                                     PT[1][0:64, 0:512].rearrange("p (h t) -> p h t", h=4)),
                      reads=[PT[1]], writes=[memKT])

        def prompt_tile(l, b, i, smp=False):
            src = xp if l == 0 else xmid
            dst = xmid if l == 0 else y_p
            DBG2 = int(os.environ.get("KDBG2", "99"))
            xin = x_t
            xs_src = xs if l == 0 else xsmid
            if smp:
                kb.dma("sp", x_t, x_t[0:SB_, :], xs_src, xs_src[:, :])
            else:
                kb.dma("sp", x_t, x_t[:], src, src[b, i * 128:(i + 1) * 128, :])
            if i >= 1 and DBG2 < 1:
                return
            norm_transpose(xin, gcol)
            if i >= 1 and DBG2 < 2:
                return
            for n0 in range(0, NW, 512):
                if i >= 1 and DBG2 < 3 + n0 // 512:
                    return
                n1 = min(NW, n0 + 512)
                pb = PF[(n0 // 512) % 2]
                for kc in range(8):
                    kb.mm(lambda e, kc=kc, pb=pb, n0=n0, n1=n1: e.matmul(pb[:, 0:n1 - n0], hT[:, kc, :], W_in[:, kc, n0:n1],
                                                                        start=(kc == 0), stop=(kc == 7)),
                          reads=[hT, W_in], writes=[pb], last=(kc == 7))
                evac(proj, proj[:, n0:n1], pb, pb[:, 0:n1 - n0])
            if STAGE < 3:
                return
            kb.op("dve", lambda e: e.tensor_tensor(sq11[:], proj[:, 0:704], proj[:, 0:704], ALU.mult), reads=[proj], writes=[sq11])
            kb.op("dve", lambda e: e.tensor_reduce(s11[0][:], sq11[:].rearrange("p (h d) -> p h d", h=11), AX.X, ALU.add),
                  reads=[sq11], writes=[s11[0]])
            r = rms_rstd(s11[0], 64, s11[1:4])
            kb.op("dve", lambda e: e.tensor_tensor(qkn[:], proj[:, 0:704].rearrange("p (h d) -> p h d", h=11),
                                                   V(r, 0, [[1, 11], [0, 64]]), ALU.mult), reads=[proj, r], writes=[qkn])
            kb.op(PENG, lambda e: e.tensor_tensor(qkg[:], qkn[:], gain11[:], ALU.mult), reads=[qkn, gain11], writes=[qkg])
            if SUB < 1:
                return
            if smp:
                kb.dma("sp", rope_t, rope_t[:], cd["cs_s"], bass.AP(cd["cs_s"].h, 0, [[0, 128], [32, 2], [1, 32]]))
            else:
                kb.dma("sp", rope_t, rope_t[:, 0, :], cd["cos_p"], cd["cos_p"][:, i, :])
                kb.dma("sp", rope_t, rope_t[:, 1, :], cd["sin_p"], cd["sin_p"][:, i, :])
                kb.dma("sp", cmpb_t, cmpb_t[:], cd["cmpb"], cd["cmpb"][:, i, :])
                kb.dma("sp", selb_t, selb_t[:], cd["selb"], cd["selb"][:, i, :])
            cosb = V(rope_t, 0, [[0, 7], [1, 32]])
            sinb = V(rope_t, 32, [[0, 7], [1, 32]])
            x1 = qkg[:, 0:7, 0:32]
            x2 = qkg[:, 0:7, 32:64]
            kb.op("dve", lambda e: e.tensor_tensor(rt[0][:], x1, cosb, ALU.mult), reads=[qkg, rope_t], writes=[rt[0]])
            kb.op(PENG, lambda e: e.tensor_tensor(rt[1][:], x2, sinb, ALU.mult), reads=[qkg, rope_t], writes=[rt[1]])
            kb.op("dve", lambda e: e.tensor_tensor(qkr[:, :, 0:32], rt[0][:], rt[1][:], ALU.subtract), reads=[rt[0], rt[1]], writes=[qkr])
            kb.op(PENG, lambda e: e.tensor_tensor(rt[2][:], x2, cosb, ALU.mult), reads=[qkg, rope_t], writes=[rt[2]])
            kb.op("dve", lambda e: e.tensor_tensor(rt[3][:], x1, sinb, ALU.mult), reads=[qkg, rope_t], writes=[rt[3]])
            kb.op(PENG, lambda e: e.tensor_tensor(qkr[:, :, 32:64], rt[2][:], rt[3][:], ALU.add), reads=[rt[2], rt[3]], writes=[qkr])
            if SUB < 2:
                return
            kb.op("pool", lambda e: e.tensor_copy(rows[:, 0:4:2, :], qkr[:, 4:6, :]), reads=[qkr], writes=[rows])
            kb.op("pool", lambda e: e.tensor_copy(rows[:, 1:4:2, :], proj[:, C_VC:C_VC + 128].rearrange("p (a d) -> p a d", a=2)),
                  reads=[proj], writes=[rows])
            if smp:
                kb.dma("sp", o_nsa_s, o_nsa_s[l], rows, rows[0:SB_].rearrange("p a d -> p (a d)"), home=rows, is_output=True)
                kb.op("pool", lambda e: e.tensor_copy(wrows[:, 0, :], qkr[:, 6, :]), reads=[qkr], writes=[wrows])
                kb.op("pool", lambda e: e.tensor_copy(wrows[:, 1, :], proj[:, C_VW:C_VW + 64]), reads=[proj], writes=[wrows])
                for s in range(SB_):
                    kb.dma("sp", o_win_s, o_win_s[l, s, 0:511, :], cwin, cwin[l, s, 1:512, :], home=wrows, is_output=True)
                    kb.dma("sp", o_win_s, o_win_s[l, s, 511:512, :], wrows, wrows[s:s + 1].rearrange("p a d -> p (a d)"), home=wrows, is_output=True)
                    kb.dma("sp", o_pool_s, o_pool_s[l, s, 0:14, :], spool, spool[l, s, 1:15, :], home=wrows, is_output=True)
                    kb.dma("sp", o_pool_s, o_pool_s[l, s, 14:15, :], proj, proj[s:s + 1, C_U:C_U + 256], home=proj, is_output=True)
            else:
                kb.dma("sp", o_nsa_p, o_nsa_p[l, b, i * 128:(i + 1) * 128, :], rows, rows[:].rearrange("p a d -> p (a d)"),
                       home=rows, is_output=True)
            if (not smp) and i >= NT - 4:
                kb.op("pool", lambda e: e.tensor_copy(wrows[:, 0, :], qkr[:, 6, :]), reads=[qkr], writes=[wrows])
                kb.op("pool", lambda e: e.tensor_copy(wrows[:, 1, :], proj[:, C_VW:C_VW + 64]), reads=[proj], writes=[wrows])
                kb.dma("sp", o_win_p, o_win_p[l, b, (i - (NT - 4)) * 128:(i - (NT - 4) + 1) * 128, :], wrows,
                       wrows[:].rearrange("p a d -> p (a d)"), home=wrows, is_output=True)
            if (not smp) and i == NT - 1:
                kb.dma("sp", o_pool_p, o_pool_p[l, b], proj, proj[113:128, C_U:C_U + 256], home=proj, is_output=True)
            if SUB < 3:
                return
            kb.op("dve", lambda e: e.tensor_scalar(qk_bf[:, 0:4, :], qkr[:, 0:4, :], 0.125, None, ALU.mult), reads=[qkr], writes=[qk_bf])
            kb.op("dve", lambda e: e.tensor_copy(qk_bf[:, 4:7, :], qkr[:, 4:7, :]), reads=[qkr], writes=[qk_bf])
            kb.op("dve", lambda e: e.tensor_scalar(qk_bf[:, 7:11, :], qkg[:, 7:11, :], 0.125, None, ALU.mult), reads=[qkg], writes=[qk_bf])
            if SUB < 4:
                return
            qk2d = qk_bf[:].rearrange('p h d -> p (h d)')
            for h in range(7):
                kb.mm(lambda e, h=h: e.transpose(PT[1][0:64, h * 128:(h + 1) * 128], qk2d[:, h * 64:(h + 1) * 64], identb[:]),
                      reads=[qk_bf, identb], writes=[PT[1]], last=(h == 6))
            if SUB < 5:
                return
            kb.op("dve", lambda e: e.tensor_copy(qT_all[:], PT[1][0:64, 0:512].rearrange("p (h t) -> p h t", h=4)), reads=[PT[1]], writes=[qT_all])
            if SUB < 6:
                return
            if smp:
                kb.op("dve", lambda e: e.tensor_copy(kT3[:], PT[1][0:64, 512:896].rearrange("p (h t) -> p h t", h=3)), reads=[PT[1]], writes=[kT3])
            else:
                kb.op("dve", lambda e: e.tensor_copy(KTW[0:64, :].rearrange("p (a s) -> p a s", a=2)[:, :, i * 128:(i + 1) * 128],
                                                     PT[1][0:64, 640:896].rearrange("p (h t) -> p h t", h=2)), reads=[PT[1]], writes=[KTW])
            if SUB < 7:
                return
            for h in range(4):
                kb.mm(lambda e, h=h: e.transpose(PT[1][0:64, h * 128:(h + 1) * 128], qk2d[:, (7 + h) * 64:(8 + h) * 64], identb[:]),
                      reads=[qk_bf, identb], writes=[PT[1]], last=(h == 3))
            kb.op("dve", lambda e: e.tensor_copy(qmT[:], PT[1][0:64, 0:512].rearrange("p (h t) -> p h t", h=4)), reads=[PT[1]], writes=[qmT])
            if smp:
                sample_mixers(l)
                kb.dma("sp", x_t, x_t[0:SB_, :], xs_src, xs_src[:, :])
            if not smp:
                kb.op("dve", lambda e: e.tensor_copy(V1[:, :, i, 0:64], proj[:, C_VS:C_VS + 128].rearrange("p (a d) -> p a d", a=2)),
                      reads=[proj], writes=[V1])
                if STAGE < 4:
                    return
                kb.op("dve", lambda e: e.tensor_copy(stg[:, 0], V(qkr, 4 * 64, [[0, 2], [1, 64]])), reads=[qkr], writes=[stg])
                kb.op("pool", lambda e: e.tensor_copy(stg[:, 1], V(proj, C_VC, [[0, 2], [1, 64]])), reads=[proj], writes=[stg])
                for a in range(2):
                    kb.mm(lambda e, a=a: e.transpose(PT[0][:, a * 128:(a + 1) * 128], stg[:, a].rearrange("p j d -> p (j d)"), identb[:]),
                          reads=[stg, identb], writes=[PT[0]], last=(a == 1))
                ptv = PT[0][:, 0:256].rearrange("p (a m j) -> p a m j", a=2, j=2)
                kb.op("dve", lambda e: e.tensor_copy(kT2[0:64], ptv[0:64, :, :, 0]), reads=[PT[0]], writes=[kT2])
                kb.op("dve", lambda e: e.tensor_copy(kT2[64:128], ptv[64:128, :, :, 1]), reads=[PT[0]], writes=[kT2])
                for a in range(2):
                    for hc in range(2):
                        for c in range(16):
                            kb.mm(lambda e, a=a, hc=hc, c=c: e.matmul(PF[2][:, (a * 2 + hc) * 4:(a * 2 + hc) * 4 + 4],
                                                                      W1[:, a, c, hc * 128:(hc + 1) * 128],
                                                                      V(kT2, a * 64 + c, [[16, 4]]), start=(c == 0), stop=(c == 15)),
                                  reads=[W1, kT2], writes=[PF[2]], last=(c == 15))
                for a in range(2):
                    for hc in range(2):
                        kb.op("act", lambda e, a=a, hc=hc: e.activation(hidT[:, a, hc, :], PF[2][:, (a * 2 + hc) * 4:(a * 2 + hc) * 4 + 4],
                                                                        AF.Silu, bias=biasT[:, a, hc:hc + 1]),
                              reads=[PF[2], biasT], writes=[hidT])
                for a in range(2):
                    for hc in range(2):
                        kb.mm(lambda e, a=a, hc=hc: e.matmul(PF[4][0:64, a * 4:a * 4 + 4], W2[:, a, hc, :], hidT[:, a, hc, :],
                                                             start=(hc == 0), stop=(hc == 1)),
                              reads=[W2, hidT], writes=[PF[4]], last=(hc == 1))
                kb.op("dve", lambda e: e.tensor_copy(CC[:, :, 4 * i:4 * i + 4], PF[4][0:64, 0:8].rearrange("p (a n) -> p a n", a=2)),
                      reads=[PF[4]], writes=[CC])
                if STAGE < 5:
                    return
                for h in range(4):
                    kb.mm(lambda e, h=h: e.matmul(PF[3][:, h * 64:(h + 1) * 64], qT_all[:, h, :], CC[:, 0, :], start=True, stop=True),
                          reads=[qT_all, CC], writes=[PF[3]], last=(h == 3))
                kb.op("dve", lambda e: e.tensor_tensor(sm[0][:], PF[3][:, 0:256].rearrange("p (h n) -> p h n", h=4),
                                                       V(cmpb_t, 0, [[0, 4], [1, 64]]), ALU.add), reads=[PF[3], cmpb_t], writes=[sm[0]])
                kb.op("dve", lambda e: e.tensor_reduce(s4[0][:], sm[0][:], AX.X, ALU.max), reads=[sm[0]], writes=[s4[0]])
                kb.op("dve", lambda e: e.tensor_scalar(s4[1][:], s4[0][:], -1000.0, None, ALU.max), reads=[s4[0]], writes=[s4[1]])
                kb.op("dve", lambda e: e.tensor_tensor(sm[1][:], sm[0][:], V(s4[1], 0, [[1, 4], [0, 64]]), ALU.subtract),
                      reads=[sm[0], s4[1]], writes=[sm[1]])
                kb.op("act", lambda e: e.activation(sm[2][:], sm[1][:], AF.Exp), reads=[sm[1]], writes=[sm[2]])
                kb.op("dve", lambda e: e.tensor_reduce(s4[2][:], sm[2][:], AX.X, ALU.add), reads=[sm[2]], writes=[s4[2]])
                kb.op("dve", lambda e: e.tensor_scalar(s4[3][:], s4[2][:], 1e-30, None, ALU.max), reads=[s4[2]], writes=[s4[3]])
                kb.op("dve", lambda e: e.reciprocal(s4[4][:], s4[3][:]), reads=[s4[3]], writes=[s4[4]])
                kb.op("dve", lambda e: e.tensor_tensor(pcf[:], sm[2][:], V(s4[4], 0, [[1, 4], [0, 64]]), ALU.mult), reads=[sm[2], s4[4]], writes=[pcf])
                kb.op("act", lambda e: e.copy(pcb[:], pcf[:]), reads=[pcf], writes=[pcb])
                for h in range(4):
                    kb.mm(lambda e, h=h: e.transpose(PT[1][0:64, h * 128:(h + 1) * 128], pcb[:, h, :], identb[:]),
                          reads=[pcb, identb], writes=[PT[1]], last=False)
                kb.mm(lambda e: e.transpose(PT[1][0:64, 512:576], CC[:, 1, :], identb[0:64, 0:64]), reads=[CC, identb], writes=[PT[1]])
                kb.op("dve", lambda e: e.tensor_copy(pTc[:], PT[1][0:64, 0:512].rearrange("p (h t) -> p h t", h=4)), reads=[PT[1]], writes=[pTc])
                kb.op("dve", lambda e: e.tensor_copy(vcc[:], PT[1][0:64, 512:576]), reads=[PT[1]], writes=[vcc])
                for h in range(4):
                    kb.mm(lambda e, h=h: e.matmul(PF[3][:, 256 + h * 64:256 + (h + 1) * 64], pTc[:, h, :], vcc[:], start=True, stop=True),
                          reads=[pTc, vcc], writes=[PF[3]], last=(h == 3))
                kb.op("act", lambda e: e.activation(gts[:], proj[:, C_G:C_G + 12], AF.Sigmoid), reads=[proj], writes=[gts])
                kb.op("dve", lambda e: e.tensor_tensor(onsa[0][:], PF[3][:, 256:512].rearrange("p (h d) -> p h d", h=4),
                                                       V(gts, 0, [[3, 4], [0, 64]]), ALU.mult), reads=[PF[3], gts], writes=[onsa[0]])
                kb.op("dve", lambda e: e.tensor_reduce(imp[:], V(hg[3], 0, [[2, 32], [64, 4], [1, 2]]), AX.XY, ALU.add), reads=[pcf], writes=[imp])
                kb.op("dve", lambda e: e.tensor_tensor(score[:], imp[:], selb_t[:], ALU.add), reads=[imp, selb_t], writes=[score])
                kb.op("dve", lambda e: e.tensor_tensor(cmpt[:], V(score, 0, [[0, 32], [1, 32]]), V(score, 0, [[1, 32], [0, 32]]), ALU.is_gt),
                      reads=[score], writes=[cmpt])
                kb.op("dve", lambda e: e.tensor_reduce(rank[:], cmpt[:], AX.X, ALU.add), reads=[cmpt], writes=[rank])
                kb.op("dve", lambda e: e.tensor_scalar(negsel[:, 0:32], rank[:], 15.5, NEG, ALU.is_ge, ALU.mult), reads=[rank], writes=[negsel])
                kb.mm(lambda e: e.transpose(PT[1][0:64, 0:128], negsel[:], identb[:]), reads=[negsel, identb], writes=[PT[1]])
                kb.op("dve", lambda e: e.tensor_copy(negselT[:], PT[1][0:32, 0:128]), reads=[PT[1]], writes=[negselT])
                if STAGE < 6:
                    return
                qflat = qT_all[:].rearrange("p h t -> p (h t)")

                def attn(branch, kts, cache_idx, v_idx, pacc, ot):
                    nk = len(kts)
                    for n_, kt in enumerate(kts):
                        ps_ = PF[4 + (n_ % 2)] if False else PF[2 + (n_ % 2)]
                        ex = expT[n_ % 2]
                        extra = []
                        if branch == "s":
                            extra.append(("sel", None))
                        if kt == i:
                            extra.append(("mask", cs["causb"]))
                        if branch == "w" and kt == i - 4:
                            extra.append(("mask", cs["edgeb"]))
                        kb.mm(lambda e, kt=kt, ps_=ps_: e.matmul(ps_[:], KTW[0:64, cache_idx * SEQ + kt * 128:cache_idx * SEQ + (kt + 1) * 128], qflat,
                                                                start=True, stop=(len(extra) == 0)),
                              reads=[KTW, qT_all], writes=[ps_], last=(len(extra) == 0))
                        for xi, (kind, mt) in enumerate(extra):
                            lastx = xi == len(extra) - 1
                            if kind == "sel":
                                kb.mm(lambda e, kt=kt, ps_=ps_, lastx=lastx: e.matmul(
                                    ps_[:].rearrange("p (h t) -> p h t", h=4), cs["e32"][:, kt * 128:(kt + 1) * 128],
                                    V(negselT, 0, [[0, 4], [1, 128]]), start=False, stop=lastx),
                                    reads=[cs["e32"], negselT], writes=[ps_], last=lastx)
                            else:
                                kb.mm(lambda e, mt=mt, ps_=ps_, lastx=lastx: e.matmul(
                                    ps_[:].rearrange("p (h t) -> p h t", h=4), identb[:], V(mt, 0, [[0, 4], [1, 128]]),
                                    start=False, stop=lastx), reads=[identb, mt], writes=[ps_], last=lastx)
                        kb.op("act", lambda e, ps_=ps_, ex=ex: e.activation(ex[:], ps_[:], AF.Exp), reads=[ps_], writes=[ex])
                        kb.mm(lambda e, kt=kt, ex=ex, n_=n_: e.matmul(pacc[0:65, :], V1[:, v_idx, kt, :], ex[:], start=(n_ == 0), stop=(n_ == nk - 1)),
                              reads=[V1, ex], writes=[pacc], last=(n_ == nk - 1))
                    kb.op("act", lambda e: e.copy(ot[:], pacc[0:65, :]), reads=[pacc], writes=[ot])

                for bi, (br, kts, cidx) in enumerate([("s", list(range(0, i + 1)), 0), ("w", list(range(max(0, i - 4), i + 1)), 1)]):
                    attn(br, kts, cidx, bi, PF[5], oT[bi])
                    pback = PF[2 + bi]
                    for h in range(4):
                        kb.mm(lambda e, bi=bi, h=h, pback=pback: e.transpose(pback[:, h * 65:(h + 1) * 65], oT[bi][:, h * 128:(h + 1) * 128], identf[0:65, 0:65]),
                              reads=[oT[bi], identf], writes=[pback], last=(h == 3))
                    pv = pback[:, 0:260].rearrange("p (h d) -> p h d", h=4)
                    kb.op("dve", lambda e, pv=pv, bi=bi: e.reciprocal(s4[5 + bi][:], pv[:, :, 64]), reads=[pback], writes=[s4[5 + bi]])
                    kb.op("dve", lambda e, bi=bi: e.tensor_tensor(s4[bi][:], s4[5 + bi][:], V(gts, 1 + bi, [[3, 4]]), ALU.mult),
                          reads=[s4[5 + bi], gts], writes=[s4[bi]])
                    kb.op("dve", lambda e, pv=pv, bi=bi: e.tensor_tensor(onsa[1 + bi][:], pv[:, :, 0:64], V(s4[bi], 0, [[1, 4], [0, 64]]), ALU.mult),
                          reads=[pback, s4[bi]], writes=[onsa[1 + bi]])
                kb.op(PENG, lambda e: e.tensor_tensor(onsa[3][:], onsa[0][:], onsa[1][:], ALU.add), reads=[onsa[0], onsa[1]], writes=[onsa[3]])
                kb.op(PENG, lambda e: e.tensor_tensor(ycat[:, 0:256].rearrange("p (h d) -> p h d", h=4), onsa[3][:], onsa[2][:], ALU.add),
                      reads=[onsa[3], onsa[2]], writes=[ycat])

                if STAGE < 7:
                    return
                kb.op("act", lambda e: e.activation(hg[0][:], proj[:, C_HF:C_HF + 256], AF.Sigmoid), reads=[proj], writes=[hg[0]])
                kb.op("dve", lambda e: e.tensor_tensor(hg[1][:], hg[0][:], oml_b[:], ALU.mult), reads=[hg[0], oml_b], writes=[hg[1]])
                kb.op("dve", lambda e: e.tensor_tensor(hg[2][:], hg[1][:], lb_b[:], ALU.add), reads=[hg[1], lb_b], writes=[hg[2]])
                kb.op("act", lambda e: e.activation(hg[3][:], hg[2][:], AF.Ln), reads=[hg[2]], writes=[hg[3]])
                kb.op("dve", lambda e: e.tensor_scalar(hg[4][:], hg[2][:], -1.0, 1.0, ALU.mult, ALU.add), reads=[hg[2]], writes=[hg[4]])
                kb.mm(lambda e: e.matmul(PF[0][:, 0:256], cs["uincl"][:], hg[3][:], start=True, stop=True), reads=[cs["uincl"], hg[3]], writes=[PF[0]], last=False)
                kb.mm(lambda e: e.matmul(PF[0][:, 256:512], cs["urest"][:], hg[3][:], start=True, stop=True), reads=[cs["urest"], hg[3]], writes=[PF[0]])
                for h in range(4):
                    kb.mm(lambda e, h=h: e.matmul(PF[4][0:64, h * 2:h * 2 + 2], hg[3][:, h * 64:(h + 1) * 64], cs["cmask"][:], start=True, stop=True),
                          reads=[hg[3], cs["cmask"]], writes=[PF[4]], last=(h == 3))
                kb.op("act", lambda e: e.activation(eGl[:].rearrange("p h c -> p (h c)"), PF[4][0:64, 0:8], AF.Exp), reads=[PF[4]], writes=[eGl])
                kb.op("act", lambda e: e.activation(hg[0][:], PF[0][:, 0:256], AF.Exp), reads=[PF[0]], writes=[hg[0]])
                kb.op("act", lambda e: e.activation(hg[1][:], PF[0][:, 0:256], AF.Exp, scale=-1.0), reads=[PF[0]], writes=[hg[1]])
                kb.op("act", lambda e: e.activation(hg[5][:], PF[0][:, 256:512], AF.Exp), reads=[PF[0]], writes=[hg[5]])
                kb.op("dve", lambda e: e.tensor_tensor(hgb[0][:], proj[:, C_HQ:C_HQ + 256], hg[0][:], ALU.mult), reads=[proj, hg[0]], writes=[hgb[0]])
                kb.op(PENG, lambda e: e.tensor_tensor(hgb[1][:], hg[4][:], hg[1][:], ALU.mult), reads=[hg[4], hg[1]], writes=[hgb[1]])
                for c in range(2):
                    kb.op("dve", lambda e, c=c: e.scalar_tensor_tensor(khz[:, c, :], hg[4][:], cs["cmask"][:, c:c + 1], hg[5][:], ALU.mult, ALU.mult),
                          reads=[hg[4], hg[5], cs["cmask"]], writes=[khz])
                kb.op("act", lambda e: e.copy(hgb[2][:], proj[:, C_HI:C_HI + 256]), reads=[proj], writes=[hgb[2]])
                for h in range(4):
                    kb.mm(lambda e, h=h: e.transpose(PT[1][0:64, h * 128:(h + 1) * 128], hgb[0][:, h * 64:(h + 1) * 64], identb[:]),
                          reads=[hgb[0], identb], writes=[PT[1]], last=False)
                for h in range(4):
                    kb.mm(lambda e, h=h: e.transpose(PT[1][0:64, 512 + h * 128:512 + (h + 1) * 128], hgb[1][:, h * 64:(h + 1) * 64], identb[:]),
                          reads=[hgb[1], identb], writes=[PT[1]], last=(h == 3))
                pq = PT[1][0:64, 0:512].rearrange("p (h t) -> p h t", h=4)
                kb.op("dve", lambda e: e.tensor_copy(qtT[:], pq), reads=[PT[1]], writes=[qtT])
                kb.op("dve", lambda e: e.tensor_copy(qtTz[:, :, 0, 0:64], pq[:, :, 0:64]), reads=[PT[1]], writes=[qtTz])
                kb.op("dve", lambda e: e.tensor_copy(qtTz[:, :, 1, 64:128], pq[:, :, 64:128]), reads=[PT[1]], writes=[qtTz])
                kb.op("dve", lambda e: e.tensor_copy(ktT[:], PT[1][0:64, 512:1024].rearrange("p (h t) -> p h t", h=4)), reads=[PT[1]], writes=[ktT])
                for h in range(4):
                    kb.mm(lambda e, h=h: e.matmul(PF[0][:, h * 128:(h + 1) * 128], ktT[:, h, :], qtT[:, h, :], start=True, stop=True),
                          reads=[ktT, qtT], writes=[PF[0]], last=(h == 3))
                kb.op("dve", lambda e: e.tensor_tensor(ATs[:], PF[0][:].rearrange("p (h t) -> p h t", h=4), V(cs["bdmask"], 0, [[0, 4], [1, 128]]), ALU.mult),
                      reads=[PF[0], cs["bdmask"]], writes=[ATs])
                kb.op("act", lambda e: e.copy(Sbf[0][:], Sst[0][:]), reads=[Sst[0]], writes=[Sbf[0]])
                for c in range(2):
                    for h in range(4):
                        kb.mm(lambda e, c=c, h=h: e.matmul(PF[4][0:64, 64 + h * 64:64 + (h + 1) * 64],
                                                           khz[:, c, h * 64:(h + 1) * 64], hgb[2][:, h * 64:(h + 1) * 64], start=True, stop=True),
                              reads=[khz, hgb[2]], writes=[PF[4]], last=(h == 3))
                    kb.op("dve", lambda e, c=c: e.tensor_tensor(Sst[2][:], Sst[c][:], V(eGl, c, [[2, 4], [0, 64]]), ALU.mult),
                          reads=[Sst[c], eGl], writes=[Sst[2]])
                    dstS = Sst[1] if c == 0 else Sst[0]
                    kb.op("dve", lambda e, dstS=dstS: e.tensor_tensor(dstS[:], Sst[2][:], PF[4][0:64, 64:320].rearrange("p (h d) -> p h d", h=4), ALU.add),
                          reads=[Sst[2], PF[4]], writes=[dstS])
                    if c == 0:
                        kb.op("act", lambda e: e.copy(Sbf[1][:], Sst[1][:]), reads=[Sst[1]], writes=[Sbf[1]])
                for h in range(4):
                    kb.mm(lambda e, h=h: e.matmul(PF[2][:, h * 64:(h + 1) * 64], ATs[:, h, :], hgb[2][:, h * 64:(h + 1) * 64],
                                                  start=True, stop=False), reads=[ATs, hgb[2]], writes=[PF[2]], last=False)
                    kb.mm(lambda e, h=h: e.matmul(PF[2][:, h * 64:(h + 1) * 64], qtTz[:, h, 0, :], Sbf[0][:, h, :], start=False, stop=False),
                          reads=[qtTz, Sbf[0]], writes=[PF[2]], last=False)
                    kb.mm(lambda e, h=h: e.matmul(PF[2][:, h * 64:(h + 1) * 64], qtTz[:, h, 1, :], Sbf[1][:, h, :], start=False, stop=True),
                          reads=[qtTz, Sbf[1]], writes=[PF[2]], last=(h == 3))
                if i == NT - 1:
                    kb.dma("sp", o_hg_p, o_hg_p[l, b].rearrange("h k v -> k h v"), Sst[0], Sst[0][:], home=Sst[0], is_output=True)
                kb.op("act", lambda e: e.copy(hg[0][:], PF[2][:, 0:256]), reads=[PF[2]], writes=[hg[0]])
                kb.op("dve", lambda e: e.tensor_tensor(hg[1][:], hg[0][:], hg[0][:], ALU.mult), reads=[hg[0]], writes=[hg[1]])
                kb.op("dve", lambda e: e.tensor_reduce(s4[0][:], hg[1][:].rearrange("p (h d) -> p h d", h=4), AX.X, ALU.add), reads=[hg[1]], writes=[s4[0]])
                r = rms_rstd(s4[0], 64, s4[1:4])
                kb.op("dve", lambda e: e.tensor_tensor(hg[5][:].rearrange("p (h d) -> p h d", h=4), hg[0][:].rearrange("p (h d) -> p h d", h=4),
                                                       V(r, 0, [[1, 4], [0, 64]]), ALU.mult), reads=[hg[0], r], writes=[hg[5]])
                kb.op(PENG, lambda e: e.tensor_tensor(ycat[:, 256:512], hg[5][:], hgon_b[:], ALU.mult), reads=[hg[5], hgon_b], writes=[ycat])

                if STAGE < 8:
                    return
                ucur = ubf[i % 2]
                uprev = ubf[(i + 1) % 2]
                kb.op("act", lambda e: e.copy(ucur[:], proj[:, C_U:C_U + 256]), reads=[proj], writes=[ucur])
                bc_ = cs["band0"] if i == 0 else cs["bandg"]
                for g in range(4):
                    kb.mm(lambda e, g=g: e.matmul(PF[4][0:64, g * 128:(g + 1) * 128], ucur[:, g * 64:(g + 1) * 64], bc_[:, g, :], start=True, stop=(i == 0)),
                          reads=[ucur, bc_], writes=[PF[4]], last=(i == 0 and g == 3))
                    if i > 0:
                        kb.mm(lambda e, g=g: e.matmul(PF[4][0:64, g * 128:(g + 1) * 128], uprev[:, g * 64:(g + 1) * 64], cs["bandp"][:, g, :], start=False, stop=True),
                              reads=[uprev, cs["bandp"]], writes=[PF[4]], last=(g == 3))
                kb.op("act", lambda e: e.copy(dltT[:], PF[4][0:64, :].rearrange("p (g t) -> p g t", g=4)), reads=[PF[4]], writes=[dltT])
                for g in range(4):
                    kb.mm(lambda e, g=g: e.matmul(PF[1][:, g * 64:(g + 1) * 64], dltT[:, g, :], PW[:, g, :], start=True, stop=True),
                          reads=[dltT, PW], writes=[PF[1]], last=(g == 3))
                kb.op("dve", lambda e: e.tensor_tensor(ycat[:, 512:768], PF[1][:, 0:256], psc_b[:], ALU.mult), reads=[PF[1], psc_b], writes=[ycat])

                for mc in range(2):
                    for h in range(4):
                        kb.mm(lambda e, mc=mc, h=h: e.matmul(PF[2 + mc][:, h * 128:(h + 1) * 128], memKT[:, h, mc * 128:(mc + 1) * 128], qmT[:, h, :],
                                                             start=True, stop=True), reads=[memKT, qmT], writes=[PF[2 + mc]], last=(h == 3))
                    kb.op("act", lambda e, mc=mc: e.activation(eM[mc][:], PF[2 + mc][:], AF.Exp), reads=[PF[2 + mc]], writes=[eM[mc]])
                for h in range(4):
                    for mc in range(2):
                        kb.mm(lambda e, mc=mc, h=h: e.matmul(PF[0][:, h * 65:(h + 1) * 65], eM[mc][:, h * 128:(h + 1) * 128], memV1[:, mc, h, :],
                                                             start=(mc == 0), stop=(mc == 1)), reads=[eM[mc], memV1], writes=[PF[0]],
                              last=(mc == 1 and h == 3))
                pv = PF[0][:, 0:260].rearrange("p (h d) -> p h d", h=4)
                kb.op("dve", lambda e: e.reciprocal(s4[7][:], pv[:, :, 64]), reads=[PF[0]], writes=[s4[7]])
                kb.op("dve", lambda e: e.tensor_tensor(ycat[:, 768:1024].rearrange("p (h d) -> p h d", h=4), pv[:, :, 0:64],
                                                       V(s4[7], 0, [[1, 4], [0, 64]]), ALU.mult), reads=[PF[0], s4[7]], writes=[ycat])

                if STAGE < 9:
                    return
            if DBG and l == 0 and not smp:
                kb.dma("sp", dbg_ycat, dbg_ycat[b, i * 128:(i + 1) * 128, :], ycat, ycat[:], home=ycat, is_output=True)
            kb.op("act", lambda e: e.activation(sz[:], proj[:, C_Z:C_Z + D], AF.Silu), reads=[proj], writes=[sz])
            kb.op("dve", lambda e: e.tensor_tensor(yg[:], ycat[:], sz[:], ALU.mult), reads=[ycat, sz], writes=[yg])
            for kc in range(8):
                kb.mm(lambda e, kc=kc: e.transpose(PT[1][:, kc * 128:(kc + 1) * 128], yg[:, kc * 128:(kc + 1) * 128], identb[:]),
                      reads=[yg, identb], writes=[PT[1]], last=(kc == 7))
            kb.op("dve", lambda e: e.tensor_copy(yT[:], PT[1][:].rearrange("p (k t) -> p k t", k=8)), reads=[PT[1]], writes=[yT])
            for ncn in range(2):
                pb = PF[ncn]
                for kc in range(8):
                    kb.mm(lambda e, kc=kc, pb=pb, ncn=ncn: e.matmul(pb[:], yT[:, kc, :], W_out[:, kc, ncn * 512:(ncn + 1) * 512],
                                                                   start=(kc == 0), stop=(kc == 7)), reads=[yT, W_out], writes=[pb], last=(kc == 7))
                kb.op("dve", lambda e, pb=pb, ncn=ncn: e.tensor_tensor(xo[:, ncn * 512:(ncn + 1) * 512], pb[:], xin[:, ncn * 512:(ncn + 1) * 512], ALU.add),
                      reads=[pb, xin], writes=[xo])
            if smp:
                if l == DEPTH - 1:
                    kb.dma("sp", y_s, y_s[:], xo, xo[0:SB_, :], home=xo, is_output=True)
                else:
                    kb.dma("sp", xsmid, xsmid[:, :], xo, xo[0:SB_, :], home=xo)
            else:
                kb.dma("sp", dst, dst[b, i * 128:(i + 1) * 128, :], xo, xo[:], home=xo, is_output=(l == DEPTH - 1))

        def prompt_seq(l, b):
            kb.op("pool", lambda e: e.memset(V1[:], 1.0), writes=[V1])
            kb.op("pool", lambda e: e.memset(memV1[:], 1.0), writes=[memV1])
            kb.op("pool", lambda e: e.memset(CC[:], 0.0), writes=[CC])
            kb.op("pool", lambda e: e.memset(qtTz[:], 0.0), writes=[qtTz])
            kb.op("pool", lambda e: e.memset(Sst[0][:], 0.0), writes=[Sst[0]])
            kb.dma("pool", KTW, KTW[:].rearrange("p (kc n) -> p kc n", kc=8), w_mem_kv, w_mem_kv[l].rearrange("(kc p) n -> p kc n", p=128))
            for _rep in range(int(os.environ.get("KMEMREP", "1"))):
                prompt_mem(l, b)
            if STAGE < 2:
                return
            for i in range(min(NT, NTILES_DBG)):
                prompt_tile(l, b, i)

        if do_sample:
            sample_init()
        for l in range(DEPTH):
            load_weights(l)
            if STAGE < 1:
                break
            if do_sample:
                sample_idx(l)
                prompt_tile(l, 0, 0, smp=True)
            if not int(os.environ.get("KPROMPT", "1")):
                continue
            for b in range(PB):
                prompt_seq(l, b)
            if STAGE < 10:
                break
        kb.finish()
        print("instructions:", kb.n_inst, "dma sems:", kb.nd, flush=True)
    return nc, [k for k in di.keys() if k not in SKIP_IN]


_CACHE = {}


def kernel(x_prompt, x_sample, mem_prompt, cache_nsa, cache_nsa_win, state_hgrn, state_pool, cache_mem,
           page_table, norm_g, w_in, w_out, nsa_qn, nsa_kn, cmp_pe, cmp_w1, cmp_w2, hg_lb, hg_on,
           pool_w, pool_scale, mem_norm, w_mem_kv, mem_qn, mem_kn):
    f = lambda a: np.ascontiguousarray(np.asarray(a, dtype=np.float32))
    if "nc" not in _CACHE:
        _CACHE["nc"], _CACHE["names"] = build_program()
        _CACHE["consts"] = host_consts()
    nc = _CACHE["nc"]
    consts = _CACHE["consts"]
    x_prompt, x_sample, mem_prompt = f(x_prompt), f(x_sample), f(mem_prompt)
    cnsa_full = f(cache_nsa).reshape(DEPTH * NPHYS * 64, 512)
    cache_nsa_win, state_hgrn, state_pool, cache_mem = f(cache_nsa_win), f(state_hgrn), f(state_pool), f(cache_mem)
    pt = np.asarray(page_table, dtype=np.int32)
    shared = {
        "norm_g": f(norm_g), "w_in": f(w_in), "w_out": f(w_out), "nsa_qn": f(nsa_qn), "nsa_kn": f(nsa_kn),
        "cmp_pe": f(cmp_pe), "cmp_w1": f(cmp_w1), "cmp_w2": f(cmp_w2), "hg_lb": f(hg_lb), "hg_on": f(hg_on),
        "pool_w": f(pool_w), "pool_scale": f(pool_scale), "mem_norm": f(mem_norm), "w_mem_kv": f(w_mem_kv),
        "mem_qn": f(mem_qn), "mem_kn": f(mem_kn), "cnsa": cnsa_full,
    }
    for name, _, _ in CONST_SPECS:
        shared["c_" + name] = consts[name]
    in_maps = []
    for c in range(NCORES):
        ps = slice(PB * c, PB * (c + 1))
        ss = slice(SB_ * c, SB_ * (c + 1))
        m = dict(shared)
        m["xp"] = np.ascontiguousarray(x_prompt[ps])
        m["memp"] = np.ascontiguousarray(mem_prompt[ps])
        m["xs"] = np.ascontiguousarray(x_sample[ss, 0, :])
        m["cwin"] = np.ascontiguousarray(cache_nsa_win[:, ss].reshape(DEPTH, SB_, 512, 128))
        m["shg"] = np.ascontiguousarray(state_hgrn[:, ss])
        m["spool"] = np.ascontiguousarray(state_pool[:, ss])
        m["cmem"] = np.ascontiguousarray(cache_mem[:, ss].reshape(DEPTH, SB_, 256, 512))
        ptc = pt[ss]
        ptb = ptc.reshape(SB_, 64, 2).transpose(2, 0, 1)
        m["ptb"] = np.ascontiguousarray(np.repeat(ptb, 64, axis=0).astype(np.int32))
        in_maps.append(m)
    declared = set(_CACHE["names"])
    in_maps = [{k: v for k, v in m.items() if k in declared} for m in in_maps]
    ncores = int(os.environ.get("KCORES", str(NCORES)))
    res = run_bass_kernel_spmd(nc, in_maps[:ncores], core_ids=list(range(ncores)))
    R = list(res.results)
    while len(R) < NCORES:
        R.append(R[0])

    def cat(name, axis):
        return np.concatenate([np.asarray(r[name]) for r in R], axis=axis)

    y_prompt = cat("y_p", 0)
    y_sample = cat("y_s", 0).reshape(32, 1, D)
    new_nsa_p = cat("o_nsa_p", 1).reshape(DEPTH, 16, SEQ, 4, 1, 64)
    new_win_p = cat("o_win_p", 1).reshape(DEPTH, 16, 512, 2, 1, 64)
    new_hgrn_p = cat("o_hg_p", 1)
    new_pool_p = cat("o_pool_p", 1)
    new_mem_p = cat("o_mem_p", 1).reshape(DEPTH, 16, 256, 2, 4, 64)
    new_nsa_s = cat("o_nsa_s", 1).reshape(DEPTH, 32, 1, 4, 1, 64)
    new_win_s = cat("o_win_s", 1).reshape(DEPTH, 32, 512, 2, 1, 64)
    new_hgrn_s = cat("o_hg_s", 1)
    new_pool_s = cat("o_pool_s", 1)
    return (y_prompt, y_sample, new_nsa_p, new_win_p, new_hgrn_p, new_pool_p, new_mem_p,
            new_nsa_s, new_win_s, new_hgrn_s, new_pool_s)
```

```python
import contextlib
import math
import numpy as np
import ml_dtypes
import concourse.bass as bass
import concourse.mybir as mybir
from concourse.bass_utils import run_bass_kernel_spmd

F32 = mybir.dt.float32
BF16 = mybir.dt.bfloat16
I32 = mybir.dt.int32
ALU = mybir.AluOpType
AF = mybir.ActivationFunctionType
AX = mybir.AxisListType

NCORES = 8
D = 1024
NW = 2956
SEQ = 2048
NT = SEQ // 128
DEPTH = 2
EPS = 1e-6
NEG = -30000.0
PB = 2
SB_ = 4
PAST = 16384
NPHYS = 5120

WPERM = [(0, 0, 320), (320, 384, 64), (384, 512, 64), (448, 1676, 256), (704, 320, 64), (768, 448, 64),
         (832, 576, 64), (896, 640, 12), (908, 652, 1024), (1932, 1932, 1024)]
C_Q, C_KC, C_KS, C_KW, C_QM, C_VC, C_VS, C_VW, C_G, C_HQ, C_HF, C_HI, C_U, C_Z = (
    0, 256, 320, 384, 448, 704, 768, 832, 896, 908, 1164, 1420, 1676, 1932)


class T:
    __slots__ = ("h", "name", "last_w", "readers", "dsem", "dcnt")

    def __init__(self, h, name):
        self.h = h
        self.name = name
        self.last_w = None
        self.readers = {}
        self.dsem = None
        self.dcnt = 0

    def __getitem__(self, k):
        return self.h[k]


class KB:
    def __init__(self, nc, stack):
        self.nc = nc
        self.stack = stack
        self.eng = {"pe": nc.tensor, "act": nc.scalar, "dve": nc.vector, "pool": nc.gpsimd, "sp": nc.sync}
        self.sem = {}
        self.cnt = {}
        self.waited = {}
        for en in self.eng:
            self.sem[en] = stack.enter_context(nc.semaphore("sem_" + en))
            self.cnt[en] = 0
            self.waited[en] = {}
        self.nd = 0
        self.out_marks = []
        self.n_inst = 0
        self.free_dsems = []

    def sb(self, name, shape, dt, stack=None):
        t = T((stack or self.stack).enter_context(self.nc.sbuf_tensor(name, list(shape), dt)), name)
        return t

    def ps(self, name, shape, dt=F32):
        return T(self.stack.enter_context(self.nc.psum_tensor(name, list(shape), dt)), name)

    def dram(self, name, shape, dt, kind=None):
        if kind is None:
            h = self.nc.dram_tensor(name, list(shape), dt)
        else:
            h = self.nc.dram_tensor(name, list(shape), dt, kind=kind)
        return T(h, name)

    def _wait(self, en, deps):
        w = self.waited[en]
        e = self.eng[en]
        best = {}
        for d in deps:
            if d is None:
                continue
            k, c = d
            if best.get(k, 0) < c:
                best[k] = c
        for k, c in best.items():
            if k == "pe" and en == "pe":
                continue
            if w.get(k, 0) < c:
                e.wait_ge(self.sem[k], c)
                w[k] = c

    def _deps(self, reads, writes):
        deps = []
        for t in reads:
            deps.append(t.last_w)
        for t in writes:
            deps.append(t.last_w)
            deps.extend(t.readers.items())
        return deps

    def _mark(self, mark, reads, writes):
        k, c = mark
        for t in writes:
            t.last_w = mark
            t.readers = {}
        for t in reads:
            if not any(t is w_ for w_ in writes):
                if t.readers.get(k, 0) < c:
                    t.readers[k] = c

    @staticmethod
    def _n(ts):
        return [getattr(t, "t", t) for t in ts]

    def op(self, en, fn, reads=(), writes=()):
        reads, writes = self._n(reads), self._n(writes)
        self._wait(en, self._deps(reads, writes))
        inst = fn(self.eng[en])
        self.cnt[en] += 1
        inst.then_inc(self.sem[en], 1)
        self._mark((en, self.cnt[en]), reads, writes)
        self.n_inst += 1
        return inst

    def mm(self, fn, reads=(), writes=(), last=True):
        reads, writes = self._n(reads), self._n(writes)
        self._wait("pe", self._deps(reads, writes))
        inst = fn(self.eng["pe"])
        self.n_inst += 1
        if last:
            self.cnt["pe"] += 1
            inst.then_inc(self.sem["pe"], 1)
            self._mark(("pe", self.cnt["pe"]), reads, writes)
        else:
            self._mark(("pe", self.cnt["pe"] + 1), reads, ())
        return inst

    def _dsem(self, t):
        if t.dsem is None:
            key = "d%d" % self.nd
            self.nd += 1
            self.sem[key] = self.stack.enter_context(self.nc.semaphore("sem_" + key))
            t.dsem = key
        return t.dsem

    def dma(self, q, out_t, out_ap, in_t, in_ap, home=None, is_output=False, **kw):
        out_t, in_t = getattr(out_t, "t", out_t), getattr(in_t, "t", in_t)
        if home is not None:
            home = getattr(home, "t", home)
        self._wait(q, self._deps([in_t], [out_t]))
        if home is None:
            home = out_t
        key = self._dsem(home)
        inst = self.eng[q].dma_start(out=out_ap, in_=in_ap, **kw)
        home.dcnt += 16
        inst.then_inc(self.sem[key], 16)
        mark = (key, home.dcnt)
        self._mark(mark, [in_t], [out_t])
        if is_output:
            self.out_marks.append(mark)
        self.n_inst += 1
        return inst

    def gather(self, out_t, out_ap, in_t, in_ap, idx_t, idx_ap):
        self._wait("pool", self._deps([in_t, idx_t], [out_t]))
        key = self._dsem(out_t)
        inst = self.nc.gpsimd.indirect_dma_start(out=out_ap, out_offset=None, in_=in_ap,
                                                 in_offset=bass.IndirectOffsetOnAxis(ap=idx_ap, axis=0))
        out_t.dcnt += 16
        inst.then_inc(self.sem[key], 16)
        self._mark((key, out_t.dcnt), [in_t, idx_t], [out_t])
        self.n_inst += 1

    def barrier(self):
        marks = [(en, c) for en, c in self.cnt.items() if c > 0]
        for en in self.eng:
            self._wait(en, marks)

    def finish(self):
        self._wait("sp", self.out_marks)
        self._wait("sp", [(en, c) for en, c in self.cnt.items() if c > 0 and en != "sp"])


def V(t, off, dims, p0=0, npart=None):
    full = t.h[:].ap
    pstep = full[0][0]
    if npart is None:
        npart = full[0][1] - p0
    return bass.AP(t.h, p0 * pstep + off, [[pstep, npart]] + [list(d) for d in dims])


def host_consts():
    c = {}
    half = 32
    inv = (10000.0 ** (-np.arange(half, dtype=np.float32) / half)).astype(np.float32)
    pos = np.arange(SEQ, dtype=np.float32)
    ang = (pos[:, None] * inv[None, :]).astype(np.float32)
    cs = np.cos(ang).astype(np.float32).reshape(NT, 128, half).transpose(1, 0, 2)
    sn = np.sin(ang).astype(np.float32).reshape(NT, 128, half).transpose(1, 0, 2)
    c["cos_p"] = np.ascontiguousarray(cs)
    c["sin_p"] = np.ascontiguousarray(sn)
    angs = (np.float32(PAST) * inv).astype(np.float32)
    c["cs_s"] = np.stack([np.cos(angs), np.sin(angs)]).astype(np.float32).reshape(1, 2, half)
    t = np.arange(SEQ)[:, None]
    n = np.arange(64)[None, :]
    done = ((n + 1) * 32 - 1) <= t
    c["cmpb"] = np.ascontiguousarray(np.where(done, 0.0, NEG).astype(np.float32).reshape(NT, 128, 64).transpose(1, 0, 2))
    j = np.arange(32)[None, :]
    blk = (t // 64)
    forced = (j == 0) | (j == blk) | (j == blk - 1)
    avail = j <= blk
    sb = np.where(avail, np.where(forced, 1e4, 0.0), -1e4).astype(np.float32)
    c["selb"] = np.ascontiguousarray(sb.reshape(NT, 128, 32).transpose(1, 0, 2))
    key = np.arange(SEQ)[None, :]
    c["e32"] = (np.arange(32)[:, None] == key // 64).astype(ml_dtypes.bfloat16)
    kk = np.arange(128)[:, None]
    tt = np.arange(128)[None, :]
    c["causb"] = np.where(kk > tt, NEG, 0.0).astype(ml_dtypes.bfloat16)
    c["edgeb"] = np.where(kk <= tt, NEG, 0.0).astype(ml_dtypes.bfloat16)
    c["identb"] = np.eye(128, dtype=np.float32).astype(ml_dtypes.bfloat16)
    c["identf"] = np.eye(128, dtype=np.float32)
    same = (kk // 64) == (tt // 64)
    c["bdmask"] = (same & (kk <= tt)).astype(np.float32)
    c["uincl"] = (same & (kk <= tt)).astype(np.float32)
    c["urest"] = (same & (kk > tt)).astype(np.float32)
    c["cmask"] = np.stack([(np.arange(128) < 64), (np.arange(128) >= 64)], 1).astype(np.float32)
    ws = (2, 4, 8, 16)
    bg = np.zeros((128, 4, 128), np.float32)
    b0 = np.zeros((128, 4, 128), np.float32)
    bp = np.zeros((128, 4, 128), np.float32)
    for g, w in enumerate(ws):
        for tq in range(128):
            for s in range(max(0, tq - w + 1), tq + 1):
                bg[s, g, tq] += 1.0 / w
                b0[s, g, tq] += 1.0 / min(w, tq + 1)
            bg[tq, g, tq] -= 1.0
            b0[tq, g, tq] -= 1.0
            for s in range(128):
                if s + 128 - 128 >= 0 and (s - 128) > tq - w:
                    bp[s, g, tq] = 1.0 / w
    c["pmod"] = np.stack([(np.arange(128) % 64) + l_ * NPHYS * 64 for l_ in range(DEPTH)], 1).astype(np.float32)
    c["nident"] = (1.0 - np.eye(128)).astype(np.float32)
    r4 = np.zeros((128, 128), np.float32)
    r4[0:4, :] = 1.0
    c["rows4"] = r4
    bs = np.zeros((60, 4, 128), np.float32)
    bself = np.zeros((128, 4, 128), np.float32)
    for g, w in enumerate(ws):
        for s_ in range(4):
            for r_ in range(15):
                if r_ >= 16 - w:
                    bs[s_ * 15 + r_, g, s_] = 1.0 / w
        bself[:, g, :] = np.eye(128) * (1.0 / w - 1.0)
    c["bs_s"] = bs.astype(ml_dtypes.bfloat16)
    c["bself"] = bself.astype(ml_dtypes.bfloat16)
    sbs = np.zeros((128, 256), np.float32)
    sbs[:, 0] = 1e4
    sbs[:, 255] = 1e4
    c["selb_s"] = sbs
    c["bandg"] = bg.astype(ml_dtypes.bfloat16)
    c["band0"] = b0.astype(ml_dtypes.bfloat16)
    c["bandp"] = bp.astype(ml_dtypes.bfloat16)
    return c


CONST_SPECS = [("cos_p", [128, NT, 32], F32), ("sin_p", [128, NT, 32], F32), ("cs_s", [1, 2, 32], F32),
               ("cmpb", [128, NT, 64], F32), ("selb", [128, NT, 32], F32), ("e32", [32, SEQ], BF16),
               ("causb", [128, 128], BF16), ("edgeb", [128, 128], BF16), ("identb", [128, 128], BF16),
               ("identf", [128, 128], F32), ("bdmask", [128, 128], F32), ("uincl", [128, 128], F32),
               ("urest", [128, 128], F32), ("cmask", [128, 2], F32), ("bandg", [128, 4, 128], BF16),
               ("band0", [128, 4, 128], BF16), ("bandp", [128, 4, 128], BF16), ("pmod", [128, 2], F32),
               ("nident", [128, 128], F32), ("rows4", [128, 128], F32), ("bs_s", [60, 4, 128], BF16),
               ("bself", [128, 4, 128], BF16), ("selb_s", [128, 256], F32)]


import os
STAGE = int(os.environ.get("KSTAGE", "99"))
NTILES_DBG = int(os.environ.get("KTILES", "16"))


SKIP_IN = set()
SUB = int(os.environ.get('KSUB', '99'))
PENG = os.environ.get('KPENG', 'dve')


def build_program(do_sample=bool(int(os.environ.get("KSAMPLE", "1")))):
    nc = bass.Bass("TRN2", target_bir_lowering=False)
    with contextlib.ExitStack() as st:
        kb = KB(nc, st)
        di = {}

        def din(name, shape, dt=F32):
            di[name] = kb.dram(name, shape, dt, kind="ExternalInput")
            return di[name]

        def dout(name, shape):
            di[name] = kb.dram(name, shape, F32, kind="ExternalOutput")
            return di[name]

        xp = din("xp", [PB, SEQ, D])
        memp = din("memp", [PB, 256, D])
        xs = din("xs", [SB_, D])
        cnsa = din("cnsa", [DEPTH * NPHYS * 64, 512]) if do_sample else None
        cwin = din("cwin", [DEPTH, SB_, 512, 128])
        shg = din("shg", [DEPTH, SB_, 4, 64, 64])
        spool = din("spool", [DEPTH, SB_, 15, 256])
        cmem = din("cmem", [DEPTH, SB_, 256, 512])
        ptb = din("ptb", [128, SB_, 64], I32)
        norm_g = din("norm_g", [DEPTH, D])
        w_in = din("w_in", [DEPTH, D, NW])
        w_out = din("w_out", [DEPTH, D, D])
        nsa_qn = din("nsa_qn", [DEPTH, 64])
        nsa_kn = din("nsa_kn", [DEPTH, 3, 64])
        cmp_pe = din("cmp_pe", [DEPTH, 2, 32, 64])
        cmp_w1 = din("cmp_w1", [DEPTH, 2, 2048, 256])
        cmp_w2 = din("cmp_w2", [DEPTH, 2, 256, 64])
        hg_lb = din("hg_lb", [DEPTH, 256])
        hg_on = din("hg_on", [DEPTH, 256])
        pool_w = din("pool_w", [DEPTH, 4, 64, 64])
        pool_scale = din("pool_scale", [DEPTH, 256])
        mem_norm = din("mem_norm", [DEPTH, D])
        w_mem_kv = din("w_mem_kv", [DEPTH, D, 512])
        mem_qn = din("mem_qn", [DEPTH, 64])
        mem_kn = din("mem_kn", [DEPTH, 64])
        cd = {}
        for name, shape, dt in CONST_SPECS:
            cd[name] = din("c_" + name, shape, dt)

        y_p = dout("y_p", [PB, SEQ, D])
        y_s = dout("y_s", [SB_, D])
        o_nsa_p = dout("o_nsa_p", [DEPTH, PB, SEQ, 256])
        o_win_p = dout("o_win_p", [DEPTH, PB, 512, 128])
        o_hg_p = dout("o_hg_p", [DEPTH, PB, 4, 64, 64])
        o_pool_p = dout("o_pool_p", [DEPTH, PB, 15, 256])
        o_mem_p = dout("o_mem_p", [DEPTH, PB, 256, 512])
        o_nsa_s = dout("o_nsa_s", [DEPTH, SB_, 256])
        o_win_s = dout("o_win_s", [DEPTH, SB_, 512, 128])
        o_hg_s = dout("o_hg_s", [DEPTH, SB_, 4, 64, 64])
        o_pool_s = dout("o_pool_s", [DEPTH, SB_, 15, 256])
        DBG = False
        xmid = kb.dram("xmid", [PB, SEQ, D], F32)
        global SKIP_IN
        SKIP_IN = set() if do_sample else {"cnsa"}

        cs = {}
        NONRES = ("cos_p", "sin_p", "cmpb", "selb", "uincl", "cs_s")
        for name, shape, dt in CONST_SPECS:
            if name in NONRES:
                continue
            cs[name] = kb.sb("k_" + name, shape, dt)
            kb.dma("sp", cs[name], cs[name][:], cd[name], cd[name][:])
        cs["uincl"] = cs["bdmask"]
        rope_t = kb.sb("rope_t", [128, 2, 32], F32)
        cmpb_t = kb.sb("cmpb_t", [128, 64], F32)
        selb_t = kb.sb("selb_t", [128, 32], F32)
        identb, identf = cs["identb"], cs["identf"]

        PF = [kb.ps("pf%d" % i, [128, 512], F32) for i in range(6)]
        PT = [kb.ps("pt%d" % i, [128, 1024], BF16) for i in range(2)]

        W_in = kb.sb("W_in", [128, 8, NW], BF16)
        W_out = kb.sb("W_out", [128, 8, D], BF16)
        W1 = kb.sb("W1", [128, 2, 16, 256], BF16)
        W2 = kb.sb("W2", [128, 2, 2, 64], BF16)
        PW = kb.sb("PW", [64, 4, 64], BF16)
        PE2 = kb.sb("PE2", [128, 2, 16], BF16)
        gcol = kb.sb("gcol", [128, 8], F32)
        mgcol = kb.sb("mgcol", [128, 8], F32)
        gain11 = kb.sb("gain11", [128, 11, 64], F32)
        mkn_b = kb.sb("mkn_b", [128, 64], F32)
        hgon_b = kb.sb("hgon_b", [128, 256], F32)
        psc_b = kb.sb("psc_b", [128, 256], F32)
        lbraw = kb.sb("lbraw", [128, 2, 256], F32)
        lb_b = kb.sb("lb_b", [128, 256], F32)
        oml_b = kb.sb("oml_b", [128, 256], F32)
        biasT = kb.sb("biasT", [128, 2, 2], F32)
        lbt = [kb.sb("lbt%d" % i, [128, 256], F32) for i in range(3)]

        def load_weights(l):
            for (d0, s0, n) in WPERM:
                kb.dma("pool", W_in, W_in[:, :, d0:d0 + n], w_in,
                       w_in[l, :, s0:s0 + n].rearrange("(kc p) n -> p kc n", p=128))
            kb.dma("pool", W_out, W_out[:], w_out, w_out[l].rearrange("(kc p) n -> p kc n", p=128))
            for a in range(2):
                kb.dma("pool", W1, W1[:, a], cmp_w1, cmp_w1[l, a].rearrange("(c p) h -> p c h", p=128))
                kb.dma("pool", W2, W2[:, a], cmp_w2, cmp_w2[l, a].rearrange("(hc p) d -> p hc d", p=128))
                for j in range(2):
                    kb.dma("pool", PE2, PE2[j * 64:(j + 1) * 64, a, :], cmp_pe,
                           cmp_pe[l, a, j::2, :].rearrange("c d -> d c"), allow_slow_non_contiguous=True)
            kb.dma("pool", PW, PW[:], pool_w, pool_w[l].rearrange("g c d -> c g d"))
            kb.dma("sp", gcol, gcol[:], norm_g, norm_g[l].rearrange("(kc p) -> p kc", p=128), allow_slow_non_contiguous=True)
            kb.dma("sp", mgcol, mgcol[:], mem_norm, mem_norm[l].rearrange("(kc p) -> p kc", p=128), allow_slow_non_contiguous=True)
            for h in range(4):
                kb.dma("sp", gain11, gain11[:, h, :], nsa_qn, bass.AP(nsa_qn.h, l * 64, [[0, 128], [1, 64]]))
                kb.dma("sp", gain11, gain11[:, 7 + h, :], mem_qn, bass.AP(mem_qn.h, l * 64, [[0, 128], [1, 64]]))
            kb.dma("sp", gain11, gain11[:, 4:7, :], nsa_kn, bass.AP(nsa_kn.h, l * 192, [[0, 128], [64, 3], [1, 64]]))
            kb.dma("sp", mkn_b, mkn_b[:], mem_kn, bass.AP(mem_kn.h, l * 64, [[0, 128], [1, 64]]))
            kb.dma("sp", hgon_b, hgon_b[:], hg_on, bass.AP(hg_on.h, l * 256, [[0, 128], [1, 256]]))
            kb.dma("sp", psc_b, psc_b[:], pool_scale, bass.AP(pool_scale.h, l * 256, [[0, 128], [1, 256]]))
            kb.dma("sp", lbraw, lbraw[:], hg_lb, bass.AP(hg_lb.h, 0, [[0, 128], [256, 2], [1, 256]]))
            if l == 0:
                kb.op("dve", lambda e: e.memset(lb_b[:], 0.0), writes=[lb_b])
                kb.op("dve", lambda e: e.memset(oml_b[:], 1.0), writes=[oml_b])
            else:
                kb.op("dve", lambda e: e.tensor_tensor(lbt[0][:], lbraw[:, 1, :], lbraw[:, 0, :], ALU.subtract),
                      reads=[lbraw], writes=[lbt[0]])
                kb.op("act", lambda e: e.activation(lb_b[:], lbt[0][:], AF.Sigmoid), reads=[lbt[0]], writes=[lb_b])
                kb.op("dve", lambda e: e.tensor_scalar(oml_b[:], lb_b[:], -1.0, 1.0, ALU.mult, ALU.add),
                      reads=[lb_b], writes=[oml_b])
            for a in range(2):
                for hc in range(2):
                    for c in range(16):
                        kb.mm(lambda e, a=a, hc=hc, c=c: e.matmul(PF[0][:, a * 2 + hc:a * 2 + hc + 1],
                                                                  W1[:, a, c, hc * 128:(hc + 1) * 128],
                                                                  PE2[:, a, c:c + 1], start=(c == 0), stop=(c == 15)),
                              reads=[W1, PE2], writes=[PF[0]], last=(c == 15))
            kb.op("act", lambda e: e.copy(biasT[:].rearrange("p a h -> p (a h)"), PF[0][:, 0:4]), reads=[PF[0]], writes=[biasT])

        x_t = kb.sb("x_t", [128, D], F32)
        xn = kb.sb("xn", [128, D], BF16)
        hT = kb.sb("hT", [128, 8, 128], BF16)
        proj = kb.sb("proj", [128, NW], F32)
        st1 = [kb.sb("st1_%d" % i, [128, 1], F32) for i in range(4)]
        s11 = [kb.sb("s11_%d" % i, [128, 11], F32) for i in range(4)]
        qkg = kb.sb("qkg", [128, 11, 64], F32)
        qkr = kb.sb("qkr", [128, 7, 64], F32)
        rows = kb.sb("rows", [128, 4, 64], F32)
        wrows = kb.sb("wrows", [128, 2, 64], F32)
        qk_bf = kb.sb("qk_bf", [128, 11, 64], BF16)
        qT_all = kb.sb("qT_all", [64, 4, 128], BF16)
        qmT = kb.sb("qmT", [64, 4, 128], BF16)
        KTW = kb.sb("KTW", [128, 2 * SEQ], BF16)
        V1 = kb.sb("V1", [128, 2, NT, 65], BF16)
        CC = kb.sb("CC", [64, 2, 64], BF16)
        stg = kb.sb("stg", [128, 2, 2, 64], BF16)
        kT2 = kb.sb("kT2", [128, 2, 64], BF16)
        hidT = kb.sb("hidT", [128, 2, 2, 4], BF16)
        pcb = kb.sb("pcb", [128, 4, 64], BF16)
        s4 = [kb.sb("s4_%d" % i, [128, 4], F32) for i in range(8)]
        pTc = kb.sb("pTc", [64, 4, 128], BF16)
        vcc = kb.sb("vcc", [64, 64], BF16)
        imp = kb.sb("imp", [128, 32], F32)
        score = kb.sb("score", [128, 32], F32)
        rank = kb.sb("rank", [128, 32], F32)
        negsel = kb.sb("negsel", [128, 64], BF16)
        kb.op("pool", lambda e: e.memset(negsel[:], 0.0), writes=[negsel])
        negselT = kb.sb("negselT", [32, 128], BF16)
        expT = [kb.sb("expT%d" % i, [128, 512], BF16) for i in range(2)]
        oT = [kb.sb("oT0", [65, 512], F32)] * 2
        gts = kb.sb("gts", [128, 12], F32)
        onsa = [kb.sb("onsa%d" % i, [128, 4, 64], F32) for i in range(4)]
        ycat = kb.sb("ycat", [128, D], F32)
        hg = [kb.sb("hg%d" % i, [128, 256], F32) for i in range(6)]
        hgb = [kb.sb("hgb%d" % i, [128, 256], BF16) for i in range(3)]
        khz = kb.sb("khz", [128, 2, 256], BF16)
        qtT = kb.sb("qtT", [64, 4, 128], BF16)
        qtTz = kb.sb("qtTz", [64, 4, 2, 128], BF16)
        ktT = kb.sb("ktT", [64, 4, 128], BF16)
        ATs = kb.sb("ATs", [128, 4, 128], BF16)
        Sst = [kb.sb("Sst%d" % i, [64, 4, 64], F32) for i in range(3)]
        Sbf = [kb.sb("Sbf%d" % i, [64, 4, 64], BF16) for i in range(2)]
        eGl = kb.sb("eGl", [64, 4, 2], F32)
        ubf = [kb.sb("ubf%d" % i, [128, 256], BF16) for i in range(2)]
        dltT = kb.sb("dltT", [64, 4, 128], BF16)
        memKT = kb.sb("memKT", [64, 4, 256], BF16)
        memV1 = kb.sb("memV1", [128, 2, 4, 65], BF16)
        sz = kb.sb("sz", [128, D], F32)
        xo = kb.sb("xo", [128, D], F32)

        class AV:
            def __init__(self, t, ap):
                self.t = t
                self.ap = ap

            def __getitem__(self, k):
                return self.ap[k]

        def alias(t, ap):
            v = AV(t, ap)
            return v
        junk, junk_t = alias(sz, sz[:]), sz
        sq11, sq11_t = alias(sz, sz[:, 0:704]), sz
        cmpt, cmpt_t = alias(sz, sz[:].rearrange("p (a b) -> p a b", a=32)), sz
        qkn, qkn_t = alias(ycat, ycat[:, 0:704].rearrange("p (h d) -> p h d", h=11)), ycat
        mrow, mrow_t = alias(xo, xo[:, 0:512].rearrange("p (a h d) -> p a h d", a=2, h=4)), xo
        mkv, mkv_t = alias(ycat, ycat[:, 0:512]), ycat
        rt = [alias(hg[i], hg[i][:, 0:224].rearrange("p (h d) -> p h d", h=7)) for i in range(4)]
        sm = [alias(hg[i], hg[i][:].rearrange("p (h d) -> p h d", h=4)) for i in range(3)]
        pcf = alias(hg[3], hg[3][:].rearrange("p (h d) -> p h d", h=4))
        eM = expT
        yg = xn
        yT = hT

        def rms_rstd(src_ssq, n, eps_t):
            a, b, c_ = eps_t
            kb.op("dve", lambda e: e.tensor_scalar(a[:], src_ssq[:], 1.0 / n, EPS, ALU.mult, ALU.add), reads=[src_ssq], writes=[a])
            kb.op("act", lambda e: e.activation(b[:], a[:], AF.Sqrt), reads=[a], writes=[b])
            kb.op("dve", lambda e: e.reciprocal(c_[:], b[:]), reads=[b], writes=[c_])
            return c_

        def norm_transpose(src_t, gc):
            kb.op("act", lambda e: e.activation(junk[:], src_t[:], AF.Square, accum_out=st1[0][:]), reads=[src_t], writes=[junk, st1[0]])
            r = rms_rstd(st1[0], D, st1[1:4])
            kb.op("dve", lambda e: e.tensor_scalar(xn[:], src_t[:], r[:], None, ALU.mult), reads=[src_t, r], writes=[xn])
            for kc in range(8):
                kb.mm(lambda e, kc=kc: e.transpose(PT[0][:, kc * 128:(kc + 1) * 128], xn[:, kc * 128:(kc + 1) * 128], identb[:]),
                      reads=[xn, identb], writes=[PT[0]], last=(kc == 7))
            kb.op("dve", lambda e: e.tensor_tensor(hT[:], PT[0][:].rearrange("p (k t) -> p k t", k=8),
                                                   V(gc, 0, [[1, 8], [0, 128]]), ALU.mult), reads=[PT[0], gc], writes=[hT])

        evac_flip = [0]

        def evac(dst_t, dst_ap, src_t, src_ap):
            evac_flip[0] ^= 1
            if evac_flip[0] or os.environ.get("KEVAC", "act") == "act":
                kb.op("act", lambda e: e.copy(dst_ap, src_ap), reads=[src_t], writes=[dst_t])
            else:
                kb.op("dve", lambda e: e.tensor_copy(dst_ap, src_ap), reads=[src_t], writes=[dst_t])

        kT3 = kb.sb("kT3", [64, 3, 128], BF16)
        qT_pad = kb.sb("qT_pad", [64, 128], BF16)
        G2 = kb.sb("G2", [128, 512], BF16)
        G3 = kb.sb("G3", [128, 512], BF16)
        Gs = [khz, ATs, G2, G3]
        Gv = [khz[:].rearrange("p a b -> p (a b)"), ATs[:].rearrange("p a b -> p (a b)"), G2[:], G3[:]]
        stg1 = stg
        hidc = kb.sb("hidc", [128, 2, 2, 128], BF16)
        kccT_1 = alias(ktT, ktT[:].rearrange("p h t -> p (h t)"))
        vcc_1 = pcb
        idx_l = kb.sb("idx_l", [128, SB_, 64], I32)
        ptb_sb = kb.sb("ptb_sb", [128, SB_, 64], I32)
        Ef = kb.sb("Ef", [128, 128], F32)
        ETs = [kb.sb("ETs%d" % i_, [128, 128], BF16) for i_ in range(2)]
        Wt = alias(hgb[1], hgb[1][:].rearrange("p (k d) -> p k d", k=4))
        KwT = pTc
        V1w = kb.sb("V1w", [128, 4, 65], BF16)
        V1n = kb.sb("V1n", [128, 2, 65], BF16)
        Mt = kb.sb("Mt", [128, 2, 256], BF16)
        spb = kb.sb("spb", [60, 256], BF16)
        pT_s = alias(expT[0], expT[0][:].rearrange("p (c t) -> p c t", c=4))
        m8a = kb.sb("m8a", [128, 16], F32)
        m8b = kb.sb("m8b", [128, 16], F32)
        selk = kb.sb("selk", [128, 64], F32)
        sh_f = alias(onsa[2], onsa[2][:].rearrange("p h d -> p (h d)"))
        sh_k = alias(onsa[3], onsa[3][:].rearrange("p h d -> p (h d)"))
        fkT = kb.sb("fkT", [64, 2, 4, 4], F32)
        ETn = kb.sb("ETn", [128, 32], BF16)
        den4 = [kb.sb("den4_%d" % i_, [128, 4], F32) for i_ in range(3)]
        xsmid = kb.dram("xsmid", [SB_, D], F32)

        def acc(k, par):
            if k < 4:
                t = x_t if par == 0 else xo
                return t, t[:, k * 256:(k + 1) * 256]
            return sz, sz[:, par * 256:(par + 1) * 256]

        def sample_init():
            kb.op("pool", lambda e: e.memset(x_t[:], 0.0), writes=[x_t])
            kb.dma("sp", ptb_sb, ptb_sb[:], ptb, ptb[:])
            for l_ in range(DEPTH):
                pass

        def sample_idx(l):
            kb.op("dve", lambda e: e.tensor_scalar(idx_l[:], ptb_sb[:], 64.0, cs["pmod"][:, l:l + 1], ALU.mult, ALU.add),
                  reads=[ptb_sb, cs["pmod"]], writes=[idx_l])

        def sample_pass1(l, s):
            KC2 = KTW[:].rearrange("p (a m) -> p a m", a=2)
            if True:
                for ch in range(4):
                    for ppl in range(16):
                        pp = ch * 16 + ppl
                        g = Gs[pp % 4]
                        gq = Gv[pp % 4]
                        kb.gather(g, gq, cnsa, cnsa[:, :], idx_l, idx_l[:, s, pp:pp + 1])
                        gv = gq.rearrange("p (j r d) -> p r j d", j=2, r=4)
                        kb.op("dve", lambda e, gv=gv: e.tensor_copy(stg1[:], gv[:, 0:2, :, :]), reads=[g], writes=[stg1])
                        for a in range(2):
                            kb.mm(lambda e, a=a: e.transpose(PT[0][:, a * 128:(a + 1) * 128], stg1[:, a].rearrange("p j d -> p (j d)"), identb[:]),
                                  reads=[stg1, identb], writes=[PT[0]], last=(a == 1))
                        kb.op("dve", lambda e, ppl=ppl: e.tensor_copy(KC2[:, :, ppl * 128:(ppl + 1) * 128],
                                                                      PT[0][:, 0:256].rearrange("p (a m) -> p a m", a=2)),
                              reads=[PT[0]], writes=[KTW])
                    for a in range(2):
                        for hc in range(2):
                            pb_ = PF[(a * 2 + hc) % 2]
                            for c in range(16):
                                kb.mm(lambda e, a=a, hc=hc, c=c, pb_=pb_: e.matmul(pb_[:, 0:128], W1[:, a, c, hc * 128:(hc + 1) * 128],
                                                                                   V(KTW, a * 2048 + c, [[16, 128]]), start=(c == 0), stop=(c == 15)),
                                      reads=[W1, KTW], writes=[pb_], last=(c == 15))
                            kb.op("act", lambda e, a=a, hc=hc, pb_=pb_: e.activation(hidc[:, a, hc, :], pb_[:, 0:128], AF.Silu, bias=biasT[:, a, hc:hc + 1]),
                                  reads=[pb_, biasT], writes=[hidc])
                    for hc in range(2):
                        kb.mm(lambda e, hc=hc: e.matmul(PF[4][0:64, 0:128], W2[:, 0, hc, :], hidc[:, 0, hc, :], start=(hc == 0), stop=(hc == 1)),
                              reads=[W2, hidc], writes=[PF[4]], last=(hc == 1))
                    kb.op("dve", lambda e, s=s, ch=ch: e.tensor_copy(kccT_1[:, ch * 128:(ch + 1) * 128], PF[4][0:64, 0:128]), reads=[PF[4]], writes=[kccT_1])
                    for hc in range(2):
                        kb.mm(lambda e, hc=hc: e.matmul(PF[2][:, 0:64], hidc[:, 1, hc, :], W2[:, 1, hc, :], start=(hc == 0), stop=(hc == 1)),
                              reads=[W2, hidc], writes=[PF[2]], last=(hc == 1))
                    kb.op("dve", lambda e, s=s, ch=ch: e.tensor_copy(vcc_1[:, ch, :], PF[2][:, 0:64]), reads=[PF[2]], writes=[vcc_1])

        def nsa_branch_finish(s, k):
            kb.op("pool", lambda e: e.memset(oT[0][:], 0.0), writes=[oT[0]])
            kb.op("dve", lambda e: e.tensor_copy(V(oT[0], s, [[128, 4]], npart=65), PF[5][0:65, 0:4]), reads=[PF[5]], writes=[oT[0]])
            for h in range(4):
                kb.mm(lambda e, h=h: e.transpose(PF[2][:, h * 65:(h + 1) * 65], oT[0][:, h * 128:(h + 1) * 128], identf[0:65, 0:65]),
                      reads=[oT[0], identf], writes=[PF[2]], last=(h == 3))
            pv = PF[2][:, 0:260].rearrange("p (h d) -> p h d", h=4)
            kb.op("dve", lambda e: e.tensor_scalar(den4[0][:], pv[:, :, 64], cs["nident"][:, s:s + 1], None, ALU.add),
                  reads=[PF[2], cs["nident"]], writes=[den4[0]])
            kb.op("dve", lambda e: e.reciprocal(den4[1][:], den4[0][:]), reads=[den4[0]], writes=[den4[1]])
            kb.op("dve", lambda e: e.tensor_tensor(onsa[0][:], pv[:, :, 0:64], V(den4[1], 0, [[1, 4], [0, 64]]), ALU.mult),
                  reads=[PF[2], den4[1]], writes=[onsa[0]])
            to, ao = acc(k, s % 2)
            tn, an = acc(k, (s + 1) % 2)
            kb.op("dve", lambda e: e.tensor_tensor(an, ao, onsa[0][:].rearrange("p h d -> p (h d)"), ALU.add), reads=[to, onsa[0]], writes=[tn])

        def sample_mixers(l):
            for k in range(5):
                t0_, a0_ = acc(k, 0)
                kb.op("pool", lambda e, a0_=a0_: e.memset(a0_, 0.0), writes=[t0_])
            kb.op("pool", lambda e: e.memset(qT_pad[:], 0.0), writes=[qT_pad])
            kb.op("pool", lambda e: e.memset(V1n[:], 1.0), writes=[V1n])
            kb.op("dve", lambda e: e.tensor_copy(V1n[:, :, 0:64], proj[:, C_VS:C_VS + 128].rearrange("p (a d) -> p a d", a=2)), reads=[proj], writes=[V1n])
            kb.op("act", lambda e: e.activation(hg[0][:], proj[:, C_HF:C_HF + 256], AF.Sigmoid), reads=[proj], writes=[hg[0]])
            kb.op("dve", lambda e: e.tensor_tensor(hg[5][:], hg[0][:], oml_b[:], ALU.mult), reads=[hg[0], oml_b], writes=[hg[5]])
            kb.op("dve", lambda e: e.tensor_tensor(sh_f[:], hg[5][:], lb_b[:], ALU.add), reads=[hg[5], lb_b], writes=[sh_f])
            kb.op("dve", lambda e: e.tensor_scalar(sh_k[:], sh_f[:], -1.0, 1.0, ALU.mult, ALU.add), reads=[sh_f], writes=[sh_k])
            for ti, tsrc in enumerate([sh_f, sh_k]):
                for h in range(4):
                    kb.mm(lambda e, h=h, tsrc=tsrc: e.transpose(PF[4][0:64, h * 128:(h + 1) * 128], tsrc[:, h * 64:(h + 1) * 64], identf[:]),
                          reads=[tsrc, identf], writes=[PF[4]], last=(h == 3))
                kb.op("dve", lambda e, ti=ti: e.tensor_copy(fkT[:, ti], PF[4][0:64, :].rearrange("p (h t) -> p h t", h=4)[:, :, 0:4]), reads=[PF[4]], writes=[fkT])
            kb.op("dve", lambda e: e.tensor_copy(hgb[0][:], proj[:, C_HQ:C_HQ + 256]), reads=[proj], writes=[hgb[0]])
            for h in range(4):
                kb.mm(lambda e, h=h: e.transpose(PT[1][0:64, h * 128:(h + 1) * 128], hgb[0][:, h * 64:(h + 1) * 64], identb[:]),
                      reads=[hgb[0], identb], writes=[PT[1]], last=(h == 3))
            kb.op("dve", lambda e: e.tensor_copy(qtT[:], PT[1][0:64, 0:512].rearrange("p (h t) -> p h t", h=4)), reads=[PT[1]], writes=[qtT])

            sm_s = ycat[:, 0:512]
            p_s = ycat[:, 512:1024]
            pb_s = xn[:, 0:512]
            for s in range(SB_):
                oh = identf[:, s:s + 1]
                sample_pass1(l, s)
                kb.op("dve", lambda e, s=s: e.tensor_copy(qT_pad[:, 0:4], V(qT_all, s, [[128, 4]])), reads=[qT_all], writes=[qT_pad])
                kb.mm(lambda e, s=s: e.matmul(PF[3][:], qT_pad[:], kccT_1[:], start=True, stop=True), reads=[qT_pad, kccT_1], writes=[PF[3]])
                kb.op("dve", lambda e: e.tensor_reduce(st1[0][:], PF[3][:], AX.X, ALU.max), reads=[PF[3]], writes=[st1[0]])
                kb.op("dve", lambda e: e.tensor_scalar(sm_s, PF[3][:], st1[0][:], None, ALU.subtract), reads=[PF[3], st1[0]], writes=[ycat])
                kb.op("act", lambda e: e.activation(p_s, sm_s, AF.Exp, accum_out=st1[1][:]), reads=[ycat], writes=[ycat, st1[1]])
                kb.op("dve", lambda e: e.reciprocal(st1[2][:], st1[1][:]), reads=[st1[1]], writes=[st1[2]])
                kb.op("dve", lambda e: e.tensor_scalar(sm_s, p_s, st1[2][:], None, ALU.mult), reads=[ycat, st1[2]], writes=[ycat])
                kb.op("dve", lambda e: e.tensor_copy(pb_s, sm_s), reads=[ycat], writes=[xn])
                pv2 = sm_s.rearrange("p (j r) -> p j r", r=2)
                kb.op("dve", lambda e: e.tensor_tensor(hg[4][:], pv2[:, :, 0], pv2[:, :, 1], ALU.add), reads=[ycat], writes=[hg[4]])
                kb.mm(lambda e: e.matmul(PF[0][:, 0:256], cs["rows4"][:], hg[4][:], start=True, stop=True), reads=[cs["rows4"], hg[4]], writes=[PF[0]])
                kb.op("dve", lambda e: e.tensor_tensor(hg[1][:], PF[0][:, 0:256], cs["selb_s"][:], ALU.add), reads=[PF[0], cs["selb_s"]], writes=[hg[1]])
                kb.op("dve", lambda e: e.max(m8a[:, 0:8], hg[1][:]), reads=[hg[1]], writes=[m8a])
                kb.op("dve", lambda e: e.match_replace(hg[2][:], m8a[:, 0:8], hg[1][:], -1e9), reads=[hg[1], m8a], writes=[hg[2]])
                kb.op("dve", lambda e: e.max(m8b[:, 0:8], hg[2][:]), reads=[hg[2]], writes=[m8b])
                kb.op("dve", lambda e: e.tensor_scalar(hg[3][:], hg[1][:], m8b[:, 6:7], None, ALU.is_ge), reads=[hg[1], m8b], writes=[hg[3]])
                for g4 in range(4):
                    kb.op("dve", lambda e, g4=g4: e.tensor_copy(selk[32 * g4:32 * (g4 + 1), :], V(hg[3], g4, [[4, 64]], p0=32 * g4, npart=32)),
                          reads=[hg[3]], writes=[selk])
                for c in range(4):
                    kb.mm(lambda e, c=c: e.transpose(PT[0][:, c * 128:(c + 1) * 128], pb_s[:, c * 128:(c + 1) * 128], identb[:]),
                          reads=[xn, identb], writes=[PT[0]], last=(c == 3))
                kb.op("dve", lambda e: e.tensor_copy(pT_s[:], PT[0][:, 0:512].rearrange("p (c t) -> p c t", c=4)), reads=[PT[0]], writes=[pT_s])
                for c in range(4):
                    kb.mm(lambda e, c=c, s=s: e.matmul(PF[4][0:64, 0:4], vcc_1[:, c, :], pT_s[:, c, 0:4], start=(c == 0), stop=(c == 3)),
                          reads=[vcc_1, pT_s], writes=[PF[4]], last=(c == 3))
                kb.op("pool", lambda e: e.memset(oT[0][:], 0.0), writes=[oT[0]])
                kb.op("dve", lambda e, s=s: e.tensor_copy(V(oT[0], s, [[128, 4]], npart=64), PF[4][0:64, 0:4]), reads=[PF[4]], writes=[oT[0]])
                for h in range(4):
                    kb.mm(lambda e, h=h: e.transpose(PF[3][:, h * 64:(h + 1) * 64], oT[0][0:64, h * 128:(h + 1) * 128], identf[0:64, 0:64]),
                          reads=[oT[0], identf], writes=[PF[3]], last=(h == 3))
                to, ao = acc(0, s % 2)
                tn, an = acc(0, (s + 1) % 2)
                kb.op("dve", lambda e, ao=ao, an=an: e.tensor_tensor(an, ao, PF[3][:, 0:256], ALU.add), reads=[to, PF[3]], writes=[tn])
                kb.op("pool", lambda e: e.memset(V1[:], 1.0), writes=[V1])
                V1c = V1[:].rearrange("p a t d -> p (a t) d")
                KsT = KTW[0:64, :].rearrange("p (j m) -> p j m", j=2)
                first = True
                for ch in range(4):
                    for ppl in range(16):
                        pp = ch * 16 + ppl
                        g = Gs[pp % 4]
                        gq = Gv[pp % 4]
                        kb.gather(g, gq, cnsa, cnsa[:, :], idx_l, idx_l[:, s, pp:pp + 1])
                        for j in range(2):
                            kb.mm(lambda e, j=j, gq=gq: e.transpose(PT[1][0:64, j * 128:(j + 1) * 128], gq[:, j * 256 + 128:j * 256 + 192], identb[:]),
                                  reads=[g, identb], writes=[PT[1]], last=(j == 1))
                        kb.op("dve", lambda e, ppl=ppl: e.tensor_copy(KsT[:, :, ppl * 128:(ppl + 1) * 128],
                                                                      PT[1][0:64, 0:256].rearrange("p (j t) -> p j t", j=2)), reads=[PT[1]], writes=[KTW])
                        g4v = gq.rearrange("p (j r d) -> p j r d", j=2, r=4)
                        kb.op("dve", lambda e, ppl=ppl, g4v=g4v: e.tensor_copy(V1c[:, ppl * 2:ppl * 2 + 2, 0:64], g4v[:, :, 3, :]), reads=[g], writes=[V1])
                    psc = PF[ch % 2]
                    for t_ in range(32):
                        ppl, j = divmod(t_, 2)
                        kb.mm(lambda e, t_=t_, ppl=ppl, j=j, psc=psc: e.matmul(psc[:, t_ * 4:(t_ + 1) * 4], KsT[:, j, ppl * 128:(ppl + 1) * 128], qT_pad[:, 0:4],
                                                                            start=True, stop=True), reads=[KTW, qT_pad], writes=[psc], last=(t_ == 31))
                    kb.op("act", lambda e, psc=psc: e.activation(Ef[:], psc[:, 0:128], AF.Exp), reads=[psc], writes=[Ef])
                    et = ETs[ch % 2]
                    kb.op("dve", lambda e, et=et, ch=ch: e.tensor_tensor(et[:].rearrange("p (a b) -> p a b", b=8), Ef[:].rearrange("p (a b) -> p a b", b=8),
                                                                       V(selk, ch * 16, [[1, 16], [0, 8]]), ALU.mult), reads=[Ef, selk], writes=[et])
                    for t_ in range(32):
                        kb.mm(lambda e, t_=t_, et=et, first=first: e.matmul(PF[5][0:65, 0:4], V1c[:, t_, :], et[:, t_ * 4:(t_ + 1) * 4],
                                                                          start=(first and t_ == 0), stop=False), reads=[V1, et], writes=[PF[5]], last=False)
                    first = False
                kb.mm(lambda e: e.matmul(PF[3][:, 0:4], kT3[:, 1, :], qT_pad[:, 0:4], start=True, stop=True), reads=[kT3, qT_pad], writes=[PF[3]])
                kb.op("act", lambda e: e.activation(Ef[:, 0:4], PF[3][:, 0:4], AF.Exp), reads=[PF[3]], writes=[Ef])
                kb.op("dve", lambda e, oh=oh: e.tensor_scalar(ETn[:, 0:4], Ef[:, 0:4], oh, None, ALU.mult), reads=[Ef, identf], writes=[ETn])
                kb.mm(lambda e: e.matmul(PF[5][0:65, 0:4], V1n[:, 0, :], ETn[:, 0:4], start=False, stop=True), reads=[V1n, ETn], writes=[PF[5]])
                nsa_branch_finish(s, 1)
                kb.dma("pool", Wt, Wt[:], cwin, cwin[l, s].rearrange("(kt p) c -> p kt c", p=128)[:, :, 0:64])
                for kt in range(4):
                    kb.mm(lambda e, kt=kt: e.transpose(PT[1][0:64, kt * 128:(kt + 1) * 128], Wt[:, kt, :], identb[:]),
                          reads=[Wt, identb], writes=[PT[1]], last=(kt == 3))
                kb.op("dve", lambda e: e.tensor_copy(KwT[:], PT[1][0:64, 0:512].rearrange("p (k t) -> p k t", k=4)), reads=[PT[1]], writes=[KwT])
                kb.op("pool", lambda e: e.memset(V1w[:], 1.0), writes=[V1w])
                kb.dma("pool", V1w, V1w[:, :, 0:64], cwin, cwin[l, s].rearrange("(kt p) c -> p kt c", p=128)[:, :, 64:128])
                kb.op("pool", lambda e: e.memset(V1w[0:1, 0, :], 0.0), writes=[V1w])
                for kt in range(4):
                    kb.mm(lambda e, kt=kt: e.matmul(PF[3][:, kt * 4:(kt + 1) * 4], KwT[:, kt, :], qT_pad[:, 0:4], start=True, stop=True),
                          reads=[KwT, qT_pad], writes=[PF[3]], last=False)
                kb.mm(lambda e: e.matmul(PF[3][:, 16:20], kT3[:, 2, :], qT_pad[:, 0:4], start=True, stop=True), reads=[kT3, qT_pad], writes=[PF[3]])
                kb.op("act", lambda e: e.activation(Ef[:, 0:20], PF[3][:, 0:20], AF.Exp), reads=[PF[3]], writes=[Ef])
                kb.op("dve", lambda e: e.tensor_copy(ETn[:, 0:16], Ef[:, 0:16]), reads=[Ef], writes=[ETn])
                kb.op("dve", lambda e, oh=oh: e.tensor_scalar(ETn[:, 16:20], Ef[:, 16:20], oh, None, ALU.mult), reads=[Ef, identf], writes=[ETn])
                for kt in range(4):
                    kb.mm(lambda e, kt=kt: e.matmul(PF[5][0:65, 0:4], V1w[:, kt, :], ETn[:, kt * 4:(kt + 1) * 4], start=(kt == 0), stop=False),
                          reads=[V1w, ETn], writes=[PF[5]], last=False)
                kb.mm(lambda e: e.matmul(PF[5][0:65, 0:4], V1n[:, 1, :], ETn[:, 16:20], start=False, stop=True), reads=[V1n, ETn], writes=[PF[5]])
                nsa_branch_finish(s, 2)
                kb.dma("sp", Sst[0], Sst[0][:], shg, shg[l, s].rearrange("h k v -> k h v"))
                kb.mm(lambda e, s=s: e.matmul(PF[4][0:64, 0:256], V(identf, s, [[0, 64]]), proj[:, C_HI:C_HI + 256], start=True, stop=True),
                      reads=[identf, proj], writes=[PF[4]])
                kb.op("dve", lambda e, s=s: e.tensor_tensor(Sst[1][:], Sst[0][:], V(fkT, s, [[4, 4], [0, 64]]), ALU.mult), reads=[Sst[0], fkT], writes=[Sst[1]])
                kb.op("dve", lambda e, s=s: e.tensor_tensor(Sst[2][:], PF[4][0:64, 0:256].rearrange("p (h d) -> p h d", h=4),
                                                           V(fkT, 16 + s, [[4, 4], [0, 64]]), ALU.mult), reads=[PF[4], fkT], writes=[Sst[2]])
                kb.op("dve", lambda e: e.tensor_tensor(Sst[0][:], Sst[1][:], Sst[2][:], ALU.add), reads=[Sst[1], Sst[2]], writes=[Sst[0]])
                kb.dma("sp", o_hg_s, o_hg_s[l, s].rearrange("h k v -> k h v"), Sst[0], Sst[0][:], home=Sst[0], is_output=True)
                kb.op("act", lambda e: e.copy(Sbf[0][:], Sst[0][:]), reads=[Sst[0]], writes=[Sbf[0]])
                for h in range(4):
                    kb.mm(lambda e, h=h: e.matmul(PF[2][:, h * 64:(h + 1) * 64], qtT[:, h, :], Sbf[0][:, h, :], start=True, stop=True),
                          reads=[qtT, Sbf[0]], writes=[PF[2]], last=(h == 3))
                to, ao = acc(3, s % 2)
                tn, an = acc(3, (s + 1) % 2)
                kb.op("dve", lambda e, ao=ao, an=an, oh=oh: e.scalar_tensor_tensor(an, PF[2][:, 0:256], oh, ao, ALU.mult, ALU.add),
                      reads=[PF[2], identf, to], writes=[tn])
                kb.dma("pool", Mt, Mt[:], cmem, cmem[l, s].rearrange("(mc p) c -> p mc c", p=128)[:, :, 0:256])
                for mc in range(2):
                    for h in range(4):
                        kb.mm(lambda e, mc=mc, h=h: e.transpose(PT[1][0:64, (mc * 4 + h) * 128:(mc * 4 + h + 1) * 128], Mt[:, mc, h * 64:(h + 1) * 64], identb[:]),
                              reads=[Mt, identb], writes=[PT[1]], last=(mc == 1 and h == 3))
                kb.op("dve", lambda e: e.tensor_copy(memKT[:].rearrange("p h (mc t) -> p mc h t", mc=2),
                                                     PT[1][0:64, :].rearrange("p (mc h t) -> p mc h t", mc=2, h=4)), reads=[PT[1]], writes=[memKT])
                kb.op("pool", lambda e: e.memset(memV1[:], 1.0), writes=[memV1])
                for mc_ in range(2):
                    kb.dma("pool", memV1, memV1[:, mc_, :, 0:64], cmem,
                           cmem[l, s, mc_ * 128:(mc_ + 1) * 128, 256:512].rearrange("p (h d) -> p h d", h=4))
                for mc in range(2):
                    for h in range(4):
                        kb.mm(lambda e, mc=mc, h=h: e.matmul(PF[mc][:, h * 128:(h + 1) * 128], memKT[:, h, mc * 128:(mc + 1) * 128], qmT[:, h, :],
                                                             start=True, stop=True), reads=[memKT, qmT], writes=[PF[mc]], last=(h == 3))
                    kb.op("act", lambda e, mc=mc: e.activation(eM[mc][:], PF[mc][:], AF.Exp), reads=[PF[mc]], writes=[eM[mc]])
                for h in range(4):
                    for mc in range(2):
                        kb.mm(lambda e, mc=mc, h=h: e.matmul(PF[3][:, h * 65:(h + 1) * 65], eM[mc][:, h * 128:(h + 1) * 128], memV1[:, mc, h, :],
                                                             start=(mc == 0), stop=(mc == 1)), reads=[eM[mc], memV1], writes=[PF[3]], last=(mc == 1 and h == 3))
                pvm = PF[3][:, 0:260].rearrange("p (h d) -> p h d", h=4)
                kb.op("dve", lambda e: e.reciprocal(den4[2][:], pvm[:, :, 64]), reads=[PF[3]], writes=[den4[2]])
                kb.op("dve", lambda e: e.tensor_tensor(onsa[1][:], pvm[:, :, 0:64], V(den4[2], 0, [[1, 4], [0, 64]]), ALU.mult), reads=[PF[3], den4[2]], writes=[onsa[1]])
                to, ao = acc(4, s % 2)
                tn, an = acc(4, (s + 1) % 2)
                kb.op("dve", lambda e, ao=ao, an=an, oh=oh: e.scalar_tensor_tensor(an, onsa[1][:].rearrange("p h d -> p (h d)"), oh, ao, ALU.mult, ALU.add),
                      reads=[onsa[1], identf, to], writes=[tn])
            t0, aC = acc(0, 0)
            _, aS = acc(1, 0)
            _, aW = acc(2, 0)
            _, aH = acc(3, 0)
            tm, aM = acc(4, 0)
            kb.op("act", lambda e: e.activation(gts[:], proj[:, C_G:C_G + 12], AF.Sigmoid), reads=[proj], writes=[gts])
            for bi, ab in enumerate([aC, aS, aW]):
                kb.op("dve", lambda e, bi=bi, ab=ab: e.tensor_tensor(onsa[bi][:], ab.rearrange("p (h d) -> p h d", h=4), V(gts, bi, [[3, 4], [0, 64]]), ALU.mult),
                      reads=[t0, gts], writes=[onsa[bi]])
            kb.op("dve", lambda e: e.tensor_tensor(onsa[3][:], onsa[0][:], onsa[1][:], ALU.add), reads=[onsa[0], onsa[1]], writes=[onsa[3]])
            kb.op("dve", lambda e: e.tensor_tensor(ycat[:, 0:256].rearrange("p (h d) -> p h d", h=4), onsa[3][:], onsa[2][:], ALU.add),
                  reads=[onsa[3], onsa[2]], writes=[ycat])
            kb.op("dve", lambda e: e.tensor_tensor(hg[1][:], aH, aH, ALU.mult), reads=[t0], writes=[hg[1]])
            kb.op("dve", lambda e: e.tensor_reduce(s4[0][:], hg[1][:].rearrange("p (h d) -> p h d", h=4), AX.X, ALU.add), reads=[hg[1]], writes=[s4[0]])
            r_ = rms_rstd(s4[0], 64, s4[1:4])
            kb.op("dve", lambda e: e.tensor_tensor(hg[5][:].rearrange("p (h d) -> p h d", h=4), aH.rearrange("p (h d) -> p h d", h=4),
                                                   V(r_, 0, [[1, 4], [0, 64]]), ALU.mult), reads=[t0, r_], writes=[hg[5]])
            kb.op("dve", lambda e: e.tensor_tensor(ycat[:, 256:512], hg[5][:], hgon_b[:], ALU.mult), reads=[hg[5], hgon_b], writes=[ycat])
            kb.op("dve", lambda e: e.tensor_copy(ycat[:, 768:1024], aM), reads=[tm], writes=[ycat])
            kb.dma("pool", spb, spb[:], spool, spool[l].rearrange("s r c -> (s r) c"))
            kb.op("act", lambda e: e.copy(ubf[0][:], proj[:, C_U:C_U + 256]), reads=[proj], writes=[ubf[0]])
            for g in range(4):
                kb.mm(lambda e, g=g: e.matmul(PF[4][0:64, g * 128:(g + 1) * 128], spb[:, g * 64:(g + 1) * 64], cs["bs_s"][:, g, :], start=True, stop=False),
                      reads=[spb, cs["bs_s"]], writes=[PF[4]], last=False)
                kb.mm(lambda e, g=g: e.matmul(PF[4][0:64, g * 128:(g + 1) * 128], ubf[0][:, g * 64:(g + 1) * 64], cs["bself"][:, g, :], start=False, stop=True),
                      reads=[ubf[0], cs["bself"]], writes=[PF[4]], last=(g == 3))
            kb.op("act", lambda e: e.copy(dltT[:], PF[4][0:64, :].rearrange("p (g t) -> p g t", g=4)), reads=[PF[4]], writes=[dltT])
            for g in range(4):
                kb.mm(lambda e, g=g: e.matmul(PF[1][:, g * 64:(g + 1) * 64], dltT[:, g, :], PW[:, g, :], start=True, stop=True),
                      reads=[dltT, PW], writes=[PF[1]], last=(g == 3))
            kb.op("dve", lambda e: e.tensor_tensor(ycat[:, 512:768], PF[1][:, 0:256], psc_b[:], ALU.mult), reads=[PF[1], psc_b], writes=[ycat])

        def prompt_mem(l, b):
            for mc in range(2):
                kb.dma("sp", x_t, x_t[:], memp, memp[b, mc * 128:(mc + 1) * 128, :])
                norm_transpose(x_t, mgcol)
                for kc in range(8):
                    kb.mm(lambda e, kc=kc: e.matmul(PF[0][:], hT[:, kc, :], KTW[:, kc * 512:(kc + 1) * 512], start=(kc == 0), stop=(kc == 7)),
                          reads=[hT, KTW], writes=[PF[0]], last=(kc == 7))
                kb.op("act", lambda e: e.copy(mkv[:], PF[0][:]), reads=[PF[0]], writes=[mkv])
                kb.op("dve", lambda e: e.tensor_tensor(hg[0][:], mkv[:, 0:256], mkv[:, 0:256], ALU.mult), reads=[mkv], writes=[hg[0]])
                kb.op("dve", lambda e: e.tensor_reduce(s4[0][:], hg[0][:].rearrange("p (h d) -> p h d", h=4), AX.X, ALU.add),
                      reads=[hg[0]], writes=[s4[0]])
                r = rms_rstd(s4[0], 64, s4[1:4])
                kb.op("dve", lambda e: e.tensor_tensor(hg[1][:].rearrange("p (h d) -> p h d", h=4),
                                                       mkv[:, 0:256].rearrange("p (h d) -> p h d", h=4),
                                                       V(r, 0, [[1, 4], [0, 64]]), ALU.mult), reads=[mkv, r], writes=[hg[1]])
                kb.op("dve", lambda e: e.tensor_tensor(mrow[:, 0], hg[1][:].rearrange("p (h d) -> p h d", h=4),
                                                       V(mkn_b, 0, [[0, 4], [1, 64]]), ALU.mult), reads=[hg[1], mkn_b], writes=[mrow])
                kb.op("pool", lambda e: e.tensor_copy(mrow[:, 1], mkv[:, 256:512].rearrange("p (h d) -> p h d", h=4)),
                      reads=[mkv], writes=[mrow])
                kb.dma("sp", o_mem_p, o_mem_p[l, b, mc * 128:(mc + 1) * 128, :], mrow, mrow[:].rearrange("p a h d -> p (a h d)"),
                       home=mrow, is_output=True)
                kb.op("act", lambda e: e.copy(hgb[0][:].rearrange("p (h d) -> p h d", h=4), mrow[:, 0]), reads=[mrow], writes=[hgb[0]])
                kb.op("dve", lambda e, mc=mc: e.tensor_copy(memV1[:, mc, :, 0:64], mrow[:, 1]), reads=[mrow], writes=[memV1])
                for h in range(4):
                    kb.mm(lambda e, h=h: e.transpose(PT[1][0:64, h * 128:(h + 1) * 128], hgb[0][:, h * 64:(h + 1) * 64], identb[:]),
                          reads=[hgb[0], identb], writes=[PT[1]], last=(h == 3))
                kb.op("act", lambda e, mc=mc: e.copy(memKT[:, :, mc * 128:(mc + 1) * 128],
                                                     PT[1][0:64, 0:512].rearrange("p (h t) -> p h t", h=4)),
                      reads=[PT[1]], writes=[memKT])

        def prompt_tile(l, b, i, smp=False):
            src = xp if l == 0 else xmid
            dst = xmid if l == 0 else y_p
            DBG2 = int(os.environ.get("KDBG2", "99"))
            xin = x_t
            xs_src = xs if l == 0 else xsmid
            if smp:
                kb.dma("sp", x_t, x_t[0:SB_, :], xs_src, xs_src[:, :])
            else:
                kb.dma("sp", x_t, x_t[:], src, src[b, i * 128:(i + 1) * 128, :])
            if i >= 1 and DBG2 < 1:
                return
            norm_transpose(xin, gcol)
            if i >= 1 and DBG2 < 2:
                return
            for n0 in range(0, NW, 512):
                if i >= 1 and DBG2 < 3 + n0 // 512:
                    return
                n1 = min(NW, n0 + 512)
                pb = PF[(n0 // 512) % 2]
                for kc in range(8):
                    kb.mm(lambda e, kc=kc, pb=pb, n0=n0, n1=n1: e.matmul(pb[:, 0:n1 - n0], hT[:, kc, :], W_in[:, kc, n0:n1],
                                                                        start=(kc == 0), stop=(kc == 7)),
                          reads=[hT, W_in], writes=[pb], last=(kc == 7))
                evac(proj, proj[:, n0:n1], pb, pb[:, 0:n1 - n0])
            if STAGE < 3:
                return
            kb.op("dve", lambda e: e.tensor_tensor(sq11[:], proj[:, 0:704], proj[:, 0:704], ALU.mult), reads=[proj], writes=[sq11])
            kb.op("dve", lambda e: e.tensor_reduce(s11[0][:], sq11[:].rearrange("p (h d) -> p h d", h=11), AX.X, ALU.add),
                  reads=[sq11], writes=[s11[0]])
            r = rms_rstd(s11[0], 64, s11[1:4])
            kb.op("dve", lambda e: e.tensor_tensor(qkn[:], proj[:, 0:704].rearrange("p (h d) -> p h d", h=11),
                                                   V(r, 0, [[1, 11], [0, 64]]), ALU.mult), reads=[proj, r], writes=[qkn])
            kb.op(PENG, lambda e: e.tensor_tensor(qkg[:], qkn[:], gain11[:], ALU.mult), reads=[qkn, gain11], writes=[qkg])
            if SUB < 1:
                return
            if smp:
                kb.dma("sp", rope_t, rope_t[:], cd["cs_s"], bass.AP(cd["cs_s"].h, 0, [[0, 128], [32, 2], [1, 32]]))
            else:
                kb.dma("sp", rope_t, rope_t[:, 0, :], cd["cos_p"], cd["cos_p"][:, i, :])
                kb.dma("sp", rope_t, rope_t[:, 1, :], cd["sin_p"], cd["sin_p"][:, i, :])
                kb.dma("sp", cmpb_t, cmpb_t[:], cd["cmpb"], cd["cmpb"][:, i, :])
                kb.dma("sp", selb_t, selb_t[:], cd["selb"], cd["selb"][:, i, :])
            cosb = V(rope_t, 0, [[0, 7], [1, 32]])
            sinb = V(rope_t, 32, [[0, 7], [1, 32]])
            x1 = qkg[:, 0:7, 0:32]
            x2 = qkg[:, 0:7, 32:64]
            kb.op("dve", lambda e: e.tensor_tensor(rt[0][:], x1, cosb, ALU.mult), reads=[qkg, rope_t], writes=[rt[0]])
            kb.op(PENG, lambda e: e.tensor_tensor(rt[1][:], x2, sinb, ALU.mult), reads=[qkg, rope_t], writes=[rt[1]])
            kb.op("dve", lambda e: e.tensor_tensor(qkr[:, :, 0:32], rt[0][:], rt[1][:], ALU.subtract), reads=[rt[0], rt[1]], writes=[qkr])
            kb.op(PENG, lambda e: e.tensor_tensor(rt[2][:], x2, cosb, ALU.mult), reads=[qkg, rope_t], writes=[rt[2]])
            kb.op("dve", lambda e: e.tensor_tensor(rt[3][:], x1, sinb, ALU.mult), reads=[qkg, rope_t], writes=[rt[3]])
            kb.op(PENG, lambda e: e.tensor_tensor(qkr[:, :, 32:64], rt[2][:], rt[3][:], ALU.add), reads=[rt[2], rt[3]], writes=[qkr])
            if SUB < 2:
                return
            kb.op("pool", lambda e: e.tensor_copy(rows[:, 0:4:2, :], qkr[:, 4:6, :]), reads=[qkr], writes=[rows])
            kb.op("pool", lambda e: e.tensor_copy(rows[:, 1:4:2, :], proj[:, C_VC:C_VC + 128].rearrange("p (a d) -> p a d", a=2)),
                  reads=[proj], writes=[rows])
            if smp:
                kb.dma("sp", o_nsa_s, o_nsa_s[l], rows, rows[0:SB_].rearrange("p a d -> p (a d)"), home=rows, is_output=True)
                kb.op("pool", lambda e: e.tensor_copy(wrows[:, 0, :], qkr[:, 6, :]), reads=[qkr], writes=[wrows])
                kb.op("pool", lambda e: e.tensor_copy(wrows[:, 1, :], proj[:, C_VW:C_VW + 64]), reads=[proj], writes=[wrows])
                for s in range(SB_):
                    kb.dma("sp", o_win_s, o_win_s[l, s, 0:511, :], cwin, cwin[l, s, 1:512, :], home=wrows, is_output=True)
                    kb.dma("sp", o_win_s, o_win_s[l, s, 511:512, :], wrows, wrows[s:s + 1].rearrange("p a d -> p (a d)"), home=wrows, is_output=True)
                    kb.dma("sp", o_pool_s, o_pool_s[l, s, 0:14, :], spool, spool[l, s, 1:15, :], home=wrows, is_output=True)
                    kb.dma("sp", o_pool_s, o_pool_s[l, s, 14:15, :], proj, proj[s:s + 1, C_U:C_U + 256], home=proj, is_output=True)
            else:
                kb.dma("sp", o_nsa_p, o_nsa_p[l, b, i * 128:(i + 1) * 128, :], rows, rows[:].rearrange("p a d -> p (a d)"),
                       home=rows, is_output=True)
            if (not smp) and i >= NT - 4:
                kb.op("pool", lambda e: e.tensor_copy(wrows[:, 0, :], qkr[:, 6, :]), reads=[qkr], writes=[wrows])
                kb.op("pool", lambda e: e.tensor_copy(wrows[:, 1, :], proj[:, C_VW:C_VW + 64]), reads=[proj], writes=[wrows])
                kb.dma("sp", o_win_p, o_win_p[l, b, (i - (NT - 4)) * 128:(i - (NT - 4) + 1) * 128, :], wrows,
                       wrows[:].rearrange("p a d -> p (a d)"), home=wrows, is_output=True)
            if (not smp) and i == NT - 1:
                kb.dma("sp", o_pool_p, o_pool_p[l, b], proj, proj[113:128, C_U:C_U + 256], home=proj, is_output=True)
            if SUB < 3:
                return
            kb.op("dve", lambda e: e.tensor_scalar(qk_bf[:, 0:4, :], qkr[:, 0:4, :], 0.125, None, ALU.mult), reads=[qkr], writes=[qk_bf])
            kb.op("dve", lambda e: e.tensor_copy(qk_bf[:, 4:7, :], qkr[:, 4:7, :]), reads=[qkr], writes=[qk_bf])
            kb.op("dve", lambda e: e.tensor_scalar(qk_bf[:, 7:11, :], qkg[:, 7:11, :], 0.125, None, ALU.mult), reads=[qkg], writes=[qk_bf])
            if SUB < 4:
                return
            qk2d = qk_bf[:].rearrange('p h d -> p (h d)')
            for h in range(7):
                kb.mm(lambda e, h=h: e.transpose(PT[1][0:64, h * 128:(h + 1) * 128], qk2d[:, h * 64:(h + 1) * 64], identb[:]),
                      reads=[qk_bf, identb], writes=[PT[1]], last=(h == 6))
            if SUB < 5:
                return
            kb.op("dve", lambda e: e.tensor_copy(qT_all[:], PT[1][0:64, 0:512].rearrange("p (h t) -> p h t", h=4)), reads=[PT[1]], writes=[qT_all])
            if SUB < 6:
                return
            if smp:
                kb.op("dve", lambda e: e.tensor_copy(kT3[:], PT[1][0:64, 512:896].rearrange("p (h t) -> p h t", h=3)), reads=[PT[1]], writes=[kT3])
            else:
                kb.op("dve", lambda e: e.tensor_copy(KTW[0:64, :].rearrange("p (a s) -> p a s", a=2)[:, :, i * 128:(i + 1) * 128],
                                                     PT[1][0:64, 640:896].rearrange("p (h t) -> p h t", h=2)), reads=[PT[1]], writes=[KTW])
            if SUB < 7:
                return
            for h in range(4):
                kb.mm(lambda e, h=h: e.transpose(PT[1][0:64, h * 128:(h + 1) * 128], qk2d[:, (7 + h) * 64:(8 + h) * 64], identb[:]),
                      reads=[qk_bf, identb], writes=[PT[1]], last=(h == 3))
            kb.op("dve", lambda e: e.tensor_copy(qmT[:], PT[1][0:64, 0:512].rearrange("p (h t) -> p h t", h=4)), reads=[PT[1]], writes=[qmT])
            if smp:
                sample_mixers(l)
                kb.dma("sp", x_t, x_t[0:SB_, :], xs_src, xs_src[:, :])
            if not smp:
                kb.op("dve", lambda e: e.tensor_copy(V1[:, :, i, 0:64], proj[:, C_VS:C_VS + 128].rearrange("p (a d) -> p a d", a=2)),
                      reads=[proj], writes=[V1])
                if STAGE < 4:
                    return
                kb.op("dve", lambda e: e.tensor_copy(stg[:, 0], V(qkr, 4 * 64, [[0, 2], [1, 64]])), reads=[qkr], writes=[stg])
                kb.op("pool", lambda e: e.tensor_copy(stg[:, 1], V(proj, C_VC, [[0, 2], [1, 64]])), reads=[proj], writes=[stg])
                for a in range(2):
                    kb.mm(lambda e, a=a: e.transpose(PT[0][:, a * 128:(a + 1) * 128], stg[:, a].rearrange("p j d -> p (j d)"), identb[:]),
                          reads=[stg, identb], writes=[PT[0]], last=(a == 1))
                ptv = PT[0][:, 0:256].rearrange("p (a m j) -> p a m j", a=2, j=2)
                kb.op("dve", lambda e: e.tensor_copy(kT2[0:64], ptv[0:64, :, :, 0]), reads=[PT[0]], writes=[kT2])
                kb.op("dve", lambda e: e.tensor_copy(kT2[64:128], ptv[64:128, :, :, 1]), reads=[PT[0]], writes=[kT2])
                for a in range(2):
                    for hc in range(2):
                        for c in range(16):
                            kb.mm(lambda e, a=a, hc=hc, c=c: e.matmul(PF[2][:, (a * 2 + hc) * 4:(a * 2 + hc) * 4 + 4],
                                                                      W1[:, a, c, hc * 128:(hc + 1) * 128],
                                                                      V(kT2, a * 64 + c, [[16, 4]]), start=(c == 0), stop=(c == 15)),
                                  reads=[W1, kT2], writes=[PF[2]], last=(c == 15))
                for a in range(2):
                    for hc in range(2):
                        kb.op("act", lambda e, a=a, hc=hc: e.activation(hidT[:, a, hc, :], PF[2][:, (a * 2 + hc) * 4:(a * 2 + hc) * 4 + 4],
                                                                        AF.Silu, bias=biasT[:, a, hc:hc + 1]),
                              reads=[PF[2], biasT], writes=[hidT])
                for a in range(2):
                    for hc in range(2):
                        kb.mm(lambda e, a=a, hc=hc: e.matmul(PF[4][0:64, a * 4:a * 4 + 4], W2[:, a, hc, :], hidT[:, a, hc, :],
                                                             start=(hc == 0), stop=(hc == 1)),
                              reads=[W2, hidT], writes=[PF[4]], last=(hc == 1))
                kb.op("dve", lambda e: e.tensor_copy(CC[:, :, 4 * i:4 * i + 4], PF[4][0:64, 0:8].rearrange("p (a n) -> p a n", a=2)),
                      reads=[PF[4]], writes=[CC])
                if STAGE < 5:
                    return
                for h in range(4):
                    kb.mm(lambda e, h=h: e.matmul(PF[3][:, h * 64:(h + 1) * 64], qT_all[:, h, :], CC[:, 0, :], start=True, stop=True),
                          reads=[qT_all, CC], writes=[PF[3]], last=(h == 3))
                kb.op("dve", lambda e: e.tensor_tensor(sm[0][:], PF[3][:, 0:256].rearrange("p (h n) -> p h n", h=4),
                                                       V(cmpb_t, 0, [[0, 4], [1, 64]]), ALU.add), reads=[PF[3], cmpb_t], writes=[sm[0]])
                kb.op("dve", lambda e: e.tensor_reduce(s4[0][:], sm[0][:], AX.X, ALU.max), reads=[sm[0]], writes=[s4[0]])
                kb.op("dve", lambda e: e.tensor_scalar(s4[1][:], s4[0][:], -1000.0, None, ALU.max), reads=[s4[0]], writes=[s4[1]])
                kb.op("dve", lambda e: e.tensor_tensor(sm[1][:], sm[0][:], V(s4[1], 0, [[1, 4], [0, 64]]), ALU.subtract),
                      reads=[sm[0], s4[1]], writes=[sm[1]])
                kb.op("act", lambda e: e.activation(sm[2][:], sm[1][:], AF.Exp), reads=[sm[1]], writes=[sm[2]])
                kb.op("dve", lambda e: e.tensor_reduce(s4[2][:], sm[2][:], AX.X, ALU.add), reads=[sm[2]], writes=[s4[2]])
                kb.op("dve", lambda e: e.tensor_scalar(s4[3][:], s4[2][:], 1e-30, None, ALU.max), reads=[s4[2]], writes=[s4[3]])
                kb.op("dve", lambda e: e.reciprocal(s4[4][:], s4[3][:]), reads=[s4[3]], writes=[s4[4]])
                kb.op("dve", lambda e: e.tensor_tensor(pcf[:], sm[2][:], V(s4[4], 0, [[1, 4], [0, 64]]), ALU.mult), reads=[sm[2], s4[4]], writes=[pcf])
                kb.op("act", lambda e: e.copy(pcb[:], pcf[:]), reads=[pcf], writes=[pcb])
                for h in range(4):
                    kb.mm(lambda e, h=h: e.transpose(PT[1][0:64, h * 128:(h + 1) * 128], pcb[:, h, :], identb[:]),
                          reads=[pcb, identb], writes=[PT[1]], last=False)
                kb.mm(lambda e: e.transpose(PT[1][0:64, 512:576], CC[:, 1, :], identb[0:64, 0:64]), reads=[CC, identb], writes=[PT[1]])
                kb.op("dve", lambda e: e.tensor_copy(pTc[:], PT[1][0:64, 0:512].rearrange("p (h t) -> p h t", h=4)), reads=[PT[1]], writes=[pTc])
                kb.op("dve", lambda e: e.tensor_copy(vcc[:], PT[1][0:64, 512:576]), reads=[PT[1]], writes=[vcc])
                for h in range(4):
                    kb.mm(lambda e, h=h: e.matmul(PF[3][:, 256 + h * 64:256 + (h + 1) * 64], pTc[:, h, :], vcc[:], start=True, stop=True),
                          reads=[pTc, vcc], writes=[PF[3]], last=(h == 3))
                kb.op("act", lambda e: e.activation(gts[:], proj[:, C_G:C_G + 12], AF.Sigmoid), reads=[proj], writes=[gts])
                kb.op("dve", lambda e: e.tensor_tensor(onsa[0][:], PF[3][:, 256:512].rearrange("p (h d) -> p h d", h=4),
                                                       V(gts, 0, [[3, 4], [0, 64]]), ALU.mult), reads=[PF[3], gts], writes=[onsa[0]])
                kb.op("dve", lambda e: e.tensor_reduce(imp[:], V(hg[3], 0, [[2, 32], [64, 4], [1, 2]]), AX.XY, ALU.add), reads=[pcf], writes=[imp])
                kb.op("dve", lambda e: e.tensor_tensor(score[:], imp[:], selb_t[:], ALU.add), reads=[imp, selb_t], writes=[score])
                kb.op("dve", lambda e: e.tensor_tensor(cmpt[:], V(score, 0, [[0, 32], [1, 32]]), V(score, 0, [[1, 32], [0, 32]]), ALU.is_gt),
                      reads=[score], writes=[cmpt])
                kb.op("dve", lambda e: e.tensor_reduce(rank[:], cmpt[:], AX.X, ALU.add), reads=[cmpt], writes=[rank])
                kb.op("dve", lambda e: e.tensor_scalar(negsel[:, 0:32], rank[:], 15.5, NEG, ALU.is_ge, ALU.mult), reads=[rank], writes=[negsel])
                kb.mm(lambda e: e.transpose(PT[1][0:64, 0:128], negsel[:], identb[:]), reads=[negsel, identb], writes=[PT[1]])
                kb.op("dve", lambda e: e.tensor_copy(negselT[:], PT[1][0:32, 0:128]), reads=[PT[1]], writes=[negselT])
                if STAGE < 6:
                    return
                qflat = qT_all[:].rearrange("p h t -> p (h t)")

                def attn(branch, kts, cache_idx, v_idx, pacc, ot):
                    nk = len(kts)

                    def score(n_):
                        kt = kts[n_]
                        ps_ = PF[2 + (n_ % 2)]
                        ex = expT[n_ % 2]
                        extra = []
                        if branch == "s":
                            extra.append(("sel", None))
                        if kt == i:
                            extra.append(("mask", cs["causb"]))
                        if branch == "w" and kt == i - 4:
                            extra.append(("mask", cs["edgeb"]))
                        kb.mm(lambda e: e.matmul(ps_[:], KTW[0:64, cache_idx * SEQ + kt * 128:cache_idx * SEQ + (kt + 1) * 128], qflat,
                                                 start=True, stop=(len(extra) == 0)),
                              reads=[KTW, qT_all], writes=[ps_], last=(len(extra) == 0))
                        for xi, (kind, mt) in enumerate(extra):
                            lastx = xi == len(extra) - 1
                            if kind == "sel":
                                kb.mm(lambda e: e.matmul(ps_[:].rearrange("p (h t) -> p h t", h=4), cs["e32"][:, kt * 128:(kt + 1) * 128],
                                                         V(negselT, 0, [[0, 4], [1, 128]]), start=False, stop=lastx),
                                      reads=[cs["e32"], negselT], writes=[ps_], last=lastx)
                            else:
                                kb.mm(lambda e: e.matmul(ps_[:].rearrange("p (h t) -> p h t", h=4), identb[:], V(mt, 0, [[0, 4], [1, 128]]),
                                                         start=False, stop=lastx), reads=[identb, mt], writes=[ps_], last=lastx)
                        kb.op("act", lambda e: e.activation(ex[:], ps_[:], AF.Exp), reads=[ps_], writes=[ex])

                    def pv(n_):
                        kt = kts[n_]
                        ex = expT[n_ % 2]
                        kb.mm(lambda e: e.matmul(pacc[0:65, :], V1[:, v_idx, kt, :], ex[:], start=(n_ == 0), stop=(n_ == nk - 1)),
                              reads=[V1, ex], writes=[pacc], last=(n_ == nk - 1))

                    score(0)
                    for n_ in range(nk):
                        if n_ + 1 < nk:
                            score(n_ + 1)
                        pv(n_)
                    kb.op("act", lambda e: e.copy(ot[:], pacc[0:65, :]), reads=[pacc], writes=[ot])

                for bi, (br, kts, cidx) in enumerate([("s", list(range(0, i + 1)), 0), ("w", list(range(max(0, i - 4), i + 1)), 1)]):
                    attn(br, kts, cidx, bi, PF[5], oT[bi])
                    pback = PF[2 + bi]
                    for h in range(4):
                        kb.mm(lambda e, bi=bi, h=h, pback=pback: e.transpose(pback[:, h * 65:(h + 1) * 65], oT[bi][:, h * 128:(h + 1) * 128], identf[0:65, 0:65]),
                              reads=[oT[bi], identf], writes=[pback], last=(h == 3))
                    pv = pback[:, 0:260].rearrange("p (h d) -> p h d", h=4)
                    kb.op("dve", lambda e, pv=pv, bi=bi: e.reciprocal(s4[5 + bi][:], pv[:, :, 64]), reads=[pback], writes=[s4[5 + bi]])
                    kb.op("dve", lambda e, bi=bi: e.tensor_tensor(s4[bi][:], s4[5 + bi][:], V(gts, 1 + bi, [[3, 4]]), ALU.mult),
                          reads=[s4[5 + bi], gts], writes=[s4[bi]])
                    kb.op("dve", lambda e, pv=pv, bi=bi: e.tensor_tensor(onsa[1 + bi][:], pv[:, :, 0:64], V(s4[bi], 0, [[1, 4], [0, 64]]), ALU.mult),
                          reads=[pback, s4[bi]], writes=[onsa[1 + bi]])
                kb.op(PENG, lambda e: e.tensor_tensor(onsa[3][:], onsa[0][:], onsa[1][:], ALU.add), reads=[onsa[0], onsa[1]], writes=[onsa[3]])
                kb.op(PENG, lambda e: e.tensor_tensor(ycat[:, 0:256].rearrange("p (h d) -> p h d", h=4), onsa[3][:], onsa[2][:], ALU.add),
                      reads=[onsa[3], onsa[2]], writes=[ycat])

                if STAGE < 7:
                    return
                kb.op("act", lambda e: e.activation(hg[0][:], proj[:, C_HF:C_HF + 256], AF.Sigmoid), reads=[proj], writes=[hg[0]])
                kb.op("dve", lambda e: e.tensor_tensor(hg[1][:], hg[0][:], oml_b[:], ALU.mult), reads=[hg[0], oml_b], writes=[hg[1]])
                kb.op("dve", lambda e: e.tensor_tensor(hg[2][:], hg[1][:], lb_b[:], ALU.add), reads=[hg[1], lb_b], writes=[hg[2]])
                kb.op("act", lambda e: e.activation(hg[3][:], hg[2][:], AF.Ln), reads=[hg[2]], writes=[hg[3]])
                kb.op("dve", lambda e: e.tensor_scalar(hg[4][:], hg[2][:], -1.0, 1.0, ALU.mult, ALU.add), reads=[hg[2]], writes=[hg[4]])
                kb.mm(lambda e: e.matmul(PF[0][:, 0:256], cs["uincl"][:], hg[3][:], start=True, stop=True), reads=[cs["uincl"], hg[3]], writes=[PF[0]], last=False)
                kb.mm(lambda e: e.matmul(PF[0][:, 256:512], cs["urest"][:], hg[3][:], start=True, stop=True), reads=[cs["urest"], hg[3]], writes=[PF[0]])
                for h in range(4):
                    kb.mm(lambda e, h=h: e.matmul(PF[4][0:64, h * 2:h * 2 + 2], hg[3][:, h * 64:(h + 1) * 64], cs["cmask"][:], start=True, stop=True),
                          reads=[hg[3], cs["cmask"]], writes=[PF[4]], last=(h == 3))
                kb.op("act", lambda e: e.activation(eGl[:].rearrange("p h c -> p (h c)"), PF[4][0:64, 0:8], AF.Exp), reads=[PF[4]], writes=[eGl])
                kb.op("act", lambda e: e.activation(hg[0][:], PF[0][:, 0:256], AF.Exp), reads=[PF[0]], writes=[hg[0]])
                kb.op("act", lambda e: e.activation(hg[1][:], PF[0][:, 0:256], AF.Exp, scale=-1.0), reads=[PF[0]], writes=[hg[1]])
                kb.op("act", lambda e: e.activation(hg[5][:], PF[0][:, 256:512], AF.Exp), reads=[PF[0]], writes=[hg[5]])
                kb.op("dve", lambda e: e.tensor_tensor(hgb[0][:], proj[:, C_HQ:C_HQ + 256], hg[0][:], ALU.mult), reads=[proj, hg[0]], writes=[hgb[0]])
                kb.op(PENG, lambda e: e.tensor_tensor(hgb[1][:], hg[4][:], hg[1][:], ALU.mult), reads=[hg[4], hg[1]], writes=[hgb[1]])
                for c in range(2):
                    kb.op("dve", lambda e, c=c: e.scalar_tensor_tensor(khz[:, c, :], hg[4][:], cs["cmask"][:, c:c + 1], hg[5][:], ALU.mult, ALU.mult),
                          reads=[hg[4], hg[5], cs["cmask"]], writes=[khz])
                kb.op("act", lambda e: e.copy(hgb[2][:], proj[:, C_HI:C_HI + 256]), reads=[proj], writes=[hgb[2]])
                for h in range(4):
                    kb.mm(lambda e, h=h: e.transpose(PT[1][0:64, h * 128:(h + 1) * 128], hgb[0][:, h * 64:(h + 1) * 64], identb[:]),
                          reads=[hgb[0], identb], writes=[PT[1]], last=False)
                for h in range(4):
                    kb.mm(lambda e, h=h: e.transpose(PT[1][0:64, 512 + h * 128:512 + (h + 1) * 128], hgb[1][:, h * 64:(h + 1) * 64], identb[:]),
                          reads=[hgb[1], identb], writes=[PT[1]], last=(h == 3))
                pq = PT[1][0:64, 0:512].rearrange("p (h t) -> p h t", h=4)
                kb.op("dve", lambda e: e.tensor_copy(qtT[:], pq), reads=[PT[1]], writes=[qtT])
                kb.op("dve", lambda e: e.tensor_copy(qtTz[:, :, 0, 0:64], pq[:, :, 0:64]), reads=[PT[1]], writes=[qtTz])
                kb.op("dve", lambda e: e.tensor_copy(qtTz[:, :, 1, 64:128], pq[:, :, 64:128]), reads=[PT[1]], writes=[qtTz])
                kb.op("dve", lambda e: e.tensor_copy(ktT[:], PT[1][0:64, 512:1024].rearrange("p (h t) -> p h t", h=4)), reads=[PT[1]], writes=[ktT])
                for h in range(4):
                    kb.mm(lambda e, h=h: e.matmul(PF[0][:, h * 128:(h + 1) * 128], ktT[:, h, :], qtT[:, h, :], start=True, stop=True),
                          reads=[ktT, qtT], writes=[PF[0]], last=(h == 3))
                kb.op("dve", lambda e: e.tensor_tensor(ATs[:], PF[0][:].rearrange("p (h t) -> p h t", h=4), V(cs["bdmask"], 0, [[0, 4], [1, 128]]), ALU.mult),
                      reads=[PF[0], cs["bdmask"]], writes=[ATs])
                kb.op("act", lambda e: e.copy(Sbf[0][:], Sst[0][:]), reads=[Sst[0]], writes=[Sbf[0]])
                for c in range(2):
                    for h in range(4):
                        kb.mm(lambda e, c=c, h=h: e.matmul(PF[4][0:64, 64 + h * 64:64 + (h + 1) * 64],
                                                           khz[:, c, h * 64:(h + 1) * 64], hgb[2][:, h * 64:(h + 1) * 64], start=True, stop=True),
                              reads=[khz, hgb[2]], writes=[PF[4]], last=(h == 3))
                    kb.op("dve", lambda e, c=c: e.tensor_tensor(Sst[2][:], Sst[c][:], V(eGl, c, [[2, 4], [0, 64]]), ALU.mult),
                          reads=[Sst[c], eGl], writes=[Sst[2]])
                    dstS = Sst[1] if c == 0 else Sst[0]
                    kb.op("dve", lambda e, dstS=dstS: e.tensor_tensor(dstS[:], Sst[2][:], PF[4][0:64, 64:320].rearrange("p (h d) -> p h d", h=4), ALU.add),
                          reads=[Sst[2], PF[4]], writes=[dstS])
                    if c == 0:
                        kb.op("act", lambda e: e.copy(Sbf[1][:], Sst[1][:]), reads=[Sst[1]], writes=[Sbf[1]])
                for h in range(4):
                    kb.mm(lambda e, h=h: e.matmul(PF[2][:, h * 64:(h + 1) * 64], ATs[:, h, :], hgb[2][:, h * 64:(h + 1) * 64],
                                                  start=True, stop=False), reads=[ATs, hgb[2]], writes=[PF[2]], last=False)
                    kb.mm(lambda e, h=h: e.matmul(PF[2][:, h * 64:(h + 1) * 64], qtTz[:, h, 0, :], Sbf[0][:, h, :], start=False, stop=False),
                          reads=[qtTz, Sbf[0]], writes=[PF[2]], last=False)
                    kb.mm(lambda e, h=h: e.matmul(PF[2][:, h * 64:(h + 1) * 64], qtTz[:, h, 1, :], Sbf[1][:, h, :], start=False, stop=True),
                          reads=[qtTz, Sbf[1]], writes=[PF[2]], last=(h == 3))
                if i == NT - 1:
                    kb.dma("sp", o_hg_p, o_hg_p[l, b].rearrange("h k v -> k h v"), Sst[0], Sst[0][:], home=Sst[0], is_output=True)
                kb.op("act", lambda e: e.copy(hg[0][:], PF[2][:, 0:256]), reads=[PF[2]], writes=[hg[0]])
                kb.op("dve", lambda e: e.tensor_tensor(hg[1][:], hg[0][:], hg[0][:], ALU.mult), reads=[hg[0]], writes=[hg[1]])
                kb.op("dve", lambda e: e.tensor_reduce(s4[0][:], hg[1][:].rearrange("p (h d) -> p h d", h=4), AX.X, ALU.add), reads=[hg[1]], writes=[s4[0]])
                r = rms_rstd(s4[0], 64, s4[1:4])
                kb.op("dve", lambda e: e.tensor_tensor(hg[5][:].rearrange("p (h d) -> p h d", h=4), hg[0][:].rearrange("p (h d) -> p h d", h=4),
                                                       V(r, 0, [[1, 4], [0, 64]]), ALU.mult), reads=[hg[0], r], writes=[hg[5]])
                kb.op(PENG, lambda e: e.tensor_tensor(ycat[:, 256:512], hg[5][:], hgon_b[:], ALU.mult), reads=[hg[5], hgon_b], writes=[ycat])

                if STAGE < 8:
                    return
                ucur = ubf[i % 2]
                uprev = ubf[(i + 1) % 2]
                kb.op("act", lambda e: e.copy(ucur[:], proj[:, C_U:C_U + 256]), reads=[proj], writes=[ucur])
                bc_ = cs["band0"] if i == 0 else cs["bandg"]
                for g in range(4):
                    kb.mm(lambda e, g=g: e.matmul(PF[4][0:64, g * 128:(g + 1) * 128], ucur[:, g * 64:(g + 1) * 64], bc_[:, g, :], start=True, stop=(i == 0)),
                          reads=[ucur, bc_], writes=[PF[4]], last=(i == 0 and g == 3))
                    if i > 0:
                        kb.mm(lambda e, g=g: e.matmul(PF[4][0:64, g * 128:(g + 1) * 128], uprev[:, g * 64:(g + 1) * 64], cs["bandp"][:, g, :], start=False, stop=True),
                              reads=[uprev, cs["bandp"]], writes=[PF[4]], last=(g == 3))
                kb.op("act", lambda e: e.copy(dltT[:], PF[4][0:64, :].rearrange("p (g t) -> p g t", g=4)), reads=[PF[4]], writes=[dltT])
                for g in range(4):
                    kb.mm(lambda e, g=g: e.matmul(PF[1][:, g * 64:(g + 1) * 64], dltT[:, g, :], PW[:, g, :], start=True, stop=True),
                          reads=[dltT, PW], writes=[PF[1]], last=(g == 3))
                kb.op("dve", lambda e: e.tensor_tensor(ycat[:, 512:768], PF[1][:, 0:256], psc_b[:], ALU.mult), reads=[PF[1], psc_b], writes=[ycat])

                for mc in range(2):
                    for h in range(4):
                        kb.mm(lambda e, mc=mc, h=h: e.matmul(PF[2 + mc][:, h * 128:(h + 1) * 128], memKT[:, h, mc * 128:(mc + 1) * 128], qmT[:, h, :],
                                                             start=True, stop=True), reads=[memKT, qmT], writes=[PF[2 + mc]], last=(h == 3))
                    kb.op("act", lambda e, mc=mc: e.activation(eM[mc][:], PF[2 + mc][:], AF.Exp), reads=[PF[2 + mc]], writes=[eM[mc]])
                for h in range(4):
                    for mc in range(2):
                        kb.mm(lambda e, mc=mc, h=h: e.matmul(PF[0][:, h * 65:(h + 1) * 65], eM[mc][:, h * 128:(h + 1) * 128], memV1[:, mc, h, :],
                                                             start=(mc == 0), stop=(mc == 1)), reads=[eM[mc], memV1], writes=[PF[0]],
                              last=(mc == 1 and h == 3))
                pv = PF[0][:, 0:260].rearrange("p (h d) -> p h d", h=4)
                kb.op("dve", lambda e: e.reciprocal(s4[7][:], pv[:, :, 64]), reads=[PF[0]], writes=[s4[7]])
                kb.op("dve", lambda e: e.tensor_tensor(ycat[:, 768:1024].rearrange("p (h d) -> p h d", h=4), pv[:, :, 0:64],
                                                       V(s4[7], 0, [[1, 4], [0, 64]]), ALU.mult), reads=[PF[0], s4[7]], writes=[ycat])

                if STAGE < 9:
                    return
            if DBG and l == 0 and not smp:
                kb.dma("sp", dbg_ycat, dbg_ycat[b, i * 128:(i + 1) * 128, :], ycat, ycat[:], home=ycat, is_output=True)
            kb.op("act", lambda e: e.activation(sz[:], proj[:, C_Z:C_Z + D], AF.Silu), reads=[proj], writes=[sz])
            kb.op("dve", lambda e: e.tensor_tensor(yg[:], ycat[:], sz[:], ALU.mult), reads=[ycat, sz], writes=[yg])
            for kc in range(8):
                kb.mm(lambda e, kc=kc: e.transpose(PT[1][:, kc * 128:(kc + 1) * 128], yg[:, kc * 128:(kc + 1) * 128], identb[:]),
                      reads=[yg, identb], writes=[PT[1]], last=(kc == 7))
            kb.op("dve", lambda e: e.tensor_copy(yT[:], PT[1][:].rearrange("p (k t) -> p k t", k=8)), reads=[PT[1]], writes=[yT])
            for ncn in range(2):
                pb = PF[ncn]
                for kc in range(8):
                    kb.mm(lambda e, kc=kc, pb=pb, ncn=ncn: e.matmul(pb[:], yT[:, kc, :], W_out[:, kc, ncn * 512:(ncn + 1) * 512],
                                                                   start=(kc == 0), stop=(kc == 7)), reads=[yT, W_out], writes=[pb], last=(kc == 7))
                kb.op("dve", lambda e, pb=pb, ncn=ncn: e.tensor_tensor(xo[:, ncn * 512:(ncn + 1) * 512], pb[:], xin[:, ncn * 512:(ncn + 1) * 512], ALU.add),
                      reads=[pb, xin], writes=[xo])
            if smp:
                if l == DEPTH - 1:
                    kb.dma("sp", y_s, y_s[:], xo, xo[0:SB_, :], home=xo, is_output=True)
                else:
                    kb.dma("sp", xsmid, xsmid[:, :], xo, xo[0:SB_, :], home=xo)
            else:
                kb.dma("sp", dst, dst[b, i * 128:(i + 1) * 128, :], xo, xo[:], home=xo, is_output=(l == DEPTH - 1))

        def prompt_seq(l, b):
            kb.op("pool", lambda e: e.memset(V1[:], 1.0), writes=[V1])
            kb.op("pool", lambda e: e.memset(memV1[:], 1.0), writes=[memV1])
            kb.op("pool", lambda e: e.memset(CC[:], 0.0), writes=[CC])
            kb.op("pool", lambda e: e.memset(qtTz[:], 0.0), writes=[qtTz])
            kb.op("pool", lambda e: e.memset(Sst[0][:], 0.0), writes=[Sst[0]])
            kb.dma("pool", KTW, KTW[:].rearrange("p (kc n) -> p kc n", kc=8), w_mem_kv, w_mem_kv[l].rearrange("(kc p) n -> p kc n", p=128))
            for _rep in range(int(os.environ.get("KMEMREP", "1"))):
                prompt_mem(l, b)
            if STAGE < 2:
                return
            for i in range(min(NT, NTILES_DBG)):
                prompt_tile(l, b, i)

        if do_sample:
            sample_init()
        for l in range(DEPTH):
            load_weights(l)
            if STAGE < 1:
                break
            if do_sample:
                sample_idx(l)
                prompt_tile(l, 0, 0, smp=True)
            if not int(os.environ.get("KPROMPT", "1")):
                continue
            for b in range(PB):
                prompt_seq(l, b)
            if STAGE < 10:
                break
        kb.finish()
        print("instructions:", kb.n_inst, "dma sems:", kb.nd, flush=True)
    return nc, [k for k in di.keys() if k not in SKIP_IN]


_CACHE = {}


def kernel(x_prompt, x_sample, mem_prompt, cache_nsa, cache_nsa_win, state_hgrn, state_pool, cache_mem,
           page_table, norm_g, w_in, w_out, nsa_qn, nsa_kn, cmp_pe, cmp_w1, cmp_w2, hg_lb, hg_on,
           pool_w, pool_scale, mem_norm, w_mem_kv, mem_qn, mem_kn):
    f = lambda a: np.ascontiguousarray(np.asarray(a, dtype=np.float32))
    if "nc" not in _CACHE:
        _CACHE["nc"], _CACHE["names"] = build_program()
        _CACHE["consts"] = host_consts()
    nc = _CACHE["nc"]
    consts = _CACHE["consts"]
    x_prompt, x_sample, mem_prompt = f(x_prompt), f(x_sample), f(mem_prompt)
    cnsa_full = f(cache_nsa).reshape(DEPTH * NPHYS * 64, 512)
    cache_nsa_win, state_hgrn, state_pool, cache_mem = f(cache_nsa_win), f(state_hgrn), f(state_pool), f(cache_mem)
    pt = np.asarray(page_table, dtype=np.int32)
    shared = {
        "norm_g": f(norm_g), "w_in": f(w_in), "w_out": f(w_out), "nsa_qn": f(nsa_qn), "nsa_kn": f(nsa_kn),
        "cmp_pe": f(cmp_pe), "cmp_w1": f(cmp_w1), "cmp_w2": f(cmp_w2), "hg_lb": f(hg_lb), "hg_on": f(hg_on),
        "pool_w": f(pool_w), "pool_scale": f(pool_scale), "mem_norm": f(mem_norm), "w_mem_kv": f(w_mem_kv),
        "mem_qn": f(mem_qn), "mem_kn": f(mem_kn), "cnsa": cnsa_full,
    }
    for name, _, _ in CONST_SPECS:
        shared["c_" + name] = consts[name]
    in_maps = []
    for c in range(NCORES):
        ps = slice(PB * c, PB * (c + 1))
        ss = slice(SB_ * c, SB_ * (c + 1))
        m = dict(shared)
        m["xp"] = np.ascontiguousarray(x_prompt[ps])
        m["memp"] = np.ascontiguousarray(mem_prompt[ps])
        m["xs"] = np.ascontiguousarray(x_sample[ss, 0, :])
        m["cwin"] = np.ascontiguousarray(cache_nsa_win[:, ss].reshape(DEPTH, SB_, 512, 128))
        m["shg"] = np.ascontiguousarray(state_hgrn[:, ss])
        m["spool"] = np.ascontiguousarray(state_pool[:, ss])
        m["cmem"] = np.ascontiguousarray(cache_mem[:, ss].reshape(DEPTH, SB_, 256, 512))
        ptc = pt[ss]
        ptb = ptc.reshape(SB_, 64, 2).transpose(2, 0, 1)
        m["ptb"] = np.ascontiguousarray(np.repeat(ptb, 64, axis=0).astype(np.int32))
        in_maps.append(m)
    declared = set(_CACHE["names"])
    in_maps = [{k: v for k, v in m.items() if k in declared} for m in in_maps]
    ncores = int(os.environ.get("KCORES", str(NCORES)))
    res = run_bass_kernel_spmd(nc, in_maps[:ncores], core_ids=list(range(ncores)))
    R = list(res.results)
    while len(R) < NCORES:
        R.append(R[0])

    def cat(name, axis):
        return np.concatenate([np.asarray(r[name]) for r in R], axis=axis)

    y_prompt = cat("y_p", 0)
    y_sample = cat("y_s", 0).reshape(32, 1, D)
    new_nsa_p = cat("o_nsa_p", 1).reshape(DEPTH, 16, SEQ, 4, 1, 64)
    new_win_p = cat("o_win_p", 1).reshape(DEPTH, 16, 512, 2, 1, 64)
    new_hgrn_p = cat("o_hg_p", 1)
    new_pool_p = cat("o_pool_p", 1)
    new_mem_p = cat("o_mem_p", 1).reshape(DEPTH, 16, 256, 2, 4, 64)
    new_nsa_s = cat("o_nsa_s", 1).reshape(DEPTH, 32, 1, 4, 1, 64)
    new_win_s = cat("o_win_s", 1).reshape(DEPTH, 32, 512, 2, 1, 64)
    new_hgrn_s = cat("o_hg_s", 1)
    new_pool_s = cat("o_pool_s", 1)
    return (y_prompt, y_sample, new_nsa_p, new_win_p, new_hgrn_p, new_pool_p, new_mem_p,
            new_nsa_s, new_win_s, new_hgrn_s, new_pool_s)
```

```python
import contextlib
import math
import numpy as np
import ml_dtypes
import concourse.bass as bass
import concourse.mybir as mybir
from concourse.bass_utils import run_bass_kernel_spmd

F32 = mybir.dt.float32
BF16 = mybir.dt.bfloat16
I32 = mybir.dt.int32
ALU = mybir.AluOpType
AF = mybir.ActivationFunctionType
AX = mybir.AxisListType

NCORES = 8
D = 1024
NW = 2956
SEQ = 2048
NT = SEQ // 128
DEPTH = 2
EPS = 1e-6
NEG = -30000.0
PB = 2
SB_ = 4
PAST = 16384
NPHYS = 5120

WPERM = [(0, 0, 320), (320, 384, 64), (384, 512, 64), (448, 1676, 256), (704, 320, 64), (768, 448, 64),
         (832, 576, 64), (896, 640, 12), (908, 652, 1024), (1932, 1932, 1024)]
C_Q, C_KC, C_KS, C_KW, C_QM, C_VC, C_VS, C_VW, C_G, C_HQ, C_HF, C_HI, C_U, C_Z = (
    0, 256, 320, 384, 448, 704, 768, 832, 896, 908, 1164, 1420, 1676, 1932)


class T:
    __slots__ = ("h", "name", "last_w", "readers", "dsem", "dcnt")

    def __init__(self, h, name):
        self.h = h
        self.name = name
        self.last_w = None
        self.readers = {}
        self.dsem = None
        self.dcnt = 0

    def __getitem__(self, k):
        return self.h[k]


class KB:
    def __init__(self, nc, stack):
        self.nc = nc
        self.stack = stack
        self.eng = {"pe": nc.tensor, "act": nc.scalar, "dve": nc.vector, "pool": nc.gpsimd, "sp": nc.sync}
        self.sem = {}
        self.cnt = {}
        self.waited = {}
        for en in self.eng:
            self.sem[en] = stack.enter_context(nc.semaphore("sem_" + en))
            self.cnt[en] = 0
            self.waited[en] = {}
        self.nd = 0
        self.out_marks = []
        self.n_inst = 0
        self.free_dsems = []

    def sb(self, name, shape, dt, stack=None):
        t = T((stack or self.stack).enter_context(self.nc.sbuf_tensor(name, list(shape), dt)), name)
        return t

    def ps(self, name, shape, dt=F32):
        return T(self.stack.enter_context(self.nc.psum_tensor(name, list(shape), dt)), name)

    def dram(self, name, shape, dt, kind=None):
        if kind is None:
            h = self.nc.dram_tensor(name, list(shape), dt)
        else:
            h = self.nc.dram_tensor(name, list(shape), dt, kind=kind)
        return T(h, name)

    def _wait(self, en, deps):
        w = self.waited[en]
        e = self.eng[en]
        best = {}
        for d in deps:
            if d is None:
                continue
            k, c = d
            if best.get(k, 0) < c:
                best[k] = c
        for k, c in best.items():
            if k == "pe" and en == "pe":
                continue
            if w.get(k, 0) < c:
                e.wait_ge(self.sem[k], c)
                w[k] = c

    def _deps(self, reads, writes):
        deps = []
        for t in reads:
            deps.append(t.last_w)
        for t in writes:
            deps.append(t.last_w)
            deps.extend(t.readers.items())
        return deps

    def _mark(self, mark, reads, writes):
        k, c = mark
        for t in writes:
            t.last_w = mark
            t.readers = {}
        for t in reads:
            if not any(t is w_ for w_ in writes):
                if t.readers.get(k, 0) < c:
                    t.readers[k] = c

    @staticmethod
    def _n(ts):
        return [getattr(t, "t", t) for t in ts]

    def op(self, en, fn, reads=(), writes=()):
        reads, writes = self._n(reads), self._n(writes)
        self._wait(en, self._deps(reads, writes))
        inst = fn(self.eng[en])
        self.cnt[en] += 1
        inst.then_inc(self.sem[en], 1)
        self._mark((en, self.cnt[en]), reads, writes)
        self.n_inst += 1
        return inst

    def mm(self, fn, reads=(), writes=(), last=True):
        reads, writes = self._n(reads), self._n(writes)
        self._wait("pe", self._deps(reads, writes))
        inst = fn(self.eng["pe"])
        self.n_inst += 1
        if last:
            self.cnt["pe"] += 1
            inst.then_inc(self.sem["pe"], 1)
            self._mark(("pe", self.cnt["pe"]), reads, writes)
        else:
            self._mark(("pe", self.cnt["pe"] + 1), reads, ())
        return inst

    def _dsem(self, t):
        if t.dsem is None:
            key = "d%d" % self.nd
            self.nd += 1
            self.sem[key] = self.stack.enter_context(self.nc.semaphore("sem_" + key))
            t.dsem = key
        return t.dsem

    def dma(self, q, out_t, out_ap, in_t, in_ap, home=None, is_output=False, **kw):
        out_t, in_t = getattr(out_t, "t", out_t), getattr(in_t, "t", in_t)
        if home is not None:
            home = getattr(home, "t", home)
        self._wait(q, self._deps([in_t], [out_t]))
        if home is None:
            home = out_t
        key = self._dsem(home)
        inst = self.eng[q].dma_start(out=out_ap, in_=in_ap, **kw)
        home.dcnt += 16
        inst.then_inc(self.sem[key], 16)
        mark = (key, home.dcnt)
        self._mark(mark, [in_t], [out_t])
        if is_output:
            self.out_marks.append(mark)
        self.n_inst += 1
        return inst

    def gather(self, out_t, out_ap, in_t, in_ap, idx_t, idx_ap):
        self._wait("pool", self._deps([in_t, idx_t], [out_t]))
        key = self._dsem(out_t)
        inst = self.nc.gpsimd.indirect_dma_start(out=out_ap, out_offset=None, in_=in_ap,
                                                 in_offset=bass.IndirectOffsetOnAxis(ap=idx_ap, axis=0))
        out_t.dcnt += 16
        inst.then_inc(self.sem[key], 16)
        self._mark((key, out_t.dcnt), [in_t, idx_t], [out_t])
        self.n_inst += 1

    def barrier(self):
        marks = [(en, c) for en, c in self.cnt.items() if c > 0]
        for en in self.eng:
            self._wait(en, marks)

    def finish(self):
        self._wait("sp", self.out_marks)
        self._wait("sp", [(en, c) for en, c in self.cnt.items() if c > 0 and en != "sp"])


def V(t, off, dims, p0=0, npart=None):
    full = t.h[:].ap
    pstep = full[0][0]
    if npart is None:
        npart = full[0][1] - p0
    return bass.AP(t.h, p0 * pstep + off, [[pstep, npart]] + [list(d) for d in dims])


def host_consts():
    c = {}
    half = 32
    inv = (10000.0 ** (-np.arange(half, dtype=np.float32) / half)).astype(np.float32)
    pos = np.arange(SEQ, dtype=np.float32)
    ang = (pos[:, None] * inv[None, :]).astype(np.float32)
    cs = np.cos(ang).astype(np.float32).reshape(NT, 128, half).transpose(1, 0, 2)
    sn = np.sin(ang).astype(np.float32).reshape(NT, 128, half).transpose(1, 0, 2)
    c["cos_p"] = np.ascontiguousarray(cs)
    c["sin_p"] = np.ascontiguousarray(sn)
    angs = (np.float32(PAST) * inv).astype(np.float32)
    c["cs_s"] = np.stack([np.cos(angs), np.sin(angs)]).astype(np.float32).reshape(1, 2, half)
    t = np.arange(SEQ)[:, None]
    n = np.arange(64)[None, :]
    done = ((n + 1) * 32 - 1) <= t
    c["cmpb"] = np.ascontiguousarray(np.where(done, 0.0, NEG).astype(np.float32).reshape(NT, 128, 64).transpose(1, 0, 2))
    j = np.arange(32)[None, :]
    blk = (t // 64)
    forced = (j == 0) | (j == blk) | (j == blk - 1)
    avail = j <= blk
    sb = np.where(avail, np.where(forced, 1e4, 0.0), -1e4).astype(np.float32)
    c["selb"] = np.ascontiguousarray(sb.reshape(NT, 128, 32).transpose(1, 0, 2))
    key = np.arange(SEQ)[None, :]
    c["e32"] = (np.arange(32)[:, None] == key // 64).astype(ml_dtypes.bfloat16)
    kk = np.arange(128)[:, None]
    tt = np.arange(128)[None, :]
    c["causb"] = np.where(kk > tt, NEG, 0.0).astype(ml_dtypes.bfloat16)
    c["edgeb"] = np.where(kk <= tt, NEG, 0.0).astype(ml_dtypes.bfloat16)
    c["identb"] = np.eye(128, dtype=np.float32).astype(ml_dtypes.bfloat16)
    c["identf"] = np.eye(128, dtype=np.float32)
    same = (kk // 64) == (tt // 64)
    c["bdmask"] = (same & (kk <= tt)).astype(np.float32)
    c["uincl"] = (same & (kk <= tt)).astype(np.float32)
    c["urest"] = (same & (kk > tt)).astype(np.float32)
    c["cmask"] = np.stack([(np.arange(128) < 64), (np.arange(128) >= 64)], 1).astype(np.float32)
    ws = (2, 4, 8, 16)
    bg = np.zeros((128, 4, 128), np.float32)
    b0 = np.zeros((128, 4, 128), np.float32)
    bp = np.zeros((128, 4, 128), np.float32)
    for g, w in enumerate(ws):
        for tq in range(128):
            for s in range(max(0, tq - w + 1), tq + 1):
                bg[s, g, tq] += 1.0 / w
                b0[s, g, tq] += 1.0 / min(w, tq + 1)
            bg[tq, g, tq] -= 1.0
            b0[tq, g, tq] -= 1.0
            for s in range(128):
                if s + 128 - 128 >= 0 and (s - 128) > tq - w:
                    bp[s, g, tq] = 1.0 / w
    c["pmod"] = np.stack([(np.arange(128) % 64) + l_ * NPHYS * 64 for l_ in range(DEPTH)], 1).astype(np.float32)
    c["nident"] = (1.0 - np.eye(128)).astype(np.float32)
    r4 = np.zeros((128, 128), np.float32)
    r4[0:4, :] = 1.0
    c["rows4"] = r4
    bs = np.zeros((60, 4, 128), np.float32)
    bself = np.zeros((128, 4, 128), np.float32)
    for g, w in enumerate(ws):
        for s_ in range(4):
            for r_ in range(15):
                if r_ >= 16 - w:
                    bs[s_ * 15 + r_, g, s_] = 1.0 / w
        bself[:, g, :] = np.eye(128) * (1.0 / w - 1.0)
    c["bs_s"] = bs.astype(ml_dtypes.bfloat16)
    c["bself"] = bself.astype(ml_dtypes.bfloat16)
    sbs = np.zeros((128, 256), np.float32)
    sbs[:, 0] = 1e4
    sbs[:, 255] = 1e4
    c["selb_s"] = sbs
    c["bandg"] = bg.astype(ml_dtypes.bfloat16)
    c["band0"] = b0.astype(ml_dtypes.bfloat16)
    c["bandp"] = bp.astype(ml_dtypes.bfloat16)
    return c


CONST_SPECS = [("cos_p", [128, NT, 32], F32), ("sin_p", [128, NT, 32], F32), ("cs_s", [1, 2, 32], F32),
               ("cmpb", [128, NT, 64], F32), ("selb", [128, NT, 32], F32), ("e32", [32, SEQ], BF16),
               ("causb", [128, 128], BF16), ("edgeb", [128, 128], BF16), ("identb", [128, 128], BF16),
               ("identf", [128, 128], F32), ("bdmask", [128, 128], F32), ("uincl", [128, 128], F32),
               ("urest", [128, 128], F32), ("cmask", [128, 2], F32), ("bandg", [128, 4, 128], BF16),
               ("band0", [128, 4, 128], BF16), ("bandp", [128, 4, 128], BF16), ("pmod", [128, 2], F32),
               ("nident", [128, 128], F32), ("rows4", [128, 128], F32), ("bs_s", [60, 4, 128], BF16),
               ("bself", [128, 4, 128], BF16), ("selb_s", [128, 256], F32)]


import os
STAGE = int(os.environ.get("KSTAGE", "99"))
NTILES_DBG = int(os.environ.get("KTILES", "16"))


SKIP_IN = set()
SUB = int(os.environ.get('KSUB', '99'))
PENG = os.environ.get('KPENG', 'dve')


def build_program(do_sample=bool(int(os.environ.get("KSAMPLE", "1")))):
    nc = bass.Bass("TRN2", target_bir_lowering=False)
    with contextlib.ExitStack() as st:
        kb = KB(nc, st)
        di = {}

        def din(name, shape, dt=F32):
            di[name] = kb.dram(name, shape, dt, kind="ExternalInput")
            return di[name]

        def dout(name, shape):
            di[name] = kb.dram(name, shape, F32, kind="ExternalOutput")
            return di[name]

        xp = din("xp", [PB, SEQ, D])
        memp = din("memp", [PB, 256, D])
        xs = din("xs", [SB_, D])
        cnsa = din("cnsa", [DEPTH * NPHYS * 64, 512]) if do_sample else None
        cwin = din("cwin", [DEPTH, SB_, 512, 128])
        shg = din("shg", [DEPTH, SB_, 4, 64, 64])
        spool = din("spool", [DEPTH, SB_, 15, 256])
        cmem = din("cmem", [DEPTH, SB_, 256, 512])
        ptb = din("ptb", [128, SB_, 64], I32)
        norm_g = din("norm_g", [DEPTH, D])
        w_in = din("w_in", [DEPTH, D, NW])
        w_out = din("w_out", [DEPTH, D, D])
        nsa_qn = din("nsa_qn", [DEPTH, 64])
        nsa_kn = din("nsa_kn", [DEPTH, 3, 64])
        cmp_pe = din("cmp_pe", [DEPTH, 2, 32, 64])
        cmp_w1 = din("cmp_w1", [DEPTH, 2, 2048, 256])
        cmp_w2 = din("cmp_w2", [DEPTH, 2, 256, 64])
        hg_lb = din("hg_lb", [DEPTH, 256])
        hg_on = din("hg_on", [DEPTH, 256])
        pool_w = din("pool_w", [DEPTH, 4, 64, 64])
        pool_scale = din("pool_scale", [DEPTH, 256])
        mem_norm = din("mem_norm", [DEPTH, D])
        w_mem_kv = din("w_mem_kv", [DEPTH, D, 512])
        mem_qn = din("mem_qn", [DEPTH, 64])
        mem_kn = din("mem_kn", [DEPTH, 64])
        cd = {}
        for name, shape, dt in CONST_SPECS:
            cd[name] = din("c_" + name, shape, dt)

        y_p = dout("y_p", [PB, SEQ, D])
        y_s = dout("y_s", [SB_, D])
        o_nsa_p = dout("o_nsa_p", [DEPTH, PB, SEQ, 256])
        o_win_p = dout("o_win_p", [DEPTH, PB, 512, 128])
        o_hg_p = dout("o_hg_p", [DEPTH, PB, 4, 64, 64])
        o_pool_p = dout("o_pool_p", [DEPTH, PB, 15, 256])
        o_mem_p = dout("o_mem_p", [DEPTH, PB, 256, 512])
        o_nsa_s = dout("o_nsa_s", [DEPTH, SB_, 256])
        o_win_s = dout("o_win_s", [DEPTH, SB_, 512, 128])
        o_hg_s = dout("o_hg_s", [DEPTH, SB_, 4, 64, 64])
        o_pool_s = dout("o_pool_s", [DEPTH, SB_, 15, 256])
        DBG = False
        xmid = kb.dram("xmid", [PB, SEQ, D], F32)
        global SKIP_IN
        SKIP_IN = set() if do_sample else {"cnsa"}

        cs = {}
        NONRES = ("cos_p", "sin_p", "cmpb", "selb", "uincl", "cs_s")
        for name, shape, dt in CONST_SPECS:
            if name in NONRES:
                continue
            cs[name] = kb.sb("k_" + name, shape, dt)
            kb.dma("sp", cs[name], cs[name][:], cd[name], cd[name][:])
        cs["uincl"] = cs["bdmask"]
        rope_t = kb.sb("rope_t", [128, 2, 32], F32)
        cmpb_t = kb.sb("cmpb_t", [128, 64], F32)
        selb_t = kb.sb("selb_t", [128, 32], F32)
        identb, identf = cs["identb"], cs["identf"]

        PF = [kb.ps("pf%d" % i, [128, 512], F32) for i in range(6)]
        PT = [kb.ps("pt%d" % i, [128, 1024], BF16) for i in range(2)]

        W_in = kb.sb("W_in", [128, 8, NW], BF16)
        W_out = kb.sb("W_out", [128, 8, D], BF16)
        W1 = kb.sb("W1", [128, 2, 16, 256], BF16)
        W2 = kb.sb("W2", [128, 2, 2, 64], BF16)
        PW = kb.sb("PW", [64, 4, 64], BF16)
        PE2 = kb.sb("PE2", [128, 2, 16], BF16)
        gcol = kb.sb("gcol", [128, 8], F32)
        mgcol = kb.sb("mgcol", [128, 8], F32)
        gain11 = kb.sb("gain11", [128, 11, 64], F32)
        mkn_b = kb.sb("mkn_b", [128, 64], F32)
        hgon_b = kb.sb("hgon_b", [128, 256], F32)
        psc_b = kb.sb("psc_b", [128, 256], F32)
        lbraw = kb.sb("lbraw", [128, 2, 256], F32)
        lb_b = kb.sb("lb_b", [128, 256], F32)
        oml_b = kb.sb("oml_b", [128, 256], F32)
        biasT = kb.sb("biasT", [128, 2, 2], F32)
        lbt = [kb.sb("lbt%d" % i, [128, 256], F32) for i in range(3)]

        def load_weights(l):
            for (d0, s0, n) in WPERM:
                kb.dma("pool", W_in, W_in[:, :, d0:d0 + n], w_in,
                       w_in[l, :, s0:s0 + n].rearrange("(kc p) n -> p kc n", p=128))
            kb.dma("pool", W_out, W_out[:], w_out, w_out[l].rearrange("(kc p) n -> p kc n", p=128))
            for a in range(2):
                kb.dma("pool", W1, W1[:, a], cmp_w1, cmp_w1[l, a].rearrange("(c p) h -> p c h", p=128))
                kb.dma("pool", W2, W2[:, a], cmp_w2, cmp_w2[l, a].rearrange("(hc p) d -> p hc d", p=128))
                for j in range(2):
                    kb.dma("pool", PE2, PE2[j * 64:(j + 1) * 64, a, :], cmp_pe,
                           cmp_pe[l, a, j::2, :].rearrange("c d -> d c"), allow_slow_non_contiguous=True)
            kb.dma("pool", PW, PW[:], pool_w, pool_w[l].rearrange("g c d -> c g d"))
            kb.dma("sp", gcol, gcol[:], norm_g, norm_g[l].rearrange("(kc p) -> p kc", p=128), allow_slow_non_contiguous=True)
            kb.dma("sp", mgcol, mgcol[:], mem_norm, mem_norm[l].rearrange("(kc p) -> p kc", p=128), allow_slow_non_contiguous=True)
            for h in range(4):
                kb.dma("sp", gain11, gain11[:, h, :], nsa_qn, bass.AP(nsa_qn.h, l * 64, [[0, 128], [1, 64]]))
                kb.dma("sp", gain11, gain11[:, 7 + h, :], mem_qn, bass.AP(mem_qn.h, l * 64, [[0, 128], [1, 64]]))
            kb.dma("sp", gain11, gain11[:, 4:7, :], nsa_kn, bass.AP(nsa_kn.h, l * 192, [[0, 128], [64, 3], [1, 64]]))
            kb.dma("sp", mkn_b, mkn_b[:], mem_kn, bass.AP(mem_kn.h, l * 64, [[0, 128], [1, 64]]))
            kb.dma("sp", hgon_b, hgon_b[:], hg_on, bass.AP(hg_on.h, l * 256, [[0, 128], [1, 256]]))
            kb.dma("sp", psc_b, psc_b[:], pool_scale, bass.AP(pool_scale.h, l * 256, [[0, 128], [1, 256]]))
            kb.dma("sp", lbraw, lbraw[:], hg_lb, bass.AP(hg_lb.h, 0, [[0, 128], [256, 2], [1, 256]]))
            if l == 0:
                kb.op("dve", lambda e: e.memset(lb_b[:], 0.0), writes=[lb_b])
                kb.op("dve", lambda e: e.memset(oml_b[:], 1.0), writes=[oml_b])
            else:
                kb.op("dve", lambda e: e.tensor_tensor(lbt[0][:], lbraw[:, 1, :], lbraw[:, 0, :], ALU.subtract),
                      reads=[lbraw], writes=[lbt[0]])
                kb.op("act", lambda e: e.activation(lb_b[:], lbt[0][:], AF.Sigmoid), reads=[lbt[0]], writes=[lb_b])
                kb.op("dve", lambda e: e.tensor_scalar(oml_b[:], lb_b[:], -1.0, 1.0, ALU.mult, ALU.add),
                      reads=[lb_b], writes=[oml_b])
            for a in range(2):
                for hc in range(2):
                    for c in range(16):
                        kb.mm(lambda e, a=a, hc=hc, c=c: e.matmul(PF[0][:, a * 2 + hc:a * 2 + hc + 1],
                                                                  W1[:, a, c, hc * 128:(hc + 1) * 128],
                                                                  PE2[:, a, c:c + 1], start=(c == 0), stop=(c == 15)),
                              reads=[W1, PE2], writes=[PF[0]], last=(c == 15))
            kb.op("act", lambda e: e.copy(biasT[:].rearrange("p a h -> p (a h)"), PF[0][:, 0:4]), reads=[PF[0]], writes=[biasT])

        x_t = kb.sb("x_t", [128, D], F32)
        xn = kb.sb("xn", [128, D], BF16)
        hT = kb.sb("hT", [128, 8, 128], BF16)
        proj = kb.sb("proj", [128, NW], F32)
        st1 = [kb.sb("st1_%d" % i, [128, 1], F32) for i in range(4)]
        s11 = [kb.sb("s11_%d" % i, [128, 11], F32) for i in range(4)]
        qkg = kb.sb("qkg", [128, 11, 64], F32)
        qkr = kb.sb("qkr", [128, 7, 64], F32)
        rows = kb.sb("rows", [128, 4, 64], F32)
        wrows = kb.sb("wrows", [128, 2, 64], F32)
        qk_bf = kb.sb("qk_bf", [128, 11, 64], BF16)
        qT_all = kb.sb("qT_all", [64, 4, 128], BF16)
        qmT = kb.sb("qmT", [64, 4, 128], BF16)
        KTW = kb.sb("KTW", [128, 2 * SEQ], BF16)
        V1 = kb.sb("V1", [128, 2, NT, 65], BF16)
        CC = kb.sb("CC", [64, 2, 64], BF16)
        stg = kb.sb("stg", [128, 2, 2, 64], BF16)
        kT2 = kb.sb("kT2", [128, 2, 64], BF16)
        hidT = kb.sb("hidT", [128, 2, 2, 4], BF16)
        pcb = kb.sb("pcb", [128, 4, 64], BF16)
        s4 = [kb.sb("s4_%d" % i, [128, 4], F32) for i in range(8)]
        pTc = kb.sb("pTc", [64, 4, 128], BF16)
        vcc = kb.sb("vcc", [64, 64], BF16)
        imp = kb.sb("imp", [128, 32], F32)
        score = kb.sb("score", [128, 32], F32)
        rank = kb.sb("rank", [128, 32], F32)
        negsel = kb.sb("negsel", [128, 64], BF16)
        kb.op("pool", lambda e: e.memset(negsel[:], 0.0), writes=[negsel])
        negselT = kb.sb("negselT", [32, 128], BF16)
        expT = [kb.sb("expT%d" % i, [128, 512], BF16) for i in range(2)]
        oT = [kb.sb("oT0", [65, 512], F32)] * 2
        gts = kb.sb("gts", [128, 12], F32)
        onsa = [kb.sb("onsa%d" % i, [128, 4, 64], F32) for i in range(4)]
        ycat = kb.sb("ycat", [128, D], F32)
        hg = [kb.sb("hg%d" % i, [128, 256], F32) for i in range(6)]
        hgb = [kb.sb("hgb%d" % i, [128, 256], BF16) for i in range(3)]
        khz = kb.sb("khz", [128, 2, 256], BF16)
        qtT = kb.sb("qtT", [64, 4, 128], BF16)
        qtTz = kb.sb("qtTz", [64, 4, 2, 128], BF16)
        ktT = kb.sb("ktT", [64, 4, 128], BF16)
        ATs = kb.sb("ATs", [128, 4, 128], BF16)
        Sst = [kb.sb("Sst%d" % i, [64, 4, 64], F32) for i in range(3)]
        Sbf = [kb.sb("Sbf%d" % i, [64, 4, 64], BF16) for i in range(2)]
        eGl = kb.sb("eGl", [64, 4, 2], F32)
        ubf = [kb.sb("ubf%d" % i, [128, 256], BF16) for i in range(2)]
        dltT = kb.sb("dltT", [64, 4, 128], BF16)
        memKT = kb.sb("memKT", [64, 4, 256], BF16)
        memV1 = kb.sb("memV1", [128, 2, 4, 65], BF16)
        sz = kb.sb("sz", [128, D], F32)
        xo = kb.sb("xo", [128, D], F32)

        class AV:
            def __init__(self, t, ap):
                self.t = t
                self.ap = ap

            def __getitem__(self, k):
                return self.ap[k]

        def alias(t, ap):
            v = AV(t, ap)
            return v
        junk, junk_t = alias(sz, sz[:]), sz
        sq11, sq11_t = alias(sz, sz[:, 0:704]), sz
        cmpt, cmpt_t = alias(sz, sz[:].rearrange("p (a b) -> p a b", a=32)), sz
        qkn, qkn_t = alias(ycat, ycat[:, 0:704].rearrange("p (h d) -> p h d", h=11)), ycat
        mrow, mrow_t = alias(xo, xo[:, 0:512].rearrange("p (a h d) -> p a h d", a=2, h=4)), xo
        mkv, mkv_t = alias(ycat, ycat[:, 0:512]), ycat
        rt = [alias(hg[i], hg[i][:, 0:224].rearrange("p (h d) -> p h d", h=7)) for i in range(4)]
        sm = [alias(hg[i], hg[i][:].rearrange("p (h d) -> p h d", h=4)) for i in range(3)]
        pcf = alias(hg[3], hg[3][:].rearrange("p (h d) -> p h d", h=4))
        eM = expT
        yg = xn
        yT = hT

        def rms_rstd(src_ssq, n, eps_t):
            a, b, c_ = eps_t
            kb.op("dve", lambda e: e.tensor_scalar(a[:], src_ssq[:], 1.0 / n, EPS, ALU.mult, ALU.add), reads=[src_ssq], writes=[a])
            kb.op("act", lambda e: e.activation(b[:], a[:], AF.Sqrt), reads=[a], writes=[b])
            kb.op("dve", lambda e: e.reciprocal(c_[:], b[:]), reads=[b], writes=[c_])
            return c_

        def norm_transpose(src_t, gc):
            kb.op("act", lambda e: e.activation(junk[:], src_t[:], AF.Square, accum_out=st1[0][:]), reads=[src_t], writes=[junk, st1[0]])
            r = rms_rstd(st1[0], D, st1[1:4])
            kb.op("dve", lambda e: e.tensor_scalar(xn[:], src_t[:], r[:], None, ALU.mult), reads=[src_t, r], writes=[xn])
            for kc in range(8):
                kb.mm(lambda e, kc=kc: e.transpose(PT[0][:, kc * 128:(kc + 1) * 128], xn[:, kc * 128:(kc + 1) * 128], identb[:]),
                      reads=[xn, identb], writes=[PT[0]], last=(kc == 7))
            kb.op("dve", lambda e: e.tensor_tensor(hT[:], PT[0][:].rearrange("p (k t) -> p k t", k=8),
                                                   V(gc, 0, [[1, 8], [0, 128]]), ALU.mult), reads=[PT[0], gc], writes=[hT])

        evac_flip = [0]

        def evac(dst_t, dst_ap, src_t, src_ap):
            evac_flip[0] ^= 1
            if evac_flip[0] or os.environ.get("KEVAC", "act") == "act":
                kb.op("act", lambda e: e.copy(dst_ap, src_ap), reads=[src_t], writes=[dst_t])
            else:
                kb.op("dve", lambda e: e.tensor_copy(dst_ap, src_ap), reads=[src_t], writes=[dst_t])

        kT3 = kb.sb("kT3", [64, 3, 128], BF16)
        qT_pad = kb.sb("qT_pad", [64, 128], BF16)
        G2 = kb.sb("G2", [128, 512], BF16)
        G3 = kb.sb("G3", [128, 512], BF16)
        Gs = [khz, ATs, G2, G3]
        Gv = [khz[:].rearrange("p a b -> p (a b)"), ATs[:].rearrange("p a b -> p (a b)"), G2[:], G3[:]]
        stg1 = stg
        hidc = kb.sb("hidc", [128, 2, 2, 128], BF16)
        kccT_1 = alias(ktT, ktT[:].rearrange("p h t -> p (h t)"))
        vcc_1 = pcb
        idx_l = kb.sb("idx_l", [128, SB_, 64], I32)
        ptb_sb = kb.sb("ptb_sb", [128, SB_, 64], I32)
        Ef = kb.sb("Ef", [128, 128], F32)
        ETs = [kb.sb("ETs%d" % i_, [128, 128], BF16) for i_ in range(2)]
        Wt = alias(hgb[1], hgb[1][:].rearrange("p (k d) -> p k d", k=4))
        KwT = pTc
        V1w = kb.sb("V1w", [128, 4, 65], BF16)
        V1n = kb.sb("V1n", [128, 2, 65], BF16)
        Mt = kb.sb("Mt", [128, 2, 256], BF16)
        spb = kb.sb("spb", [60, 256], BF16)
        pT_s = alias(expT[0], expT[0][:].rearrange("p (c t) -> p c t", c=4))
        m8a = kb.sb("m8a", [128, 16], F32)
        m8b = kb.sb("m8b", [128, 16], F32)
        selk = kb.sb("selk", [128, 64], F32)
        sh_f = alias(onsa[2], onsa[2][:].rearrange("p h d -> p (h d)"))
        sh_k = alias(onsa[3], onsa[3][:].rearrange("p h d -> p (h d)"))
        fkT = kb.sb("fkT", [64, 2, 4, 4], F32)
        ETn = kb.sb("ETn", [128, 32], BF16)
        den4 = [kb.sb("den4_%d" % i_, [128, 4], F32) for i_ in range(3)]
        xsmid = kb.dram("xsmid", [SB_, D], F32)

        _padtest = int(os.environ.get("KPADTEST", "0"))
        if _padtest:
            kb.sb("padtest", [128, _padtest // 2], BF16)

        def acc(k, par):
            if k < 4:
                t = x_t if par == 0 else xo
                return t, t[:, k * 256:(k + 1) * 256]
            return sz, sz[:, par * 256:(par + 1) * 256]

        def sample_init():
            kb.op("pool", lambda e: e.memset(x_t[:], 0.0), writes=[x_t])
            kb.dma("sp", ptb_sb, ptb_sb[:], ptb, ptb[:])
            for l_ in range(DEPTH):
                pass

        def sample_idx(l):
            kb.op("dve", lambda e: e.tensor_scalar(idx_l[:], ptb_sb[:], 64.0, cs["pmod"][:, l:l + 1], ALU.mult, ALU.add),
                  reads=[ptb_sb, cs["pmod"]], writes=[idx_l])

        def sample_pass1(l, s):
            KC2 = KTW[:].rearrange("p (a m) -> p a m", a=2)
            if True:
                for ch in range(4):
                    for ppl in range(16):
                        pp = ch * 16 + ppl
                        g = Gs[pp % 4]
                        gq = Gv[pp % 4]
                        kb.gather(g, gq, cnsa, cnsa[:, :], idx_l, idx_l[:, s, pp:pp + 1])
                        gv = gq.rearrange("p (j r d) -> p r j d", j=2, r=4)
                        kb.op("dve", lambda e, gv=gv: e.tensor_copy(stg1[:], gv[:, 0:2, :, :]), reads=[g], writes=[stg1])
                        for a in range(2):
                            kb.mm(lambda e, a=a: e.transpose(PT[0][:, a * 128:(a + 1) * 128], stg1[:, a].rearrange("p j d -> p (j d)"), identb[:]),
                                  reads=[stg1, identb], writes=[PT[0]], last=(a == 1))
                        kb.op("dve", lambda e, ppl=ppl: e.tensor_copy(KC2[:, :, ppl * 128:(ppl + 1) * 128],
                                                                      PT[0][:, 0:256].rearrange("p (a m) -> p a m", a=2)),
                              reads=[PT[0]], writes=[KTW])
                    for a in range(2):
                        for hc in range(2):
                            pb_ = PF[(a * 2 + hc) % 2]
                            for c in range(16):
                                kb.mm(lambda e, a=a, hc=hc, c=c, pb_=pb_: e.matmul(pb_[:, 0:128], W1[:, a, c, hc * 128:(hc + 1) * 128],
                                                                                   V(KTW, a * 2048 + c, [[16, 128]]), start=(c == 0), stop=(c == 15)),
                                      reads=[W1, KTW], writes=[pb_], last=(c == 15))
                            kb.op("act", lambda e, a=a, hc=hc, pb_=pb_: e.activation(hidc[:, a, hc, :], pb_[:, 0:128], AF.Silu, bias=biasT[:, a, hc:hc + 1]),
                                  reads=[pb_, biasT], writes=[hidc])
                    for hc in range(2):
                        kb.mm(lambda e, hc=hc: e.matmul(PF[4][0:64, 0:128], W2[:, 0, hc, :], hidc[:, 0, hc, :], start=(hc == 0), stop=(hc == 1)),
                              reads=[W2, hidc], writes=[PF[4]], last=(hc == 1))
                    kb.op("dve", lambda e, s=s, ch=ch: e.tensor_copy(kccT_1[:, ch * 128:(ch + 1) * 128], PF[4][0:64, 0:128]), reads=[PF[4]], writes=[kccT_1])
                    for hc in range(2):
                        kb.mm(lambda e, hc=hc: e.matmul(PF[2][:, 0:64], hidc[:, 1, hc, :], W2[:, 1, hc, :], start=(hc == 0), stop=(hc == 1)),
                              reads=[W2, hidc], writes=[PF[2]], last=(hc == 1))
                    kb.op("dve", lambda e, s=s, ch=ch: e.tensor_copy(vcc_1[:, ch, :], PF[2][:, 0:64]), reads=[PF[2]], writes=[vcc_1])

        def nsa_branch_finish(s, k):
            kb.op("pool", lambda e: e.memset(oT[0][:], 0.0), writes=[oT[0]])
            kb.op("dve", lambda e: e.tensor_copy(V(oT[0], s, [[128, 4]], npart=65), PF[5][0:65, 0:4]), reads=[PF[5]], writes=[oT[0]])
            for h in range(4):
                kb.mm(lambda e, h=h: e.transpose(PF[2][:, h * 65:(h + 1) * 65], oT[0][:, h * 128:(h + 1) * 128], identf[0:65, 0:65]),
                      reads=[oT[0], identf], writes=[PF[2]], last=(h == 3))
            pv = PF[2][:, 0:260].rearrange("p (h d) -> p h d", h=4)
            kb.op("dve", lambda e: e.tensor_scalar(den4[0][:], pv[:, :, 64], cs["nident"][:, s:s + 1], None, ALU.add),
                  reads=[PF[2], cs["nident"]], writes=[den4[0]])
            kb.op("dve", lambda e: e.reciprocal(den4[1][:], den4[0][:]), reads=[den4[0]], writes=[den4[1]])
            kb.op("dve", lambda e: e.tensor_tensor(onsa[0][:], pv[:, :, 0:64], V(den4[1], 0, [[1, 4], [0, 64]]), ALU.mult),
                  reads=[PF[2], den4[1]], writes=[onsa[0]])
            to, ao = acc(k, s % 2)
            tn, an = acc(k, (s + 1) % 2)
            kb.op("dve", lambda e: e.tensor_tensor(an, ao, onsa[0][:].rearrange("p h d -> p (h d)"), ALU.add), reads=[to, onsa[0]], writes=[tn])

        def sample_mixers(l):
            for k in range(5):
                t0_, a0_ = acc(k, 0)
                kb.op("pool", lambda e, a0_=a0_: e.memset(a0_, 0.0), writes=[t0_])
            kb.op("pool", lambda e: e.memset(qT_pad[:], 0.0), writes=[qT_pad])
            kb.op("pool", lambda e: e.memset(V1n[:], 1.0), writes=[V1n])
            kb.op("dve", lambda e: e.tensor_copy(V1n[:, :, 0:64], proj[:, C_VS:C_VS + 128].rearrange("p (a d) -> p a d", a=2)), reads=[proj], writes=[V1n])
            kb.op("act", lambda e: e.activation(hg[0][:], proj[:, C_HF:C_HF + 256], AF.Sigmoid), reads=[proj], writes=[hg[0]])
            kb.op("dve", lambda e: e.tensor_tensor(hg[5][:], hg[0][:], oml_b[:], ALU.mult), reads=[hg[0], oml_b], writes=[hg[5]])
            kb.op("dve", lambda e: e.tensor_tensor(sh_f[:], hg[5][:], lb_b[:], ALU.add), reads=[hg[5], lb_b], writes=[sh_f])
            kb.op("dve", lambda e: e.tensor_scalar(sh_k[:], sh_f[:], -1.0, 1.0, ALU.mult, ALU.add), reads=[sh_f], writes=[sh_k])
            for ti, tsrc in enumerate([sh_f, sh_k]):
                for h in range(4):
                    kb.mm(lambda e, h=h, tsrc=tsrc: e.transpose(PF[4][0:64, h * 128:(h + 1) * 128], tsrc[:, h * 64:(h + 1) * 64], identf[:]),
                          reads=[tsrc, identf], writes=[PF[4]], last=(h == 3))
                kb.op("dve", lambda e, ti=ti: e.tensor_copy(fkT[:, ti], PF[4][0:64, :].rearrange("p (h t) -> p h t", h=4)[:, :, 0:4]), reads=[PF[4]], writes=[fkT])
            kb.op("dve", lambda e: e.tensor_copy(hgb[0][:], proj[:, C_HQ:C_HQ + 256]), reads=[proj], writes=[hgb[0]])
            for h in range(4):
                kb.mm(lambda e, h=h: e.transpose(PT[1][0:64, h * 128:(h + 1) * 128], hgb[0][:, h * 64:(h + 1) * 64], identb[:]),
                      reads=[hgb[0], identb], writes=[PT[1]], last=(h == 3))
            kb.op("dve", lambda e: e.tensor_copy(qtT[:], PT[1][0:64, 0:512].rearrange("p (h t) -> p h t", h=4)), reads=[PT[1]], writes=[qtT])

            sm_s = ycat[:, 0:512]
            p_s = ycat[:, 512:1024]
            pb_s = xn[:, 0:512]
            for s in range(SB_):
                oh = identf[:, s:s + 1]
                sample_pass1(l, s)
                kb.op("dve", lambda e, s=s: e.tensor_copy(qT_pad[:, 0:4], V(qT_all, s, [[128, 4]])), reads=[qT_all], writes=[qT_pad])
                kb.mm(lambda e, s=s: e.matmul(PF[3][:], qT_pad[:], kccT_1[:], start=True, stop=True), reads=[qT_pad, kccT_1], writes=[PF[3]])
                kb.op("dve", lambda e: e.tensor_reduce(st1[0][:], PF[3][:], AX.X, ALU.max), reads=[PF[3]], writes=[st1[0]])
                kb.op("dve", lambda e: e.tensor_scalar(sm_s, PF[3][:], st1[0][:], None, ALU.subtract), reads=[PF[3], st1[0]], writes=[ycat])
                kb.op("act", lambda e: e.activation(p_s, sm_s, AF.Exp, accum_out=st1[1][:]), reads=[ycat], writes=[ycat, st1[1]])
                kb.op("dve", lambda e: e.reciprocal(st1[2][:], st1[1][:]), reads=[st1[1]], writes=[st1[2]])
                kb.op("dve", lambda e: e.tensor_scalar(sm_s, p_s, st1[2][:], None, ALU.mult), reads=[ycat, st1[2]], writes=[ycat])
                kb.op("dve", lambda e: e.tensor_copy(pb_s, sm_s), reads=[ycat], writes=[xn])
                pv2 = sm_s.rearrange("p (j r) -> p j r", r=2)
                kb.op("dve", lambda e: e.tensor_tensor(hg[4][:], pv2[:, :, 0], pv2[:, :, 1], ALU.add), reads=[ycat], writes=[hg[4]])
                kb.mm(lambda e: e.matmul(PF[0][:, 0:256], cs["rows4"][:], hg[4][:], start=True, stop=True), reads=[cs["rows4"], hg[4]], writes=[PF[0]])
                kb.op("dve", lambda e: e.tensor_tensor(hg[1][:], PF[0][:, 0:256], cs["selb_s"][:], ALU.add), reads=[PF[0], cs["selb_s"]], writes=[hg[1]])
                kb.op("dve", lambda e: e.max(m8a[:, 0:8], hg[1][:]), reads=[hg[1]], writes=[m8a])
                kb.op("dve", lambda e: e.match_replace(hg[2][:], m8a[:, 0:8], hg[1][:], -1e9), reads=[hg[1], m8a], writes=[hg[2]])
                kb.op("dve", lambda e: e.max(m8b[:, 0:8], hg[2][:]), reads=[hg[2]], writes=[m8b])
                kb.op("dve", lambda e: e.tensor_scalar(hg[3][:], hg[1][:], m8b[:, 6:7], None, ALU.is_ge), reads=[hg[1], m8b], writes=[hg[3]])
                for g4 in range(4):
                    kb.op("dve", lambda e, g4=g4: e.tensor_copy(selk[32 * g4:32 * (g4 + 1), :], V(hg[3], g4, [[4, 64]], p0=32 * g4, npart=32)),
                          reads=[hg[3]], writes=[selk])
                for c in range(4):
                    kb.mm(lambda e, c=c: e.transpose(PT[0][:, c * 128:(c + 1) * 128], pb_s[:, c * 128:(c + 1) * 128], identb[:]),
                          reads=[xn, identb], writes=[PT[0]], last=(c == 3))
                kb.op("dve", lambda e: e.tensor_copy(pT_s[:], PT[0][:, 0:512].rearrange("p (c t) -> p c t", c=4)), reads=[PT[0]], writes=[pT_s])
                for c in range(4):
                    kb.mm(lambda e, c=c, s=s: e.matmul(PF[4][0:64, 0:4], vcc_1[:, c, :], pT_s[:, c, 0:4], start=(c == 0), stop=(c == 3)),
                          reads=[vcc_1, pT_s], writes=[PF[4]], last=(c == 3))
                kb.op("pool", lambda e: e.memset(oT[0][:], 0.0), writes=[oT[0]])
                kb.op("dve", lambda e, s=s: e.tensor_copy(V(oT[0], s, [[128, 4]], npart=64), PF[4][0:64, 0:4]), reads=[PF[4]], writes=[oT[0]])
                for h in range(4):
                    kb.mm(lambda e, h=h: e.transpose(PF[3][:, h * 64:(h + 1) * 64], oT[0][0:64, h * 128:(h + 1) * 128], identf[0:64, 0:64]),
                          reads=[oT[0], identf], writes=[PF[3]], last=(h == 3))
                to, ao = acc(0, s % 2)
                tn, an = acc(0, (s + 1) % 2)
                kb.op("dve", lambda e, ao=ao, an=an: e.tensor_tensor(an, ao, PF[3][:, 0:256], ALU.add), reads=[to, PF[3]], writes=[tn])
                kb.op("pool", lambda e: e.memset(V1[:], 1.0), writes=[V1])
                V1c = V1[:].rearrange("p a t d -> p (a t) d")
                KsT = KTW[0:64, :].rearrange("p (j m) -> p j m", j=2)
                first = True
                for ch in range(4):
                    for ppl in range(16):
                        pp = ch * 16 + ppl
                        g = Gs[pp % 4]
                        gq = Gv[pp % 4]
                        kb.gather(g, gq, cnsa, cnsa[:, :], idx_l, idx_l[:, s, pp:pp + 1])
                        for j in range(2):
                            kb.mm(lambda e, j=j, gq=gq: e.transpose(PT[1][0:64, j * 128:(j + 1) * 128], gq[:, j * 256 + 128:j * 256 + 192], identb[:]),
                                  reads=[g, identb], writes=[PT[1]], last=(j == 1))
                        kb.op("dve", lambda e, ppl=ppl: e.tensor_copy(KsT[:, :, ppl * 128:(ppl + 1) * 128],
                                                                      PT[1][0:64, 0:256].rearrange("p (j t) -> p j t", j=2)), reads=[PT[1]], writes=[KTW])
                        g4v = gq.rearrange("p (j r d) -> p j r d", j=2, r=4)
                        kb.op("dve", lambda e, ppl=ppl, g4v=g4v: e.tensor_copy(V1c[:, ppl * 2:ppl * 2 + 2, 0:64], g4v[:, :, 3, :]), reads=[g], writes=[V1])
                    psc = PF[ch % 2]
                    for t_ in range(32):
                        ppl, j = divmod(t_, 2)
                        kb.mm(lambda e, t_=t_, ppl=ppl, j=j, psc=psc: e.matmul(psc[:, t_ * 4:(t_ + 1) * 4], KsT[:, j, ppl * 128:(ppl + 1) * 128], qT_pad[:, 0:4],
                                                                            start=True, stop=True), reads=[KTW, qT_pad], writes=[psc], last=(t_ == 31))
                    kb.op("act", lambda e, psc=psc: e.activation(Ef[:], psc[:, 0:128], AF.Exp), reads=[psc], writes=[Ef])
                    et = ETs[ch % 2]
                    kb.op("dve", lambda e, et=et, ch=ch: e.tensor_tensor(et[:].rearrange("p (a b) -> p a b", b=8), Ef[:].rearrange("p (a b) -> p a b", b=8),
                                                                       V(selk, ch * 16, [[1, 16], [0, 8]]), ALU.mult), reads=[Ef, selk], writes=[et])
                    for t_ in range(32):
                        kb.mm(lambda e, t_=t_, et=et, first=first: e.matmul(PF[5][0:65, 0:4], V1c[:, t_, :], et[:, t_ * 4:(t_ + 1) * 4],
                                                                          start=(first and t_ == 0), stop=False), reads=[V1, et], writes=[PF[5]], last=False)
                    first = False
                kb.mm(lambda e: e.matmul(PF[3][:, 0:4], kT3[:, 1, :], qT_pad[:, 0:4], start=True, stop=True), reads=[kT3, qT_pad], writes=[PF[3]])
                kb.op("act", lambda e: e.activation(Ef[:, 0:4], PF[3][:, 0:4], AF.Exp), reads=[PF[3]], writes=[Ef])
                kb.op("dve", lambda e, oh=oh: e.tensor_scalar(ETn[:, 0:4], Ef[:, 0:4], oh, None, ALU.mult), reads=[Ef, identf], writes=[ETn])
                kb.mm(lambda e: e.matmul(PF[5][0:65, 0:4], V1n[:, 0, :], ETn[:, 0:4], start=False, stop=True), reads=[V1n, ETn], writes=[PF[5]])
                nsa_branch_finish(s, 1)
                kb.dma("pool", Wt, Wt[:], cwin, cwin[l, s].rearrange("(kt p) c -> p kt c", p=128)[:, :, 0:64])
                for kt in range(4):
                    kb.mm(lambda e, kt=kt: e.transpose(PT[1][0:64, kt * 128:(kt + 1) * 128], Wt[:, kt, :], identb[:]),
                          reads=[Wt, identb], writes=[PT[1]], last=(kt == 3))
                kb.op("dve", lambda e: e.tensor_copy(KwT[:], PT[1][0:64, 0:512].rearrange("p (k t) -> p k t", k=4)), reads=[PT[1]], writes=[KwT])
                kb.op("pool", lambda e: e.memset(V1w[:], 1.0), writes=[V1w])
                kb.dma("pool", V1w, V1w[:, :, 0:64], cwin, cwin[l, s].rearrange("(kt p) c -> p kt c", p=128)[:, :, 64:128])
                kb.op("pool", lambda e: e.memset(V1w[0:1, 0, :], 0.0), writes=[V1w])
                for kt in range(4):
                    kb.mm(lambda e, kt=kt: e.matmul(PF[3][:, kt * 4:(kt + 1) * 4], KwT[:, kt, :], qT_pad[:, 0:4], start=True, stop=True),
                          reads=[KwT, qT_pad], writes=[PF[3]], last=False)
                kb.mm(lambda e: e.matmul(PF[3][:, 16:20], kT3[:, 2, :], qT_pad[:, 0:4], start=True, stop=True), reads=[kT3, qT_pad], writes=[PF[3]])
                kb.op("act", lambda e: e.activation(Ef[:, 0:20], PF[3][:, 0:20], AF.Exp), reads=[PF[3]], writes=[Ef])
                kb.op("dve", lambda e: e.tensor_copy(ETn[:, 0:16], Ef[:, 0:16]), reads=[Ef], writes=[ETn])
                kb.op("dve", lambda e, oh=oh: e.tensor_scalar(ETn[:, 16:20], Ef[:, 16:20], oh, None, ALU.mult), reads=[Ef, identf], writes=[ETn])
                for kt in range(4):
                    kb.mm(lambda e, kt=kt: e.matmul(PF[5][0:65, 0:4], V1w[:, kt, :], ETn[:, kt * 4:(kt + 1) * 4], start=(kt == 0), stop=False),
                          reads=[V1w, ETn], writes=[PF[5]], last=False)
                kb.mm(lambda e: e.matmul(PF[5][0:65, 0:4], V1n[:, 1, :], ETn[:, 16:20], start=False, stop=True), reads=[V1n, ETn], writes=[PF[5]])
                nsa_branch_finish(s, 2)
                kb.dma("sp", Sst[0], Sst[0][:], shg, shg[l, s].rearrange("h k v -> k h v"))
                kb.mm(lambda e, s=s: e.matmul(PF[4][0:64, 0:256], V(identf, s, [[0, 64]]), proj[:, C_HI:C_HI + 256], start=True, stop=True),
                      reads=[identf, proj], writes=[PF[4]])
                kb.op("dve", lambda e, s=s: e.tensor_tensor(Sst[1][:], Sst[0][:], V(fkT, s, [[4, 4], [0, 64]]), ALU.mult), reads=[Sst[0], fkT], writes=[Sst[1]])
                kb.op("dve", lambda e, s=s: e.tensor_tensor(Sst[2][:], PF[4][0:64, 0:256].rearrange("p (h d) -> p h d", h=4),
                                                           V(fkT, 16 + s, [[4, 4], [0, 64]]), ALU.mult), reads=[PF[4], fkT], writes=[Sst[2]])
                kb.op("dve", lambda e: e.tensor_tensor(Sst[0][:], Sst[1][:], Sst[2][:], ALU.add), reads=[Sst[1], Sst[2]], writes=[Sst[0]])
                kb.dma("sp", o_hg_s, o_hg_s[l, s].rearrange("h k v -> k h v"), Sst[0], Sst[0][:], home=Sst[0], is_output=True)
                kb.op("act", lambda e: e.copy(Sbf[0][:], Sst[0][:]), reads=[Sst[0]], writes=[Sbf[0]])
                for h in range(4):
                    kb.mm(lambda e, h=h: e.matmul(PF[2][:, h * 64:(h + 1) * 64], qtT[:, h, :], Sbf[0][:, h, :], start=True, stop=True),
                          reads=[qtT, Sbf[0]], writes=[PF[2]], last=(h == 3))
                to, ao = acc(3, s % 2)
                tn, an = acc(3, (s + 1) % 2)
                kb.op("dve", lambda e, ao=ao, an=an, oh=oh: e.scalar_tensor_tensor(an, PF[2][:, 0:256], oh, ao, ALU.mult, ALU.add),
                      reads=[PF[2], identf, to], writes=[tn])
                kb.dma("pool", Mt, Mt[:], cmem, cmem[l, s].rearrange("(mc p) c -> p mc c", p=128)[:, :, 0:256])
                for mc in range(2):
                    for h in range(4):
                        kb.mm(lambda e, mc=mc, h=h: e.transpose(PT[1][0:64, (mc * 4 + h) * 128:(mc * 4 + h + 1) * 128], Mt[:, mc, h * 64:(h + 1) * 64], identb[:]),
                              reads=[Mt, identb], writes=[PT[1]], last=(mc == 1 and h == 3))
                kb.op("dve", lambda e: e.tensor_copy(memKT[:].rearrange("p h (mc t) -> p mc h t", mc=2),
                                                     PT[1][0:64, :].rearrange("p (mc h t) -> p mc h t", mc=2, h=4)), reads=[PT[1]], writes=[memKT])
                kb.op("pool", lambda e: e.memset(memV1[:], 1.0), writes=[memV1])
                for mc_ in range(2):
                    kb.dma("pool", memV1, memV1[:, mc_, :, 0:64], cmem,
                           cmem[l, s, mc_ * 128:(mc_ + 1) * 128, 256:512].rearrange("p (h d) -> p h d", h=4))
                for mc in range(2):
                    for h in range(4):
                        kb.mm(lambda e, mc=mc, h=h: e.matmul(PF[mc][:, h * 128:(h + 1) * 128], memKT[:, h, mc * 128:(mc + 1) * 128], qmT[:, h, :],
                                                             start=True, stop=True), reads=[memKT, qmT], writes=[PF[mc]], last=(h == 3))
                    kb.op("act", lambda e, mc=mc: e.activation(eM[mc][:], PF[mc][:], AF.Exp), reads=[PF[mc]], writes=[eM[mc]])
                for h in range(4):
                    for mc in range(2):
                        kb.mm(lambda e, mc=mc, h=h: e.matmul(PF[3][:, h * 65:(h + 1) * 65], eM[mc][:, h * 128:(h + 1) * 128], memV1[:, mc, h, :],
                                                             start=(mc == 0), stop=(mc == 1)), reads=[eM[mc], memV1], writes=[PF[3]], last=(mc == 1 and h == 3))
                pvm = PF[3][:, 0:260].rearrange("p (h d) -> p h d", h=4)
                kb.op("dve", lambda e: e.reciprocal(den4[2][:], pvm[:, :, 64]), reads=[PF[3]], writes=[den4[2]])
                kb.op("dve", lambda e: e.tensor_tensor(onsa[1][:], pvm[:, :, 0:64], V(den4[2], 0, [[1, 4], [0, 64]]), ALU.mult), reads=[PF[3], den4[2]], writes=[onsa[1]])
                to, ao = acc(4, s % 2)
                tn, an = acc(4, (s + 1) % 2)
                kb.op("dve", lambda e, ao=ao, an=an, oh=oh: e.scalar_tensor_tensor(an, onsa[1][:].rearrange("p h d -> p (h d)"), oh, ao, ALU.mult, ALU.add),
                      reads=[onsa[1], identf, to], writes=[tn])
            t0, aC = acc(0, 0)
            _, aS = acc(1, 0)
            _, aW = acc(2, 0)
            _, aH = acc(3, 0)
            tm, aM = acc(4, 0)
            kb.op("act", lambda e: e.activation(gts[:], proj[:, C_G:C_G + 12], AF.Sigmoid), reads=[proj], writes=[gts])
            for bi, ab in enumerate([aC, aS, aW]):
                kb.op("dve", lambda e, bi=bi, ab=ab: e.tensor_tensor(onsa[bi][:], ab.rearrange("p (h d) -> p h d", h=4), V(gts, bi, [[3, 4], [0, 64]]), ALU.mult),
                      reads=[t0, gts], writes=[onsa[bi]])
            kb.op("dve", lambda e: e.tensor_tensor(onsa[3][:], onsa[0][:], onsa[1][:], ALU.add), reads=[onsa[0], onsa[1]], writes=[onsa[3]])
            kb.op("dve", lambda e: e.tensor_tensor(ycat[:, 0:256].rearrange("p (h d) -> p h d", h=4), onsa[3][:], onsa[2][:], ALU.add),
                  reads=[onsa[3], onsa[2]], writes=[ycat])
            kb.op("dve", lambda e: e.tensor_tensor(hg[1][:], aH, aH, ALU.mult), reads=[t0], writes=[hg[1]])
            kb.op("dve", lambda e: e.tensor_reduce(s4[0][:], hg[1][:].rearrange("p (h d) -> p h d", h=4), AX.X, ALU.add), reads=[hg[1]], writes=[s4[0]])
            r_ = rms_rstd(s4[0], 64, s4[1:4])
            kb.op("dve", lambda e: e.tensor_tensor(hg[5][:].rearrange("p (h d) -> p h d", h=4), aH.rearrange("p (h d) -> p h d", h=4),
                                                   V(r_, 0, [[1, 4], [0, 64]]), ALU.mult), reads=[t0, r_], writes=[hg[5]])
            kb.op("dve", lambda e: e.tensor_tensor(ycat[:, 256:512], hg[5][:], hgon_b[:], ALU.mult), reads=[hg[5], hgon_b], writes=[ycat])
            kb.op("dve", lambda e: e.tensor_copy(ycat[:, 768:1024], aM), reads=[tm], writes=[ycat])
            kb.dma("pool", spb, spb[:], spool, spool[l].rearrange("s r c -> (s r) c"))
            kb.op("act", lambda e: e.copy(ubf[0][:], proj[:, C_U:C_U + 256]), reads=[proj], writes=[ubf[0]])
            for g in range(4):
                kb.mm(lambda e, g=g: e.matmul(PF[4][0:64, g * 128:(g + 1) * 128], spb[:, g * 64:(g + 1) * 64], cs["bs_s"][:, g, :], start=True, stop=False),
                      reads=[spb, cs["bs_s"]], writes=[PF[4]], last=False)
                kb.mm(lambda e, g=g: e.matmul(PF[4][0:64, g * 128:(g + 1) * 128], ubf[0][:, g * 64:(g + 1) * 64], cs["bself"][:, g, :], start=False, stop=True),
                      reads=[ubf[0], cs["bself"]], writes=[PF[4]], last=(g == 3))
            kb.op("act", lambda e: e.copy(dltT[:], PF[4][0:64, :].rearrange("p (g t) -> p g t", g=4)), reads=[PF[4]], writes=[dltT])
            for g in range(4):
                kb.mm(lambda e, g=g: e.matmul(PF[1][:, g * 64:(g + 1) * 64], dltT[:, g, :], PW[:, g, :], start=True, stop=True),
                      reads=[dltT, PW], writes=[PF[1]], last=(g == 3))
            kb.op("dve", lambda e: e.tensor_tensor(ycat[:, 512:768], PF[1][:, 0:256], psc_b[:], ALU.mult), reads=[PF[1], psc_b], writes=[ycat])

        WARM = [int(v) for v in os.environ.get("KWARM", "14,10,8,8").split(",")]

        def warm(n, bank):
            for k_ in range(n):
                kb.mm(lambda e: e.matmul(bank[:, 0:512], identb[:], W_in[:, 0, 0:512], start=True, stop=True),
                      reads=[identb, W_in], writes=[bank], last=(k_ == n - 1))

        def prompt_mem(l, b):
            for mc in range(2):
                kb.dma("sp", x_t, x_t[:], memp, memp[b, mc * 128:(mc + 1) * 128, :])
                norm_transpose(x_t, mgcol)
                for kc in range(8):
                    kb.mm(lambda e, kc=kc: e.matmul(PF[0][:], hT[:, kc, :], KTW[:, kc * 512:(kc + 1) * 512], start=(kc == 0), stop=(kc == 7)),
                          reads=[hT, KTW], writes=[PF[0]], last=(kc == 7))
                kb.op("act", lambda e: e.copy(mkv[:], PF[0][:]), reads=[PF[0]], writes=[mkv])
                kb.op("dve", lambda e: e.tensor_tensor(hg[0][:], mkv[:, 0:256], mkv[:, 0:256], ALU.mult), reads=[mkv], writes=[hg[0]])
                kb.op("dve", lambda e: e.tensor_reduce(s4[0][:], hg[0][:].rearrange("p (h d) -> p h d", h=4), AX.X, ALU.add),
                      reads=[hg[0]], writes=[s4[0]])
                r = rms_rstd(s4[0], 64, s4[1:4])
                kb.op("dve", lambda e: e.tensor_tensor(hg[1][:].rearrange("p (h d) -> p h d", h=4),
                                                       mkv[:, 0:256].rearrange("p (h d) -> p h d", h=4),
                                                       V(r, 0, [[1, 4], [0, 64]]), ALU.mult), reads=[mkv, r], writes=[hg[1]])
                kb.op("dve", lambda e: e.tensor_tensor(mrow[:, 0], hg[1][:].rearrange("p (h d) -> p h d", h=4),
                                                       V(mkn_b, 0, [[0, 4], [1, 64]]), ALU.mult), reads=[hg[1], mkn_b], writes=[mrow])
                kb.op("pool", lambda e: e.tensor_copy(mrow[:, 1], mkv[:, 256:512].rearrange("p (h d) -> p h d", h=4)),
                      reads=[mkv], writes=[mrow])
                kb.dma("sp", o_mem_p, o_mem_p[l, b, mc * 128:(mc + 1) * 128, :], mrow, mrow[:].rearrange("p a h d -> p (a h d)"),
                       home=mrow, is_output=True)
                kb.op("act", lambda e: e.copy(hgb[0][:].rearrange("p (h d) -> p h d", h=4), mrow[:, 0]), reads=[mrow], writes=[hgb[0]])
                kb.op("dve", lambda e, mc=mc: e.tensor_copy(memV1[:, mc, :, 0:64], mrow[:, 1]), reads=[mrow], writes=[memV1])
                for h in range(4):
                    kb.mm(lambda e, h=h: e.transpose(PT[1][0:64, h * 128:(h + 1) * 128], hgb[0][:, h * 64:(h + 1) * 64], identb[:]),
                          reads=[hgb[0], identb], writes=[PT[1]], last=(h == 3))
                kb.op("act", lambda e, mc=mc: e.copy(memKT[:, :, mc * 128:(mc + 1) * 128],
                                                     PT[1][0:64, 0:512].rearrange("p (h t) -> p h t", h=4)),
                      reads=[PT[1]], writes=[memKT])

        def prompt_tile(l, b, i, smp=False):
            src = xp if l == 0 else xmid
            dst = xmid if l == 0 else y_p
            DBG2 = int(os.environ.get("KDBG2", "99"))
            xin = x_t
            xs_src = xs if l == 0 else xsmid
            if smp:
                kb.dma("sp", x_t, x_t[0:SB_, :], xs_src, xs_src[:, :])
            else:
                kb.dma("sp", x_t, x_t[:], src, src[b, i * 128:(i + 1) * 128, :])
            if i >= 1 and DBG2 < 1:
                return
            if not smp:
                warm(WARM[1], PF[0])
            norm_transpose(xin, gcol)
            if i >= 1 and DBG2 < 2:
                return
            for n0 in range(0, NW, 512):
                if i >= 1 and DBG2 < 3 + n0 // 512:
                    return
                n1 = min(NW, n0 + 512)
                pb = PF[(n0 // 512) % 2]
                for kc in range(8):
                    kb.mm(lambda e, kc=kc, pb=pb, n0=n0, n1=n1: e.matmul(pb[:, 0:n1 - n0], hT[:, kc, :], W_in[:, kc, n0:n1],
                                                                        start=(kc == 0), stop=(kc == 7)),
                          reads=[hT, W_in], writes=[pb], last=(kc == 7))
                evac(proj, proj[:, n0:n1], pb, pb[:, 0:n1 - n0])
            if STAGE < 3:
                return
            if not smp:
                warm(WARM[0], PF[1])
            kb.op("dve", lambda e: e.tensor_tensor(sq11[:], proj[:, 0:704], proj[:, 0:704], ALU.mult), reads=[proj], writes=[sq11])
            kb.op("dve", lambda e: e.tensor_reduce(s11[0][:], sq11[:].rearrange("p (h d) -> p h d", h=11), AX.X, ALU.add),
                  reads=[sq11], writes=[s11[0]])
            r = rms_rstd(s11[0], 64, s11[1:4])
            kb.op("dve", lambda e: e.tensor_tensor(qkn[:], proj[:, 0:704].rearrange("p (h d) -> p h d", h=11),
                                                   V(r, 0, [[1, 11], [0, 64]]), ALU.mult), reads=[proj, r], writes=[qkn])
            kb.op(PENG, lambda e: e.tensor_tensor(qkg[:], qkn[:], gain11[:], ALU.mult), reads=[qkn, gain11], writes=[qkg])
            if SUB < 1:
                return
            if smp:
                kb.dma("sp", rope_t, rope_t[:], cd["cs_s"], bass.AP(cd["cs_s"].h, 0, [[0, 128], [32, 2], [1, 32]]))
            else:
                kb.dma("sp", rope_t, rope_t[:, 0, :], cd["cos_p"], cd["cos_p"][:, i, :])
                kb.dma("sp", rope_t, rope_t[:, 1, :], cd["sin_p"], cd["sin_p"][:, i, :])
                kb.dma("sp", cmpb_t, cmpb_t[:], cd["cmpb"], cd["cmpb"][:, i, :])
                kb.dma("sp", selb_t, selb_t[:], cd["selb"], cd["selb"][:, i, :])
            cosb = V(rope_t, 0, [[0, 7], [1, 32]])
            sinb = V(rope_t, 32, [[0, 7], [1, 32]])
            x1 = qkg[:, 0:7, 0:32]
            x2 = qkg[:, 0:7, 32:64]
            kb.op("dve", lambda e: e.tensor_tensor(rt[0][:], x1, cosb, ALU.mult), reads=[qkg, rope_t], writes=[rt[0]])
            kb.op(PENG, lambda e: e.tensor_tensor(rt[1][:], x2, sinb, ALU.mult), reads=[qkg, rope_t], writes=[rt[1]])
            kb.op("dve", lambda e: e.tensor_tensor(qkr[:, :, 0:32], rt[0][:], rt[1][:], ALU.subtract), reads=[rt[0], rt[1]], writes=[qkr])
            kb.op(PENG, lambda e: e.tensor_tensor(rt[2][:], x2, cosb, ALU.mult), reads=[qkg, rope_t], writes=[rt[2]])
            kb.op("dve", lambda e: e.tensor_tensor(rt[3][:], x1, sinb, ALU.mult), reads=[qkg, rope_t], writes=[rt[3]])
            kb.op(PENG, lambda e: e.tensor_tensor(qkr[:, :, 32:64], rt[2][:], rt[3][:], ALU.add), reads=[rt[2], rt[3]], writes=[qkr])
            if SUB < 2:
                return
            kb.op("pool", lambda e: e.tensor_copy(rows[:, 0:4:2, :], qkr[:, 4:6, :]), reads=[qkr], writes=[rows])
            kb.op("pool", lambda e: e.tensor_copy(rows[:, 1:4:2, :], proj[:, C_VC:C_VC + 128].rearrange("p (a d) -> p a d", a=2)),
                  reads=[proj], writes=[rows])
            if smp:
                kb.dma("sp", o_nsa_s, o_nsa_s[l], rows, rows[0:SB_].rearrange("p a d -> p (a d)"), home=rows, is_output=True)
                kb.op("pool", lambda e: e.tensor_copy(wrows[:, 0, :], qkr[:, 6, :]), reads=[qkr], writes=[wrows])
                kb.op("pool", lambda e: e.tensor_copy(wrows[:, 1, :], proj[:, C_VW:C_VW + 64]), reads=[proj], writes=[wrows])
                for s in range(SB_):
                    kb.dma("sp", o_win_s, o_win_s[l, s, 0:511, :], cwin, cwin[l, s, 1:512, :], home=wrows, is_output=True)
                    kb.dma("sp", o_win_s, o_win_s[l, s, 511:512, :], wrows, wrows[s:s + 1].rearrange("p a d -> p (a d)"), home=wrows, is_output=True)
                    kb.dma("sp", o_pool_s, o_pool_s[l, s, 0:14, :], spool, spool[l, s, 1:15, :], home=wrows, is_output=True)
                    kb.dma("sp", o_pool_s, o_pool_s[l, s, 14:15, :], proj, proj[s:s + 1, C_U:C_U + 256], home=proj, is_output=True)
            else:
                kb.dma("sp", o_nsa_p, o_nsa_p[l, b, i * 128:(i + 1) * 128, :], rows, rows[:].rearrange("p a d -> p (a d)"),
                       home=rows, is_output=True)
            if (not smp) and i >= NT - 4:
                kb.op("pool", lambda e: e.tensor_copy(wrows[:, 0, :], qkr[:, 6, :]), reads=[qkr], writes=[wrows])
                kb.op("pool", lambda e: e.tensor_copy(wrows[:, 1, :], proj[:, C_VW:C_VW + 64]), reads=[proj], writes=[wrows])
                kb.dma("sp", o_win_p, o_win_p[l, b, (i - (NT - 4)) * 128:(i - (NT - 4) + 1) * 128, :], wrows,
                       wrows[:].rearrange("p a d -> p (a d)"), home=wrows, is_output=True)
            if (not smp) and i == NT - 1:
                kb.dma("sp", o_pool_p, o_pool_p[l, b], proj, proj[113:128, C_U:C_U + 256], home=proj, is_output=True)
            if SUB < 3:
                return
            kb.op("dve", lambda e: e.tensor_scalar(qk_bf[:, 0:4, :], qkr[:, 0:4, :], 0.125, None, ALU.mult), reads=[qkr], writes=[qk_bf])
            kb.op("dve", lambda e: e.tensor_copy(qk_bf[:, 4:7, :], qkr[:, 4:7, :]), reads=[qkr], writes=[qk_bf])
            kb.op("dve", lambda e: e.tensor_scalar(qk_bf[:, 7:11, :], qkg[:, 7:11, :], 0.125, None, ALU.mult), reads=[qkg], writes=[qk_bf])
            if SUB < 4:
                return
            qk2d = qk_bf[:].rearrange('p h d -> p (h d)')
            for h in range(7):
                kb.mm(lambda e, h=h: e.transpose(PT[1][0:64, h * 128:(h + 1) * 128], qk2d[:, h * 64:(h + 1) * 64], identb[:]),
                      reads=[qk_bf, identb], writes=[PT[1]], last=(h == 6))
            if SUB < 5:
                return
            kb.op("dve", lambda e: e.tensor_copy(qT_all[:], PT[1][0:64, 0:512].rearrange("p (h t) -> p h t", h=4)), reads=[PT[1]], writes=[qT_all])
            if SUB < 6:
                return
            if smp:
                kb.op("dve", lambda e: e.tensor_copy(kT3[:], PT[1][0:64, 512:896].rearrange("p (h t) -> p h t", h=3)), reads=[PT[1]], writes=[kT3])
            else:
                kb.op("dve", lambda e: e.tensor_copy(KTW[0:64, :].rearrange("p (a s) -> p a s", a=2)[:, :, i * 128:(i + 1) * 128],
                                                     PT[1][0:64, 640:896].rearrange("p (h t) -> p h t", h=2)), reads=[PT[1]], writes=[KTW])
            if SUB < 7:
                return
            for h in range(4):
                kb.mm(lambda e, h=h: e.transpose(PT[1][0:64, h * 128:(h + 1) * 128], qk2d[:, (7 + h) * 64:(8 + h) * 64], identb[:]),
                      reads=[qk_bf, identb], writes=[PT[1]], last=(h == 3))
            kb.op("dve", lambda e: e.tensor_copy(qmT[:], PT[1][0:64, 0:512].rearrange("p (h t) -> p h t", h=4)), reads=[PT[1]], writes=[qmT])
            if smp:
                sample_mixers(l)
                kb.dma("sp", x_t, x_t[0:SB_, :], xs_src, xs_src[:, :])
            if not smp:
                kb.op("dve", lambda e: e.tensor_copy(V1[:, :, i, 0:64], proj[:, C_VS:C_VS + 128].rearrange("p (a d) -> p a d", a=2)),
                      reads=[proj], writes=[V1])
                if STAGE < 4:
                    return
                kb.op("dve", lambda e: e.tensor_copy(stg[:, 0], V(qkr, 4 * 64, [[0, 2], [1, 64]])), reads=[qkr], writes=[stg])
                kb.op("pool", lambda e: e.tensor_copy(stg[:, 1], V(proj, C_VC, [[0, 2], [1, 64]])), reads=[proj], writes=[stg])
                for a in range(2):
                    kb.mm(lambda e, a=a: e.transpose(PT[0][:, a * 128:(a + 1) * 128], stg[:, a].rearrange("p j d -> p (j d)"), identb[:]),
                          reads=[stg, identb], writes=[PT[0]], last=(a == 1))
                ptv = PT[0][:, 0:256].rearrange("p (a m j) -> p a m j", a=2, j=2)
                kb.op("dve", lambda e: e.tensor_copy(kT2[0:64], ptv[0:64, :, :, 0]), reads=[PT[0]], writes=[kT2])
                kb.op("dve", lambda e: e.tensor_copy(kT2[64:128], ptv[64:128, :, :, 1]), reads=[PT[0]], writes=[kT2])
                for a in range(2):
                    for hc in range(2):
                        for c in range(16):
                            kb.mm(lambda e, a=a, hc=hc, c=c: e.matmul(PF[2][:, (a * 2 + hc) * 4:(a * 2 + hc) * 4 + 4],
                                                                      W1[:, a, c, hc * 128:(hc + 1) * 128],
                                                                      V(kT2, a * 64 + c, [[16, 4]]), start=(c == 0), stop=(c == 15)),
                                  reads=[W1, kT2], writes=[PF[2]], last=(c == 15))
                for a in range(2):
                    for hc in range(2):
                        kb.op("act", lambda e, a=a, hc=hc: e.activation(hidT[:, a, hc, :], PF[2][:, (a * 2 + hc) * 4:(a * 2 + hc) * 4 + 4],
                                                                        AF.Silu, bias=biasT[:, a, hc:hc + 1]),
                              reads=[PF[2], biasT], writes=[hidT])
                for a in range(2):
                    for hc in range(2):
                        kb.mm(lambda e, a=a, hc=hc: e.matmul(PF[4][0:64, a * 4:a * 4 + 4], W2[:, a, hc, :], hidT[:, a, hc, :],
                                                             start=(hc == 0), stop=(hc == 1)),
                              reads=[W2, hidT], writes=[PF[4]], last=(hc == 1))
                kb.op("dve", lambda e: e.tensor_copy(CC[:, :, 4 * i:4 * i + 4], PF[4][0:64, 0:8].rearrange("p (a n) -> p a n", a=2)),
                      reads=[PF[4]], writes=[CC])
                if STAGE < 5:
                    return
                for h in range(4):
                    kb.mm(lambda e, h=h: e.matmul(PF[3][:, h * 64:(h + 1) * 64], qT_all[:, h, :], CC[:, 0, :], start=True, stop=True),
                          reads=[qT_all, CC], writes=[PF[3]], last=(h == 3))
                kb.op("dve", lambda e: e.tensor_tensor(sm[0][:], PF[3][:, 0:256].rearrange("p (h n) -> p h n", h=4),
                                                       V(cmpb_t, 0, [[0, 4], [1, 64]]), ALU.add), reads=[PF[3], cmpb_t], writes=[sm[0]])
                ucur = ubf[i % 2]
                uprev = ubf[(i + 1) % 2]
                kb.op("act", lambda e: e.copy(ucur[:], proj[:, C_U:C_U + 256]), reads=[proj], writes=[ucur])
                bc_ = cs["band0"] if i == 0 else cs["bandg"]
                for g in range(4):
                    kb.mm(lambda e, g=g: e.matmul(PF[4][0:64, g * 128:(g + 1) * 128], ucur[:, g * 64:(g + 1) * 64], bc_[:, g, :], start=True, stop=(i == 0)),
                          reads=[ucur, bc_], writes=[PF[4]], last=(i == 0 and g == 3))
                    if i > 0:
                        kb.mm(lambda e, g=g: e.matmul(PF[4][0:64, g * 128:(g + 1) * 128], uprev[:, g * 64:(g + 1) * 64], cs["bandp"][:, g, :], start=False, stop=True),
                              reads=[uprev, cs["bandp"]], writes=[PF[4]], last=(g == 3))
                kb.op("act", lambda e: e.copy(dltT[:], PF[4][0:64, :].rearrange("p (g t) -> p g t", g=4)), reads=[PF[4]], writes=[dltT])
                for g in range(4):
                    kb.mm(lambda e, g=g: e.matmul(PF[1][:, g * 64:(g + 1) * 64], dltT[:, g, :], PW[:, g, :], start=True, stop=True),
                          reads=[dltT, PW], writes=[PF[1]], last=(g == 3))

                for mc in range(2):
                    for h in range(4):
                        kb.mm(lambda e, mc=mc, h=h: e.matmul(PF[2 + mc][:, h * 128:(h + 1) * 128], memKT[:, h, mc * 128:(mc + 1) * 128], qmT[:, h, :],
                                                             start=True, stop=True), reads=[memKT, qmT], writes=[PF[2 + mc]], last=(h == 3))
                    kb.op("act", lambda e, mc=mc: e.activation(eM[mc][:], PF[2 + mc][:], AF.Exp), reads=[PF[2 + mc]], writes=[eM[mc]])
                for h in range(4):
                    for mc in range(2):
                        kb.mm(lambda e, mc=mc, h=h: e.matmul(PF[0][:, h * 65:(h + 1) * 65], eM[mc][:, h * 128:(h + 1) * 128], memV1[:, mc, h, :],
                                                             start=(mc == 0), stop=(mc == 1)), reads=[eM[mc], memV1], writes=[PF[0]],
                              last=(mc == 1 and h == 3))
                warm(WARM[2], PF[2])
                kb.op("dve", lambda e: e.tensor_reduce(s4[0][:], sm[0][:], AX.X, ALU.max), reads=[sm[0]], writes=[s4[0]])
                kb.op("dve", lambda e: e.tensor_scalar(s4[1][:], s4[0][:], -1000.0, None, ALU.max), reads=[s4[0]], writes=[s4[1]])
                kb.op("dve", lambda e: e.tensor_tensor(sm[1][:], sm[0][:], V(s4[1], 0, [[1, 4], [0, 64]]), ALU.subtract),
                      reads=[sm[0], s4[1]], writes=[sm[1]])
                kb.op("act", lambda e: e.activation(sm[2][:], sm[1][:], AF.Exp), reads=[sm[1]], writes=[sm[2]])
                kb.op("dve", lambda e: e.tensor_reduce(s4[2][:], sm[2][:], AX.X, ALU.add), reads=[sm[2]], writes=[s4[2]])
                kb.op("dve", lambda e: e.tensor_scalar(s4[3][:], s4[2][:], 1e-30, None, ALU.max), reads=[s4[2]], writes=[s4[3]])
                kb.op("dve", lambda e: e.reciprocal(s4[4][:], s4[3][:]), reads=[s4[3]], writes=[s4[4]])
                kb.op("dve", lambda e: e.tensor_tensor(pcf[:], sm[2][:], V(s4[4], 0, [[1, 4], [0, 64]]), ALU.mult), reads=[sm[2], s4[4]], writes=[pcf])
                kb.op("act", lambda e: e.copy(pcb[:], pcf[:]), reads=[pcf], writes=[pcb])
                for h in range(4):
                    kb.mm(lambda e, h=h: e.transpose(PT[1][0:64, h * 128:(h + 1) * 128], pcb[:, h, :], identb[:]),
                          reads=[pcb, identb], writes=[PT[1]], last=False)
                kb.mm(lambda e: e.transpose(PT[1][0:64, 512:576], CC[:, 1, :], identb[0:64, 0:64]), reads=[CC, identb], writes=[PT[1]])
                kb.op("dve", lambda e: e.tensor_copy(pTc[:], PT[1][0:64, 0:512].rearrange("p (h t) -> p h t", h=4)), reads=[PT[1]], writes=[pTc])
                kb.op("dve", lambda e: e.tensor_copy(vcc[:], PT[1][0:64, 512:576]), reads=[PT[1]], writes=[vcc])
                for h in range(4):
                    kb.mm(lambda e, h=h: e.matmul(PF[3][:, 256 + h * 64:256 + (h + 1) * 64], pTc[:, h, :], vcc[:], start=True, stop=True),
                          reads=[pTc, vcc], writes=[PF[3]], last=(h == 3))
                kb.op("act", lambda e: e.activation(gts[:], proj[:, C_G:C_G + 12], AF.Sigmoid), reads=[proj], writes=[gts])
                kb.op("dve", lambda e: e.tensor_tensor(onsa[0][:], PF[3][:, 256:512].rearrange("p (h d) -> p h d", h=4),
                                                       V(gts, 0, [[3, 4], [0, 64]]), ALU.mult), reads=[PF[3], gts], writes=[onsa[0]])
                kb.op("dve", lambda e: e.tensor_reduce(imp[:], V(hg[3], 0, [[2, 32], [64, 4], [1, 2]]), AX.XY, ALU.add), reads=[pcf], writes=[imp])
                kb.op("dve", lambda e: e.tensor_tensor(score[:], imp[:], selb_t[:], ALU.add), reads=[imp, selb_t], writes=[score])
                kb.op("dve", lambda e: e.tensor_tensor(cmpt[:], V(score, 0, [[0, 32], [1, 32]]), V(score, 0, [[1, 32], [0, 32]]), ALU.is_gt),
                      reads=[score], writes=[cmpt])
                kb.op("dve", lambda e: e.tensor_reduce(rank[:], cmpt[:], AX.X, ALU.add), reads=[cmpt], writes=[rank])
                kb.op("dve", lambda e: e.tensor_scalar(negsel[:, 0:32], rank[:], 15.5, NEG, ALU.is_ge, ALU.mult), reads=[rank], writes=[negsel])
                kb.mm(lambda e: e.transpose(PT[1][0:64, 0:128], negsel[:], identb[:]), reads=[negsel, identb], writes=[PT[1]])
                kb.op("dve", lambda e: e.tensor_copy(negselT[:], PT[1][0:32, 0:128]), reads=[PT[1]], writes=[negselT])
                if STAGE < 6:
                    return
                qflat = qT_all[:].rearrange("p h t -> p (h t)")

                def attn(branch, kts, cache_idx, v_idx, pacc, ot):
                    nk = len(kts)

                    def score(n_):
                        kt = kts[n_]
                        ps_ = PF[2 + (n_ % 2)]
                        ex = expT[n_ % 2]
                        extra = []
                        if branch == "s":
                            extra.append(("sel", None))
                        if kt == i:
                            extra.append(("mask", cs["causb"]))
                        if branch == "w" and kt == i - 4:
                            extra.append(("mask", cs["edgeb"]))
                        kb.mm(lambda e: e.matmul(ps_[:], KTW[0:64, cache_idx * SEQ + kt * 128:cache_idx * SEQ + (kt + 1) * 128], qflat,
                                                 start=True, stop=(len(extra) == 0)),
                              reads=[KTW, qT_all], writes=[ps_], last=(len(extra) == 0))
                        for xi, (kind, mt) in enumerate(extra):
                            lastx = xi == len(extra) - 1
                            if kind == "sel":
                                kb.mm(lambda e: e.matmul(ps_[:].rearrange("p (h t) -> p h t", h=4), cs["e32"][:, kt * 128:(kt + 1) * 128],
                                                         V(negselT, 0, [[0, 4], [1, 128]]), start=False, stop=lastx),
                                      reads=[cs["e32"], negselT], writes=[ps_], last=lastx)
                            else:
                                kb.mm(lambda e: e.matmul(ps_[:].rearrange("p (h t) -> p h t", h=4), identb[:], V(mt, 0, [[0, 4], [1, 128]]),
                                                         start=False, stop=lastx), reads=[identb, mt], writes=[ps_], last=lastx)
                        kb.op("act", lambda e: e.activation(ex[:], ps_[:], AF.Exp), reads=[ps_], writes=[ex])

                    def pv(n_):
                        kt = kts[n_]
                        ex = expT[n_ % 2]
                        kb.mm(lambda e: e.matmul(pacc[0:65, :], V1[:, v_idx, kt, :], ex[:], start=(n_ == 0), stop=(n_ == nk - 1)),
                              reads=[V1, ex], writes=[pacc], last=(n_ == nk - 1))

                    score(0)
                    for n_ in range(nk):
                        if n_ + 1 < nk:
                            score(n_ + 1)
                        pv(n_)
                    kb.op("act", lambda e: e.copy(ot[:], pacc[0:65, :]), reads=[pacc], writes=[ot])

                for bi, (br, kts, cidx) in enumerate([("s", list(range(0, i + 1)), 0), ("w", list(range(max(0, i - 4), i + 1)), 1)]):
                    attn(br, kts, cidx, bi, PF[5], oT[bi])
                    pback = PF[2 + bi]
                    for h in range(4):
                        kb.mm(lambda e, bi=bi, h=h, pback=pback: e.transpose(pback[:, h * 65:(h + 1) * 65], oT[bi][:, h * 128:(h + 1) * 128], identf[0:65, 0:65]),
                              reads=[oT[bi], identf], writes=[pback], last=(h == 3))
                    pv = pback[:, 0:260].rearrange("p (h d) -> p h d", h=4)
                    kb.op("dve", lambda e, pv=pv, bi=bi: e.reciprocal(s4[5 + bi][:], pv[:, :, 64]), reads=[pback], writes=[s4[5 + bi]])
                    kb.op("dve", lambda e, bi=bi: e.tensor_tensor(s4[bi][:], s4[5 + bi][:], V(gts, 1 + bi, [[3, 4]]), ALU.mult),
                          reads=[s4[5 + bi], gts], writes=[s4[bi]])
                    kb.op("dve", lambda e, pv=pv, bi=bi: e.tensor_tensor(onsa[1 + bi][:], pv[:, :, 0:64], V(s4[bi], 0, [[1, 4], [0, 64]]), ALU.mult),
                          reads=[pback, s4[bi]], writes=[onsa[1 + bi]])
                kb.op(PENG, lambda e: e.tensor_tensor(onsa[3][:], onsa[0][:], onsa[1][:], ALU.add), reads=[onsa[0], onsa[1]], writes=[onsa[3]])
                kb.op(PENG, lambda e: e.tensor_tensor(ycat[:, 0:256].rearrange("p (h d) -> p h d", h=4), onsa[3][:], onsa[2][:], ALU.add),
                      reads=[onsa[3], onsa[2]], writes=[ycat])

                if STAGE < 7:
                    return
                kb.op("dve", lambda e: e.tensor_tensor(ycat[:, 512:768], PF[1][:, 0:256], psc_b[:], ALU.mult), reads=[PF[1], psc_b], writes=[ycat])
                pv = PF[0][:, 0:260].rearrange("p (h d) -> p h d", h=4)
                kb.op("dve", lambda e: e.reciprocal(s4[7][:], pv[:, :, 64]), reads=[PF[0]], writes=[s4[7]])
                kb.op("dve", lambda e: e.tensor_tensor(ycat[:, 768:1024].rearrange("p (h d) -> p h d", h=4), pv[:, :, 0:64],
                                                       V(s4[7], 0, [[1, 4], [0, 64]]), ALU.mult), reads=[PF[0], s4[7]], writes=[ycat])


                warm(WARM[3], PF[1])
                kb.op("act", lambda e: e.activation(hg[0][:], proj[:, C_HF:C_HF + 256], AF.Sigmoid), reads=[proj], writes=[hg[0]])
                kb.op("dve", lambda e: e.tensor_tensor(hg[1][:], hg[0][:], oml_b[:], ALU.mult), reads=[hg[0], oml_b], writes=[hg[1]])
                kb.op("dve", lambda e: e.tensor_tensor(hg[2][:], hg[1][:], lb_b[:], ALU.add), reads=[hg[1], lb_b], writes=[hg[2]])
                kb.op("act", lambda e: e.activation(hg[3][:], hg[2][:], AF.Ln), reads=[hg[2]], writes=[hg[3]])
                kb.op("dve", lambda e: e.tensor_scalar(hg[4][:], hg[2][:], -1.0, 1.0, ALU.mult, ALU.add), reads=[hg[2]], writes=[hg[4]])
                kb.mm(lambda e: e.matmul(PF[0][:, 0:256], cs["uincl"][:], hg[3][:], start=True, stop=True), reads=[cs["uincl"], hg[3]], writes=[PF[0]], last=False)
                kb.mm(lambda e: e.matmul(PF[0][:, 256:512], cs["urest"][:], hg[3][:], start=True, stop=True), reads=[cs["urest"], hg[3]], writes=[PF[0]])
                for h in range(4):
                    kb.mm(lambda e, h=h: e.matmul(PF[4][0:64, h * 2:h * 2 + 2], hg[3][:, h * 64:(h + 1) * 64], cs["cmask"][:], start=True, stop=True),
                          reads=[hg[3], cs["cmask"]], writes=[PF[4]], last=(h == 3))
                kb.op("act", lambda e: e.activation(eGl[:].rearrange("p h c -> p (h c)"), PF[4][0:64, 0:8], AF.Exp), reads=[PF[4]], writes=[eGl])
                kb.op("act", lambda e: e.activation(hg[0][:], PF[0][:, 0:256], AF.Exp), reads=[PF[0]], writes=[hg[0]])
                kb.op("act", lambda e: e.activation(hg[1][:], PF[0][:, 0:256], AF.Exp, scale=-1.0), reads=[PF[0]], writes=[hg[1]])
                kb.op("act", lambda e: e.activation(hg[5][:], PF[0][:, 256:512], AF.Exp), reads=[PF[0]], writes=[hg[5]])
                kb.op("dve", lambda e: e.tensor_tensor(hgb[0][:], proj[:, C_HQ:C_HQ + 256], hg[0][:], ALU.mult), reads=[proj, hg[0]], writes=[hgb[0]])
                kb.op(PENG, lambda e: e.tensor_tensor(hgb[1][:], hg[4][:], hg[1][:], ALU.mult), reads=[hg[4], hg[1]], writes=[hgb[1]])
                for c in range(2):
                    kb.op("dve", lambda e, c=c: e.scalar_tensor_tensor(khz[:, c, :], hg[4][:], cs["cmask"][:, c:c + 1], hg[5][:], ALU.mult, ALU.mult),
                          reads=[hg[4], hg[5], cs["cmask"]], writes=[khz])
                kb.op("act", lambda e: e.copy(hgb[2][:], proj[:, C_HI:C_HI + 256]), reads=[proj], writes=[hgb[2]])
                for h in range(4):
                    kb.mm(lambda e, h=h: e.transpose(PT[1][0:64, h * 128:(h + 1) * 128], hgb[0][:, h * 64:(h + 1) * 64], identb[:]),
                          reads=[hgb[0], identb], writes=[PT[1]], last=False)
                for h in range(4):
                    kb.mm(lambda e, h=h: e.transpose(PT[1][0:64, 512 + h * 128:512 + (h + 1) * 128], hgb[1][:, h * 64:(h + 1) * 64], identb[:]),
                          reads=[hgb[1], identb], writes=[PT[1]], last=(h == 3))
                pq = PT[1][0:64, 0:512].rearrange("p (h t) -> p h t", h=4)
                kb.op("dve", lambda e: e.tensor_copy(qtT[:], pq), reads=[PT[1]], writes=[qtT])
                kb.op("dve", lambda e: e.tensor_copy(qtTz[:, :, 0, 0:64], pq[:, :, 0:64]), reads=[PT[1]], writes=[qtTz])
                kb.op("dve", lambda e: e.tensor_copy(qtTz[:, :, 1, 64:128], pq[:, :, 64:128]), reads=[PT[1]], writes=[qtTz])
                kb.op("dve", lambda e: e.tensor_copy(ktT[:], PT[1][0:64, 512:1024].rearrange("p (h t) -> p h t", h=4)), reads=[PT[1]], writes=[ktT])
                for h in range(4):
                    kb.mm(lambda e, h=h: e.matmul(PF[0][:, h * 128:(h + 1) * 128], ktT[:, h, :], qtT[:, h, :], start=True, stop=True),
                          reads=[ktT, qtT], writes=[PF[0]], last=(h == 3))
                kb.op("dve", lambda e: e.tensor_tensor(ATs[:], PF[0][:].rearrange("p (h t) -> p h t", h=4), V(cs["bdmask"], 0, [[0, 4], [1, 128]]), ALU.mult),
                      reads=[PF[0], cs["bdmask"]], writes=[ATs])
                kb.op("act", lambda e: e.copy(Sbf[0][:], Sst[0][:]), reads=[Sst[0]], writes=[Sbf[0]])
                for c in range(2):
                    for h in range(4):
                        kb.mm(lambda e, c=c, h=h: e.matmul(PF[4][0:64, 64 + h * 64:64 + (h + 1) * 64],
                                                           khz[:, c, h * 64:(h + 1) * 64], hgb[2][:, h * 64:(h + 1) * 64], start=True, stop=True),
                              reads=[khz, hgb[2]], writes=[PF[4]], last=(h == 3))
                    kb.op("dve", lambda e, c=c: e.tensor_tensor(Sst[2][:], Sst[c][:], V(eGl, c, [[2, 4], [0, 64]]), ALU.mult),
                          reads=[Sst[c], eGl], writes=[Sst[2]])
                    dstS = Sst[1] if c == 0 else Sst[0]
                    kb.op("dve", lambda e, dstS=dstS: e.tensor_tensor(dstS[:], Sst[2][:], PF[4][0:64, 64:320].rearrange("p (h d) -> p h d", h=4), ALU.add),
                          reads=[Sst[2], PF[4]], writes=[dstS])
                    if c == 0:
                        kb.op("act", lambda e: e.copy(Sbf[1][:], Sst[1][:]), reads=[Sst[1]], writes=[Sbf[1]])
                for h in range(4):
                    kb.mm(lambda e, h=h: e.matmul(PF[2][:, h * 64:(h + 1) * 64], ATs[:, h, :], hgb[2][:, h * 64:(h + 1) * 64],
                                                  start=True, stop=False), reads=[ATs, hgb[2]], writes=[PF[2]], last=False)
                    kb.mm(lambda e, h=h: e.matmul(PF[2][:, h * 64:(h + 1) * 64], qtTz[:, h, 0, :], Sbf[0][:, h, :], start=False, stop=False),
                          reads=[qtTz, Sbf[0]], writes=[PF[2]], last=False)
                    kb.mm(lambda e, h=h: e.matmul(PF[2][:, h * 64:(h + 1) * 64], qtTz[:, h, 1, :], Sbf[1][:, h, :], start=False, stop=True),
                          reads=[qtTz, Sbf[1]], writes=[PF[2]], last=(h == 3))
                if i == NT - 1:
                    kb.dma("sp", o_hg_p, o_hg_p[l, b].rearrange("h k v -> k h v"), Sst[0], Sst[0][:], home=Sst[0], is_output=True)
                kb.op("act", lambda e: e.copy(hg[0][:], PF[2][:, 0:256]), reads=[PF[2]], writes=[hg[0]])
                kb.op("dve", lambda e: e.tensor_tensor(hg[1][:], hg[0][:], hg[0][:], ALU.mult), reads=[hg[0]], writes=[hg[1]])
                kb.op("dve", lambda e: e.tensor_reduce(s4[0][:], hg[1][:].rearrange("p (h d) -> p h d", h=4), AX.X, ALU.add), reads=[hg[1]], writes=[s4[0]])
                r = rms_rstd(s4[0], 64, s4[1:4])
                kb.op("dve", lambda e: e.tensor_tensor(hg[5][:].rearrange("p (h d) -> p h d", h=4), hg[0][:].rearrange("p (h d) -> p h d", h=4),
                                                       V(r, 0, [[1, 4], [0, 64]]), ALU.mult), reads=[hg[0], r], writes=[hg[5]])
                kb.op(PENG, lambda e: e.tensor_tensor(ycat[:, 256:512], hg[5][:], hgon_b[:], ALU.mult), reads=[hg[5], hgon_b], writes=[ycat])

                if STAGE < 8:
                    return
                if STAGE < 9:
                    return
            if DBG and l == 0 and not smp:
                kb.dma("sp", dbg_ycat, dbg_ycat[b, i * 128:(i + 1) * 128, :], ycat, ycat[:], home=ycat, is_output=True)
            kb.op("act", lambda e: e.activation(sz[:], proj[:, C_Z:C_Z + D], AF.Silu), reads=[proj], writes=[sz])
            kb.op("dve", lambda e: e.tensor_tensor(yg[:], ycat[:], sz[:], ALU.mult), reads=[ycat, sz], writes=[yg])
            for kc in range(8):
                kb.mm(lambda e, kc=kc: e.transpose(PT[1][:, kc * 128:(kc + 1) * 128], yg[:, kc * 128:(kc + 1) * 128], identb[:]),
                      reads=[yg, identb], writes=[PT[1]], last=(kc == 7))
            kb.op("dve", lambda e: e.tensor_copy(yT[:], PT[1][:].rearrange("p (k t) -> p k t", k=8)), reads=[PT[1]], writes=[yT])
            for ncn in range(2):
                pb = PF[ncn]
                for kc in range(8):
                    kb.mm(lambda e, kc=kc, pb=pb, ncn=ncn: e.matmul(pb[:], yT[:, kc, :], W_out[:, kc, ncn * 512:(ncn + 1) * 512],
                                                                   start=(kc == 0), stop=(kc == 7)), reads=[yT, W_out], writes=[pb], last=(kc == 7))
                kb.op("dve", lambda e, pb=pb, ncn=ncn: e.tensor_tensor(xo[:, ncn * 512:(ncn + 1) * 512], pb[:], xin[:, ncn * 512:(ncn + 1) * 512], ALU.add),
                      reads=[pb, xin], writes=[xo])
            if smp:
                if l == DEPTH - 1:
                    kb.dma("sp", y_s, y_s[:], xo, xo[0:SB_, :], home=xo, is_output=True)
                else:
                    kb.dma("sp", xsmid, xsmid[:, :], xo, xo[0:SB_, :], home=xo)
            else:
                kb.dma("sp", dst, dst[b, i * 128:(i + 1) * 128, :], xo, xo[:], home=xo, is_output=(l == DEPTH - 1))

        def prompt_seq(l, b):
            kb.op("pool", lambda e: e.memset(V1[:], 1.0), writes=[V1])
            kb.op("pool", lambda e: e.memset(memV1[:], 1.0), writes=[memV1])
            kb.op("pool", lambda e: e.memset(CC[:], 0.0), writes=[CC])
            kb.op("pool", lambda e: e.memset(qtTz[:], 0.0), writes=[qtTz])
            kb.op("pool", lambda e: e.memset(Sst[0][:], 0.0), writes=[Sst[0]])
            kb.dma("pool", KTW, KTW[:].rearrange("p (kc n) -> p kc n", kc=8), w_mem_kv, w_mem_kv[l].rearrange("(kc p) n -> p kc n", p=128))
            for _rep in range(int(os.environ.get("KMEMREP", "1"))):
                prompt_mem(l, b)
            if STAGE < 2:
                return
            for i in range(min(NT, NTILES_DBG)):
                prompt_tile(l, b, i)

        if do_sample:
            sample_init()
        for l in range(DEPTH):
            load_weights(l)
            if STAGE < 1:
                break
            if do_sample:
                sample_idx(l)
                prompt_tile(l, 0, 0, smp=True)
            if not int(os.environ.get("KPROMPT", "1")):
                continue
            for b in range(PB):
                prompt_seq(l, b)
            if STAGE < 10:
                break
        kb.finish()
        print("instructions:", kb.n_inst, "dma sems:", kb.nd, flush=True)
    return nc, [k for k in di.keys() if k not in SKIP_IN]


_CACHE = {}


def kernel(x_prompt, x_sample, mem_prompt, cache_nsa, cache_nsa_win, state_hgrn, state_pool, cache_mem,
           page_table, norm_g, w_in, w_out, nsa_qn, nsa_kn, cmp_pe, cmp_w1, cmp_w2, hg_lb, hg_on,
           pool_w, pool_scale, mem_norm, w_mem_kv, mem_qn, mem_kn):
    f = lambda a: np.ascontiguousarray(np.asarray(a, dtype=np.float32))
    if "nc" not in _CACHE:
        _CACHE["nc"], _CACHE["names"] = build_program()
        _CACHE["consts"] = host_consts()
    nc = _CACHE["nc"]
    consts = _CACHE["consts"]
    x_prompt, x_sample, mem_prompt = f(x_prompt), f(x_sample), f(mem_prompt)
    cnsa_full = f(cache_nsa).reshape(DEPTH * NPHYS * 64, 512)
    cache_nsa_win, state_hgrn, state_pool, cache_mem = f(cache_nsa_win), f(state_hgrn), f(state_pool), f(cache_mem)
    pt = np.asarray(page_table, dtype=np.int32)
    shared = {
        "norm_g": f(norm_g), "w_in": f(w_in), "w_out": f(w_out), "nsa_qn": f(nsa_qn), "nsa_kn": f(nsa_kn),
        "cmp_pe": f(cmp_pe), "cmp_w1": f(cmp_w1), "cmp_w2": f(cmp_w2), "hg_lb": f(hg_lb), "hg_on": f(hg_on),
        "pool_w": f(pool_w), "pool_scale": f(pool_scale), "mem_norm": f(mem_norm), "w_mem_kv": f(w_mem_kv),
        "mem_qn": f(mem_qn), "mem_kn": f(mem_kn), "cnsa": cnsa_full,
    }
    for name, _, _ in CONST_SPECS:
        shared["c_" + name] = consts[name]
    in_maps = []
    for c in range(NCORES):
        ps = slice(PB * c, PB * (c + 1))
        ss = slice(SB_ * c, SB_ * (c + 1))
        m = dict(shared)
        m["xp"] = np.ascontiguousarray(x_prompt[ps])
        m["memp"] = np.ascontiguousarray(mem_prompt[ps])
        m["xs"] = np.ascontiguousarray(x_sample[ss, 0, :])
        m["cwin"] = np.ascontiguousarray(cache_nsa_win[:, ss].reshape(DEPTH, SB_, 512, 128))
        m["shg"] = np.ascontiguousarray(state_hgrn[:, ss])
        m["spool"] = np.ascontiguousarray(state_pool[:, ss])
        m["cmem"] = np.ascontiguousarray(cache_mem[:, ss].reshape(DEPTH, SB_, 256, 512))
        ptc = pt[ss]
        ptb = ptc.reshape(SB_, 64, 2).transpose(2, 0, 1)
        m["ptb"] = np.ascontiguousarray(np.repeat(ptb, 64, axis=0).astype(np.int32))
        in_maps.append(m)
    declared = set(_CACHE["names"])
    in_maps = [{k: v for k, v in m.items() if k in declared} for m in in_maps]
    ncores = int(os.environ.get("KCORES", str(NCORES)))
    res = run_bass_kernel_spmd(nc, in_maps[:ncores], core_ids=list(range(ncores)))
    R = list(res.results)
    while len(R) < NCORES:
        R.append(R[0])

    def cat(name, axis):
        return np.concatenate([np.asarray(r[name]) for r in R], axis=axis)

    y_prompt = cat("y_p", 0)
    y_sample = cat("y_s", 0).reshape(32, 1, D)
    new_nsa_p = cat("o_nsa_p", 1).reshape(DEPTH, 16, SEQ, 4, 1, 64)
    new_win_p = cat("o_win_p", 1).reshape(DEPTH, 16, 512, 2, 1, 64)
    new_hgrn_p = cat("o_hg_p", 1)
    new_pool_p = cat("o_pool_p", 1)
    new_mem_p = cat("o_mem_p", 1).reshape(DEPTH, 16, 256, 2, 4, 64)
    new_nsa_s = cat("o_nsa_s", 1).reshape(DEPTH, 32, 1, 4, 1, 64)
    new_win_s = cat("o_win_s", 1).reshape(DEPTH, 32, 512, 2, 1, 64)
    new_hgrn_s = cat("o_hg_s", 1)
    new_pool_s = cat("o_pool_s", 1)
    return (y_prompt, y_sample, new_nsa_p, new_win_p, new_hgrn_p, new_pool_p, new_mem_p,
            new_nsa_s, new_win_s, new_hgrn_s, new_pool_s)
```

```python
import contextlib
import math
import numpy as np
import ml_dtypes
import concourse.bass as bass
import concourse.mybir as mybir
from concourse.bass_utils import run_bass_kernel_spmd

F32 = mybir.dt.float32
BF16 = mybir.dt.bfloat16
I32 = mybir.dt.int32
ALU = mybir.AluOpType
AF = mybir.ActivationFunctionType
AX = mybir.AxisListType

NCORES = 8
D = 1024
NW = 2956
SEQ = 2048
NT = SEQ // 128
DEPTH = 2
EPS = 1e-6
NEG = -30000.0
PB = 2
SB_ = 4
PAST = 16384
NPHYS = 5120

WPERM = [(0, 0, 320), (320, 384, 64), (384, 512, 64), (448, 1676, 256), (704, 320, 64), (768, 448, 64),
         (832, 576, 64), (896, 640, 12), (908, 652, 1024), (1932, 1932, 1024)]
C_Q, C_KC, C_KS, C_KW, C_QM, C_VC, C_VS, C_VW, C_G, C_HQ, C_HF, C_HI, C_U, C_Z = (
    0, 256, 320, 384, 448, 704, 768, 832, 896, 908, 1164, 1420, 1676, 1932)


class T:
    __slots__ = ("h", "name", "last_w", "readers", "dsem", "dcnt")

    def __init__(self, h, name):
        self.h = h
        self.name = name
        self.last_w = None
        self.readers = {}
        self.dsem = None
        self.dcnt = 0

    def __getitem__(self, k):
        return self.h[k]


class KB:
    def __init__(self, nc, stack):
        self.nc = nc
        self.stack = stack
        self.eng = {"pe": nc.tensor, "act": nc.scalar, "dve": nc.vector, "pool": nc.gpsimd, "sp": nc.sync}
        self.sem = {}
        self.cnt = {}
        self.waited = {}
        for en in self.eng:
            self.sem[en] = stack.enter_context(nc.semaphore("sem_" + en))
            self.cnt[en] = 0
            self.waited[en] = {}
        self.nd = 0
        self.out_marks = []
        self.n_inst = 0
        self.free_dsems = []

    def sb(self, name, shape, dt, stack=None):
        t = T((stack or self.stack).enter_context(self.nc.sbuf_tensor(name, list(shape), dt)), name)
        return t

    def ps(self, name, shape, dt=F32):
        return T(self.stack.enter_context(self.nc.psum_tensor(name, list(shape), dt)), name)

    def dram(self, name, shape, dt, kind=None):
        if kind is None:
            h = self.nc.dram_tensor(name, list(shape), dt)
        else:
            h = self.nc.dram_tensor(name, list(shape), dt, kind=kind)
        return T(h, name)

    def _wait(self, en, deps):
        w = self.waited[en]
        e = self.eng[en]
        best = {}
        for d in deps:
            if d is None:
                continue
            k, c = d
            if best.get(k, 0) < c:
                best[k] = c
        for k, c in best.items():
            if k == "pe" and en == "pe":
                continue
            if w.get(k, 0) < c:
                e.wait_ge(self.sem[k], c)
                w[k] = c

    def _deps(self, reads, writes):
        deps = []
        for t in reads:
            deps.append(t.last_w)
        for t in writes:
            deps.append(t.last_w)
            deps.extend(t.readers.items())
        return deps

    def _mark(self, mark, reads, writes):
        k, c = mark
        for t in writes:
            t.last_w = mark
            t.readers = {}
        for t in reads:
            if not any(t is w_ for w_ in writes):
                if t.readers.get(k, 0) < c:
                    t.readers[k] = c

    @staticmethod
    def _n(ts):
        return [getattr(t, "t", t) for t in ts]

    def op(self, en, fn, reads=(), writes=()):
        reads, writes = self._n(reads), self._n(writes)
        self._wait(en, self._deps(reads, writes))
        inst = fn(self.eng[en])
        self.cnt[en] += 1
        inst.then_inc(self.sem[en], 1)
        self._mark((en, self.cnt[en]), reads, writes)
        self.n_inst += 1
        return inst

    def mm(self, fn, reads=(), writes=(), last=True):
        reads, writes = self._n(reads), self._n(writes)
        self._wait("pe", self._deps(reads, writes))
        inst = fn(self.eng["pe"])
        self.n_inst += 1
        if last:
            self.cnt["pe"] += 1
            inst.then_inc(self.sem["pe"], 1)
            self._mark(("pe", self.cnt["pe"]), reads, writes)
        else:
            self._mark(("pe", self.cnt["pe"] + 1), reads, ())
        return inst

    def _dsem(self, t):
        if t.dsem is None:
            key = "d%d" % self.nd
            self.nd += 1
            self.sem[key] = self.stack.enter_context(self.nc.semaphore("sem_" + key))
            t.dsem = key
        return t.dsem

    def dma(self, q, out_t, out_ap, in_t, in_ap, home=None, is_output=False, **kw):
        out_t, in_t = getattr(out_t, "t", out_t), getattr(in_t, "t", in_t)
        if home is not None:
            home = getattr(home, "t", home)
        self._wait(q, self._deps([in_t], [out_t]))
        if home is None:
            home = out_t
        key = self._dsem(home)
        inst = self.eng[q].dma_start(out=out_ap, in_=in_ap, **kw)
        home.dcnt += 16
        inst.then_inc(self.sem[key], 16)
        mark = (key, home.dcnt)
        self._mark(mark, [in_t], [out_t])
        if is_output:
            self.out_marks.append(mark)
        self.n_inst += 1
        return inst

    def gather(self, out_t, out_ap, in_t, in_ap, idx_t, idx_ap):
        self._wait("pool", self._deps([in_t, idx_t], [out_t]))
        key = self._dsem(out_t)
        inst = self.nc.gpsimd.indirect_dma_start(out=out_ap, out_offset=None, in_=in_ap,
                                                 in_offset=bass.IndirectOffsetOnAxis(ap=idx_ap, axis=0))
        out_t.dcnt += 16
        inst.then_inc(self.sem[key], 16)
        self._mark((key, out_t.dcnt), [in_t, idx_t], [out_t])
        self.n_inst += 1

    def barrier(self):
        marks = [(en, c) for en, c in self.cnt.items() if c > 0]
        for en in self.eng:
            self._wait(en, marks)

    def finish(self):
        self._wait("sp", self.out_marks)
        self._wait("sp", [(en, c) for en, c in self.cnt.items() if c > 0 and en != "sp"])


def V(t, off, dims, p0=0, npart=None):
    full = t.h[:].ap
    pstep = full[0][0]
    if npart is None:
        npart = full[0][1] - p0
    return bass.AP(t.h, p0 * pstep + off, [[pstep, npart]] + [list(d) for d in dims])


def host_consts():
    c = {}
    half = 32
    inv = (10000.0 ** (-np.arange(half, dtype=np.float32) / half)).astype(np.float32)
    pos = np.arange(SEQ, dtype=np.float32)
    ang = (pos[:, None] * inv[None, :]).astype(np.float32)
    cs = np.cos(ang).astype(np.float32).reshape(NT, 128, half).transpose(1, 0, 2)
    sn = np.sin(ang).astype(np.float32).reshape(NT, 128, half).transpose(1, 0, 2)
    c["cos_p"] = np.ascontiguousarray(cs)
    c["sin_p"] = np.ascontiguousarray(sn)
    angs = (np.float32(PAST) * inv).astype(np.float32)
    c["cs_s"] = np.stack([np.cos(angs), np.sin(angs)]).astype(np.float32).reshape(1, 2, half)
    t = np.arange(SEQ)[:, None]
    n = np.arange(64)[None, :]
    done = ((n + 1) * 32 - 1) <= t
    c["cmpb"] = np.ascontiguousarray(np.where(done, 0.0, NEG).astype(np.float32).reshape(NT, 128, 64).transpose(1, 0, 2))
    j = np.arange(32)[None, :]
    blk = (t // 64)
    forced = (j == 0) | (j == blk) | (j == blk - 1)
    avail = j <= blk
    sb = np.where(avail, np.where(forced, 1e4, 0.0), -1e4).astype(np.float32)
    c["selb"] = np.ascontiguousarray(sb.reshape(NT, 128, 32).transpose(1, 0, 2))
    key = np.arange(SEQ)[None, :]
    c["e32"] = (np.arange(32)[:, None] == key // 64).astype(ml_dtypes.bfloat16)
    kk = np.arange(128)[:, None]
    tt = np.arange(128)[None, :]
    c["causb"] = np.where(kk > tt, NEG, 0.0).astype(ml_dtypes.bfloat16)
    c["edgeb"] = np.where(kk <= tt, NEG, 0.0).astype(ml_dtypes.bfloat16)
    c["identb"] = np.eye(128, dtype=np.float32).astype(ml_dtypes.bfloat16)
    c["identf"] = np.eye(128, dtype=np.float32)
    same = (kk // 64) == (tt // 64)
    c["bdmask"] = (same & (kk <= tt)).astype(np.float32)
    c["uincl"] = (same & (kk <= tt)).astype(np.float32)
    c["urest"] = (same & (kk > tt)).astype(np.float32)
    c["cmask"] = np.stack([(np.arange(128) < 64), (np.arange(128) >= 64)], 1).astype(np.float32)
    ws = (2, 4, 8, 16)
    bg = np.zeros((128, 4, 128), np.float32)
    b0 = np.zeros((128, 4, 128), np.float32)
    bp = np.zeros((128, 4, 128), np.float32)
    for g, w in enumerate(ws):
        for tq in range(128):
            for s in range(max(0, tq - w + 1), tq + 1):
                bg[s, g, tq] += 1.0 / w
                b0[s, g, tq] += 1.0 / min(w, tq + 1)
            bg[tq, g, tq] -= 1.0
            b0[tq, g, tq] -= 1.0
            for s in range(128):
                if s + 128 - 128 >= 0 and (s - 128) > tq - w:
                    bp[s, g, tq] = 1.0 / w
    c["pmod"] = np.stack([(np.arange(128) % 64) + l_ * NPHYS * 64 for l_ in range(DEPTH)], 1).astype(np.float32)
    c["nident"] = (1.0 - np.eye(128)).astype(np.float32)
    r4 = np.zeros((128, 128), np.float32)
    r4[0:4, :] = 1.0
    c["rows4"] = r4
    bs = np.zeros((60, 4, 128), np.float32)
    bself = np.zeros((128, 4, 128), np.float32)
    for g, w in enumerate(ws):
        for s_ in range(4):
            for r_ in range(15):
                if r_ >= 16 - w:
                    bs[s_ * 15 + r_, g, s_] = 1.0 / w
        bself[:, g, :] = np.eye(128) * (1.0 / w - 1.0)
    c["bs_s"] = bs.astype(ml_dtypes.bfloat16)
    c["bself"] = bself.astype(ml_dtypes.bfloat16)
    sbs = np.zeros((128, 256), np.float32)
    sbs[:, 0] = 1e4
    sbs[:, 255] = 1e4
    c["selb_s"] = sbs
    c["bandg"] = bg.astype(ml_dtypes.bfloat16)
    c["band0"] = b0.astype(ml_dtypes.bfloat16)
    c["bandp"] = bp.astype(ml_dtypes.bfloat16)
    return c


CONST_SPECS = [("cos_p", [128, NT, 32], F32), ("sin_p", [128, NT, 32], F32), ("cs_s", [1, 2, 32], F32),
               ("cmpb", [128, NT, 64], F32), ("selb", [128, NT, 32], F32), ("e32", [32, SEQ], BF16),
               ("causb", [128, 128], BF16), ("edgeb", [128, 128], BF16), ("identb", [128, 128], BF16),
               ("identf", [128, 128], F32), ("bdmask", [128, 128], F32), ("uincl", [128, 128], F32),
               ("urest", [128, 128], F32), ("cmask", [128, 2], F32), ("bandg", [128, 4, 128], BF16),
               ("band0", [128, 4, 128], BF16), ("bandp", [128, 4, 128], BF16), ("pmod", [128, 2], F32),
               ("nident", [128, 128], F32), ("rows4", [128, 128], F32), ("bs_s", [60, 4, 128], BF16),
               ("bself", [128, 4, 128], BF16), ("selb_s", [128, 256], F32)]


import os
STAGE = int(os.environ.get("KSTAGE", "99"))
NTILES_DBG = int(os.environ.get("KTILES", "16"))


SKIP_IN = set()
SUB = int(os.environ.get('KSUB', '99'))
PENG = os.environ.get('KPENG', 'dve')


def build_program(do_sample=bool(int(os.environ.get("KSAMPLE", "1")))):
    nc = bass.Bass("TRN2", target_bir_lowering=False)
    with contextlib.ExitStack() as st:
        kb = KB(nc, st)
        di = {}

        def din(name, shape, dt=F32):
            di[name] = kb.dram(name, shape, dt, kind="ExternalInput")
            return di[name]

        def dout(name, shape):
            di[name] = kb.dram(name, shape, F32, kind="ExternalOutput")
            return di[name]

        xp = din("xp", [PB, SEQ, D])
        memp = din("memp", [PB, 256, D])
        xs = din("xs", [SB_, D])
        cnsa = din("cnsa", [DEPTH * NPHYS * 64, 512]) if do_sample else None
        cwin = din("cwin", [DEPTH, SB_, 512, 128])
        shg = din("shg", [DEPTH, SB_, 4, 64, 64])
        spool = din("spool", [DEPTH, SB_, 15, 256])
        cmem = din("cmem", [DEPTH, SB_, 256, 512])
        ptb = din("ptb", [128, SB_, 64], I32)
        norm_g = din("norm_g", [DEPTH, D])
        w_in = din("w_in", [DEPTH, D, NW])
        w_out = din("w_out", [DEPTH, D, D])
        nsa_qn = din("nsa_qn", [DEPTH, 64])
        nsa_kn = din("nsa_kn", [DEPTH, 3, 64])
        cmp_pe = din("cmp_pe", [DEPTH, 2, 32, 64])
        cmp_w1 = din("cmp_w1", [DEPTH, 2, 2048, 256])
        cmp_w2 = din("cmp_w2", [DEPTH, 2, 256, 64])
        hg_lb = din("hg_lb", [DEPTH, 256])
        hg_on = din("hg_on", [DEPTH, 256])
        pool_w = din("pool_w", [DEPTH, 4, 64, 64])
        pool_scale = din("pool_scale", [DEPTH, 256])
        mem_norm = din("mem_norm", [DEPTH, D])
        w_mem_kv = din("w_mem_kv", [DEPTH, D, 512])
        mem_qn = din("mem_qn", [DEPTH, 64])
        mem_kn = din("mem_kn", [DEPTH, 64])
        cd = {}
        for name, shape, dt in CONST_SPECS:
            cd[name] = din("c_" + name, shape, dt)

        y_p = dout("y_p", [PB, SEQ, D])
        y_s = dout("y_s", [SB_, D])
        o_nsa_p = dout("o_nsa_p", [DEPTH, PB, SEQ, 256])
        o_win_p = dout("o_win_p", [DEPTH, PB, 512, 128])
        o_hg_p = dout("o_hg_p", [DEPTH, PB, 4, 64, 64])
        o_pool_p = dout("o_pool_p", [DEPTH, PB, 15, 256])
        o_mem_p = dout("o_mem_p", [DEPTH, PB, 256, 512])
        o_nsa_s = dout("o_nsa_s", [DEPTH, SB_, 256])
        o_win_s = dout("o_win_s", [DEPTH, SB_, 512, 128])
        o_hg_s = dout("o_hg_s", [DEPTH, SB_, 4, 64, 64])
        o_pool_s = dout("o_pool_s", [DEPTH, SB_, 15, 256])
        DBG = False
        xmid = kb.dram("xmid", [PB, SEQ, D], F32)
        global SKIP_IN
        SKIP_IN = set() if do_sample else {"cnsa"}

        cs = {}
        NONRES = ("cos_p", "sin_p", "cmpb", "selb", "uincl", "cs_s")
        for name, shape, dt in CONST_SPECS:
            if name in NONRES:
                continue
            cs[name] = kb.sb("k_" + name, shape, dt)
            kb.dma("sp", cs[name], cs[name][:], cd[name], cd[name][:])
        cs["uincl"] = cs["bdmask"]
        rope_t = kb.sb("rope_t", [128, 2, 32], F32)
        cmpb_t = kb.sb("cmpb_t", [128, 64], F32)
        selb_t = kb.sb("selb_t", [128, 32], F32)
        identb, identf = cs["identb"], cs["identf"]

        PF = [kb.ps("pf%d" % i, [128, 512], F32) for i in range(6)]
        PT = [kb.ps("pt%d" % i, [128, 1024], BF16) for i in range(2)]

        W_in = kb.sb("W_in", [128, 8, NW], BF16)
        W_out = kb.sb("W_out", [128, 8, D], BF16)
        W1 = kb.sb("W1", [128, 2, 16, 256], BF16)
        W2 = kb.sb("W2", [128, 2, 2, 64], BF16)
        PW = kb.sb("PW", [64, 4, 64], BF16)
        PE2 = kb.sb("PE2", [128, 2, 16], BF16)
        gcol = kb.sb("gcol", [128, 8], F32)
        mgcol = kb.sb("mgcol", [128, 8], F32)
        gain11 = kb.sb("gain11", [128, 11, 64], F32)
        mkn_b = kb.sb("mkn_b", [128, 64], F32)
        hgon_b = kb.sb("hgon_b", [128, 256], F32)
        psc_b = kb.sb("psc_b", [128, 256], F32)
        lbraw = kb.sb("lbraw", [128, 2, 256], F32)
        lb_b = kb.sb("lb_b", [128, 256], F32)
        oml_b = kb.sb("oml_b", [128, 256], F32)
        biasT = kb.sb("biasT", [128, 2, 2], F32)
        lbt = [kb.sb("lbt%d" % i, [128, 256], F32) for i in range(3)]

        def load_weights(l):
            for (d0, s0, n) in WPERM:
                kb.dma("pool", W_in, W_in[:, :, d0:d0 + n], w_in,
                       w_in[l, :, s0:s0 + n].rearrange("(kc p) n -> p kc n", p=128))
            kb.dma("pool", W_out, W_out[:], w_out, w_out[l].rearrange("(kc p) n -> p kc n", p=128))
            for a in range(2):
                kb.dma("pool", W1, W1[:, a], cmp_w1, cmp_w1[l, a].rearrange("(c p) h -> p c h", p=128))
                kb.dma("pool", W2, W2[:, a], cmp_w2, cmp_w2[l, a].rearrange("(hc p) d -> p hc d", p=128))
                for j in range(2):
                    kb.dma("pool", PE2, PE2[j * 64:(j + 1) * 64, a, :], cmp_pe,
                           cmp_pe[l, a, j::2, :].rearrange("c d -> d c"), allow_slow_non_contiguous=True)
            kb.dma("pool", PW, PW[:], pool_w, pool_w[l].rearrange("g c d -> c g d"))
            kb.dma("sp", gcol, gcol[:], norm_g, norm_g[l].rearrange("(kc p) -> p kc", p=128), allow_slow_non_contiguous=True)
            kb.dma("sp", mgcol, mgcol[:], mem_norm, mem_norm[l].rearrange("(kc p) -> p kc", p=128), allow_slow_non_contiguous=True)
            for h in range(4):
                kb.dma("sp", gain11, gain11[:, h, :], nsa_qn, bass.AP(nsa_qn.h, l * 64, [[0, 128], [1, 64]]))
                kb.dma("sp", gain11, gain11[:, 7 + h, :], mem_qn, bass.AP(mem_qn.h, l * 64, [[0, 128], [1, 64]]))
            kb.dma("sp", gain11, gain11[:, 4:7, :], nsa_kn, bass.AP(nsa_kn.h, l * 192, [[0, 128], [64, 3], [1, 64]]))
            kb.dma("sp", mkn_b, mkn_b[:], mem_kn, bass.AP(mem_kn.h, l * 64, [[0, 128], [1, 64]]))
            kb.dma("sp", hgon_b, hgon_b[:], hg_on, bass.AP(hg_on.h, l * 256, [[0, 128], [1, 256]]))
            kb.dma("sp", psc_b, psc_b[:], pool_scale, bass.AP(pool_scale.h, l * 256, [[0, 128], [1, 256]]))
            kb.dma("sp", lbraw, lbraw[:], hg_lb, bass.AP(hg_lb.h, 0, [[0, 128], [256, 2], [1, 256]]))
            if l == 0:
                kb.op("dve", lambda e: e.memset(lb_b[:], 0.0), writes=[lb_b])
                kb.op("dve", lambda e: e.memset(oml_b[:], 1.0), writes=[oml_b])
            else:
                kb.op("dve", lambda e: e.tensor_tensor(lbt[0][:], lbraw[:, 1, :], lbraw[:, 0, :], ALU.subtract),
                      reads=[lbraw], writes=[lbt[0]])
                kb.op("act", lambda e: e.activation(lb_b[:], lbt[0][:], AF.Sigmoid), reads=[lbt[0]], writes=[lb_b])
                kb.op("dve", lambda e: e.tensor_scalar(oml_b[:], lb_b[:], -1.0, 1.0, ALU.mult, ALU.add),
                      reads=[lb_b], writes=[oml_b])
            for a in range(2):
                for hc in range(2):
                    for c in range(16):
                        kb.mm(lambda e, a=a, hc=hc, c=c: e.matmul(PF[0][:, a * 2 + hc:a * 2 + hc + 1],
                                                                  W1[:, a, c, hc * 128:(hc + 1) * 128],
                                                                  PE2[:, a, c:c + 1], start=(c == 0), stop=(c == 15)),
                              reads=[W1, PE2], writes=[PF[0]], last=(c == 15))
            kb.op("act", lambda e: e.copy(biasT[:].rearrange("p a h -> p (a h)"), PF[0][:, 0:4]), reads=[PF[0]], writes=[biasT])

        x_t = kb.sb("x_t", [128, D], F32)
        xn = kb.sb("xn", [128, D], BF16)
        hT = kb.sb("hT", [128, 8, 128], BF16)
        proj = kb.sb("proj", [128, NW], F32)
        st1 = [kb.sb("st1_%d" % i, [128, 1], F32) for i in range(4)]
        s11 = [kb.sb("s11_%d" % i, [128, 11], F32) for i in range(4)]
        qkg = kb.sb("qkg", [128, 11, 64], F32)
        qkr = kb.sb("qkr", [128, 7, 64], F32)
        rows = kb.sb("rows", [128, 4, 64], F32)
        wrows = kb.sb("wrows", [128, 2, 64], F32)
        qk_bf = kb.sb("qk_bf", [128, 11, 64], BF16)
        qT_all = kb.sb("qT_all", [64, 4, 128], BF16)
        qmT = kb.sb("qmT", [64, 4, 128], BF16)
        KTW = kb.sb("KTW", [128, 2 * SEQ], BF16)
        V1 = kb.sb("V1", [128, 2, NT, 65], BF16)
        CC = kb.sb("CC", [64, 2, 64], BF16)
        stg = kb.sb("stg", [128, 2, 2, 64], BF16)
        kT2 = kb.sb("kT2", [128, 2, 64], BF16)
        hidT = kb.sb("hidT", [128, 2, 2, 4], BF16)
        pcb = kb.sb("pcb", [128, 4, 64], BF16)
        s4 = [kb.sb("s4_%d" % i, [128, 4], F32) for i in range(8)]
        pTc = kb.sb("pTc", [64, 4, 128], BF16)
        vcc = kb.sb("vcc", [64, 64], BF16)
        imp = kb.sb("imp", [128, 32], F32)
        score = kb.sb("score", [128, 32], F32)
        rank = kb.sb("rank", [128, 32], F32)
        negsel = kb.sb("negsel", [128, 64], BF16)
        kb.op("pool", lambda e: e.memset(negsel[:], 0.0), writes=[negsel])
        negselT = kb.sb("negselT", [32, 128], BF16)
        expT = [kb.sb("expT%d" % i, [128, 512], BF16) for i in range(2)]
        oT = [kb.sb("oT0", [65, 512], F32)] * 2
        gts = kb.sb("gts", [128, 12], F32)
        onsa = [kb.sb("onsa%d" % i, [128, 4, 64], F32) for i in range(4)]
        ycat = kb.sb("ycat", [128, D], F32)
        hg = [kb.sb("hg%d" % i, [128, 256], F32) for i in range(6)]
        hgb = [kb.sb("hgb%d" % i, [128, 256], BF16) for i in range(3)]
        khz = kb.sb("khz", [128, 2, 256], BF16)
        qtT = kb.sb("qtT", [64, 4, 128], BF16)
        qtTz = kb.sb("qtTz", [64, 4, 2, 128], BF16)
        ktT = kb.sb("ktT", [64, 4, 128], BF16)
        ATs = kb.sb("ATs", [128, 4, 128], BF16)
        Sst = [kb.sb("Sst%d" % i, [64, 4, 64], F32) for i in range(3)]
        Sbf = [kb.sb("Sbf%d" % i, [64, 4, 64], BF16) for i in range(2)]
        eGl = kb.sb("eGl", [64, 4, 2], F32)
        ubf = [kb.sb("ubf%d" % i, [128, 256], BF16) for i in range(2)]
        dltT = kb.sb("dltT", [64, 4, 128], BF16)
        memKT = kb.sb("memKT", [64, 4, 256], BF16)
        memV1 = kb.sb("memV1", [128, 2, 4, 65], BF16)
        sz = kb.sb("sz", [128, D], F32)
        xo = kb.sb("xo", [128, D], F32)

        class AV:
            def __init__(self, t, ap):
                self.t = t
                self.ap = ap

            def __getitem__(self, k):
                return self.ap[k]

        def alias(t, ap):
            v = AV(t, ap)
            return v
        junk, junk_t = alias(sz, sz[:]), sz
        sq11, sq11_t = alias(sz, sz[:, 0:704]), sz
        cmpt, cmpt_t = alias(sz, sz[:].rearrange("p (a b) -> p a b", a=32)), sz
        qkn, qkn_t = alias(ycat, ycat[:, 0:704].rearrange("p (h d) -> p h d", h=11)), ycat
        mrow, mrow_t = alias(xo, xo[:, 0:512].rearrange("p (a h d) -> p a h d", a=2, h=4)), xo
        mkv, mkv_t = alias(ycat, ycat[:, 0:512]), ycat
        rt = [alias(hg[i], hg[i][:, 0:224].rearrange("p (h d) -> p h d", h=7)) for i in range(4)]
        sm = [alias(hg[i], hg[i][:].rearrange("p (h d) -> p h d", h=4)) for i in range(3)]
        pcf = alias(hg[3], hg[3][:].rearrange("p (h d) -> p h d", h=4))
        eM = expT
        yg = xn
        yT = hT

        def rms_rstd(src_ssq, n, eps_t):
            a, b, c_ = eps_t
            kb.op("dve", lambda e: e.tensor_scalar(a[:], src_ssq[:], 1.0 / n, EPS, ALU.mult, ALU.add), reads=[src_ssq], writes=[a])
            kb.op("act", lambda e: e.activation(b[:], a[:], AF.Sqrt), reads=[a], writes=[b])
            kb.op("dve", lambda e: e.reciprocal(c_[:], b[:]), reads=[b], writes=[c_])
            return c_

        def norm_transpose(src_t, gc):
            kb.op("act", lambda e: e.activation(junk[:], src_t[:], AF.Square, accum_out=st1[0][:]), reads=[src_t], writes=[junk, st1[0]])
            r = rms_rstd(st1[0], D, st1[1:4])
            kb.op("dve", lambda e: e.tensor_scalar(xn[:], src_t[:], r[:], None, ALU.mult), reads=[src_t, r], writes=[xn])
            for kc in range(8):
                kb.mm(lambda e, kc=kc: e.transpose(PT[0][:, kc * 128:(kc + 1) * 128], xn[:, kc * 128:(kc + 1) * 128], identb[:]),
                      reads=[xn, identb], writes=[PT[0]], last=(kc == 7))
            kb.op("dve", lambda e: e.tensor_tensor(hT[:], PT[0][:].rearrange("p (k t) -> p k t", k=8),
                                                   V(gc, 0, [[1, 8], [0, 128]]), ALU.mult), reads=[PT[0], gc], writes=[hT])

        evac_flip = [0]

        def evac(dst_t, dst_ap, src_t, src_ap):
            evac_flip[0] ^= 1
            if evac_flip[0] or os.environ.get("KEVAC", "act") == "act":
                kb.op("act", lambda e: e.copy(dst_ap, src_ap), reads=[src_t], writes=[dst_t])
            else:
                kb.op("dve", lambda e: e.tensor_copy(dst_ap, src_ap), reads=[src_t], writes=[dst_t])

        kT3 = kb.sb("kT3", [64, 3, 128], BF16)
        qT_pad = kb.sb("qT_pad", [64, 128], BF16)
        G2 = kb.sb("G2", [128, 512], BF16)
        G3 = kb.sb("G3", [128, 512], BF16)
        Gs = [khz, ATs, G2, G3]
        Gv = [khz[:].rearrange("p a b -> p (a b)"), ATs[:].rearrange("p a b -> p (a b)"), G2[:], G3[:]]
        stg1 = stg
        hidc = kb.sb("hidc", [128, 2, 2, 128], BF16)
        kccT_1 = alias(ktT, ktT[:].rearrange("p h t -> p (h t)"))
        vcc_1 = pcb
        idx_l = kb.sb("idx_l", [128, SB_, 64], I32)
        ptb_sb = kb.sb("ptb_sb", [128, SB_, 64], I32)
        Ef = kb.sb("Ef", [128, 128], F32)
        ETs = [kb.sb("ETs%d" % i_, [128, 128], BF16) for i_ in range(2)]
        Wt = alias(hgb[1], hgb[1][:].rearrange("p (k d) -> p k d", k=4))
        KwT = pTc
        V1w = kb.sb("V1w", [128, 4, 65], BF16)
        V1n = kb.sb("V1n", [128, 2, 65], BF16)
        Mt = kb.sb("Mt", [128, 2, 256], BF16)
        spb = kb.sb("spb", [60, 256], BF16)
        pT_s = alias(expT[0], expT[0][:].rearrange("p (c t) -> p c t", c=4))
        m8a = kb.sb("m8a", [128, 16], F32)
        m8b = kb.sb("m8b", [128, 16], F32)
        selk = kb.sb("selk", [128, 64], F32)
        sh_f = alias(onsa[2], onsa[2][:].rearrange("p h d -> p (h d)"))
        sh_k = alias(onsa[3], onsa[3][:].rearrange("p h d -> p (h d)"))
        fkT = kb.sb("fkT", [64, 2, 4, 4], F32)
        ETn = kb.sb("ETn", [128, 32], BF16)
        den4 = [kb.sb("den4_%d" % i_, [128, 4], F32) for i_ in range(3)]
        xsmid = kb.dram("xsmid", [SB_, D], F32)

        _padtest = int(os.environ.get("KPADTEST", "0"))
        if _padtest:
            kb.sb("padtest", [128, _padtest // 2], BF16)

        def acc(k, par):
            if k < 4:
                t = x_t if par == 0 else xo
                return t, t[:, k * 256:(k + 1) * 256]
            return sz, sz[:, par * 256:(par + 1) * 256]

        def sample_init():
            kb.op("pool", lambda e: e.memset(x_t[:], 0.0), writes=[x_t])
            kb.dma("sp", ptb_sb, ptb_sb[:], ptb, ptb[:])
            for l_ in range(DEPTH):
                pass

        def sample_idx(l):
            kb.op("dve", lambda e: e.tensor_scalar(idx_l[:], ptb_sb[:], 64.0, cs["pmod"][:, l:l + 1], ALU.mult, ALU.add),
                  reads=[ptb_sb, cs["pmod"]], writes=[idx_l])

        def sample_pass1(l, s):
            KC2 = KTW[:].rearrange("p (a m) -> p a m", a=2)
            if True:
                for ch in range(4):
                    for ppl in range(16):
                        pp = ch * 16 + ppl
                        g = Gs[pp % 4]
                        gq = Gv[pp % 4]
                        kb.gather(g, gq, cnsa, cnsa[:, :], idx_l, idx_l[:, s, pp:pp + 1])
                        gv = gq.rearrange("p (j r d) -> p r j d", j=2, r=4)
                        kb.op("dve", lambda e, gv=gv: e.tensor_copy(stg1[:], gv[:, 0:2, :, :]), reads=[g], writes=[stg1])
                        for a in range(2):
                            kb.mm(lambda e, a=a: e.transpose(PT[0][:, a * 128:(a + 1) * 128], stg1[:, a].rearrange("p j d -> p (j d)"), identb[:]),
                                  reads=[stg1, identb], writes=[PT[0]], last=(a == 1))
                        kb.op("dve", lambda e, ppl=ppl: e.tensor_copy(KC2[:, :, ppl * 128:(ppl + 1) * 128],
                                                                      PT[0][:, 0:256].rearrange("p (a m) -> p a m", a=2)),
                              reads=[PT[0]], writes=[KTW])
                    for a in range(2):
                        for hc in range(2):
                            pb_ = PF[(a * 2 + hc) % 2]
                            for c in range(16):
                                kb.mm(lambda e, a=a, hc=hc, c=c, pb_=pb_: e.matmul(pb_[:, 0:128], W1[:, a, c, hc * 128:(hc + 1) * 128],
                                                                                   V(KTW, a * 2048 + c, [[16, 128]]), start=(c == 0), stop=(c == 15)),
                                      reads=[W1, KTW], writes=[pb_], last=(c == 15))
                            kb.op("act", lambda e, a=a, hc=hc, pb_=pb_: e.activation(hidc[:, a, hc, :], pb_[:, 0:128], AF.Silu, bias=biasT[:, a, hc:hc + 1]),
                                  reads=[pb_, biasT], writes=[hidc])
                    for hc in range(2):
                        kb.mm(lambda e, hc=hc: e.matmul(PF[4][0:64, 0:128], W2[:, 0, hc, :], hidc[:, 0, hc, :], start=(hc == 0), stop=(hc == 1)),
                              reads=[W2, hidc], writes=[PF[4]], last=(hc == 1))
                    kb.op("dve", lambda e, s=s, ch=ch: e.tensor_copy(kccT_1[:, ch * 128:(ch + 1) * 128], PF[4][0:64, 0:128]), reads=[PF[4]], writes=[kccT_1])
                    for hc in range(2):
                        kb.mm(lambda e, hc=hc: e.matmul(PF[2][:, 0:64], hidc[:, 1, hc, :], W2[:, 1, hc, :], start=(hc == 0), stop=(hc == 1)),
                              reads=[W2, hidc], writes=[PF[2]], last=(hc == 1))
                    kb.op("dve", lambda e, s=s, ch=ch: e.tensor_copy(vcc_1[:, ch, :], PF[2][:, 0:64]), reads=[PF[2]], writes=[vcc_1])

        def nsa_branch_finish(s, k):
            kb.op("pool", lambda e: e.memset(oT[0][:], 0.0), writes=[oT[0]])
            kb.op("dve", lambda e: e.tensor_copy(V(oT[0], s, [[128, 4]], npart=65), PF[5][0:65, 0:4]), reads=[PF[5]], writes=[oT[0]])
            for h in range(4):
                kb.mm(lambda e, h=h: e.transpose(PF[2][:, h * 65:(h + 1) * 65], oT[0][:, h * 128:(h + 1) * 128], identf[0:65, 0:65]),
                      reads=[oT[0], identf], writes=[PF[2]], last=(h == 3))
            pv = PF[2][:, 0:260].rearrange("p (h d) -> p h d", h=4)
            kb.op("dve", lambda e: e.tensor_scalar(den4[0][:], pv[:, :, 64], cs["nident"][:, s:s + 1], None, ALU.add),
                  reads=[PF[2], cs["nident"]], writes=[den4[0]])
            kb.op("dve", lambda e: e.reciprocal(den4[1][:], den4[0][:]), reads=[den4[0]], writes=[den4[1]])
            kb.op("dve", lambda e: e.tensor_tensor(onsa[0][:], pv[:, :, 0:64], V(den4[1], 0, [[1, 4], [0, 64]]), ALU.mult),
                  reads=[PF[2], den4[1]], writes=[onsa[0]])
            to, ao = acc(k, s % 2)
            tn, an = acc(k, (s + 1) % 2)
            kb.op("dve", lambda e: e.tensor_tensor(an, ao, onsa[0][:].rearrange("p h d -> p (h d)"), ALU.add), reads=[to, onsa[0]], writes=[tn])

        def sample_mixers(l):
            for k in range(5):
                t0_, a0_ = acc(k, 0)
                kb.op("pool", lambda e, a0_=a0_: e.memset(a0_, 0.0), writes=[t0_])
            kb.op("pool", lambda e: e.memset(qT_pad[:], 0.0), writes=[qT_pad])
            kb.op("pool", lambda e: e.memset(V1n[:], 1.0), writes=[V1n])
            kb.op("dve", lambda e: e.tensor_copy(V1n[:, :, 0:64], proj[:, C_VS:C_VS + 128].rearrange("p (a d) -> p a d", a=2)), reads=[proj], writes=[V1n])
            kb.op("act", lambda e: e.activation(hg[0][:], proj[:, C_HF:C_HF + 256], AF.Sigmoid), reads=[proj], writes=[hg[0]])
            kb.op("dve", lambda e: e.tensor_tensor(hg[5][:], hg[0][:], oml_b[:], ALU.mult), reads=[hg[0], oml_b], writes=[hg[5]])
            kb.op("dve", lambda e: e.tensor_tensor(sh_f[:], hg[5][:], lb_b[:], ALU.add), reads=[hg[5], lb_b], writes=[sh_f])
            kb.op("dve", lambda e: e.tensor_scalar(sh_k[:], sh_f[:], -1.0, 1.0, ALU.mult, ALU.add), reads=[sh_f], writes=[sh_k])
            for ti, tsrc in enumerate([sh_f, sh_k]):
                for h in range(4):
                    kb.mm(lambda e, h=h, tsrc=tsrc: e.transpose(PF[4][0:64, h * 128:(h + 1) * 128], tsrc[:, h * 64:(h + 1) * 64], identf[:]),
                          reads=[tsrc, identf], writes=[PF[4]], last=(h == 3))
                kb.op("dve", lambda e, ti=ti: e.tensor_copy(fkT[:, ti], PF[4][0:64, :].rearrange("p (h t) -> p h t", h=4)[:, :, 0:4]), reads=[PF[4]], writes=[fkT])
            kb.op("dve", lambda e: e.tensor_copy(hgb[0][:], proj[:, C_HQ:C_HQ + 256]), reads=[proj], writes=[hgb[0]])
            for h in range(4):
                kb.mm(lambda e, h=h: e.transpose(PT[1][0:64, h * 128:(h + 1) * 128], hgb[0][:, h * 64:(h + 1) * 64], identb[:]),
                      reads=[hgb[0], identb], writes=[PT[1]], last=(h == 3))
            kb.op("dve", lambda e: e.tensor_copy(qtT[:], PT[1][0:64, 0:512].rearrange("p (h t) -> p h t", h=4)), reads=[PT[1]], writes=[qtT])

            sm_s = ycat[:, 0:512]
            p_s = ycat[:, 512:1024]
            pb_s = xn[:, 0:512]
            for s in range(SB_):
                oh = identf[:, s:s + 1]
                sample_pass1(l, s)
                kb.op("dve", lambda e, s=s: e.tensor_copy(qT_pad[:, 0:4], V(qT_all, s, [[128, 4]])), reads=[qT_all], writes=[qT_pad])
                kb.mm(lambda e, s=s: e.matmul(PF[3][:], qT_pad[:], kccT_1[:], start=True, stop=True), reads=[qT_pad, kccT_1], writes=[PF[3]])
                kb.op("dve", lambda e: e.tensor_reduce(st1[0][:], PF[3][:], AX.X, ALU.max), reads=[PF[3]], writes=[st1[0]])
                kb.op("dve", lambda e: e.tensor_scalar(sm_s, PF[3][:], st1[0][:], None, ALU.subtract), reads=[PF[3], st1[0]], writes=[ycat])
                kb.op("act", lambda e: e.activation(p_s, sm_s, AF.Exp, accum_out=st1[1][:]), reads=[ycat], writes=[ycat, st1[1]])
                kb.op("dve", lambda e: e.reciprocal(st1[2][:], st1[1][:]), reads=[st1[1]], writes=[st1[2]])
                kb.op("dve", lambda e: e.tensor_scalar(sm_s, p_s, st1[2][:], None, ALU.mult), reads=[ycat, st1[2]], writes=[ycat])
                kb.op("dve", lambda e: e.tensor_copy(pb_s, sm_s), reads=[ycat], writes=[xn])
                pv2 = sm_s.rearrange("p (j r) -> p j r", r=2)
                kb.op("dve", lambda e: e.tensor_tensor(hg[4][:], pv2[:, :, 0], pv2[:, :, 1], ALU.add), reads=[ycat], writes=[hg[4]])
                kb.mm(lambda e: e.matmul(PF[0][:, 0:256], cs["rows4"][:], hg[4][:], start=True, stop=True), reads=[cs["rows4"], hg[4]], writes=[PF[0]])
                kb.op("dve", lambda e: e.tensor_tensor(hg[1][:], PF[0][:, 0:256], cs["selb_s"][:], ALU.add), reads=[PF[0], cs["selb_s"]], writes=[hg[1]])
                kb.op("dve", lambda e: e.max(m8a[:, 0:8], hg[1][:]), reads=[hg[1]], writes=[m8a])
                kb.op("dve", lambda e: e.match_replace(hg[2][:], m8a[:, 0:8], hg[1][:], -1e9), reads=[hg[1], m8a], writes=[hg[2]])
                kb.op("dve", lambda e: e.max(m8b[:, 0:8], hg[2][:]), reads=[hg[2]], writes=[m8b])
                kb.op("dve", lambda e: e.tensor_scalar(hg[3][:], hg[1][:], m8b[:, 6:7], None, ALU.is_ge), reads=[hg[1], m8b], writes=[hg[3]])
                for g4 in range(4):
                    kb.op("dve", lambda e, g4=g4: e.tensor_copy(selk[32 * g4:32 * (g4 + 1), :], V(hg[3], g4, [[4, 64]], p0=32 * g4, npart=32)),
                          reads=[hg[3]], writes=[selk])
                for c in range(4):
                    kb.mm(lambda e, c=c: e.transpose(PT[0][:, c * 128:(c + 1) * 128], pb_s[:, c * 128:(c + 1) * 128], identb[:]),
                          reads=[xn, identb], writes=[PT[0]], last=(c == 3))
                kb.op("dve", lambda e: e.tensor_copy(pT_s[:], PT[0][:, 0:512].rearrange("p (c t) -> p c t", c=4)), reads=[PT[0]], writes=[pT_s])
                for c in range(4):
                    kb.mm(lambda e, c=c, s=s: e.matmul(PF[4][0:64, 0:4], vcc_1[:, c, :], pT_s[:, c, 0:4], start=(c == 0), stop=(c == 3)),
                          reads=[vcc_1, pT_s], writes=[PF[4]], last=(c == 3))
                kb.op("pool", lambda e: e.memset(oT[0][:], 0.0), writes=[oT[0]])
                kb.op("dve", lambda e, s=s: e.tensor_copy(V(oT[0], s, [[128, 4]], npart=64), PF[4][0:64, 0:4]), reads=[PF[4]], writes=[oT[0]])
                for h in range(4):
                    kb.mm(lambda e, h=h: e.transpose(PF[3][:, h * 64:(h + 1) * 64], oT[0][0:64, h * 128:(h + 1) * 128], identf[0:64, 0:64]),
                          reads=[oT[0], identf], writes=[PF[3]], last=(h == 3))
                to, ao = acc(0, s % 2)
                tn, an = acc(0, (s + 1) % 2)
                kb.op("dve", lambda e, ao=ao, an=an: e.tensor_tensor(an, ao, PF[3][:, 0:256], ALU.add), reads=[to, PF[3]], writes=[tn])
                kb.op("pool", lambda e: e.memset(V1[:], 1.0), writes=[V1])
                V1c = V1[:].rearrange("p a t d -> p (a t) d")
                KsT = KTW[0:64, :].rearrange("p (j m) -> p j m", j=2)
                first = True
                for ch in range(4):
                    for ppl in range(16):
                        pp = ch * 16 + ppl
                        g = Gs[pp % 4]
                        gq = Gv[pp % 4]
                        kb.gather(g, gq, cnsa, cnsa[:, :], idx_l, idx_l[:, s, pp:pp + 1])
                        for j in range(2):
                            kb.mm(lambda e, j=j, gq=gq: e.transpose(PT[1][0:64, j * 128:(j + 1) * 128], gq[:, j * 256 + 128:j * 256 + 192], identb[:]),
                                  reads=[g, identb], writes=[PT[1]], last=(j == 1))
                        kb.op("dve", lambda e, ppl=ppl: e.tensor_copy(KsT[:, :, ppl * 128:(ppl + 1) * 128],
                                                                      PT[1][0:64, 0:256].rearrange("p (j t) -> p j t", j=2)), reads=[PT[1]], writes=[KTW])
                        g4v = gq.rearrange("p (j r d) -> p j r d", j=2, r=4)
                        kb.op("dve", lambda e, ppl=ppl, g4v=g4v: e.tensor_copy(V1c[:, ppl * 2:ppl * 2 + 2, 0:64], g4v[:, :, 3, :]), reads=[g], writes=[V1])
                    psc = PF[ch % 2]
                    for t_ in range(32):
                        ppl, j = divmod(t_, 2)
                        kb.mm(lambda e, t_=t_, ppl=ppl, j=j, psc=psc: e.matmul(psc[:, t_ * 4:(t_ + 1) * 4], KsT[:, j, ppl * 128:(ppl + 1) * 128], qT_pad[:, 0:4],
                                                                            start=True, stop=True), reads=[KTW, qT_pad], writes=[psc], last=(t_ == 31))
                    kb.op("act", lambda e, psc=psc: e.activation(Ef[:], psc[:, 0:128], AF.Exp), reads=[psc], writes=[Ef])
                    et = ETs[ch % 2]
                    kb.op("dve", lambda e, et=et, ch=ch: e.tensor_tensor(et[:].rearrange("p (a b) -> p a b", b=8), Ef[:].rearrange("p (a b) -> p a b", b=8),
                                                                       V(selk, ch * 16, [[1, 16], [0, 8]]), ALU.mult), reads=[Ef, selk], writes=[et])
                    for t_ in range(32):
                        kb.mm(lambda e, t_=t_, et=et, first=first: e.matmul(PF[5][0:65, 0:4], V1c[:, t_, :], et[:, t_ * 4:(t_ + 1) * 4],
                                                                          start=(first and t_ == 0), stop=False), reads=[V1, et], writes=[PF[5]], last=False)
                    first = False
                kb.mm(lambda e: e.matmul(PF[3][:, 0:4], kT3[:, 1, :], qT_pad[:, 0:4], start=True, stop=True), reads=[kT3, qT_pad], writes=[PF[3]])
                kb.op("act", lambda e: e.activation(Ef[:, 0:4], PF[3][:, 0:4], AF.Exp), reads=[PF[3]], writes=[Ef])
                kb.op("dve", lambda e, oh=oh: e.tensor_scalar(ETn[:, 0:4], Ef[:, 0:4], oh, None, ALU.mult), reads=[Ef, identf], writes=[ETn])
                kb.mm(lambda e: e.matmul(PF[5][0:65, 0:4], V1n[:, 0, :], ETn[:, 0:4], start=False, stop=True), reads=[V1n, ETn], writes=[PF[5]])
                nsa_branch_finish(s, 1)
                kb.dma("pool", Wt, Wt[:], cwin, cwin[l, s].rearrange("(kt p) c -> p kt c", p=128)[:, :, 0:64])
                for kt in range(4):
                    kb.mm(lambda e, kt=kt: e.transpose(PT[1][0:64, kt * 128:(kt + 1) * 128], Wt[:, kt, :], identb[:]),
                          reads=[Wt, identb], writes=[PT[1]], last=(kt == 3))
                kb.op("dve", lambda e: e.tensor_copy(KwT[:], PT[1][0:64, 0:512].rearrange("p (k t) -> p k t", k=4)), reads=[PT[1]], writes=[KwT])
                kb.op("pool", lambda e: e.memset(V1w[:], 1.0), writes=[V1w])
                kb.dma("pool", V1w, V1w[:, :, 0:64], cwin, cwin[l, s].rearrange("(kt p) c -> p kt c", p=128)[:, :, 64:128])
                kb.op("pool", lambda e: e.memset(V1w[0:1, 0, :], 0.0), writes=[V1w])
                for kt in range(4):
                    kb.mm(lambda e, kt=kt: e.matmul(PF[3][:, kt * 4:(kt + 1) * 4], KwT[:, kt, :], qT_pad[:, 0:4], start=True, stop=True),
                          reads=[KwT, qT_pad], writes=[PF[3]], last=False)
                kb.mm(lambda e: e.matmul(PF[3][:, 16:20], kT3[:, 2, :], qT_pad[:, 0:4], start=True, stop=True), reads=[kT3, qT_pad], writes=[PF[3]])
                kb.op("act", lambda e: e.activation(Ef[:, 0:20], PF[3][:, 0:20], AF.Exp), reads=[PF[3]], writes=[Ef])
                kb.op("dve", lambda e: e.tensor_copy(ETn[:, 0:16], Ef[:, 0:16]), reads=[Ef], writes=[ETn])
                kb.op("dve", lambda e, oh=oh: e.tensor_scalar(ETn[:, 16:20], Ef[:, 16:20], oh, None, ALU.mult), reads=[Ef, identf], writes=[ETn])
                for kt in range(4):
                    kb.mm(lambda e, kt=kt: e.matmul(PF[5][0:65, 0:4], V1w[:, kt, :], ETn[:, kt * 4:(kt + 1) * 4], start=(kt == 0), stop=False),
                          reads=[V1w, ETn], writes=[PF[5]], last=False)
                kb.mm(lambda e: e.matmul(PF[5][0:65, 0:4], V1n[:, 1, :], ETn[:, 16:20], start=False, stop=True), reads=[V1n, ETn], writes=[PF[5]])
                nsa_branch_finish(s, 2)
                kb.dma("sp", Sst[0], Sst[0][:], shg, shg[l, s].rearrange("h k v -> k h v"))
                kb.mm(lambda e, s=s: e.matmul(PF[4][0:64, 0:256], V(identf, s, [[0, 64]]), proj[:, C_HI:C_HI + 256], start=True, stop=True),
                      reads=[identf, proj], writes=[PF[4]])
                kb.op("dve", lambda e, s=s: e.tensor_tensor(Sst[1][:], Sst[0][:], V(fkT, s, [[4, 4], [0, 64]]), ALU.mult), reads=[Sst[0], fkT], writes=[Sst[1]])
                kb.op("dve", lambda e, s=s: e.tensor_tensor(Sst[2][:], PF[4][0:64, 0:256].rearrange("p (h d) -> p h d", h=4),
                                                           V(fkT, 16 + s, [[4, 4], [0, 64]]), ALU.mult), reads=[PF[4], fkT], writes=[Sst[2]])
                kb.op("dve", lambda e: e.tensor_tensor(Sst[0][:], Sst[1][:], Sst[2][:], ALU.add), reads=[Sst[1], Sst[2]], writes=[Sst[0]])
                kb.dma("sp", o_hg_s, o_hg_s[l, s].rearrange("h k v -> k h v"), Sst[0], Sst[0][:], home=Sst[0], is_output=True)
                kb.op("act", lambda e: e.copy(Sbf[0][:], Sst[0][:]), reads=[Sst[0]], writes=[Sbf[0]])
                for h in range(4):
                    kb.mm(lambda e, h=h: e.matmul(PF[2][:, h * 64:(h + 1) * 64], qtT[:, h, :], Sbf[0][:, h, :], start=True, stop=True),
                          reads=[qtT, Sbf[0]], writes=[PF[2]], last=(h == 3))
                to, ao = acc(3, s % 2)
                tn, an = acc(3, (s + 1) % 2)
                kb.op("dve", lambda e, ao=ao, an=an, oh=oh: e.scalar_tensor_tensor(an, PF[2][:, 0:256], oh, ao, ALU.mult, ALU.add),
                      reads=[PF[2], identf, to], writes=[tn])
                kb.dma("pool", Mt, Mt[:], cmem, cmem[l, s].rearrange("(mc p) c -> p mc c", p=128)[:, :, 0:256])
                for mc in range(2):
                    for h in range(4):
                        kb.mm(lambda e, mc=mc, h=h: e.transpose(PT[1][0:64, (mc * 4 + h) * 128:(mc * 4 + h + 1) * 128], Mt[:, mc, h * 64:(h + 1) * 64], identb[:]),
                              reads=[Mt, identb], writes=[PT[1]], last=(mc == 1 and h == 3))
                kb.op("dve", lambda e: e.tensor_copy(memKT[:].rearrange("p h (mc t) -> p mc h t", mc=2),
                                                     PT[1][0:64, :].rearrange("p (mc h t) -> p mc h t", mc=2, h=4)), reads=[PT[1]], writes=[memKT])
                kb.op("pool", lambda e: e.memset(memV1[:], 1.0), writes=[memV1])
                for mc_ in range(2):
                    kb.dma("pool", memV1, memV1[:, mc_, :, 0:64], cmem,
                           cmem[l, s, mc_ * 128:(mc_ + 1) * 128, 256:512].rearrange("p (h d) -> p h d", h=4))
                for mc in range(2):
                    for h in range(4):
                        kb.mm(lambda e, mc=mc, h=h: e.matmul(PF[mc][:, h * 128:(h + 1) * 128], memKT[:, h, mc * 128:(mc + 1) * 128], qmT[:, h, :],
                                                             start=True, stop=True), reads=[memKT, qmT], writes=[PF[mc]], last=(h == 3))
                    kb.op("act", lambda e, mc=mc: e.activation(eM[mc][:], PF[mc][:], AF.Exp), reads=[PF[mc]], writes=[eM[mc]])
                for h in range(4):
                    for mc in range(2):
                        kb.mm(lambda e, mc=mc, h=h: e.matmul(PF[3][:, h * 65:(h + 1) * 65], eM[mc][:, h * 128:(h + 1) * 128], memV1[:, mc, h, :],
                                                             start=(mc == 0), stop=(mc == 1)), reads=[eM[mc], memV1], writes=[PF[3]], last=(mc == 1 and h == 3))
                pvm = PF[3][:, 0:260].rearrange("p (h d) -> p h d", h=4)
                kb.op("dve", lambda e: e.reciprocal(den4[2][:], pvm[:, :, 64]), reads=[PF[3]], writes=[den4[2]])
                kb.op("dve", lambda e: e.tensor_tensor(onsa[1][:], pvm[:, :, 0:64], V(den4[2], 0, [[1, 4], [0, 64]]), ALU.mult), reads=[PF[3], den4[2]], writes=[onsa[1]])
                to, ao = acc(4, s % 2)
                tn, an = acc(4, (s + 1) % 2)
                kb.op("dve", lambda e, ao=ao, an=an, oh=oh: e.scalar_tensor_tensor(an, onsa[1][:].rearrange("p h d -> p (h d)"), oh, ao, ALU.mult, ALU.add),
                      reads=[onsa[1], identf, to], writes=[tn])
            t0, aC = acc(0, 0)
            _, aS = acc(1, 0)
            _, aW = acc(2, 0)
            _, aH = acc(3, 0)
            tm, aM = acc(4, 0)
            kb.op("act", lambda e: e.activation(gts[:], proj[:, C_G:C_G + 12], AF.Sigmoid), reads=[proj], writes=[gts])
            for bi, ab in enumerate([aC, aS, aW]):
                kb.op("dve", lambda e, bi=bi, ab=ab: e.tensor_tensor(onsa[bi][:], ab.rearrange("p (h d) -> p h d", h=4), V(gts, bi, [[3, 4], [0, 64]]), ALU.mult),
                      reads=[t0, gts], writes=[onsa[bi]])
            kb.op("dve", lambda e: e.tensor_tensor(onsa[3][:], onsa[0][:], onsa[1][:], ALU.add), reads=[onsa[0], onsa[1]], writes=[onsa[3]])
            kb.op("dve", lambda e: e.tensor_tensor(ycat[:, 0:256].rearrange("p (h d) -> p h d", h=4), onsa[3][:], onsa[2][:], ALU.add),
                  reads=[onsa[3], onsa[2]], writes=[ycat])
            kb.op("dve", lambda e: e.tensor_tensor(hg[1][:], aH, aH, ALU.mult), reads=[t0], writes=[hg[1]])
            kb.op("dve", lambda e: e.tensor_reduce(s4[0][:], hg[1][:].rearrange("p (h d) -> p h d", h=4), AX.X, ALU.add), reads=[hg[1]], writes=[s4[0]])
            r_ = rms_rstd(s4[0], 64, s4[1:4])
            kb.op("dve", lambda e: e.tensor_tensor(hg[5][:].rearrange("p (h d) -> p h d", h=4), aH.rearrange("p (h d) -> p h d", h=4),
                                                   V(r_, 0, [[1, 4], [0, 64]]), ALU.mult), reads=[t0, r_], writes=[hg[5]])
            kb.op("dve", lambda e: e.tensor_tensor(ycat[:, 256:512], hg[5][:], hgon_b[:], ALU.mult), reads=[hg[5], hgon_b], writes=[ycat])
            kb.op("dve", lambda e: e.tensor_copy(ycat[:, 768:1024], aM), reads=[tm], writes=[ycat])
            kb.dma("pool", spb, spb[:], spool, spool[l].rearrange("s r c -> (s r) c"))
            kb.op("act", lambda e: e.copy(ubf[0][:], proj[:, C_U:C_U + 256]), reads=[proj], writes=[ubf[0]])
            for g in range(4):
                kb.mm(lambda e, g=g: e.matmul(PF[4][0:64, g * 128:(g + 1) * 128], spb[:, g * 64:(g + 1) * 64], cs["bs_s"][:, g, :], start=True, stop=False),
                      reads=[spb, cs["bs_s"]], writes=[PF[4]], last=False)
                kb.mm(lambda e, g=g: e.matmul(PF[4][0:64, g * 128:(g + 1) * 128], ubf[0][:, g * 64:(g + 1) * 64], cs["bself"][:, g, :], start=False, stop=True),
                      reads=[ubf[0], cs["bself"]], writes=[PF[4]], last=(g == 3))
            kb.op("act", lambda e: e.copy(dltT[:], PF[4][0:64, :].rearrange("p (g t) -> p g t", g=4)), reads=[PF[4]], writes=[dltT])
            for g in range(4):
                kb.mm(lambda e, g=g: e.matmul(PF[1][:, g * 64:(g + 1) * 64], dltT[:, g, :], PW[:, g, :], start=True, stop=True),
                      reads=[dltT, PW], writes=[PF[1]], last=(g == 3))
            kb.op("dve", lambda e: e.tensor_tensor(ycat[:, 512:768], PF[1][:, 0:256], psc_b[:], ALU.mult), reads=[PF[1], psc_b], writes=[ycat])

        def prompt_mem(l, b):
            for mc in range(2):
                kb.dma("sp", x_t, x_t[:], memp, memp[b, mc * 128:(mc + 1) * 128, :])
                norm_transpose(x_t, mgcol)
                for kc in range(8):
                    kb.mm(lambda e, kc=kc: e.matmul(PF[0][:], hT[:, kc, :], KTW[:, kc * 512:(kc + 1) * 512], start=(kc == 0), stop=(kc == 7)),
                          reads=[hT, KTW], writes=[PF[0]], last=(kc == 7))
                kb.op("act", lambda e: e.copy(mkv[:], PF[0][:]), reads=[PF[0]], writes=[mkv])
                kb.op("dve", lambda e: e.tensor_tensor(hg[0][:], mkv[:, 0:256], mkv[:, 0:256], ALU.mult), reads=[mkv], writes=[hg[0]])
                kb.op("dve", lambda e: e.tensor_reduce(s4[0][:], hg[0][:].rearrange("p (h d) -> p h d", h=4), AX.X, ALU.add),
                      reads=[hg[0]], writes=[s4[0]])
                r = rms_rstd(s4[0], 64, s4[1:4])
                kb.op("dve", lambda e: e.tensor_tensor(hg[1][:].rearrange("p (h d) -> p h d", h=4),
                                                       mkv[:, 0:256].rearrange("p (h d) -> p h d", h=4),
                                                       V(r, 0, [[1, 4], [0, 64]]), ALU.mult), reads=[mkv, r], writes=[hg[1]])
                kb.op("dve", lambda e: e.tensor_tensor(mrow[:, 0], hg[1][:].rearrange("p (h d) -> p h d", h=4),
                                                       V(mkn_b, 0, [[0, 4], [1, 64]]), ALU.mult), reads=[hg[1], mkn_b], writes=[mrow])
                kb.op("pool", lambda e: e.tensor_copy(mrow[:, 1], mkv[:, 256:512].rearrange("p (h d) -> p h d", h=4)),
                      reads=[mkv], writes=[mrow])
                kb.dma("sp", o_mem_p, o_mem_p[l, b, mc * 128:(mc + 1) * 128, :], mrow, mrow[:].rearrange("p a h d -> p (a h d)"),
                       home=mrow, is_output=True)
                kb.op("act", lambda e: e.copy(hgb[0][:].rearrange("p (h d) -> p h d", h=4), mrow[:, 0]), reads=[mrow], writes=[hgb[0]])
                kb.op("dve", lambda e, mc=mc: e.tensor_copy(memV1[:, mc, :, 0:64], mrow[:, 1]), reads=[mrow], writes=[memV1])
                for h in range(4):
                    kb.mm(lambda e, h=h: e.transpose(PT[1][0:64, h * 128:(h + 1) * 128], hgb[0][:, h * 64:(h + 1) * 64], identb[:]),
                          reads=[hgb[0], identb], writes=[PT[1]], last=(h == 3))
                kb.op("act", lambda e, mc=mc: e.copy(memKT[:, :, mc * 128:(mc + 1) * 128],
                                                     PT[1][0:64, 0:512].rearrange("p (h t) -> p h t", h=4)),
                      reads=[PT[1]], writes=[memKT])

        def prompt_tile(l, b, i, smp=False):
            src = xp if l == 0 else xmid
            dst = xmid if l == 0 else y_p
            DBG2 = int(os.environ.get("KDBG2", "99"))
            xin = x_t
            xs_src = xs if l == 0 else xsmid
            if smp:
                kb.dma("sp", x_t, x_t[0:SB_, :], xs_src, xs_src[:, :])
            else:
                kb.dma("sp", x_t, x_t[:], src, src[b, i * 128:(i + 1) * 128, :])
            if i >= 1 and DBG2 < 1:
                return
            norm_transpose(xin, gcol)
            if i >= 1 and DBG2 < 2:
                return
            for n0 in range(0, NW, 512):
                if i >= 1 and DBG2 < 3 + n0 // 512:
                    return
                n1 = min(NW, n0 + 512)
                pb = PF[(n0 // 512) % 2]
                for kc in range(8):
                    kb.mm(lambda e, kc=kc, pb=pb, n0=n0, n1=n1: e.matmul(pb[:, 0:n1 - n0], hT[:, kc, :], W_in[:, kc, n0:n1],
                                                                        start=(kc == 0), stop=(kc == 7)),
                          reads=[hT, W_in], writes=[pb], last=(kc == 7))
                evac(proj, proj[:, n0:n1], pb, pb[:, 0:n1 - n0])
            if STAGE < 3:
                return
            kb.op("dve", lambda e: e.tensor_tensor(sq11[:], proj[:, 0:704], proj[:, 0:704], ALU.mult), reads=[proj], writes=[sq11])
            kb.op("dve", lambda e: e.tensor_reduce(s11[0][:], sq11[:].rearrange("p (h d) -> p h d", h=11), AX.X, ALU.add),
                  reads=[sq11], writes=[s11[0]])
            r = rms_rstd(s11[0], 64, s11[1:4])
            kb.op("dve", lambda e: e.tensor_tensor(qkn[:], proj[:, 0:704].rearrange("p (h d) -> p h d", h=11),
                                                   V(r, 0, [[1, 11], [0, 64]]), ALU.mult), reads=[proj, r], writes=[qkn])
            kb.op(PENG, lambda e: e.tensor_tensor(qkg[:], qkn[:], gain11[:], ALU.mult), reads=[qkn, gain11], writes=[qkg])
            if SUB < 1:
                return
            if smp:
                kb.dma("sp", rope_t, rope_t[:], cd["cs_s"], bass.AP(cd["cs_s"].h, 0, [[0, 128], [32, 2], [1, 32]]))
            else:
                kb.dma("sp", rope_t, rope_t[:, 0, :], cd["cos_p"], cd["cos_p"][:, i, :])
                kb.dma("sp", rope_t, rope_t[:, 1, :], cd["sin_p"], cd["sin_p"][:, i, :])
                kb.dma("sp", cmpb_t, cmpb_t[:], cd["cmpb"], cd["cmpb"][:, i, :])
                kb.dma("sp", selb_t, selb_t[:], cd["selb"], cd["selb"][:, i, :])
            cosb = V(rope_t, 0, [[0, 7], [1, 32]])
            sinb = V(rope_t, 32, [[0, 7], [1, 32]])
            x1 = qkg[:, 0:7, 0:32]
            x2 = qkg[:, 0:7, 32:64]
            kb.op("dve", lambda e: e.tensor_tensor(rt[0][:], x1, cosb, ALU.mult), reads=[qkg, rope_t], writes=[rt[0]])
            kb.op(PENG, lambda e: e.tensor_tensor(rt[1][:], x2, sinb, ALU.mult), reads=[qkg, rope_t], writes=[rt[1]])
            kb.op("dve", lambda e: e.tensor_tensor(qkr[:, :, 0:32], rt[0][:], rt[1][:], ALU.subtract), reads=[rt[0], rt[1]], writes=[qkr])
            kb.op(PENG, lambda e: e.tensor_tensor(rt[2][:], x2, cosb, ALU.mult), reads=[qkg, rope_t], writes=[rt[2]])
            kb.op("dve", lambda e: e.tensor_tensor(rt[3][:], x1, sinb, ALU.mult), reads=[qkg, rope_t], writes=[rt[3]])
            kb.op(PENG, lambda e: e.tensor_tensor(qkr[:, :, 32:64], rt[2][:], rt[3][:], ALU.add), reads=[rt[2], rt[3]], writes=[qkr])
            if SUB < 2:
                return
            kb.op("pool", lambda e: e.tensor_copy(rows[:, 0:4:2, :], qkr[:, 4:6, :]), reads=[qkr], writes=[rows])
            kb.op("pool", lambda e: e.tensor_copy(rows[:, 1:4:2, :], proj[:, C_VC:C_VC + 128].rearrange("p (a d) -> p a d", a=2)),
                  reads=[proj], writes=[rows])
            if smp:
                kb.dma("sp", o_nsa_s, o_nsa_s[l], rows, rows[0:SB_].rearrange("p a d -> p (a d)"), home=rows, is_output=True)
                kb.op("pool", lambda e: e.tensor_copy(wrows[:, 0, :], qkr[:, 6, :]), reads=[qkr], writes=[wrows])
                kb.op("pool", lambda e: e.tensor_copy(wrows[:, 1, :], proj[:, C_VW:C_VW + 64]), reads=[proj], writes=[wrows])
                for s in range(SB_):
                    kb.dma("sp", o_win_s, o_win_s[l, s, 0:511, :], cwin, cwin[l, s, 1:512, :], home=wrows, is_output=True)
                    kb.dma("sp", o_win_s, o_win_s[l, s, 511:512, :], wrows, wrows[s:s + 1].rearrange("p a d -> p (a d)"), home=wrows, is_output=True)
                    kb.dma("sp", o_pool_s, o_pool_s[l, s, 0:14, :], spool, spool[l, s, 1:15, :], home=wrows, is_output=True)
                    kb.dma("sp", o_pool_s, o_pool_s[l, s, 14:15, :], proj, proj[s:s + 1, C_U:C_U + 256], home=proj, is_output=True)
            else:
                kb.dma("sp", o_nsa_p, o_nsa_p[l, b, i * 128:(i + 1) * 128, :], rows, rows[:].rearrange("p a d -> p (a d)"),
                       home=rows, is_output=True)
            if (not smp) and i >= NT - 4:
                kb.op("pool", lambda e: e.tensor_copy(wrows[:, 0, :], qkr[:, 6, :]), reads=[qkr], writes=[wrows])
                kb.op("pool", lambda e: e.tensor_copy(wrows[:, 1, :], proj[:, C_VW:C_VW + 64]), reads=[proj], writes=[wrows])
                kb.dma("sp", o_win_p, o_win_p[l, b, (i - (NT - 4)) * 128:(i - (NT - 4) + 1) * 128, :], wrows,
                       wrows[:].rearrange("p a d -> p (a d)"), home=wrows, is_output=True)
            if (not smp) and i == NT - 1:
                kb.dma("sp", o_pool_p, o_pool_p[l, b], proj, proj[113:128, C_U:C_U + 256], home=proj, is_output=True)
            if SUB < 3:
                return
            kb.op("dve", lambda e: e.tensor_scalar(qk_bf[:, 0:4, :], qkr[:, 0:4, :], 0.125, None, ALU.mult), reads=[qkr], writes=[qk_bf])
            kb.op("dve", lambda e: e.tensor_copy(qk_bf[:, 4:7, :], qkr[:, 4:7, :]), reads=[qkr], writes=[qk_bf])
            kb.op("dve", lambda e: e.tensor_scalar(qk_bf[:, 7:11, :], qkg[:, 7:11, :], 0.125, None, ALU.mult), reads=[qkg], writes=[qk_bf])
            if SUB < 4:
                return
            qk2d = qk_bf[:].rearrange('p h d -> p (h d)')
            for h in range(7):
                kb.mm(lambda e, h=h: e.transpose(PT[1][0:64, h * 128:(h + 1) * 128], qk2d[:, h * 64:(h + 1) * 64], identb[:]),
                      reads=[qk_bf, identb], writes=[PT[1]], last=(h == 6))
            if SUB < 5:
                return
            kb.op("dve", lambda e: e.tensor_copy(qT_all[:], PT[1][0:64, 0:512].rearrange("p (h t) -> p h t", h=4)), reads=[PT[1]], writes=[qT_all])
            if SUB < 6:
                return
            if smp:
                kb.op("dve", lambda e: e.tensor_copy(kT3[:], PT[1][0:64, 512:896].rearrange("p (h t) -> p h t", h=3)), reads=[PT[1]], writes=[kT3])
            else:
                kb.op("dve", lambda e: e.tensor_copy(KTW[0:64, :].rearrange("p (a s) -> p a s", a=2)[:, :, i * 128:(i + 1) * 128],
                                                     PT[1][0:64, 640:896].rearrange("p (h t) -> p h t", h=2)), reads=[PT[1]], writes=[KTW])
            if SUB < 7:
                return
            for h in range(4):
                kb.mm(lambda e, h=h: e.transpose(PT[1][0:64, h * 128:(h + 1) * 128], qk2d[:, (7 + h) * 64:(8 + h) * 64], identb[:]),
                      reads=[qk_bf, identb], writes=[PT[1]], last=(h == 3))
            kb.op("dve", lambda e: e.tensor_copy(qmT[:], PT[1][0:64, 0:512].rearrange("p (h t) -> p h t", h=4)), reads=[PT[1]], writes=[qmT])
            if smp:
                sample_mixers(l)
                kb.dma("sp", x_t, x_t[0:SB_, :], xs_src, xs_src[:, :])
            if not smp:
                kb.op("dve", lambda e: e.tensor_copy(V1[:, :, i, 0:64], proj[:, C_VS:C_VS + 128].rearrange("p (a d) -> p a d", a=2)),
                      reads=[proj], writes=[V1])
                if STAGE < 4:
                    return
                kb.op("dve", lambda e: e.tensor_copy(stg[:, 0], V(qkr, 4 * 64, [[0, 2], [1, 64]])), reads=[qkr], writes=[stg])
                kb.op("pool", lambda e: e.tensor_copy(stg[:, 1], V(proj, C_VC, [[0, 2], [1, 64]])), reads=[proj], writes=[stg])
                for a in range(2):
                    kb.mm(lambda e, a=a: e.transpose(PT[0][:, a * 128:(a + 1) * 128], stg[:, a].rearrange("p j d -> p (j d)"), identb[:]),
                          reads=[stg, identb], writes=[PT[0]], last=(a == 1))
                ptv = PT[0][:, 0:256].rearrange("p (a m j) -> p a m j", a=2, j=2)
                kb.op("dve", lambda e: e.tensor_copy(kT2[0:64], ptv[0:64, :, :, 0]), reads=[PT[0]], writes=[kT2])
                kb.op("dve", lambda e: e.tensor_copy(kT2[64:128], ptv[64:128, :, :, 1]), reads=[PT[0]], writes=[kT2])
                for a in range(2):
                    for hc in range(2):
                        for c in range(16):
                            kb.mm(lambda e, a=a, hc=hc, c=c: e.matmul(PF[2][:, (a * 2 + hc) * 4:(a * 2 + hc) * 4 + 4],
                                                                      W1[:, a, c, hc * 128:(hc + 1) * 128],
                                                                      V(kT2, a * 64 + c, [[16, 4]]), start=(c == 0), stop=(c == 15)),
                                  reads=[W1, kT2], writes=[PF[2]], last=(c == 15))
                for a in range(2):
                    for hc in range(2):
                        kb.op("act", lambda e, a=a, hc=hc: e.activation(hidT[:, a, hc, :], PF[2][:, (a * 2 + hc) * 4:(a * 2 + hc) * 4 + 4],
                                                                        AF.Silu, bias=biasT[:, a, hc:hc + 1]),
                              reads=[PF[2], biasT], writes=[hidT])
                for a in range(2):
                    for hc in range(2):
                        kb.mm(lambda e, a=a, hc=hc: e.matmul(PF[4][0:64, a * 4:a * 4 + 4], W2[:, a, hc, :], hidT[:, a, hc, :],
                                                             start=(hc == 0), stop=(hc == 1)),
                              reads=[W2, hidT], writes=[PF[4]], last=(hc == 1))
                kb.op("dve", lambda e: e.tensor_copy(CC[:, :, 4 * i:4 * i + 4], PF[4][0:64, 0:8].rearrange("p (a n) -> p a n", a=2)),
                      reads=[PF[4]], writes=[CC])
                if STAGE < 5:
                    return
                for h in range(4):
                    kb.mm(lambda e, h=h: e.matmul(PF[3][:, h * 64:(h + 1) * 64], qT_all[:, h, :], CC[:, 0, :], start=True, stop=True),
                          reads=[qT_all, CC], writes=[PF[3]], last=(h == 3))
                kb.op("dve", lambda e: e.tensor_tensor(sm[0][:], PF[3][:, 0:256].rearrange("p (h n) -> p h n", h=4),
                                                       V(cmpb_t, 0, [[0, 4], [1, 64]]), ALU.add), reads=[PF[3], cmpb_t], writes=[sm[0]])
                ucur = ubf[i % 2]
                uprev = ubf[(i + 1) % 2]
                kb.op("act", lambda e: e.copy(ucur[:], proj[:, C_U:C_U + 256]), reads=[proj], writes=[ucur])
                bc_ = cs["band0"] if i == 0 else cs["bandg"]
                for g in range(4):
                    kb.mm(lambda e, g=g: e.matmul(PF[4][0:64, g * 128:(g + 1) * 128], ucur[:, g * 64:(g + 1) * 64], bc_[:, g, :], start=True, stop=(i == 0)),
                          reads=[ucur, bc_], writes=[PF[4]], last=(i == 0 and g == 3))
                    if i > 0:
                        kb.mm(lambda e, g=g: e.matmul(PF[4][0:64, g * 128:(g + 1) * 128], uprev[:, g * 64:(g + 1) * 64], cs["bandp"][:, g, :], start=False, stop=True),
                              reads=[uprev, cs["bandp"]], writes=[PF[4]], last=(g == 3))
                kb.op("act", lambda e: e.copy(dltT[:], PF[4][0:64, :].rearrange("p (g t) -> p g t", g=4)), reads=[PF[4]], writes=[dltT])
                for g in range(4):
                    kb.mm(lambda e, g=g: e.matmul(PF[1][:, g * 64:(g + 1) * 64], dltT[:, g, :], PW[:, g, :], start=True, stop=True),
                          reads=[dltT, PW], writes=[PF[1]], last=(g == 3))

                for mc in range(2):
                    for h in range(4):
                        kb.mm(lambda e, mc=mc, h=h: e.matmul(PF[2 + mc][:, h * 128:(h + 1) * 128], memKT[:, h, mc * 128:(mc + 1) * 128], qmT[:, h, :],
                                                             start=True, stop=True), reads=[memKT, qmT], writes=[PF[2 + mc]], last=(h == 3))
                    kb.op("act", lambda e, mc=mc: e.activation(eM[mc][:], PF[2 + mc][:], AF.Exp), reads=[PF[2 + mc]], writes=[eM[mc]])
                for h in range(4):
                    for mc in range(2):
                        kb.mm(lambda e, mc=mc, h=h: e.matmul(PF[0][:, h * 65:(h + 1) * 65], eM[mc][:, h * 128:(h + 1) * 128], memV1[:, mc, h, :],
                                                             start=(mc == 0), stop=(mc == 1)), reads=[eM[mc], memV1], writes=[PF[0]],
                              last=(mc == 1 and h == 3))
                kb.op("dve", lambda e: e.tensor_reduce(s4[0][:], sm[0][:], AX.X, ALU.max), reads=[sm[0]], writes=[s4[0]])
                kb.op("dve", lambda e: e.tensor_scalar(s4[1][:], s4[0][:], -1000.0, None, ALU.max), reads=[s4[0]], writes=[s4[1]])
                kb.op("dve", lambda e: e.tensor_tensor(sm[1][:], sm[0][:], V(s4[1], 0, [[1, 4], [0, 64]]), ALU.subtract),
                      reads=[sm[0], s4[1]], writes=[sm[1]])
                kb.op("act", lambda e: e.activation(sm[2][:], sm[1][:], AF.Exp), reads=[sm[1]], writes=[sm[2]])
                kb.op("dve", lambda e: e.tensor_reduce(s4[2][:], sm[2][:], AX.X, ALU.add), reads=[sm[2]], writes=[s4[2]])
                kb.op("dve", lambda e: e.tensor_scalar(s4[3][:], s4[2][:], 1e-30, None, ALU.max), reads=[s4[2]], writes=[s4[3]])
                kb.op("dve", lambda e: e.reciprocal(s4[4][:], s4[3][:]), reads=[s4[3]], writes=[s4[4]])
                kb.op("dve", lambda e: e.tensor_tensor(pcf[:], sm[2][:], V(s4[4], 0, [[1, 4], [0, 64]]), ALU.mult), reads=[sm[2], s4[4]], writes=[pcf])
                kb.op("act", lambda e: e.copy(pcb[:], pcf[:]), reads=[pcf], writes=[pcb])
                for h in range(4):
                    kb.mm(lambda e, h=h: e.transpose(PT[1][0:64, h * 128:(h + 1) * 128], pcb[:, h, :], identb[:]),
                          reads=[pcb, identb], writes=[PT[1]], last=False)
                kb.mm(lambda e: e.transpose(PT[1][0:64, 512:576], CC[:, 1, :], identb[0:64, 0:64]), reads=[CC, identb], writes=[PT[1]])
                kb.op("dve", lambda e: e.tensor_copy(pTc[:], PT[1][0:64, 0:512].rearrange("p (h t) -> p h t", h=4)), reads=[PT[1]], writes=[pTc])
                kb.op("dve", lambda e: e.tensor_copy(vcc[:], PT[1][0:64, 512:576]), reads=[PT[1]], writes=[vcc])
                for h in range(4):
                    kb.mm(lambda e, h=h: e.matmul(PF[3][:, 256 + h * 64:256 + (h + 1) * 64], pTc[:, h, :], vcc[:], start=True, stop=True),
                          reads=[pTc, vcc], writes=[PF[3]], last=(h == 3))
                kb.op("act", lambda e: e.activation(gts[:], proj[:, C_G:C_G + 12], AF.Sigmoid), reads=[proj], writes=[gts])
                kb.op("dve", lambda e: e.tensor_tensor(onsa[0][:], PF[3][:, 256:512].rearrange("p (h d) -> p h d", h=4),
                                                       V(gts, 0, [[3, 4], [0, 64]]), ALU.mult), reads=[PF[3], gts], writes=[onsa[0]])
                kb.op("dve", lambda e: e.tensor_reduce(imp[:], V(hg[3], 0, [[2, 32], [64, 4], [1, 2]]), AX.XY, ALU.add), reads=[pcf], writes=[imp])
                kb.op("dve", lambda e: e.tensor_tensor(score[:], imp[:], selb_t[:], ALU.add), reads=[imp, selb_t], writes=[score])
                kb.op("dve", lambda e: e.tensor_tensor(cmpt[:], V(score, 0, [[0, 32], [1, 32]]), V(score, 0, [[1, 32], [0, 32]]), ALU.is_gt),
                      reads=[score], writes=[cmpt])
                kb.op("dve", lambda e: e.tensor_reduce(rank[:], cmpt[:], AX.X, ALU.add), reads=[cmpt], writes=[rank])
                kb.op("dve", lambda e: e.tensor_scalar(negsel[:, 0:32], rank[:], 15.5, NEG, ALU.is_ge, ALU.mult), reads=[rank], writes=[negsel])
                kb.mm(lambda e: e.transpose(PT[1][0:64, 0:128], negsel[:], identb[:]), reads=[negsel, identb], writes=[PT[1]])
                kb.op("dve", lambda e: e.tensor_copy(negselT[:], PT[1][0:32, 0:128]), reads=[PT[1]], writes=[negselT])
                kb.op("act", lambda e: e.activation(hg[0][:], proj[:, C_HF:C_HF + 256], AF.Sigmoid), reads=[proj], writes=[hg[0]])
                kb.op("dve", lambda e: e.tensor_tensor(hg[1][:], hg[0][:], oml_b[:], ALU.mult), reads=[hg[0], oml_b], writes=[hg[1]])
                kb.op("dve", lambda e: e.tensor_tensor(hg[2][:], hg[1][:], lb_b[:], ALU.add), reads=[hg[1], lb_b], writes=[hg[2]])
                kb.op("act", lambda e: e.activation(hg[3][:], hg[2][:], AF.Ln), reads=[hg[2]], writes=[hg[3]])
                kb.op("dve", lambda e: e.tensor_scalar(hg[4][:], hg[2][:], -1.0, 1.0, ALU.mult, ALU.add), reads=[hg[2]], writes=[hg[4]])
                if STAGE < 6:
                    return
                qflat = qT_all[:].rearrange("p h t -> p (h t)")

                def attn(branch, kts, cache_idx, v_idx, pacc, ot):
                    nk = len(kts)

                    def score(n_):
                        kt = kts[n_]
                        ps_ = PF[2 + (n_ % 2)]
                        ex = expT[n_ % 2]
                        extra = []
                        if branch == "s":
                            extra.append(("sel", None))
                        if kt == i:
                            extra.append(("mask", cs["causb"]))
                        if branch == "w" and kt == i - 4:
                            extra.append(("mask", cs["edgeb"]))
                        kb.mm(lambda e: e.matmul(ps_[:], KTW[0:64, cache_idx * SEQ + kt * 128:cache_idx * SEQ + (kt + 1) * 128], qflat,
                                                 start=True, stop=(len(extra) == 0)),
                              reads=[KTW, qT_all], writes=[ps_], last=(len(extra) == 0))
                        for xi, (kind, mt) in enumerate(extra):
                            lastx = xi == len(extra) - 1
                            if kind == "sel":
                                kb.mm(lambda e: e.matmul(ps_[:].rearrange("p (h t) -> p h t", h=4), cs["e32"][:, kt * 128:(kt + 1) * 128],
                                                         V(negselT, 0, [[0, 4], [1, 128]]), start=False, stop=lastx),
                                      reads=[cs["e32"], negselT], writes=[ps_], last=lastx)
                            else:
                                kb.mm(lambda e: e.matmul(ps_[:].rearrange("p (h t) -> p h t", h=4), identb[:], V(mt, 0, [[0, 4], [1, 128]]),
                                                         start=False, stop=lastx), reads=[identb, mt], writes=[ps_], last=lastx)
                        kb.op("act", lambda e: e.activation(ex[:], ps_[:], AF.Exp), reads=[ps_], writes=[ex])

                    def pv(n_):
                        kt = kts[n_]
                        ex = expT[n_ % 2]
                        kb.mm(lambda e: e.matmul(pacc[0:65, :], V1[:, v_idx, kt, :], ex[:], start=(n_ == 0), stop=(n_ == nk - 1)),
                              reads=[V1, ex], writes=[pacc], last=(n_ == nk - 1))

                    score(0)
                    for n_ in range(nk):
                        if n_ + 1 < nk:
                            score(n_ + 1)
                        pv(n_)
                    kb.op("act", lambda e: e.copy(ot[:], pacc[0:65, :]), reads=[pacc], writes=[ot])

                for bi, (br, kts, cidx) in enumerate([("s", list(range(0, i + 1)), 0), ("w", list(range(max(0, i - 4), i + 1)), 1)]):
                    attn(br, kts, cidx, bi, PF[5], oT[bi])
                    pback = PF[2 + bi]
                    for h in range(4):
                        kb.mm(lambda e, bi=bi, h=h, pback=pback: e.transpose(pback[:, h * 65:(h + 1) * 65], oT[bi][:, h * 128:(h + 1) * 128], identf[0:65, 0:65]),
                              reads=[oT[bi], identf], writes=[pback], last=(h == 3))
                    pv = pback[:, 0:260].rearrange("p (h d) -> p h d", h=4)
                    kb.op("dve", lambda e, pv=pv, bi=bi: e.reciprocal(s4[5 + bi][:], pv[:, :, 64]), reads=[pback], writes=[s4[5 + bi]])
                    kb.op("dve", lambda e, bi=bi: e.tensor_tensor(s4[bi][:], s4[5 + bi][:], V(gts, 1 + bi, [[3, 4]]), ALU.mult),
                          reads=[s4[5 + bi], gts], writes=[s4[bi]])
                    kb.op("dve", lambda e, pv=pv, bi=bi: e.tensor_tensor(onsa[1 + bi][:], pv[:, :, 0:64], V(s4[bi], 0, [[1, 4], [0, 64]]), ALU.mult),
                          reads=[pback, s4[bi]], writes=[onsa[1 + bi]])
                kb.op(PENG, lambda e: e.tensor_tensor(onsa[3][:], onsa[0][:], onsa[1][:], ALU.add), reads=[onsa[0], onsa[1]], writes=[onsa[3]])
                kb.op(PENG, lambda e: e.tensor_tensor(ycat[:, 0:256].rearrange("p (h d) -> p h d", h=4), onsa[3][:], onsa[2][:], ALU.add),
                      reads=[onsa[3], onsa[2]], writes=[ycat])

                if STAGE < 7:
                    return
                kb.op("dve", lambda e: e.tensor_tensor(ycat[:, 512:768], PF[1][:, 0:256], psc_b[:], ALU.mult), reads=[PF[1], psc_b], writes=[ycat])
                pv = PF[0][:, 0:260].rearrange("p (h d) -> p h d", h=4)
                kb.op("dve", lambda e: e.reciprocal(s4[7][:], pv[:, :, 64]), reads=[PF[0]], writes=[s4[7]])
                kb.op("dve", lambda e: e.tensor_tensor(ycat[:, 768:1024].rearrange("p (h d) -> p h d", h=4), pv[:, :, 0:64],
                                                       V(s4[7], 0, [[1, 4], [0, 64]]), ALU.mult), reads=[PF[0], s4[7]], writes=[ycat])


                kb.mm(lambda e: e.matmul(PF[0][:, 0:256], cs["uincl"][:], hg[3][:], start=True, stop=True), reads=[cs["uincl"], hg[3]], writes=[PF[0]], last=False)
                kb.mm(lambda e: e.matmul(PF[0][:, 256:512], cs["urest"][:], hg[3][:], start=True, stop=True), reads=[cs["urest"], hg[3]], writes=[PF[0]])
                for h in range(4):
                    kb.mm(lambda e, h=h: e.matmul(PF[4][0:64, h * 2:h * 2 + 2], hg[3][:, h * 64:(h + 1) * 64], cs["cmask"][:], start=True, stop=True),
                          reads=[hg[3], cs["cmask"]], writes=[PF[4]], last=(h == 3))
                kb.op("act", lambda e: e.activation(eGl[:].rearrange("p h c -> p (h c)"), PF[4][0:64, 0:8], AF.Exp), reads=[PF[4]], writes=[eGl])
                kb.op("act", lambda e: e.activation(hg[0][:], PF[0][:, 0:256], AF.Exp), reads=[PF[0]], writes=[hg[0]])
                kb.op("act", lambda e: e.activation(hg[1][:], PF[0][:, 0:256], AF.Exp, scale=-1.0), reads=[PF[0]], writes=[hg[1]])
                kb.op("act", lambda e: e.activation(hg[5][:], PF[0][:, 256:512], AF.Exp), reads=[PF[0]], writes=[hg[5]])
                kb.op("dve", lambda e: e.tensor_tensor(hgb[0][:], proj[:, C_HQ:C_HQ + 256], hg[0][:], ALU.mult), reads=[proj, hg[0]], writes=[hgb[0]])
                kb.op(PENG, lambda e: e.tensor_tensor(hgb[1][:], hg[4][:], hg[1][:], ALU.mult), reads=[hg[4], hg[1]], writes=[hgb[1]])
                for c in range(2):
                    kb.op("dve", lambda e, c=c: e.scalar_tensor_tensor(khz[:, c, :], hg[4][:], cs["cmask"][:, c:c + 1], hg[5][:], ALU.mult, ALU.mult),
                          reads=[hg[4], hg[5], cs["cmask"]], writes=[khz])
                kb.op("act", lambda e: e.copy(hgb[2][:], proj[:, C_HI:C_HI + 256]), reads=[proj], writes=[hgb[2]])
                for h in range(4):
                    kb.mm(lambda e, h=h: e.transpose(PT[1][0:64, h * 128:(h + 1) * 128], hgb[0][:, h * 64:(h + 1) * 64], identb[:]),
                          reads=[hgb[0], identb], writes=[PT[1]], last=False)
                for h in range(4):
                    kb.mm(lambda e, h=h: e.transpose(PT[1][0:64, 512 + h * 128:512 + (h + 1) * 128], hgb[1][:, h * 64:(h + 1) * 64], identb[:]),
                          reads=[hgb[1], identb], writes=[PT[1]], last=(h == 3))
                pq = PT[1][0:64, 0:512].rearrange("p (h t) -> p h t", h=4)
                kb.op("dve", lambda e: e.tensor_copy(qtT[:], pq), reads=[PT[1]], writes=[qtT])
                kb.op("dve", lambda e: e.tensor_copy(qtTz[:, :, 0, 0:64], pq[:, :, 0:64]), reads=[PT[1]], writes=[qtTz])
                kb.op("dve", lambda e: e.tensor_copy(qtTz[:, :, 1, 64:128], pq[:, :, 64:128]), reads=[PT[1]], writes=[qtTz])
                kb.op("dve", lambda e: e.tensor_copy(ktT[:], PT[1][0:64, 512:1024].rearrange("p (h t) -> p h t", h=4)), reads=[PT[1]], writes=[ktT])
                for h in range(4):
                    kb.mm(lambda e, h=h: e.matmul(PF[0][:, h * 128:(h + 1) * 128], ktT[:, h, :], qtT[:, h, :], start=True, stop=True),
                          reads=[ktT, qtT], writes=[PF[0]], last=(h == 3))
                kb.op("dve", lambda e: e.tensor_tensor(ATs[:], PF[0][:].rearrange("p (h t) -> p h t", h=4), V(cs["bdmask"], 0, [[0, 4], [1, 128]]), ALU.mult),
                      reads=[PF[0], cs["bdmask"]], writes=[ATs])
                kb.op("act", lambda e: e.copy(Sbf[0][:], Sst[0][:]), reads=[Sst[0]], writes=[Sbf[0]])
                for c in range(2):
                    for h in range(4):
                        kb.mm(lambda e, c=c, h=h: e.matmul(PF[4][0:64, 64 + h * 64:64 + (h + 1) * 64],
                                                           khz[:, c, h * 64:(h + 1) * 64], hgb[2][:, h * 64:(h + 1) * 64], start=True, stop=True),
                              reads=[khz, hgb[2]], writes=[PF[4]], last=(h == 3))
                    kb.op("dve", lambda e, c=c: e.tensor_tensor(Sst[2][:], Sst[c][:], V(eGl, c, [[2, 4], [0, 64]]), ALU.mult),
                          reads=[Sst[c], eGl], writes=[Sst[2]])
                    dstS = Sst[1] if c == 0 else Sst[0]
                    kb.op("dve", lambda e, dstS=dstS: e.tensor_tensor(dstS[:], Sst[2][:], PF[4][0:64, 64:320].rearrange("p (h d) -> p h d", h=4), ALU.add),
                          reads=[Sst[2], PF[4]], writes=[dstS])
                    if c == 0:
                        kb.op("act", lambda e: e.copy(Sbf[1][:], Sst[1][:]), reads=[Sst[1]], writes=[Sbf[1]])
                for h in range(4):
                    kb.mm(lambda e, h=h: e.matmul(PF[2][:, h * 64:(h + 1) * 64], ATs[:, h, :], hgb[2][:, h * 64:(h + 1) * 64],
                                                  start=True, stop=False), reads=[ATs, hgb[2]], writes=[PF[2]], last=False)
                    kb.mm(lambda e, h=h: e.matmul(PF[2][:, h * 64:(h + 1) * 64], qtTz[:, h, 0, :], Sbf[0][:, h, :], start=False, stop=False),
                          reads=[qtTz, Sbf[0]], writes=[PF[2]], last=False)
                    kb.mm(lambda e, h=h: e.matmul(PF[2][:, h * 64:(h + 1) * 64], qtTz[:, h, 1, :], Sbf[1][:, h, :], start=False, stop=True),
                          reads=[qtTz, Sbf[1]], writes=[PF[2]], last=(h == 3))
                if i == NT - 1:
                    kb.dma("sp", o_hg_p, o_hg_p[l, b].rearrange("h k v -> k h v"), Sst[0], Sst[0][:], home=Sst[0], is_output=True)
                kb.op("act", lambda e: e.copy(hg[0][:], PF[2][:, 0:256]), reads=[PF[2]], writes=[hg[0]])
                kb.op("dve", lambda e: e.tensor_tensor(hg[1][:], hg[0][:], hg[0][:], ALU.mult), reads=[hg[0]], writes=[hg[1]])
                kb.op("dve", lambda e: e.tensor_reduce(s4[0][:], hg[1][:].rearrange("p (h d) -> p h d", h=4), AX.X, ALU.add), reads=[hg[1]], writes=[s4[0]])
                r = rms_rstd(s4[0], 64, s4[1:4])
                kb.op("dve", lambda e: e.tensor_tensor(hg[5][:].rearrange("p (h d) -> p h d", h=4), hg[0][:].rearrange("p (h d) -> p h d", h=4),
                                                       V(r, 0, [[1, 4], [0, 64]]), ALU.mult), reads=[hg[0], r], writes=[hg[5]])
                kb.op(PENG, lambda e: e.tensor_tensor(ycat[:, 256:512], hg[5][:], hgon_b[:], ALU.mult), reads=[hg[5], hgon_b], writes=[ycat])

                if STAGE < 8:
                    return
                if STAGE < 9:
                    return
            if DBG and l == 0 and not smp:
                kb.dma("sp", dbg_ycat, dbg_ycat[b, i * 128:(i + 1) * 128, :], ycat, ycat[:], home=ycat, is_output=True)
            kb.op("act", lambda e: e.activation(sz[:], proj[:, C_Z:C_Z + D], AF.Silu), reads=[proj], writes=[sz])
            kb.op("dve", lambda e: e.tensor_tensor(yg[:], ycat[:], sz[:], ALU.mult), reads=[ycat, sz], writes=[yg])
            for kc in range(8):
                kb.mm(lambda e, kc=kc: e.transpose(PT[1][:, kc * 128:(kc + 1) * 128], yg[:, kc * 128:(kc + 1) * 128], identb[:]),
                      reads=[yg, identb], writes=[PT[1]], last=(kc == 7))
            kb.op("dve", lambda e: e.tensor_copy(yT[:], PT[1][:].rearrange("p (k t) -> p k t", k=8)), reads=[PT[1]], writes=[yT])
            for ncn in range(2):
                pb = PF[ncn]
                for kc in range(8):
                    kb.mm(lambda e, kc=kc, pb=pb, ncn=ncn: e.matmul(pb[:], yT[:, kc, :], W_out[:, kc, ncn * 512:(ncn + 1) * 512],
                                                                   start=(kc == 0), stop=(kc == 7)), reads=[yT, W_out], writes=[pb], last=(kc == 7))
                kb.op("dve", lambda e, pb=pb, ncn=ncn: e.tensor_tensor(xo[:, ncn * 512:(ncn + 1) * 512], pb[:], xin[:, ncn * 512:(ncn + 1) * 512], ALU.add),
                      reads=[pb, xin], writes=[xo])
            if smp:
                if l == DEPTH - 1:
                    kb.dma("sp", y_s, y_s[:], xo, xo[0:SB_, :], home=xo, is_output=True)
                else:
                    kb.dma("sp", xsmid, xsmid[:, :], xo, xo[0:SB_, :], home=xo)
            else:
                kb.dma("sp", dst, dst[b, i * 128:(i + 1) * 128, :], xo, xo[:], home=xo, is_output=(l == DEPTH - 1))

        def prompt_seq(l, b):
            kb.op("pool", lambda e: e.memset(V1[:], 1.0), writes=[V1])
            kb.op("pool", lambda e: e.memset(memV1[:], 1.0), writes=[memV1])
            kb.op("pool", lambda e: e.memset(CC[:], 0.0), writes=[CC])
            kb.op("pool", lambda e: e.memset(qtTz[:], 0.0), writes=[qtTz])
            kb.op("pool", lambda e: e.memset(Sst[0][:], 0.0), writes=[Sst[0]])
            kb.dma("pool", KTW, KTW[:].rearrange("p (kc n) -> p kc n", kc=8), w_mem_kv, w_mem_kv[l].rearrange("(kc p) n -> p kc n", p=128))
            for _rep in range(int(os.environ.get("KMEMREP", "1"))):
                prompt_mem(l, b)
            if STAGE < 2:
                return
            for i in range(min(NT, NTILES_DBG)):
                prompt_tile(l, b, i)

        if do_sample:
            sample_init()
        for l in range(DEPTH):
            load_weights(l)
            if STAGE < 1:
                break
            if do_sample:
                sample_idx(l)
                prompt_tile(l, 0, 0, smp=True)
            if not int(os.environ.get("KPROMPT", "1")):
                continue
            for b in range(PB):
                prompt_seq(l, b)
            if STAGE < 10:
                break
        kb.finish()
        print("instructions:", kb.n_inst, "dma sems:", kb.nd, flush=True)
    return nc, [k for k in di.keys() if k not in SKIP_IN]


_CACHE = {}


def kernel(x_prompt, x_sample, mem_prompt, cache_nsa, cache_nsa_win, state_hgrn, state_pool, cache_mem,
           page_table, norm_g, w_in, w_out, nsa_qn, nsa_kn, cmp_pe, cmp_w1, cmp_w2, hg_lb, hg_on,
           pool_w, pool_scale, mem_norm, w_mem_kv, mem_qn, mem_kn):
    f = lambda a: np.ascontiguousarray(np.asarray(a, dtype=np.float32))
    if "nc" not in _CACHE:
        _CACHE["nc"], _CACHE["names"] = build_program()
        _CACHE["consts"] = host_consts()
    nc = _CACHE["nc"]
    consts = _CACHE["consts"]
    x_prompt, x_sample, mem_prompt = f(x_prompt), f(x_sample), f(mem_prompt)
    cnsa_full = f(cache_nsa).reshape(DEPTH * NPHYS * 64, 512)
    cache_nsa_win, state_hgrn, state_pool, cache_mem = f(cache_nsa_win), f(state_hgrn), f(state_pool), f(cache_mem)
    pt = np.asarray(page_table, dtype=np.int32)
    shared = {
        "norm_g": f(norm_g), "w_in": f(w_in), "w_out": f(w_out), "nsa_qn": f(nsa_qn), "nsa_kn": f(nsa_kn),
        "cmp_pe": f(cmp_pe), "cmp_w1": f(cmp_w1), "cmp_w2": f(cmp_w2), "hg_lb": f(hg_lb), "hg_on": f(hg_on),
        "pool_w": f(pool_w), "pool_scale": f(pool_scale), "mem_norm": f(mem_norm), "w_mem_kv": f(w_mem_kv),
        "mem_qn": f(mem_qn), "mem_kn": f(mem_kn), "cnsa": cnsa_full,
    }
    for name, _, _ in CONST_SPECS:
        shared["c_" + name] = consts[name]
    in_maps = []
    for c in range(NCORES):
        ps = slice(PB * c, PB * (c + 1))
        ss = slice(SB_ * c, SB_ * (c + 1))
        m = dict(shared)
        m["xp"] = np.ascontiguousarray(x_prompt[ps])
        m["memp"] = np.ascontiguousarray(mem_prompt[ps])
        m["xs"] = np.ascontiguousarray(x_sample[ss, 0, :])
        m["cwin"] = np.ascontiguousarray(cache_nsa_win[:, ss].reshape(DEPTH, SB_, 512, 128))
        m["shg"] = np.ascontiguousarray(state_hgrn[:, ss])
        m["spool"] = np.ascontiguousarray(state_pool[:, ss])
        m["cmem"] = np.ascontiguousarray(cache_mem[:, ss].reshape(DEPTH, SB_, 256, 512))
        ptc = pt[ss]
        ptb = ptc.reshape(SB_, 64, 2).transpose(2, 0, 1)
        m["ptb"] = np.ascontiguousarray(np.repeat(ptb, 64, axis=0).astype(np.int32))
        in_maps.append(m)
    declared = set(_CACHE["names"])
    in_maps = [{k: v for k, v in m.items() if k in declared} for m in in_maps]
    ncores = int(os.environ.get("KCORES", str(NCORES)))
    res = run_bass_kernel_spmd(nc, in_maps[:ncores], core_ids=list(range(ncores)))
    R = list(res.results)
    while len(R) < NCORES:
        R.append(R[0])

    def cat(name, axis):
        return np.concatenate([np.asarray(r[name]) for r in R], axis=axis)

    y_prompt = cat("y_p", 0)
    y_sample = cat("y_s", 0).reshape(32, 1, D)
    new_nsa_p = cat("o_nsa_p", 1).reshape(DEPTH, 16, SEQ, 4, 1, 64)
    new_win_p = cat("o_win_p", 1).reshape(DEPTH, 16, 512, 2, 1, 64)
    new_hgrn_p = cat("o_hg_p", 1)
    new_pool_p = cat("o_pool_p", 1)
    new_mem_p = cat("o_mem_p", 1).reshape(DEPTH, 16, 256, 2, 4, 64)
    new_nsa_s = cat("o_nsa_s", 1).reshape(DEPTH, 32, 1, 4, 1, 64)
    new_win_s = cat("o_win_s", 1).reshape(DEPTH, 32, 512, 2, 1, 64)
    new_hgrn_s = cat("o_hg_s", 1)
    new_pool_s = cat("o_pool_s", 1)
    return (y_prompt, y_sample, new_nsa_p, new_win_p, new_hgrn_p, new_pool_p, new_mem_p,
            new_nsa_s, new_win_s, new_hgrn_s, new_pool_s)
```
